# Optimizing a Trainium2 kernel written in Bass

```python
import math
import jax
import jax.numpy as jnp
from jax import lax
import numpy as np

D_MODEL = 1024
BATCH = 4
SEQ = 4096
DEPTH = 2

S5_WIDTH = 512
S5_GROUP = 16
S5_GROUPS = S5_WIDTH // S5_GROUP
S5_STATE = 64
S5_DT_MIN = 1e-3
S5_DT_MAX = 1e-1
DA_HEADS = 4
DA_HEAD_DIM = 64
DA_V_DIM = 2 * DA_HEAD_DIM
DA_QK_WIDTH = DA_HEADS * 2 * DA_HEAD_DIM
DA_V_WIDTH = DA_HEADS * DA_V_DIM
DA_SUBLN_EPS = 1e-5
Q_BLOCK = 128
RW_HEAD = 64
RW_WIDTH = 512
RW_HEADS = RW_WIDTH // RW_HEAD
RW_DECAY_LORA = 32
RW_A_LORA = 32
RW_V_LORA = 32
RW_G_LORA = 96
RW_LN_EPS = 64e-5
RW_SHIFT_WIDTH = 3 * RW_WIDTH + RW_DECAY_LORA + RW_A_LORA + RW_G_LORA
N_BRANCH = 3
BRANCH_WIDTH = 512
IN_S5_END = S5_WIDTH
IN_Q_END = IN_S5_END + DA_QK_WIDTH
IN_K_END = IN_Q_END + DA_QK_WIDTH
IN_V_END = IN_K_END + DA_V_WIDTH
IN_RW_END = IN_V_END + RW_SHIFT_WIDTH
N_IN = IN_RW_END + N_BRANCH * D_MODEL
RW_R_END = RW_WIDTH
RW_K_END = 2 * RW_WIDTH
RW_V_END = 3 * RW_WIDTH
RW_W_END = RW_V_END + RW_DECAY_LORA
RW_A_END = RW_W_END + RW_A_LORA
D_FF = ((8 * D_MODEL + 3 * 256 - 1) // (3 * 256)) * 256
NORM_EPS = 1e-6

kernel_name = 'hybrid_s5_diffattn_rwkv7_gated'


def rms_norm(x, g, eps=NORM_EPS):
    xf = x.astype(jnp.float32)
    y = xf * lax.rsqrt(jnp.mean(xf * xf, axis=-1, keepdims=True) + eps)
    return (y * g.astype(jnp.float32)).astype(x.dtype)


def _complex_affine_combine(earlier, later):
    a1r, a1i, b1r, b1i = earlier
    a2r, a2i, b2r, b2i = later
    return (a2r * a1r - a2i * a1i,
            a2r * a1i + a2i * a1r,
            a2r * b1r - a2i * b1i + b2r,
            a2r * b1i + a2i * b1r + b2i)


def s5_mixer(u, lam_re, lam_im, log_dt, b_re, b_im, c_re, c_im, d, w_glu):
    bsz, seq, _ = u.shape
    f32 = jnp.float32
    uf = u.astype(f32)
    ug = uf.reshape(bsz, seq, S5_GROUPS, S5_GROUP)
    dt = jnp.exp(log_dt.astype(f32))[:, None]
    lr = lam_re.astype(f32)
    li = lam_im.astype(f32)
    mag = jnp.exp(lr * dt)
    ar = mag * jnp.cos(li * dt)
    ai = mag * jnp.sin(li * dt)
    den = lr * lr + li * li
    nr = ar - 1.0
    kr = (nr * lr + ai * li) / den
    ki = (ai * lr - nr * li) / den
    br = b_re.astype(f32)
    bi = b_im.astype(f32)
    bbar_r = kr[..., None] * br - ki[..., None] * bi
    bbar_i = kr[..., None] * bi + ki[..., None] * br
    bu_r = jnp.einsum('blgc,gpc->blgp', ug, bbar_r)
    bu_i = jnp.einsum('blgc,gpc->blgp', ug, bbar_i)
    a_r = jnp.broadcast_to(ar, (1, seq) + ar.shape)
    a_i = jnp.broadcast_to(ai, (1, seq) + ai.shape)
    _, _, xr, xi = lax.associative_scan(_complex_affine_combine, (a_r, a_i, bu_r, bu_i), axis=1)
    y = (jnp.einsum('blgp,gcp->blgc', xr, c_re.astype(f32))
         - jnp.einsum('blgp,gcp->blgc', xi, c_im.astype(f32)))
    y = y.reshape(bsz, seq, S5_WIDTH) + d.astype(f32) * uf
    y = jax.nn.gelu(y)
    y = y * jax.nn.sigmoid(y @ w_glu.astype(f32))
    return y.astype(u.dtype)


def diff_attention(q, k, v, q_gain, k_gain, lam_vecs, subln_g, lambda_init):
    bsz, seq, _ = q.shape
    f32 = jnp.float32
    q = rms_norm(q.reshape(bsz, seq, DA_HEADS, 2, DA_HEAD_DIM), q_gain)
    k = rms_norm(k.reshape(bsz, seq, DA_HEADS, 2, DA_HEAD_DIM), k_gain)
    qf = (q.astype(f32) * (DA_HEAD_DIM ** -0.5)).transpose(0, 2, 3, 1, 4)
    kf = k.astype(f32).transpose(0, 2, 3, 1, 4)
    vf = v.astype(f32).reshape(bsz, seq, DA_HEADS, DA_V_DIM).transpose(0, 2, 1, 3)
    lv = lam_vecs.astype(f32)
    lam = jnp.exp(jnp.sum(lv[0] * lv[1])) - jnp.exp(jnp.sum(lv[2] * lv[3])) + lambda_init
    n_blk = seq // Q_BLOCK
    qb = qf.reshape(bsz, DA_HEADS, 2, n_blk, Q_BLOCK, DA_HEAD_DIM).transpose(3, 0, 1, 2, 4, 5)
    kpos = jnp.arange(seq)

    def one_block(args):
        qi, bi = args
        s = jnp.einsum('bhcqd,bhckd->bhcqk', qi, kf)
        qpos = bi * Q_BLOCK + jnp.arange(Q_BLOCK)
        causal = kpos[None, :] <= qpos[:, None]
        p = jax.nn.softmax(jnp.where(causal, s, -jnp.inf), axis=-1)
        attn = p[:, :, 0] - lam * p[:, :, 1]
        return jnp.einsum('bhqk,bhkd->bhqd', attn, vf)

    o = lax.map(one_block, (qb, jnp.arange(n_blk)))
    o = o.transpose(1, 0, 3, 2, 4).reshape(bsz, seq, DA_HEADS, DA_V_DIM)
    o = rms_norm(o, subln_g, DA_SUBLN_EPS) * (1.0 - lambda_init)
    return o.reshape(bsz, seq, DA_V_WIDTH).astype(v.dtype)


def rwkv7_scan(r, w, k, v, a, b):
    bsz, seq, nh, n = r.shape
    xs = tuple(jnp.moveaxis(t, 1, 0) for t in (r, w, k, v, a, b))

    def step(state, inp):
        r_t, w_t, k_t, v_t, a_t, b_t = inp
        sa = jnp.einsum('bhvk,bhk->bhv', state, a_t)
        state = (state * w_t[:, :, None, :] + sa[..., None] * b_t[:, :, None, :]
                 + v_t[..., None] * k_t[:, :, None, :])
        return state, jnp.einsum('bhvk,bhk->bhv', state, r_t)

    s0 = jnp.zeros((bsz, nh, n, n), jnp.float32)
    _, ys = lax.scan(step, s0, xs)
    return jnp.moveaxis(ys, 0, 1)


def rwkv7_mixer(z, mu, w0, w2, a0, a2, g2, k_k, k_a, r_k, ln_g, ln_b, v_first, v_gate):
    bsz, seq, _ = z.shape
    f32 = jnp.float32
    zf = z.astype(f32)
    prev = jnp.pad(zf, ((0, 0), (1, 0), (0, 0)))[:, :-1, :]
    zf = zf + (prev - zf) * mu.astype(f32)
    r = zf[..., :RW_R_END]
    k = zf[..., RW_R_END:RW_K_END]
    v = zf[..., RW_K_END:RW_V_END]
    zw = zf[..., RW_V_END:RW_W_END]
    za = zf[..., RW_W_END:RW_A_END]
    zg = zf[..., RW_A_END:]
    w_log = -jax.nn.softplus(-(w0.astype(f32) + jnp.tanh(zw) @ w2.astype(f32))) - 0.5
    decay = jnp.exp(-jnp.exp(w_log))
    a = jax.nn.sigmoid(a0.astype(f32) + za @ a2.astype(f32))
    g = jax.nn.sigmoid(zg) @ g2.astype(f32)
    kk = (k * k_k.astype(f32)).reshape(bsz, seq, RW_HEADS, RW_HEAD)
    kk = kk / jnp.maximum(jnp.linalg.norm(kk, axis=-1, keepdims=True), 1e-12)
    kk = kk.reshape(bsz, seq, RW_WIDTH)
    k = k * (1.0 + (a - 1.0) * k_a.astype(f32))
    if v_gate is None:
        v_first = v
    else:
        v0, v1, v2 = v_gate
        mix = jax.nn.sigmoid(v0.astype(f32) + (v @ v1.astype(f32)) @ v2.astype(f32))
        v = v + (v_first - v) * mix
    hs = (bsz, seq, RW_HEADS, RW_HEAD)
    rh, kh, vh = r.reshape(hs), k.reshape(hs), v.reshape(hs)
    y = rwkv7_scan(rh, decay.reshape(hs), kh, vh, -kk.reshape(hs), (kk * a).reshape(hs))
    mean = jnp.mean(y, axis=-1, keepdims=True)
    var = jnp.mean(jnp.square(y - mean), axis=-1, keepdims=True)
    y = ((y - mean) * lax.rsqrt(var + RW_LN_EPS)).reshape(bsz, seq, RW_WIDTH)
    y = y * ln_g.astype(f32) + ln_b.astype(f32)
    bonus = jnp.sum(rh * kh * r_k.astype(f32), axis=-1, keepdims=True) * vh
    y = (y + bonus.reshape(bsz, seq, RW_WIDTH)) * g
    return y.astype(z.dtype), v_first


def setup_inputs(seed: int = 0) -> dict:
    key = jax.random.key(seed)
    ks = iter(jax.random.split(key, 48))
    f32 = jnp.float32

    def nrm(shape, scale):
        return scale * jax.random.normal(next(ks), shape, f32)

    G, P, C = S5_GROUPS, S5_STATE, S5_GROUP
    inp = {}
    inp['x'] = nrm((BATCH, SEQ, D_MODEL), 1.0)
    inp['norm1_g'] = 1.0 + nrm((DEPTH, D_MODEL), 0.02)
    inp['w_in'] = nrm((DEPTH, D_MODEL, N_IN), D_MODEL ** -0.5)
    inp['s5_lambda_re'] = -0.5 + nrm((DEPTH, G, P), 0.01)
    inp['s5_lambda_im'] = math.pi * jnp.arange(P, dtype=f32) + nrm((DEPTH, G, P), 0.01)
    inp['s5_log_dt'] = jax.random.uniform(next(ks), (DEPTH, G), f32, math.log(S5_DT_MIN), math.log(S5_DT_MAX))
    inp['s5_b_re'] = nrm((DEPTH, G, P, C), (2 * C) ** -0.5)
    inp['s5_b_im'] = nrm((DEPTH, G, P, C), (2 * C) ** -0.5)
    inp['s5_c_re'] = nrm((DEPTH, G, C, P), P ** -0.5)
    inp['s5_c_im'] = nrm((DEPTH, G, C, P), P ** -0.5)
    inp['s5_d'] = nrm((DEPTH, S5_WIDTH), 1.0)
    inp['s5_w_glu'] = nrm((DEPTH, S5_WIDTH, S5_WIDTH), S5_WIDTH ** -0.5)
    inp['da_q_gain'] = 1.0 + nrm((DEPTH, 2, DA_HEAD_DIM), 0.02)
    inp['da_k_gain'] = 1.0 + nrm((DEPTH, 2, DA_HEAD_DIM), 0.02)
    inp['da_lambda'] = nrm((DEPTH, 4, DA_HEAD_DIM), 0.1)
    inp['da_subln_g'] = 1.0 + nrm((DEPTH, DA_V_DIM), 0.02)
    inp['rw_mu'] = jax.random.uniform(next(ks), (DEPTH, RW_SHIFT_WIDTH), f32)
    inp['rw_w0'] = jnp.linspace(-6.5, -1.5, RW_WIDTH, dtype=f32) + nrm((DEPTH, RW_WIDTH), 0.1)
    inp['rw_w2'] = nrm((DEPTH, RW_DECAY_LORA, RW_WIDTH), 0.1)
    inp['rw_a0'] = nrm((DEPTH, RW_WIDTH), 0.1)
    inp['rw_a2'] = nrm((DEPTH, RW_A_LORA, RW_WIDTH), 0.1)
    inp['rw_g2'] = nrm((DEPTH, RW_G_LORA, RW_WIDTH), RW_G_LORA ** -0.5)
    inp['rw_k_k'] = 0.85 + nrm((DEPTH, RW_WIDTH), 0.02)
    inp['rw_k_a'] = 1.0 + nrm((DEPTH, RW_WIDTH), 0.02)
    inp['rw_r_k'] = nrm((DEPTH, RW_HEADS, RW_HEAD), 0.05)
    inp['rw_ln_g'] = 1.0 + nrm((DEPTH, RW_WIDTH), 0.02)
    inp['rw_ln_b'] = nrm((DEPTH, RW_WIDTH), 0.01)
    inp['rw_v0'] = 1.0 + nrm((DEPTH - 1, RW_WIDTH), 0.02)
    inp['rw_v1'] = nrm((DEPTH - 1, RW_WIDTH, RW_V_LORA), RW_WIDTH ** -0.5)
    inp['rw_v2'] = nrm((DEPTH - 1, RW_V_LORA, RW_WIDTH), 0.1)
    inp['w_branch'] = nrm((DEPTH, N_BRANCH, BRANCH_WIDTH, D_MODEL), BRANCH_WIDTH ** -0.5)
    inp['w_out'] = nrm((DEPTH, D_MODEL, D_MODEL), D_MODEL ** -0.5)
    inp['norm2_g'] = 1.0 + nrm((DEPTH, D_MODEL), 0.02)
    inp['w_ffn_in'] = nrm((DEPTH, D_MODEL, 2 * D_FF), D_MODEL ** -0.5)
    inp['w_ffn_out'] = nrm((DEPTH, D_FF, D_MODEL), D_FF ** -0.5)
    return inp


def reference(x, norm1_g, w_in, s5_lambda_re, s5_lambda_im, s5_log_dt, s5_b_re, s5_b_im,
              s5_c_re, s5_c_im, s5_d, s5_w_glu, da_q_gain, da_k_gain, da_lambda, da_subln_g,
              rw_mu, rw_w0, rw_w2, rw_a0, rw_a2, rw_g2, rw_k_k, rw_k_a, rw_r_k, rw_ln_g, rw_ln_b,
              rw_v0, rw_v1, rw_v2, w_branch, w_out, norm2_g, w_ffn_in, w_ffn_out):
    bsz, seq, _ = x.shape
    v_first = None
    for i in range(DEPTH):
        h = rms_norm(x, norm1_g[i])
        proj = h @ w_in[i]
        u_s5 = proj[..., :IN_S5_END]
        q = proj[..., IN_S5_END:IN_Q_END]
        k = proj[..., IN_Q_END:IN_K_END]
        v = proj[..., IN_K_END:IN_V_END]
        z_rw = proj[..., IN_V_END:IN_RW_END]
        gate_logits = proj[..., IN_RW_END:]
        y_a = s5_mixer(u_s5, s5_lambda_re[i], s5_lambda_im[i], s5_log_dt[i], s5_b_re[i], s5_b_im[i],
                       s5_c_re[i], s5_c_im[i], s5_d[i], s5_w_glu[i])
        lambda_init = 0.8 - 0.6 * math.exp(-0.3 * i)
        y_b = diff_attention(q, k, v, da_q_gain[i], da_k_gain[i], da_lambda[i], da_subln_g[i], lambda_init)
        v_gate = None if i == 0 else (rw_v0[i - 1], rw_v1[i - 1], rw_v2[i - 1])
        y_c, v_first = rwkv7_mixer(z_rw, rw_mu[i], rw_w0[i], rw_w2[i], rw_a0[i], rw_a2[i], rw_g2[i],
                                   rw_k_k[i], rw_k_a[i], rw_r_k[i], rw_ln_g[i], rw_ln_b[i], v_first, v_gate)
        branches = jnp.stack([y_a, y_b, y_c.astype(y_a.dtype)], axis=2)
        branch_out = jnp.einsum('blnc,ncd->blnd', branches, w_branch[i])
        gates = jax.nn.sigmoid(gate_logits.reshape(bsz, seq, N_BRANCH, D_MODEL))
        merged = jnp.sum(gates * branch_out, axis=2)
        x = x + (merged @ w_out[i]).astype(x.dtype)
        h2 = rms_norm(x, norm2_g[i])
        gu = h2 @ w_ffn_in[i]
        x = x + ((jax.nn.silu(gu[..., :D_FF]) * gu[..., D_FF:]) @ w_ffn_out[i]).astype(x.dtype)
    return x
```

```python
from contextlib import ExitStack
import numpy as np
import concourse.bass as bass
import concourse.mybir as mybir

F32 = mybir.dt.float32
BF16 = mybir.dt.bfloat16
ALU = mybir.AluOpType
AF = mybir.ActivationFunctionType

ENGS = ["pe", "act", "dve", "pool", "sp"]


class Shared:
    NDQ = 94

    def __init__(self, nc):
        self.nc = nc
        self.stack = ExitStack()
        names = ["e_" + e for e in ENGS] + ["dq%d" % i for i in range(self.NDQ)]
        self.sems = {n: self.stack.enter_context(nc.semaphore(n)) for n in names}
        self.cnt = {n: 0 for n in names}

    def close(self):
        self.stack.close()


class V:
    def __init__(self, ap):
        self._ap = ap

    def ap(self):
        return self._ap

    def __getitem__(self, idx):
        return self._ap[idx]


class Prog:
    _pid = [0]

    def __init__(self, nc, shared=None):
        self.shared = shared
        Prog._pid[0] += 1
        self.pfx = "p%d_" % Prog._pid[0]
        self.nc = nc
        self.ins = []
        self.byeng = {e: [] for e in ENGS}
        self.lastw = {}
        self.rds = {}
        self.stack = ExitStack()
        self.n_ps = 0

    def sb(self, name, shape, dt=F32):
        return self.stack.enter_context(self.nc.sbuf_tensor(self.pfx + name, list(shape), dt))

    def ps(self, name, shape, dt=F32):
        return self.stack.enter_context(self.nc.psum_tensor(self.pfx + name, list(shape), dt))

    def dram(self, name, shape, dt=F32, kind="Internal"):
        return self.nc.dram_tensor(name, list(shape), dt, kind=kind)

    @staticmethod
    def _ov(a, b):
        return a is None or b is None or a == b

    def _norm(self, keys):
        out = []
        for k in keys:
            if isinstance(k, tuple):
                out.append((k[0], None if k[0].startswith("ps") else k[1]))
            else:
                out.append((k, None))
        return out

    def op(self, eng, fn, reads=(), writes=(), dma=False, sk=None):
        reads = self._norm(reads)
        writes = self._norm(writes)
        i = len(self.ins)
        deps = set()
        for (t, s) in reads:
            for s2, w in self.lastw.get(t, {}).items():
                if self._ov(s, s2):
                    deps.add(w)
        for (t, s) in writes:
            for s2, w in self.lastw.get(t, {}).items():
                if self._ov(s, s2):
                    deps.add(w)
            for s2, rl in self.rds.get(t, {}).items():
                if self._ov(s, s2):
                    deps.update(rl)
        deps.discard(i)
        for (t, s) in reads:
            self.rds.setdefault(t, {}).setdefault(s, []).append(i)
        for (t, s) in writes:
            d = self.lastw.setdefault(t, {})
            r = self.rds.setdefault(t, {})
            if s is None:
                d.clear()
                r.clear()
            else:
                r[s] = []
            d[s] = i
        if eng == "pe":
            deps = {d for d in deps if self.ins[d]["eng"] != "pe" or self.ins[d]["dma"]}
        self.ins.append(dict(eng=eng, fn=fn, deps=deps, dma=dma, key=(writes[0] if writes else None),
                             sk=(sk if sk is not None else (writes[0] if writes else None))))
        self.byeng[eng].append(i)
        return i

    def mm(self, out, lhsT, rhs, reads, writes, start=True, stop=True):
        return self.op("pe", lambda e: e.matmul(out, lhsT, rhs, start=start, stop=stop), reads, writes)

    def tr(self, out, in_, ident, reads, writes):
        return self.op("pe", lambda e: e.transpose(out, in_, ident), reads, writes)

    def act(self, out, in_, func, reads, writes, eng="act", **kw):
        return self.op(eng, lambda e: e.activation(out, in_, func, **kw), reads, writes)

    def tt(self, eng, out, in0, in1, op, reads, writes):
        return self.op(eng, lambda e: e.tensor_tensor(out, in0, in1, op), reads, writes)

    def ts(self, eng, out, in0, s1, s2, op0, op1, reads, writes, **kw):
        if op1 is None:
            return self.op(eng, lambda e: e.tensor_scalar(out, in0, s1, None, op0, **kw), reads, writes)
        return self.op(eng, lambda e: e.tensor_scalar(out, in0, s1, s2, op0, op1, **kw), reads, writes)

    def stt(self, out, in0, scalar, in1, op0, op1, reads, writes):
        return self.op("dve", lambda e: e.scalar_tensor_tensor(out, in0, scalar, in1, op0, op1), reads, writes)

    def cp(self, eng, out, in_, reads, writes):
        if eng == "act":
            return self.op(eng, lambda e: e.copy(out, in_), reads, writes)
        return self.op(eng, lambda e: e.tensor_copy(out, in_), reads, writes)

    def dma(self, out, in_, reads, writes, eng="sp", sk=None, **kw):
        return self.op(eng, lambda e: e.dma_start(out, in_, **kw), reads, writes, dma=True, sk=sk)

    def memset(self, eng, ap, val, writes):
        return self.op(eng, lambda e: e.memset(ap, val), (), writes)

    def emit(self, final_wait_keys=()):
        nc = self.nc
        ins = self.ins
        sh = self.shared
        own = sh is None
        if own:
            sh = Shared(nc)
        has_dep = [False] * len(ins)
        for i, d in enumerate(ins):
            for j in d["deps"]:
                has_dep[j] = True
        final_ids = [i for i, d in enumerate(ins) if d["dma"] and d["key"] is not None and d["key"][0] in final_wait_keys]
        for i, d in enumerate(ins):
            if d["dma"]:
                has_dep[i] = True
        for e in ENGS:
            for i in reversed(self.byeng[e]):
                if not ins[i]["dma"]:
                    has_dep[i] = True
                    break
        start = dict(sh.cnt)
        dma_sem_of = {}
        dma_eng = {}
        nsp, npool = [0], [0]
        sig = [None] * len(ins)
        for e in ENGS:
            for i in self.byeng[e]:
                d = ins[i]
                if not has_dep[i]:
                    continue
                if d["dma"]:
                    k = d["sk"]
                    if k not in dma_sem_of:
                        if e == "pool":
                            npool[0] += 1
                            assert npool[0] <= 34, "too many pool dma sem keys"
                            dma_sem_of[k] = "dq%d" % (Shared.NDQ - npool[0])
                        else:
                            nsp[0] += 1
                            assert nsp[0] <= Shared.NDQ - 34, "too many sp dma sem keys"
                            dma_sem_of[k] = "dq%d" % (nsp[0] - 1)
                        dma_eng[k] = e
                    assert dma_eng[k] == e, ("dma sem key used from two queues", k)
                    sn = dma_sem_of[k]
                    sh.cnt[sn] += 16
                    sig[i] = (sn, sh.cnt[sn])
                else:
                    sh.cnt["e_" + e] += 1
                    sig[i] = ("e_" + e, sh.cnt["e_" + e])
        sems = sh.sems
        block = self.stack.enter_context(nc.Block())
        engobj = dict(pe="tensor", act="scalar", dve="vector", pool="gpsimd", sp="sync")
        last_phase_final = [(sig[i][0], sig[i][1]) for i in final_ids]

        def make(e):
            def body(eng):
                waited = {}
                for sn, v in start.items():
                    if v > 0 and sn != "e_" + e:
                        eng.wait_ge(sems[sn], v)
                    waited[sn] = v
                for i in self.byeng[e]:
                    d = ins[i]
                    need = {}
                    for j in d["deps"]:
                        s_ = sig[j]
                        assert s_ is not None
                        if need.get(s_[0], 0) < s_[1]:
                            need[s_[0]] = s_[1]
                    for sn, v in need.items():
                        if waited.get(sn, 0) < v:
                            eng.wait_ge(sems[sn], v)
                            waited[sn] = v
                    inst = d["fn"](eng)
                    if sig[i] is not None:
                        inst.then_inc(sems[sig[i][0]], 16 if d["dma"] else 1)
                if e == "sp":
                    for (sn, v) in last_phase_final:
                        if waited.get(sn, 0) < v:
                            eng.wait_ge(sems[sn], v)
                            waited[sn] = v
            return body

        for e in ENGS:
            getattr(block, engobj[e])(make(e))
        self.stack.close()
        if own:
            sh.close()
        return nc


from concourse.bass_utils import run_bass_kernel_spmd

NCORES = 8
D = 1024
NIN = 6816
EPS = 1e-6


def _ctx(ctx):
    if ctx is None:
        nc = bass.Bass("TRN2", target_bir_lowering=False)
        return nc, Prog(nc), None
    nc, shared, io = ctx
    return nc, Prog(nc, shared), io


def _mkI(nc, io):
    if io is None:
        return lambda n, s: nc.dram_tensor(n, list(s), F32, kind="ExternalInput")
    return lambda n, s: io[n]


def _mkO(nc, io, n, s):
    if io is None:
        return nc.dram_tensor(n, list(s), F32, kind="ExternalOutput")
    return io[n]


def _fin(P, nc, io, outs):
    if io is None:
        P.emit(final_wait_keys=outs)
    else:
        P.emit(final_wait_keys=(outs if io.get("_final") else ()))
    return nc


def ap3(t, off, dims):
    return bass.AP(t, off, [list(d) for d in dims])


def build_L1(TOK=2048):
    nc = bass.Bass("TRN2", target_bir_lowering=False)
    P = Prog(nc)
    NT = TOK // 128
    x_tok = nc.dram_tensor("x_tok", [TOK, D], F32, kind="ExternalInput")
    xT = nc.dram_tensor("xT", [D, TOK], F32, kind="ExternalInput")
    w = nc.dram_tensor("w", [D, NIN], F32, kind="ExternalInput")
    g = nc.dram_tensor("g", [128, 8], F32, kind="ExternalInput")
    proj = nc.dram_tensor("proj", [TOK, NIN], F32, kind="ExternalOutput")

    g_sb = P.sb("g_sb", [128, 8])
    rstd = P.sb("rstd", [128, NT])
    ss = P.sb("ss", [128, NT])
    xt_in = [P.sb("xt_in%d" % i, [128, D]) for i in range(2)]
    junk = P.sb("junk", [128, D])
    xTs = P.sb("xTs", [128, 8, TOK])
    xg = P.sb("xg", [128, 8, TOK], BF16)
    wf = [P.sb("wf%d" % i, [128, 8, 512]) for i in range(2)]
    wb = [P.sb("wb%d" % i, [128, 8, 512], BF16) for i in range(2)]
    ost = [P.sb("ost%d" % i, [128, 512]) for i in range(3)]
    pss = [P.ps("ps%d" % i, [128, 512]) for i in range(4)]

    P.dma(g_sb[:, :], g[:, :], [], ["g_sb"])
    for t in range(NT):
        b = xt_in[t % 2]
        bn = "xt_in%d" % (t % 2)
        P.dma(b[:, :], x_tok[t * 128:(t + 1) * 128, :], [], [bn], eng="sp")
        P.act(junk[:, :], b[:, :], AF.Square, [bn], ["junk", ("ss", t)], accum_out=ss[:, t:t + 1])
    P.ts("dve", rstd[:, :], ss[:, :], 1.0 / D, EPS, ALU.mult, ALU.add, ["ss"], ["rstd"])
    P.act(rstd[:, :], rstd[:, :], AF.Sqrt, ["rstd"], ["rstd"])
    P.op("dve", lambda e: e.reciprocal(rstd[:, :], rstd[:, :]), ["rstd"], ["rstd"])
    xTv = xT.ap().rearrange("(c p) t -> p c t", p=128)
    for c in range(8):
        P.dma(xTs[:, c, :], xTv[:, c, :], [], [("xTs", c)], eng="pool")
        P.ts("dve", xg[:, c, :], xTs[:, c, :], g_sb[:, c:c + 1], None, ALU.mult, None,
             [("xTs", c), "g_sb"], [("xg", c)])
    wv = w.ap().rearrange("(c p) n -> p c n", p=128)
    nblk = (NIN + 511) // 512
    k = 0
    for cb in range(nblk):
        c0 = cb * 512
        cw = min(512, NIN - c0)
        wfb, wbb = wf[cb % 2], wb[cb % 2]
        wfn, wbn = "wf%d" % (cb % 2), "wb%d" % (cb % 2)
        P.dma(wfb[:, :, :cw], wv[:, :, c0:c0 + cw], [], [wfn], eng="sp")
        for c in range(8):
            P.cp("pool" if c % 2 else "dve", wbb[:, c, :cw], wfb[:, c, :cw], [wfn], [(wbn, c)])
        for t in range(NT):
            ps = pss[k % 4]
            psn = "ps%d" % (k % 4)
            for c in range(8):
                P.mm(ps[:, :cw], xg[:, c, t * 128:(t + 1) * 128], wbb[:, c, :cw],
                     [("xg", c), (wbn, c)], [psn], start=(c == 0), stop=(c == 7))
            o = ost[k % 3]
            on = "ost%d" % (k % 3)
            P.act(o[:, :cw], ps[:, :cw], AF.Copy, [psn, "rstd"], [on], scale=rstd[:, t:t + 1])
            P.dma(proj[t * 128:(t + 1) * 128, c0:c0 + cw], o[:, :cw], [on], [("proj", k)], eng="sp", sk=on)
            k += 1
    P.emit(final_wait_keys=("proj",))
    return nc


def build_L1F(L=4096, ctx=None):
    nc, P, io = _ctx(ctx)
    I = _mkI(nc, io)
    x_tok = I("x_tok", [L, D])
    w = I("w", [D, NIN])
    g = I("g", [128, 8])
    c_ident = I("c_ident", [128, 128])
    FT = _mkO(nc, io, "FT", [6304, L])
    VT = _mkO(nc, io, "VT", [L, 512])
    HT = min(L, 1024)
    NH = L // HT
    NTT = HT // 128
    NB = L // 512
    g_sb = P.sb("g_sb", [128, 8])
    P.dma(g_sb[:, :], g[:, :], [], ["g_sb"])
    identf = P.sb("identf", [128, 128])
    P.dma(identf[:, :], c_ident[:, :], [], ["identf"])
    identb = P.sb("identb", [128, 128], BF16)
    P.cp("dve", identb[:, :], identf[:, :], ["identf"], ["identb"])
    hT = P.sb("hT", [128, 8, L], BF16)
    x1 = [P.sb("x1_%d" % i, [128, D]) for i in range(3)]
    junk = P.sb("junk", [128, D], BF16)
    ss = P.sb("ss", [128, NTT])
    hb = [P.sb("hb%d" % i, [128, D], BF16) for i in range(2)]
    wf = [P.sb("wf%d" % i, [128, 8, 512]) for i in range(2)]
    wb = [P.sb("wb%d" % i, [128, 8, 512], BF16) for i in range(2)]
    ost = [P.sb("ost%d" % i, [128, 512]) for i in range(4)]
    pss = [P.ps("ps%d" % i, [128, 512]) for i in range(6)]
    psb = P.ps("psb", [128, 1024], BF16)
    for half in range(NH):
        t0 = half * HT
        xs_ = []
        for tt_ in range(NTT):
            xi, xn = x1[tt_ % 3], "x1_%d" % (tt_ % 3)
            P.dma(xi[:, :], x_tok[t0 + tt_ * 128:t0 + (tt_ + 1) * 128, :], [], [xn], eng="sp")
            P.act(junk[:, :], xi[:, :], AF.Square, [xn], ["junk", ("ss", tt_)], accum_out=ss[:, tt_:tt_ + 1])
        P.ts("dve", ss[:, :], ss[:, :], 1.0 / D, EPS, ALU.mult, ALU.add, ["ss"], ["ss"])
        P.act(ss[:, :], ss[:, :], AF.Sqrt, ["ss"], ["ss"])
        P.op("dve", lambda e: e.reciprocal(ss[:, :], ss[:, :]), ["ss"], ["ss"])
        for tt_ in range(NTT):
            xi, xn = x1[tt_ % 3], "x1_%d" % (tt_ % 3)
            P.dma(xi[:, :], x_tok[t0 + tt_ * 128:t0 + (tt_ + 1) * 128, :], [], [xn], eng="sp")
            h_, hn = hb[tt_ % 2], "hb%d" % (tt_ % 2)
            P.act(h_[:, :], xi[:, :], AF.Copy, [xn, "ss"], [hn], scale=ss[:, tt_:tt_ + 1])
            for c in range(8):
                P.tr(psb[:, c * 128:(c + 1) * 128], h_[:, c * 128:(c + 1) * 128], identb[:, :], [hn, "identb"], ["psb"])
            for c in range(8):
                P.ts("dve", hT[:, c, t0 + tt_ * 128:t0 + (tt_ + 1) * 128], psb[:, c * 128:(c + 1) * 128], g_sb[:, c:c + 1], None, ALU.mult, None,
                     ["psb", "g_sb"], [("hT", c)])
    wv = w.ap().rearrange("(c p) n -> p c n", p=128)
    k = 0
    nblk = (NIN + 511) // 512
    for cb in range(nblk):
        c0 = cb * 512
        cw = min(512, NIN - c0)
        wfb, wbb = wf[cb % 2], wb[cb % 2]
        wfn, wbn = "wf%d" % (cb % 2), "wb%d" % (cb % 2)
        P.dma(wfb[:, :, :cw], wv[:, :, c0:c0 + cw], [], [wfn], eng=("sp" if cb % 2 else "pool"))
        for c in range(8):
            P.cp("pool" if c % 2 else "dve", wbb[:, c, :cw], wfb[:, c, :cw], [wfn], [(wbn, c)])
        if c0 == 1536:
            for t in range(L // 128):
                ps, psn = pss[k % 6], "ps%d" % (k % 6)
                for c in range(8):
                    P.mm(ps[:, :], hT[:, c, t * 128:(t + 1) * 128], wbb[:, c, :], [("hT", c), (wbn, c)], [psn], start=(c == 0), stop=(c == 7))
                o, on = ost[k % 4], "ost%d" % (k % 4)
                P.cp("act" if k % 2 else "dve", o[:, :], ps[:, :], [psn], [on])
                P.dma(VT[t * 128:(t + 1) * 128, :], o[:, :], [on], [("VT", t)], sk=on, eng=("sp" if k % 2 else "pool"))
                k += 1
            continue
        r0 = c0 if c0 < 1536 else c0 - 512
        for ct in range((cw + 127) // 128):
            mw = min(128, cw - ct * 128)
            for b5 in range(NB):
                bs = slice(b5 * 512, b5 * 512 + 512)
                ps, psn = pss[k % 6], "ps%d" % (k % 6)
                for c in range(8):
                    P.mm(ps[:mw, :], wbb[:, c, ct * 128:ct * 128 + mw], hT[:, c, bs], [("hT", c), (wbn, c)], [psn], start=(c == 0), stop=(c == 7))
                o, on = ost[k % 4], "ost%d" % (k % 4)
                P.cp("act" if k % 2 else "dve", o[:mw, :], ps[:mw, :], [psn], [on])
                P.dma(FT[r0 + ct * 128:r0 + ct * 128 + mw, bs], o[:mw, :], [on], [("FT", k)], sk=on, eng=("sp" if k % 2 else "pool"))
                k += 1
    return _fin(P, nc, io, ("FT", "VT"))


def rw_consts(L):
    T = 64
    i = np.arange(128)
    same = (i[:, None] // T) == (i[None, :] // T)
    su = (same & (i[:, None] < i[None, :])).astype(np.float32)
    iu = (same & (i[:, None] <= i[None, :])).astype(np.float32)
    sl = (same & (i[:, None] > i[None, :])).astype(np.float32)
    mask4 = np.concatenate([su, iu, su, iu], 1)
    reset = np.ones((128, min(L, 1024)), np.float32)
    reset[:, ::T] = 0.0
    bones = ((i[:, None] // 64) == (i[None, :] // 64)).astype(np.float32)
    hsel = np.zeros((128, 2), np.float32)
    hsel[:64, 0] = 1
    hsel[64:, 1] = 1
    return dict(c_mask4=mask4, c_sl=sl, c_reset=reset, c_bones=bones, c_hsel=hsel,
                c_ident=np.eye(128, dtype=np.float32))


def build_L4(L=4096, has_vgate=False, LH=None, ctx=None, fm_out=False):
    nc, P, io = _ctx(ctx)
    NB = L // 128
    NCH = L // 64
    NCB = (L + 511) // 512
    I = _mkI(nc, io)
    zr, zk, zv = I("zr", [256, L]), I("zk", [256, L]), I("zv", [256, L])
    zw, za, zg = I("zw", [32, L]), I("za", [32, L]), I("zg", [96, L])
    mu3 = I("mu3", [128, 6])
    mulo = I("mulo", [96, 3])
    w2, a2, g2 = I("w2", [32, 256]), I("a2", [32, 256]), I("g2", [96, 256])
    pc = I("pc", [128, 10])
    lng, lnb = I("lng", [128, 256]), I("lnb", [128, 256])
    c_mask4, c_sl, c_reset = I("c_mask4", [128, 512]), I("c_sl", [128, 128]), I("c_reset", [128, min(L, 1024)])
    c_bones, c_hsel, c_ident = I("c_bones", [128, 128]), I("c_hsel", [128, 2]), I("c_ident", [128, 128])
    if has_vgate:
        zvfull = I("zvfull", [512, L])
        muvf = I("muvf", [128, 4])
        v1 = I("v1", [128, 4, 32])
        v2 = I("v2", [32, 256])
        v0 = I("v0", [128, 2])
        vfirst_in = I("vfirst_in", [256, L])
    y_out = _mkO(nc, io, "y_out", ([256, L] if fm_out else [L, 256]))
    vfirst_out = _mkO(nc, io, "vfirst_out", [256, L])

    def ld(name, src, shape, dt=F32, eng="sp"):
        t = P.sb(name, shape, dt)
        if len(shape) == 2:
            P.dma(t[:, :], src[:, :], [], [name], eng=eng)
        else:
            P.dma(t[:, :, :], src[:, :, :], [], [name], eng=eng)
        return t
    mu3s, mulos = ld("mu3s", mu3, [128, 6]), ld("mulos", mulo, [96, 3])
    pcs = ld("pcs", pc, [128, 10])
    lngs, lnbs = ld("lngs", lng, [128, 256]), ld("lnbs", lnb, [128, 256])
    mask4, msl, reset = ld("mask4", c_mask4, [128, 512]), ld("msl", c_sl, [128, 128]), ld("reset", c_reset, [128, min(L, 1024)])
    bones, hself, identf = ld("bones", c_bones, [128, 128]), ld("hself", c_hsel, [128, 2]), ld("identf", c_ident, [128, 128])
    w2f, a2f, g2f = ld("w2f", w2, [32, 256]), ld("a2f", a2, [32, 256]), ld("g2f", g2, [96, 256])
    identb = P.sb("identb", [128, 128], BF16)
    hselb = P.sb("hselb", [128, 2], BF16)
    w2b, a2b, g2b = P.sb("w2b", [32, 256], BF16), P.sb("a2b", [32, 256], BF16), P.sb("g2b", [96, 256], BF16)
    P.cp("dve", identb[:, :], identf[:, :], ["identf"], ["identb"])
    P.cp("dve", hselb[:, :], hself[:, :], ["hself"], ["hselb"])
    P.cp("dve", w2b[:, :], w2f[:, :], ["w2f"], ["w2b"])
    P.cp("dve", a2b[:, :], a2f[:, :], ["a2f"], ["a2b"])
    P.cp("dve", g2b[:, :], g2f[:, :], ["g2f"], ["g2b"])

    LH = LH or min(L, 1024)
    NQH = L // LH
    NCBH = (LH + 511) // 512
    S = [P.sb("S%d" % i, [128, LH + 1]) for i in range(5)]
    SN = ["S%d" % i for i in range(5)]
    AT, BT, KT, RT, VB, RK = [P.sb(n, [128, 2, L], BF16) for n in ("AT", "BT", "KT", "RT", "VB", "RK")]
    gam = P.sb("gam", [128, 2, NCH])
    lo_w, lo_a, lo_g = P.sb("lo_w", [32, LH], BF16), P.sb("lo_a", [32, LH], BF16), P.sb("lo_g", [96, L], BF16)
    resetb = P.sb("resetb", [128, LH], BF16)
    P.cp("dve", resetb[:, :], reset[:, 0:LH], ["reset"], ["resetb"])
    pss = [P.ps("ps%d" % i, [128, 512]) for i in range(7)]
    psb = P.ps("psb", [128, 1024], BF16)
    pk = [0]

    def nextps():
        k = pk[0] % 4
        pk[0] += 1
        return pss[k], "ps%d" % k

    def shift_mix(dst, dn, src_rows, rows, mucol, scratch, scn, raw, rawn, t0, eng="sp"):
        if t0 == 0:
            P.memset("pool", raw[:rows, 0:1], 0.0, [(rawn, 0)])
            P.dma(raw[:rows, 1:LH + 1], src_rows[:, 0:LH], [], [(rawn, 1)], eng=eng)
        else:
            P.dma(raw[:rows, 0:LH + 1], src_rows[:, t0 - 1:t0 + LH], [], [rawn], eng=eng)
        P.tt("dve", scratch[:rows, 0:LH], raw[:rows, 0:LH], raw[:rows, 1:LH + 1], ALU.subtract, [rawn], [scn])
        P.stt(dst[:rows, 0:LH], scratch[:rows, 0:LH], mucol, raw[:rows, 1:LH + 1], ALU.mult, ALU.add, [scn, rawn, "mu3s", "mulos", "muvfs"], [dn])

    if has_vgate:
        v1f = ld("v1f", v1, [128, 4, 32])
        v1b = P.sb("v1b", [128, 4, 32], BF16)
        P.cp("dve", v1b[:, :, :], v1f[:, :, :], ["v1f"], ["v1b"])
        v2f = ld("v2f", v2, [32, 256])
        v2b = P.sb("v2b", [32, 256], BF16)
        P.cp("dve", v2b[:, :], v2f[:, :], ["v2f"], ["v2b"])
        v0s = ld("v0s", v0, [128, 2])
        muvfs = ld("muvfs", muvf, [128, 4])
        vfb = P.sb("vfb", [128, 4, LH], BF16)
        t1b = P.sb("t1b", [32, LH], BF16)

    W = slice(0, LH)
    for hq in range(NQH):
        t0 = hq * LH
        TS = slice(t0, t0 + LH)
        shift_mix(S[2], SN[2], zw, 32, mulos[:32, 0:1], S[1], SN[1], S[0], SN[0], t0)
        P.act(lo_w[:, :], S[2][:32, W], AF.Tanh, [SN[2]], ["lo_w"])
        shift_mix(S[2], SN[2], za, 32, mulos[:32, 1:2], S[1], SN[1], S[0], SN[0], t0)
        P.cp("dve", lo_a[:, :], S[2][:32, W], [SN[2]], ["lo_a"])
        shift_mix(S[2], SN[2], zg, 96, mulos[:96, 2:3], S[1], SN[1], S[0], SN[0], t0)
        P.act(lo_g[:, TS], S[2][:96, W], AF.Sigmoid, [SN[2]], [("lo_g", hq)])
        if has_vgate:
            for c in range(4):
                shift_mix(S[2], SN[2], zvfull[c * 128:(c + 1) * 128, :], 128, muvfs[:, c:c + 1], S[1], SN[1], S[0], SN[0], t0)
                P.cp("pool", vfb[:, c, :], S[2][:, W], [SN[2]], [("vfb", c)])
            for cb in range(NCBH):
                c0 = cb * 512
                cw = min(512, LH - c0)
                ps, psn = nextps()
                for c in range(4):
                    P.mm(ps[:32, :cw], v1b[:, c, :], vfb[:, c, c0:c0 + cw], ["v1b", ("vfb", c)], [psn], start=(c == 0), stop=(c == 3))
                P.cp("act", t1b[:, c0:c0 + cw], ps[:32, :cw], [psn], [("t1b", cb)])
        for p in range(2):
            rows = slice(p * 128, (p + 1) * 128)
            col = lambda j: pcs[:, 2 * j + p:2 * j + p + 1]
            shift_mix(S[2], SN[2], zv[rows, :], 128, mu3s[:, 4 + p:5 + p], S[1], SN[1], S[0], SN[0], t0)
            if has_vgate:
                for cb in range(NCBH):
                    c0 = cb * 512
                    cw = min(512, LH - c0)
                    ps, psn = nextps()
                    P.mm(ps[:, :cw], v2b[:, p * 128:(p + 1) * 128], t1b[:, c0:c0 + cw], ["v2b", ("t1b", cb)], [psn])
                    P.act(S[3][:, c0:c0 + cw], ps[:, :cw], AF.Sigmoid, [psn, "v0s"], [(SN[3], cb)], bias=v0s[:, p:p + 1])
                P.dma(S[0][:, W], vfirst_in[rows, TS], [], [SN[0]])
                P.tt("dve", S[1][:, W], S[0][:, W], S[2][:, W], ALU.subtract, [SN[0], SN[2]], [SN[1]])
                P.tt("dve", S[1][:, W], S[1][:, W], S[3][:, W], ALU.mult, [SN[1], SN[3]], [SN[1]])
                P.tt("dve", S[2][:, W], S[2][:, W], S[1][:, W], ALU.add, [SN[2], SN[1]], [SN[2]])
                P.dma(vfirst_out[rows, TS], S[0][:, W], [SN[0]], [("vfirst_out", hq * 2 + p)], sk="vfo")
            else:
                P.dma(vfirst_out[rows, TS], S[2][:, W], [SN[2]], [("vfirst_out", hq * 2 + p)], sk="vfo")
            P.cp("pool", VB[:, p, TS], S[2][:, W], [SN[2]], [("VB", p)])
            for cb in range(NCBH):
                c0 = cb * 512
                cw = min(512, LH - c0)
                ps, psn = nextps()
                P.mm(ps[:, :cw], a2b[:, p * 128:(p + 1) * 128], lo_a[:, c0:c0 + cw], ["a2b", "lo_a"], [psn])
                P.act(S[3][:, c0:c0 + cw], ps[:, :cw], AF.Sigmoid, [psn, "pcs"], [(SN[3], cb)], bias=col(1))
                ps, psn = nextps()
                P.mm(ps[:, :cw], w2b[:, p * 128:(p + 1) * 128], lo_w[:, c0:c0 + cw], ["w2b", "lo_w"], [psn])
                P.act(S[4][:, c0:c0 + cw], ps[:, :cw], AF.Sigmoid, [psn, "pcs"], [(SN[4], cb)], bias=col(0))
            P.ts("dve", S[4][:, W], S[4][:, W], -float(np.exp(-0.5)), None, ALU.mult, None, [SN[4]], [SN[4]])
            shift_mix(S[2], SN[2], zk[rows, :], 128, mu3s[:, 2 + p:3 + p], S[1], SN[1], S[0], SN[0], t0)
            P.ts("dve", S[0][:, W], S[2][:, W], col(2), None, ALU.mult, None, [SN[2], "pcs"], [SN[0]])
            P.tt("pool", S[1][:, W], S[0][:, W], S[0][:, W], ALU.mult, [SN[0]], [SN[1]])
            for cb in range(NCBH):
                c0 = cb * 512
                cw = min(512, LH - c0)
                ps, psn = nextps()
                P.mm(ps[:, :cw], bones[:, :], S[1][:, c0:c0 + cw], ["bones", SN[1]], [psn])
                P.ts("dve", S[1][:, c0:c0 + cw], ps[:, :cw], 1e-24, None, ALU.max, None, [psn], [(SN[1], cb)])
            P.act(S[1][:, W], S[1][:, W], AF.Sqrt, [SN[1]], [SN[1]])
            P.op("dve", lambda e: e.reciprocal(S[1][:, W], S[1][:, W]), [SN[1]], [SN[1]])
            P.tt("dve", S[0][:, W], S[0][:, W], S[1][:, W], ALU.mult, [SN[0], SN[1]], [SN[0]])
            P.ts("dve", S[1][:, W], S[3][:, W], -1.0, col(3), ALU.add, ALU.mult, [SN[3], "pcs"], [SN[1]])
            P.stt(S[2][:, W], S[1][:, W], 1.0, S[2][:, W], ALU.add, ALU.mult, [SN[1], SN[2]], [SN[2]])
            P.op("dve", lambda e: e.tensor_tensor_scan(S[1][:, W], resetb[:, :], S[4][:, W], 0.0, ALU.mult, ALU.add),
                 ["resetb", SN[4]], [SN[1]])
            P.tt("pool", S[4][:, W], S[1][:, W], S[4][:, W], ALU.subtract, [SN[1], SN[4]], [SN[4]])
            P.act(S[4][:, W], S[4][:, W], AF.Exp, [SN[4]], [SN[4]])
            P.stt(AT[:, p, TS], S[0][:, W], -1.0, S[4][:, W], ALU.mult, ALU.mult, [SN[0], SN[4]], [("AT", p)])
            P.act(S[4][:, W], S[1][:, W], AF.Exp, [SN[1]], [SN[4]], scale=-1.0)
            P.tt("dve", S[0][:, W], S[0][:, W], S[3][:, W], ALU.mult, [SN[0], SN[3]], [SN[0]])
            P.tt("dve", BT[:, p, TS], S[0][:, W], S[4][:, W], ALU.mult, [SN[0], SN[4]], [("BT", p)])
            P.tt("pool", KT[:, p, TS], S[2][:, W], S[4][:, W], ALU.mult, [SN[2], SN[4]], [("KT", p)])
            P.act(S[4][:, W], S[1][:, W], AF.Exp, [SN[1]], [SN[4]])
            gv = ap3(S[4], 63, [[LH + 1, 128], [64, LH // 64]])
            P.cp("dve", gam[:, p, t0 // 64:(t0 + LH) // 64], gv, [SN[4]], [("gam", p)])
            shift_mix(S[0], SN[0], zr[rows, :], 128, mu3s[:, p:p + 1], S[1], SN[1], S[3], SN[3], t0)
            P.tt("dve", RT[:, p, TS], S[0][:, W], S[4][:, W], ALU.mult, [SN[0], SN[4]], [("RT", p)])
            P.stt(RK[:, p, TS], S[0][:, W], col(4), S[2][:, W], ALU.mult, ALU.mult, [SN[0], SN[2], "pcs"], [("RK", p)])

    A4 = [P.sb("A4_%d" % i, [128, 512], BF16) for i in range(2)]
    N0s = [P.sb("N0s_%d" % i, [128, 128], BF16) for i in range(2)]
    NL = [[P.sb("NL_%d_%d" % (i, j), [128, 256], BF16) for j in range(2)] for i in range(2)]
    TOK = [P.sb("TOK%d" % i, [128, 4, 128], BF16) for i in range(2)]
    Zb = [[P.sb("Zb_%d_%d" % (i, j), [128, 128], BF16) for j in range(2)] for i in range(2)]
    W1P = [P.sb("W1P%d" % h, [128, 128], BF16) for h in range(2)]
    XTP = [[P.sb("XTP_%d_%d" % (h, c), [128, 128]) for c in range(2)] for h in range(2)]
    Qg = [P.sb("Qg%d" % i, [128, 64]) for i in range(2)]
    RM = [[P.sb("RM_%d_%d" % (h, c), [128, 128], BF16) for c in range(2)] for h in range(2)]
    Hf = [[P.sb("Hf_%d_%d" % (p, h), [128, 64]) for h in range(2)] for p in range(2)]
    Hb = [[[P.sb("Hb_%d_%d_%d" % (p, h, c), [128, 64], BF16) for c in range(2)] for h in range(2)] for p in range(2)]
    Yall = [P.sb("Yall%d" % i, [128, 256]) for i in range(2)]
    rkc = [P.sb("rkc%d" % i, [128, 4]) for i in range(2)]
    gtk = [P.sb("gtk%d" % i, [128, 256]) for i in range(2)]
    st = P.sb("st", [128, 16])
    sq = P.sb("sq", [128, 256])
    yo = [P.sb("yo%d" % i, [128, 256]) for i in range(2)]
    yfm = [P.sb("yfm%d" % i, [128, 2, 128]) for i in range(2)]
    for h in range(2):
        P.memset("pool", W1P[h][:, :], 0.0, ["W1P%d" % h])
        for c in range(2):
            P.memset("pool", XTP[h][c][:, :], 0.0, ["XTP_%d_%d" % (h, c)])
            P.memset("pool", RM[h][c][:, :], 0.0, ["RM_%d_%d" % (h, c)])
    for p in range(2):
        for h in range(2):
            P.memset("pool", Hf[p][h][:, :], 0.0, ["Hf_%d_%d" % (p, h)])
            P.memset("pool", Hb[p][h][0][:, :], 0.0, ["Hb_%d_%d_0" % (p, h)])

    u = 0
    for blk in range(NB):
        cols = slice(blk * 128, (blk + 1) * 128)
        Y, Yn = Yall[blk % 2], "Yall%d" % (blk % 2)
        pg, pgn = pss[4], "ps4"
        P.mm(pg[:, 0:256], lo_g[:, cols], g2b[:, :], ["lo_g", "g2b"], [(pgn, 0)])
        for p in range(2):
            P.mm(pg[:, 256 + 2 * p:258 + 2 * p], RK[:, p, cols], hselb[:, :], [("RK", p), "hselb"], [(pgn, 1 + p)])
        gt, gtn = gtk[blk % 2], "gtk%d" % (blk % 2)
        rk_, rkn = rkc[blk % 2], "rkc%d" % (blk % 2)
        P.cp("act", gt[:, :], pg[:, 0:256], [(pgn, 0)], [gtn])
        P.cp("act", rk_[:, :], pg[:, 256:260], [(pgn, 1), (pgn, 2)], [rkn])
        for p in range(2):
            tk, tkn = TOK[p], "TOK%d" % p
            for j, (X, xn) in enumerate(((BT, "BT"), (KT, "KT"), (VB, "VB"), (AT, "AT"))):
                P.tr(psb[:, j * 128:(j + 1) * 128], X[:, p, cols], identb[:, :], [(xn, p), "identb"], [("psb", j)])
            P.cp("act", tk[:, :, :], psb[:, 0:512].rearrange("p (j c) -> p j c", j=4), ["psb"], [tkn])
            for h in range(2):
                o = 64 * h
                hs = slice(o, o + 64)
                bt, kt_, at, rt = BT[hs, p, cols], KT[hs, p, cols], AT[hs, p, cols], RT[hs, p, cols]
                a4, a4n = A4[u % 2], "A4_%d" % (u % 2)
                n0, n0n = N0s[u % 2], "N0s_%d" % (u % 2)
                p1, p1n = pss[5], "ps5"
                P.mm(p1[:, 0:128], bt, at, [("BT", p), ("AT", p)], [(p1n, 0)])
                P.mm(p1[:, 128:256], bt, rt, [("BT", p), ("RT", p)], [(p1n, 1)])
                P.mm(p1[:, 256:384], kt_, at, [("KT", p), ("AT", p)], [(p1n, 2)])
                P.mm(p1[:, 384:512], kt_, rt, [("KT", p), ("RT", p)], [(p1n, 3)])
                P.tt("dve", a4[:, :], p1[:, :], mask4[:, :], ALU.mult, [p1n, "mask4"], [a4n])
                p2, p2n = nextps()
                P.mm(p2[:, 0:128], at, bt, [("BT", p), ("AT", p)], [p2n])
                P.tt("dve", n0[:, :], p2[:, 0:128], msl[:, :], ALU.mult, [p2n, "msl"], [n0n])
                L0, ArbT, AakT, ArkT = a4[:, 0:128], a4[:, 128:256], a4[:, 256:384], a4[:, 384:512]
                zc, zcn = Zb[u % 2][0], "Zb_%d_0" % (u % 2)
                p3, p3n = nextps()
                P.mm(p3[:, 0:64], AakT, tk[:, 2, hs], [a4n, tkn], [p3n])
                P.cp("dve", zc[:, 0:64], tk[:, 3, hs], [tkn], [(zcn, 0)])
                P.cp("act", zc[:, 64:128], p3[:, 0:64], [p3n], [(zcn, 1)])
                Nj, Lj, njn, ljn = n0[:, :], L0, n0n, a4n
                zi = 0
                for j in range(6):
                    pz, pzn = nextps()
                    zc, zcn = Zb[u % 2][zi], "Zb_%d_%d" % (u % 2, zi)
                    zn_, znn = Zb[u % 2][1 - zi], "Zb_%d_%d" % (u % 2, 1 - zi)
                    P.mm(pz[:, 0:128], Lj, zc[:, :], [ljn, zcn], [pzn])
                    if j == 5:
                        P.tt("dve", zn_[:, :], pz[:, 0:128], zc[:, :], ALU.add, [pzn, zcn], [znn])
                        P.cp("pool", W1P[h][:, hs], zn_[:, 0:64], [znn], ["W1P%d" % h])
                    else:
                        P.tt("dve", zn_[:, :], pz[:, 0:128], zc[:, :], ALU.add, [pzn, zcn], [znn])
                        pq, pqn = nextps()
                        nl, nln = NL[u % 2][j % 2], "NL_%d_%d" % (u % 2, j % 2)
                        if j < 4:
                            P.mm(pq[:, 0:128], Lj, Nj, [ljn, njn], [(pqn, 0)])
                        P.mm(pq[:, 128:256], Nj, Lj, [ljn, njn], [(pqn, 1)])
                        if j < 4:
                            P.cp("act", nl[:, :], pq[:, 0:256], [pqn], [nln])
                        else:
                            P.cp("act", nl[:, 128:256], pq[:, 128:256], [pqn], [nln])
                        Nj, Lj, njn, ljn = nl[:, 0:128], nl[:, 128:256], nln, nln
                    zi = 1 - zi
                Zf, Zfn = Zb[u % 2][zi], "Zb_%d_%d" % (u % 2, zi)
                U0 = Zf[:, 64:128]
                w1p, w1n = W1P[h], "W1P%d" % h
                pr, prn = nextps()
                P.mm(pr[:, 0:128], w1p[:, :], ArbT, [w1n, a4n], [prn])
                for c in range(2):
                    cs = slice(64 * c, 64 * c + 64)
                    P.tt("dve", RM[h][c][hs, cs], pr[hs, cs], RT[hs, p, blk * 128 + 64 * c:blk * 128 + 64 * c + 64], ALU.add,
                         [prn, ("RT", p)], ["RM_%d_%d" % (h, c)])
                py, pyn = pss[6], "ps6"
                P.mm(py[:, 0:64], ArbT, U0, [a4n, Zfn], [pyn], start=True, stop=False)
                P.mm(py[:, 0:64], ArkT, tk[:, 2, hs], [a4n, tkn], [pyn], start=False, stop=False)
                hf, hfn = Hf[p][h], "Hf_%d_%d" % (p, h)
                for c in range(2):
                    ch = blk * 2 + c
                    rs = slice(64 * c, 64 * c + 64)
                    hb, hbn = Hb[p][h][c], "Hb_%d_%d_%d" % (p, h, c)
                    P.mm(py[:, 0:64], RM[h][c][hs, :], hb[hs, :], ["RM_%d_%d" % (h, c), hbn], [pyn], start=False, stop=(c == 1))
                    px, pxn = nextps()
                    P.mm(px[:, 0:64], w1p[rs, :], tk[rs, 0, hs], [w1n, tkn], [(pxn, 0)])
                    xt, xtn = XTP[h][c], "XTP_%d_%d" % (h, c)
                    P.tt("dve", xt[hs, hs], px[hs, 0:64], identf[hs, hs], ALU.add, [(pxn, 0), "identf"], [xtn])
                    P.mm(px[:, 64:128], tk[rs, 0, :], U0[rs, :], [tkn, Zfn], [(pxn, 1)], start=True, stop=False)
                    P.mm(px[:, 64:128], tk[rs, 1, :], tk[rs, 2, hs], [tkn], [(pxn, 1)], start=False, stop=True)
                    qg, qgn = Qg[c], "Qg%d" % c
                    P.ts("dve", qg[hs, :], px[hs, 64:128], gam[hs, p, ch:ch + 1], None, ALU.mult, None, [(pxn, 1), ("gam", p)], [qgn])
                    ph, phn = nextps()
                    P.mm(ph[:, 0:64], xt[hs, :], hf[hs, :], [xtn, hfn], [phn])
                    P.stt(hf[hs, :], ph[hs, 0:64], gam[hs, p, ch:ch + 1], qg[hs, :], ALU.mult, ALU.add, [phn, ("gam", p), qgn], [hfn])
                    nb_ = Hb[p][h][1 - c]
                    P.cp("pool", nb_[hs, :], hf[hs, :], [hfn], ["Hb_%d_%d_%d" % (p, h, 1 - c)])
                P.cp("act", Y[:, (2 * p + h) * 64:(2 * p + h + 1) * 64], py[:, 0:64], [pyn], [(Yn, 2 * p + h)])
                u += 1
        Y3 = Y[:, :].rearrange("p (h v) -> p h v", h=4)
        P.op("dve", lambda e, Y3=Y3: e.tensor_reduce(st[:, 0:4], Y3, mybir.AxisListType.X, ALU.add), [Yn], [("st", 0)])
        P.tt("pool", sq[:, :], Y[:, :], Y[:, :], ALU.mult, [Yn], ["sq"])
        sq3 = sq[:, :].rearrange("p (h v) -> p h v", h=4)
        P.op("dve", lambda e, sq3=sq3: e.tensor_reduce(st[:, 4:8], sq3, mybir.AxisListType.X, ALU.add), ["sq"], [("st", 1)])
        P.ts("dve", st[:, 0:4], st[:, 0:4], 1.0 / 64, None, ALU.mult, None, [("st", 0)], [("st", 0)])
        P.tt("dve", st[:, 8:12], st[:, 0:4], st[:, 0:4], ALU.mult, [("st", 0)], [("st", 2)])
        P.stt(st[:, 4:8], st[:, 4:8], 1.0 / 64, st[:, 8:12], ALU.mult, ALU.subtract, [("st", 1), ("st", 2)], [("st", 1)])
        P.ts("dve", st[:, 4:8], st[:, 4:8], 64e-5, None, ALU.add, None, [("st", 1)], [("st", 1)])
        P.act(st[:, 4:8], st[:, 4:8], AF.Sqrt, [("st", 1)], [("st", 1)])
        P.op("dve", lambda e: e.reciprocal(st[:, 4:8], st[:, 4:8]), [("st", 1)], [("st", 1)])
        o_, on = yo[blk % 2], "yo%d" % (blk % 2)
        for hh in range(4):
            cs = slice(hh * 64, hh * 64 + 64)
            P.ts("dve", o_[:, cs], Y[:, cs], st[:, hh:hh + 1], st[:, 4 + hh:5 + hh], ALU.subtract, ALU.mult, [Yn, "st"], [(on, hh)])
        P.tt("dve", o_[:, :], o_[:, :], lngs[:, :], ALU.mult, [on, "lngs"], [on])
        P.tt("pool", o_[:, :], o_[:, :], lnbs[:, :], ALU.add, [on, "lnbs"], [on])
        for hh in range(4):
            cs = slice(hh * 64, hh * 64 + 64)
            pp, h2 = hh // 2, hh % 2
            P.stt(o_[:, cs], TOK[pp][:, 2, 64 * h2:64 * h2 + 64], rk_[:, hh:hh + 1], o_[:, cs], ALU.mult, ALU.add,
                  ["TOK%d" % pp, rkn, on], [on])
        P.tt("dve", o_[:, :], o_[:, :], gt[:, :], ALU.mult, [on, gtn], [on])
        if fm_out:
            pt_, ptn = nextps()
            for c in range(2):
                P.tr(pt_[:, c * 128:(c + 1) * 128], o_[:, c * 128:(c + 1) * 128], identf[:, :], [on, "identf"], [ptn])
            yf, yfn = yfm[blk % 2], "yfm%d" % (blk % 2)
            P.cp("act", yf[:, :, :], pt_[:, 0:256].rearrange("p (c t) -> p c t", c=2), [ptn], [yfn])
            P.dma(y_out[0:256, blk * 128:(blk + 1) * 128].rearrange("(c p) t -> p c t", p=128), yf[:, :, :], [yfn], [("y_out", blk)], sk=yfn)
        else:
            P.dma(y_out[blk * 128:(blk + 1) * 128, :], o_[:, :], [on], [("y_out", blk)], sk=on)
    return _fin(P, nc, io, ("y_out", "vfirst_out"))


def l4_inputs(z, prm, hg, L, vfirst=None):
    c0 = 256 * hg
    f = lambda a: np.ascontiguousarray(a, dtype=np.float32)
    mu = prm["mu"]
    d = dict(zr=f(z[:, c0:c0 + 256].T), zk=f(z[:, 512 + c0:512 + c0 + 256].T), zv=f(z[:, 1024 + c0:1024 + c0 + 256].T),
             zw=f(z[:, 1536:1568].T), za=f(z[:, 1568:1600].T), zg=f(z[:, 1600:1696].T))
    mu3 = np.zeros((128, 6), np.float32)
    for j in range(3):
        for p in range(2):
            mu3[:, 2 * j + p] = mu[512 * j + c0 + 128 * p:512 * j + c0 + 128 * p + 128]
    d["mu3"] = mu3
    mulo = np.zeros((96, 3), np.float32)
    mulo[:32, 0] = mu[1536:1568]
    mulo[:32, 1] = mu[1568:1600]
    mulo[:, 2] = mu[1600:1696]
    d["mulo"] = mulo
    d["w2"] = f(prm["w2"][:, c0:c0 + 256]); d["a2"] = f(prm["a2"][:, c0:c0 + 256]); d["g2"] = f(prm["g2"][:, c0:c0 + 256])
    pc = np.zeros((128, 10), np.float32)
    for j, nm in enumerate(("w0", "a0", "k_k", "k_a", "r_k")):
        v = prm[nm].reshape(-1)
        for p in range(2):
            pc[:, 2 * j + p] = v[c0 + 128 * p:c0 + 128 * p + 128]
    d["pc"] = pc
    d["lng"] = f(np.broadcast_to(prm["ln_g"][c0:c0 + 256][None, :], (128, 256)))
    d["lnb"] = f(np.broadcast_to(prm["ln_b"][c0:c0 + 256][None, :], (128, 256)))
    d.update(rw_consts(L))
    if vfirst is not None:
        d["zvfull"] = f(z[:, 1024:1536].T)
        d["muvf"] = f(mu[1024:1536].reshape(4, 128).T)
        d["v1"] = f(prm["v1"].reshape(4, 128, 32).transpose(1, 0, 2))
        d["v2"] = f(prm["v2"][:, c0:c0 + 256])
        v0 = np.zeros((128, 2), np.float32)
        for p in range(2):
            v0[:, p] = prm["v0"][c0 + 128 * p:c0 + 128 * p + 128]
        d["v0"] = v0
        d["vfirst_in"] = f(vfirst)
    return d


def at_consts():
    tk = np.arange(128)[:, None]
    tq = np.arange(512)[None, :]
    m = np.stack([((128 * j + tk) <= tq).astype(np.float32) for j in range(4)], 1)
    i = np.arange(128)
    b64 = ((i[:, None] // 64) == (i[None, :] // 64)).astype(np.float32)
    return dict(c_cmask=m, c_b64=b64, c_ones=np.ones((128, 128), np.float32))


def build_L3(L=4096, ctx=None):
    nc, P, io = _ctx(ctx)
    NT = L // 128
    NQ = L // 512
    I = _mkI(nc, io)
    qT, kT = I("qT", [2, 128, L]), I("kT", [2, 128, L])
    vtok = I("vtok", [2, L, 128])
    gains = I("gains", [128, 4])
    lamv = I("lamv", [128, 4, 64])
    laminit = I("laminit", [128, 1])
    c_cmask, c_b64, c_ones = I("c_cmask", [128, 4, 512]), I("c_b64", [128, 128]), I("c_ones", [128, 128])
    oT = _mkO(nc, io, "oT", [2, 128, L])

    def ld(name, src, shape):
        t = P.sb(name, shape)
        if len(shape) == 2:
            P.dma(t[:, :], src[:, :], [], [name])
        else:
            P.dma(t[:, :, :], src[:, :, :], [], [name])
        return t
    gs = ld("gs", gains, [128, 4])
    lv = ld("lv", lamv, [128, 4, 64])
    li = ld("li", laminit, [128, 1])
    cmf = ld("cmf", c_cmask, [128, 4, 512])
    b64 = ld("b64", c_b64, [128, 128])
    onesf = ld("onesf", c_ones, [128, 128])
    cmb = P.sb("cmb", [128, 4, 512], BF16)
    onesb = P.sb("onesb", [128, 128], BF16)
    P.cp("dve", cmb[:, :, :], cmf[:, :, :], ["cmf"], ["cmb"])
    P.cp("dve", onesb[:, :], onesf[:, :], ["onesf"], ["onesb"])
    lt = P.sb("lt", [128, 2, 64])
    ls = P.sb("ls", [128, 4])
    P.tt("dve", lt[:, 0, :], lv[:, 0, :], lv[:, 1, :], ALU.mult, ["lv"], [("lt", 0)])
    P.tt("dve", lt[:, 1, :], lv[:, 2, :], lv[:, 3, :], ALU.mult, ["lv"], [("lt", 1)])
    P.op("dve", lambda e: e.tensor_reduce(ls[:, 0:2], lt[:, :, :], mybir.AxisListType.X, ALU.add), ["lt"], ["ls"])
    P.act(ls[:, 0:2], ls[:, 0:2], AF.Exp, ["ls"], ["ls"])
    P.tt("dve", ls[:, 2:3], ls[:, 1:2], ls[:, 0:1], ALU.subtract, ["ls"], ["ls"])
    P.tt("dve", ls[:, 3:4], ls[:, 2:3], li[:, :], ALU.subtract, ["ls", "li"], ["ls"])
    gq = P.sb("gq", [128, 1])
    P.ts("dve", gq[:, :], gs[:, 0:1], 0.125, None, ALU.mult, None, ["gs"], ["gq"])
    gsub = P.sb("gsub", [128, 1])
    P.tt("dve", gsub[:, :], gs[:, 2:3], gs[:, 3:4], ALU.mult, ["gs"], ["gsub"])

    raw = P.sb("raw", [128, L])
    sqt = P.sb("sqt", [128, L])
    qn = P.sb("qn", [128, L], BF16)
    kn = P.sb("kn", [128, L], BF16)
    vf = P.sb("vf", [128, NT, 128])
    vb = P.sb("vb", [128, NT, 128], BF16)
    pt = [P.sb("pt%d" % i, [128, 512], BF16) for i in range(4)]
    ob = P.sb("ob", [128, 512])
    o1 = P.sb("o1", [128, 512])
    rz = P.sb("rz", [128, 512])
    osq = P.sb("osq", [128, 512])
    oo = [P.sb("oo%d" % i, [128, 512]) for i in range(2)]
    psS = [P.ps("psS%d" % i, [128, 512]) for i in range(2)]
    psO = [P.ps("psO%d" % i, [128, 512]) for i in range(2)]
    psZ = [P.ps("psZ%d" % i, [128, 512]) for i in range(2)]
    psM = P.ps("psM", [128, 512])

    def qknorm(src, dst, dn, gcol):
        P.dma(raw[:, :], src, [], ["raw"])
        P.tt("pool", sqt[:, :], raw[:, :], raw[:, :], ALU.mult, ["raw"], ["sqt"])
        for cb in range(NQ):
            cs = slice(cb * 512, cb * 512 + 512)
            P.mm(psM[:, :], b64[:, :], sqt[:, cs], ["b64", "sqt"], ["psM"])
            P.ts("dve", sqt[:, cs], psM[:, :], 1.0 / 64, EPS, ALU.mult, ALU.add, ["psM"], [("sqt", cb)])
        P.act(sqt[:, :], sqt[:, :], AF.Sqrt, ["sqt"], ["sqt"])
        P.op("dve", lambda e: e.reciprocal(sqt[:, :], sqt[:, :]), ["sqt"], ["sqt"])
        P.stt(dst[:, :], raw[:, :], gcol, sqt[:, :], ALU.mult, ALU.mult, ["raw", "sqt", "gs", "gq"], [dn])

    kcount = 0
    for h in range(2):
        qknorm(qT[h, :, :], qn, "qn", gq[:, 0:1])
        qknorm(kT[h, :, :], kn, "kn", gs[:, 1:2])
        P.dma(vf[:, :, :], vtok[h, :, :].rearrange("(n p) d -> p n d", p=128), [], ["vf"])
        P.cp("pool", vb[:, :, :], vf[:, :, :], ["vf"], ["vb"])
        for qb in range(NQ):
            qs = slice(qb * 512, qb * 512 + 512)
            ntk = 4 * qb + 4
            for j in range(ntk):
                ts_ = slice(j * 128, j * 128 + 128)
                for c2 in range(2):
                    hs = slice(64 * c2, 64 * c2 + 64)
                    s, sn = psS[kcount % 2], "psS%d" % (kcount % 2)
                    p_, pn = pt[kcount % 4], "pt%d" % (kcount % 4)
                    kcount += 1
                    P.mm(s[:, :], kn[hs, ts_], qn[hs, qs], ["kn", "qn"], [sn])
                    P.act(p_[:, :], s[:, :], AF.Exp, [sn], [pn])
                    if j >= 4 * qb:
                        P.tt("dve", p_[:, :], p_[:, :], cmb[:, j - 4 * qb, :], ALU.mult, [pn, "cmb"], [pn])
                    P.mm(psO[c2][:, :], vb[:, j, :], p_[:, :], ["vb", pn], ["psO%d" % c2], start=(j == 0), stop=(j == ntk - 1))
                    P.mm(psZ[c2][:, :], onesb[:, :], p_[:, :], ["onesb", pn], ["psZ%d" % c2], start=(j == 0), stop=(j == ntk - 1))
            P.op("dve", lambda e: e.reciprocal(rz[:, :], psZ[0][:, :]), ["psZ0"], ["rz"])
            P.tt("dve", ob[:, :], psO[0][:, :], rz[:, :], ALU.mult, ["psO0", "rz"], ["ob"])
            P.op("dve", lambda e: e.reciprocal(rz[:, :], psZ[1][:, :]), ["psZ1"], ["rz"])
            P.tt("dve", o1[:, :], psO[1][:, :], rz[:, :], ALU.mult, ["psO1", "rz"], ["o1"])
            P.stt(ob[:, :], o1[:, :], ls[:, 3:4], ob[:, :], ALU.mult, ALU.add, ["o1", "ls", "ob"], ["ob"])
            P.tt("pool", osq[:, :], ob[:, :], ob[:, :], ALU.mult, ["ob"], ["osq"])
            P.mm(psM[:, :], onesf[:, :], osq[:, :], ["onesf", "osq"], ["psM"])
            P.ts("dve", osq[:, :], psM[:, :], 1.0 / 128, 1e-5, ALU.mult, ALU.add, ["psM"], ["osq"])
            P.act(osq[:, :], osq[:, :], AF.Sqrt, ["osq"], ["osq"])
            P.op("dve", lambda e: e.reciprocal(osq[:, :], osq[:, :]), ["osq"], ["osq"])
            o_, on = oo[qb % 2], "oo%d" % (qb % 2)
            P.stt(o_[:, :], ob[:, :], gsub[:, 0:1], osq[:, :], ALU.mult, ALU.mult, ["ob", "gsub", "osq"], [on])
            P.dma(oT[h, :, qs], o_[:, :], [on], [("oT", h * NQ + qb)], sk=on)
    return _fin(P, nc, io, ("oT",))


def l3_inputs(q, k, v, prm, hp, L, lam_init):
    f = lambda a: np.ascontiguousarray(a, dtype=np.float32)
    hs = [2 * hp, 2 * hp + 1]
    d = dict(qT=f(np.stack([q[:, h * 128:(h + 1) * 128].T for h in hs])),
             kT=f(np.stack([k[:, h * 128:(h + 1) * 128].T for h in hs])),
             vtok=f(np.stack([v[:, h * 128:(h + 1) * 128] for h in hs])))
    g = np.zeros((128, 4), np.float32)
    g[:, 0] = prm["q_gain"].reshape(-1)
    g[:, 1] = prm["k_gain"].reshape(-1)
    g[:, 2] = prm["subln_g"]
    g[:, 3] = 1.0 - lam_init
    d["gains"] = g
    d["lamv"] = f(np.broadcast_to(prm["lam"][None], (128, 4, 64)))
    d["laminit"] = np.full((128, 1), lam_init, np.float32)
    d.update(at_consts())
    return d


PI = float(np.pi)


def build_L2(L=4096, ctx=None):
    nc, P, io = _ctx(ctx)
    NQ = L // 512
    I = _mkI(nc, io)
    uT = I("uT", [256, L])
    bpr, bpi = I("bpr", [128, 8, 128]), I("bpi", [128, 8, 128])
    cpr, cpi = I("cpr", [128, 8, 128]), I("cpi", [128, 8, 128])
    prm = I("prm", [128, 3, 8])
    dcol = I("dcol", [128, 2])
    yT = _mkO(nc, io, "yT", [256, L])

    def ld(name, src, shape):
        t = P.sb(name, shape)
        if len(shape) == 2:
            P.dma(t[:, :], src[:, :], [], [name])
        else:
            P.dma(t[:, :, :], src[:, :, :], [], [name])
        return t
    prs = ld("prs", prm, [128, 3, 8])
    ds = ld("ds", dcol, [128, 2])
    wts = {}
    for nm, src in (("bpr", bpr), ("bpi", bpi), ("cpr", cpr), ("cpi", cpi)):
        f = ld(nm + "f", src, [128, 8, 128])
        b = P.sb(nm + "b", [128, 8, 128], BF16)
        P.cp("dve", b[:, :, :], f[:, :, :], [nm + "f"], [nm + "b"])
        wts[nm] = b
    iaf = P.sb("iaf", [128, L])
    ub = P.sb("ub", [128, 2, L], BF16)
    for t in range(2):
        P.dma(iaf[:, :], uT[t * 128:(t + 1) * 128, :], [], ["iaf"])
        P.cp("dve", ub[:, t, :], iaf[:, :], ["iaf"], [("ub", t)])
    q = P.sb("q", [128, 16, 8])
    Q = lambda i: q[:, i, :]
    lr, li_, ldt = prs[:, 0, :], prs[:, 1, :], prs[:, 2, :]
    negpi = P.sb("negpi", [128, 1])
    P.memset("pool", negpi[:, :], -PI, ["negpi"])
    P.act(Q(0), ldt, AF.Exp, ["prs"], [("q", 0)])
    P.tt("dve", Q(1), lr, Q(0), ALU.mult, ["prs", ("q", 0)], [("q", 1)])
    P.tt("dve", Q(2), li_, Q(0), ALU.mult, ["prs", ("q", 0)], [("q", 2)])
    P.act(Q(3), Q(1), AF.Exp, [("q", 1)], [("q", 3)])
    NA = L // 64
    pw = P.sb("pw", [128, 24, 2, 8])
    wt = P.sb("wt", [128, 6, 8])
    W = lambda i: wt[:, i, :]

    def square_norm(cin, sin_, cout, sout, rd, wr):
        P.tt("dve", W(0), cin, cin, ALU.mult, rd, [("wt", 0)])
        P.tt("dve", W(1), sin_, sin_, ALU.mult, rd, [("wt", 1)])
        P.tt("dve", W(2), cin, sin_, ALU.mult, rd, [("wt", 2)])
        P.tt("dve", W(0), W(0), W(1), ALU.subtract, [("wt", 0), ("wt", 1)], [("wt", 0)])
        P.ts("dve", W(2), W(2), 2.0, None, ALU.mult, None, [("wt", 2)], [("wt", 2)])
        P.tt("dve", W(3), W(0), W(0), ALU.mult, [("wt", 0)], [("wt", 3)])
        P.tt("dve", W(4), W(2), W(2), ALU.mult, [("wt", 2)], [("wt", 4)])
        P.tt("dve", W(3), W(3), W(4), ALU.add, [("wt", 3), ("wt", 4)], [("wt", 3)])
        P.act(W(3), W(3), AF.Sqrt, [("wt", 3)], [("wt", 3)])
        P.op("dve", lambda e: e.reciprocal(W(3), W(3)), [("wt", 3)], [("wt", 3)])
        P.tt("dve", cout, W(0), W(3), ALU.mult, [("wt", 0), ("wt", 3)], wr)
        P.tt("dve", sout, W(2), W(3), ALU.mult, [("wt", 2), ("wt", 3)], wr)
    P.act(pw[:, 0, 1, :], Q(2), AF.Sin, [("q", 2)], [("pw", 0)], scale=1.0 / 16)
    P.act(Q(5), Q(2), AF.Sin, [("q", 2)], [("q", 5)], scale=1.0 / 32)
    P.tt("dve", Q(5), Q(5), Q(5), ALU.mult, [("q", 5)], [("q", 5)])
    P.ts("dve", pw[:, 0, 0, :], Q(5), -2.0, 1.0, ALU.mult, ALU.add, [("q", 5)], [("pw", 0)])
    NPW = 4 + 6 + max(1, int(np.log2(NA)))
    for k_ in range(1, NPW):
        square_norm(pw[:, k_ - 1, 0, :], pw[:, k_ - 1, 1, :], pw[:, k_, 0, :], pw[:, k_, 1, :], [("pw", k_ - 1)], [("pw", k_)])
    P.cp("dve", Q(5), pw[:, 4, 0, :], [("pw", 4)], [("q", 5)])
    P.cp("dve", Q(4), pw[:, 4, 1, :], [("pw", 4)], [("q", 4)])
    EBc, EBs = P.sb("EBc", [128, 8, 64]), P.sb("EBs", [128, 8, 64])
    EAc, EAs = P.sb("EAc", [128, 8, NA]), P.sb("EAs", [128, 8, NA])
    tbl = P.sb("tbl", [128, 2, 8, 64])

    def build_table(Ec, Es, ecn, esn, n_tot, k0, rowlen):
        P.memset("pool", Ec[:, :, 0:1], 1.0, [ecn])
        P.memset("pool", Es[:, :, 0:1], 0.0, [esn])
        n = 1
        k_ = k0
        while n < n_tot:
            wc = ap3(pw, (k_ * 2 + 0) * 8, [[24 * 2 * 8, 128], [1, 8], [0, n]])
            ws = ap3(pw, (k_ * 2 + 1) * 8, [[24 * 2 * 8, 128], [1, 8], [0, n]])
            t0_, t1_ = tbl[:, 0, :, 0:n], tbl[:, 1, :, 0:n]
            rd = [ecn, esn, ("pw", k_)]
            P.tt("dve", t0_, Es[:, :, 0:n], ws, ALU.mult, rd, [("tbl", 0)])
            P.tt("dve", t1_, Ec[:, :, 0:n], ws, ALU.mult, rd, [("tbl", 1)])
            P.tt("dve", Ec[:, :, n:2 * n], Ec[:, :, 0:n], wc, ALU.mult, rd, [ecn])
            P.tt("dve", Es[:, :, n:2 * n], Es[:, :, 0:n], wc, ALU.mult, rd, [esn])
            P.tt("dve", Ec[:, :, n:2 * n], Ec[:, :, n:2 * n], t0_, ALU.subtract, [ecn, ("tbl", 0)], [ecn])
            P.tt("dve", Es[:, :, n:2 * n], Es[:, :, n:2 * n], t1_, ALU.add, [esn, ("tbl", 1)], [esn])
            n *= 2
            k_ += 1
    build_table(EBc, EBs, "EBc", "EBs", 64, 4, 64)
    build_table(EAc, EAs, "EAc", "EAs", NA, 10, NA)
    P.tt("dve", Q(6), Q(3), Q(5), ALU.mult, [("q", 3), ("q", 5)], [("q", 6)])
    P.tt("dve", Q(7), Q(3), Q(4), ALU.mult, [("q", 3), ("q", 4)], [("q", 7)])
    P.tt("dve", Q(8), lr, lr, ALU.mult, ["prs"], [("q", 8)])
    P.tt("dve", Q(9), li_, li_, ALU.mult, ["prs"], [("q", 9)])
    P.tt("dve", Q(8), Q(8), Q(9), ALU.add, [("q", 8), ("q", 9)], [("q", 8)])
    P.op("dve", lambda e: e.reciprocal(Q(8), Q(8)), [("q", 8)], [("q", 8)])
    P.ts("dve", Q(9), Q(6), -1.0, None, ALU.add, None, [("q", 6)], [("q", 9)])
    P.tt("dve", Q(10), Q(9), lr, ALU.mult, [("q", 9), "prs"], [("q", 10)])
    P.tt("dve", Q(11), Q(7), li_, ALU.mult, [("q", 7), "prs"], [("q", 11)])
    P.tt("dve", Q(10), Q(10), Q(11), ALU.add, [("q", 10), ("q", 11)], [("q", 10)])
    P.tt("dve", Q(10), Q(10), Q(8), ALU.mult, [("q", 10), ("q", 8)], [("q", 10)])
    P.tt("dve", Q(11), Q(7), lr, ALU.mult, [("q", 7), "prs"], [("q", 11)])
    P.tt("dve", Q(12), Q(9), li_, ALU.mult, [("q", 9), "prs"], [("q", 12)])
    P.tt("dve", Q(11), Q(11), Q(12), ALU.subtract, [("q", 11), ("q", 12)], [("q", 11)])
    P.tt("dve", Q(11), Q(11), Q(8), ALU.mult, [("q", 11), ("q", 8)], [("q", 11)])
    P.ts("dve", Q(14), Q(10), -1.0, None, ALU.mult, None, [("q", 10)], [("q", 14)])
    MAG, TH, KR, KI, NKR = 3, 2, 10, 11, 14

    xb = P.sb("xb", [128, 4, 2, L], BF16)
    NS = 2
    def mk(n, dt=F32):
        return [P.sb("%s%d" % (n, i), [128, 512], dt) for i in range(NS)]
    ang, cc, ss_, T1, T2, mr, mi, tmp, tmp2 = [mk(n) for n in ("ang", "cc", "ss", "T1", "T2", "mr", "mi", "tmp", "tmp2")]
    wr = [P.sb("wr%d" % i, [128, 512]) for i in range(2)]
    wi = [P.sb("wi%d" % i, [128, 512]) for i in range(2)]
    magt = P.sb("magt", [128, 512])
    onest = P.sb("onest", [128, 512])
    P.memset("pool", onest[:, :], 1.0, ["onest"])
    ust = [P.sb("ust%d" % i, [128, 512]) for i in range(2)]
    yst = [P.sb("yst%d" % i, [128, 512]) for i in range(2)]
    psr = [P.ps("psr%d" % i, [128, 512]) for i in range(2)]
    psi = [P.ps("psi%d" % i, [128, 512]) for i in range(2)]
    psy = [P.ps("psy%d" % i, [128, 512]) for i in range(2)]
    k = 0
    for tile in range(2):
        for gl in range(4):
            gp = 4 * tile + gl
            sc = lambda idx: q[:, idx, gp:gp + 1]
            P.ts("dve", magt[:, :], onest[:, :], sc(MAG), None, ALU.mult, None, ["onest", "q"], ["magt"])
            for qb in range(NQ):
                i = k % NS
                k += 1
                n = lambda s: "%s%d" % (s, i)
                cs = slice(qb * 512, qb * 512 + 512)
                P.mm(psr[i][:, :], wts["bpr"][:, gp, :], ub[:, tile, cs], ["bprb", ("ub", tile)], ["psr%d" % i])
                P.mm(psi[i][:, :], wts["bpi"][:, gp, :], ub[:, tile, cs], ["bpib", ("ub", tile)], ["psi%d" % i])
                a0 = 8 * qb
                bro = lambda T_, n_: ap3(T_, gp * n_ + a0, [[8 * n_, 128], [1, 8], [0, 64]])
                brb = lambda T_: ap3(T_, gp * 64, [[8 * 64, 128], [0, 8], [1, 64]])
                v3 = lambda t_: t_[:, :].rearrange("p (a b) -> p a b", a=8)
                P.tt("pool", v3(tmp[i]), bro(EAs, NA), brb(EBs), ALU.mult, ["EAs", "EBs"], [n("tmp")])
                P.tt("dve", v3(cc[i]), bro(EAc, NA), brb(EBc), ALU.mult, ["EAc", "EBc"], [n("cc")])
                P.tt("pool", cc[i][:, :], cc[i][:, :], tmp[i][:, :], ALU.subtract, [n("cc"), n("tmp")], [n("cc")])
                P.tt("pool", v3(tmp2[i]), bro(EAc, NA), brb(EBs), ALU.mult, ["EAc", "EBs"], [n("tmp2")])
                P.tt("dve", v3(ss_[i]), bro(EAs, NA), brb(EBc), ALU.mult, ["EAs", "EBc"], [n("ss")])
                P.tt("pool", ss_[i][:, :], ss_[i][:, :], tmp2[i][:, :], ALU.add, [n("ss"), n("tmp2")], [n("ss")])
                P.ts("dve", T1[i][:, :], cc[i][:, :], sc(KR), None, ALU.mult, None, [n("cc"), "q"], [n("T1")])
                P.stt(T1[i][:, :], ss_[i][:, :], sc(KI), T1[i][:, :], ALU.mult, ALU.add, [n("ss"), "q", n("T1")], [n("T1")])
                P.ts("dve", T2[i][:, :], cc[i][:, :], sc(KI), None, ALU.mult, None, [n("cc"), "q"], [n("T2")])
                P.stt(T2[i][:, :], ss_[i][:, :], sc(NKR), T2[i][:, :], ALU.mult, ALU.add, [n("ss"), "q", n("T2")], [n("T2")])
                P.tt("dve", tmp[i][:, :], psi[i][:, :], T2[i][:, :], ALU.mult, ["psi%d" % i, n("T2")], [n("tmp")])
                P.tt("dve", mr[i][:, :], psr[i][:, :], T1[i][:, :], ALU.mult, ["psr%d" % i, n("T1")], [n("mr")])
                P.tt("pool", mr[i][:, :], mr[i][:, :], tmp[i][:, :], ALU.subtract, [n("mr"), n("tmp")], [n("mr")])
                P.tt("dve", tmp2[i][:, :], psr[i][:, :], T2[i][:, :], ALU.mult, ["psr%d" % i, n("T2")], [n("tmp2")])
                P.tt("dve", mi[i][:, :], psi[i][:, :], T1[i][:, :], ALU.mult, ["psi%d" % i, n("T1")], [n("mi")])
                P.tt("pool", mi[i][:, :], mi[i][:, :], tmp2[i][:, :], ALU.add, [n("mi"), n("tmp2")], [n("mi")])
                w_r, w_i = wr[qb % 2], wi[qb % 2]
                wrn, win = "wr%d" % (qb % 2), "wi%d" % (qb % 2)
                if qb == 0:
                    ini_r, ini_i, extra = 0.0, 0.0, []
                else:
                    ini_r, ini_i = wr[1 - qb % 2][:, 511:512], wi[1 - qb % 2][:, 511:512]
                    extra = ["wr%d" % (1 - qb % 2), "wi%d" % (1 - qb % 2)]
                P.op("dve", lambda e, w_r=w_r, m=mr[i], ini=ini_r: e.tensor_tensor_scan(w_r[:, :], magt[:, :], m[:, :], ini, ALU.mult, ALU.add),
                     ["magt", n("mr")] + extra, [wrn])
                P.op("dve", lambda e, w_i=w_i, m=mi[i], ini=ini_i: e.tensor_tensor_scan(w_i[:, :], magt[:, :], m[:, :], ini, ALU.mult, ALU.add),
                     ["magt", n("mi")] + extra, [win])
                P.tt("pool", tmp[i][:, :], ss_[i][:, :], w_i[:, :], ALU.mult, [n("ss"), win], [n("tmp")])
                P.tt("pool", tmp2[i][:, :], cc[i][:, :], w_r[:, :], ALU.mult, [n("cc"), wrn], [n("tmp2")])
                P.tt("pool", xb[:, gl, 0, cs], tmp2[i][:, :], tmp[i][:, :], ALU.subtract, [n("tmp2"), n("tmp")], [("xb", gl * 2)])
                P.tt("pool", tmp[i][:, :], ss_[i][:, :], w_r[:, :], ALU.mult, [n("ss"), wrn], [n("tmp")])
                P.tt("pool", tmp2[i][:, :], cc[i][:, :], w_i[:, :], ALU.mult, [n("cc"), win], [n("tmp2")])
                P.stt(xb[:, gl, 1, cs], tmp[i][:, :], -1.0, tmp2[i][:, :], ALU.mult, ALU.subtract, [n("tmp"), n("tmp2")], [("xb", gl * 2 + 1)])
        for qb in range(NQ):
            cs = slice(qb * 512, qb * 512 + 512)
            py, pyn = psy[qb % 2], "psy%d" % (qb % 2)
            for gl in range(4):
                gp = 4 * tile + gl
                P.mm(py[:, :], wts["cpr"][:, gp, :], xb[:, gl, 0, cs], ["cprb", ("xb", gl * 2)], [pyn], start=(gl == 0), stop=False)
                P.mm(py[:, :], wts["cpi"][:, gp, :], xb[:, gl, 1, cs], ["cpib", ("xb", gl * 2 + 1)], [pyn], start=False, stop=(gl == 3))
            us, usn = ust[qb % 2], "ust%d" % (qb % 2)
            ys, ysn = yst[qb % 2], "yst%d" % (qb % 2)
            P.dma(us[:, :], uT[tile * 128:(tile + 1) * 128, cs], [], [usn])
            P.stt(ys[:, :], us[:, :], ds[:, tile:tile + 1], py[:, :], ALU.mult, ALU.add, [usn, "ds", pyn], [ysn])
            P.dma(yT[tile * 128:(tile + 1) * 128, cs], ys[:, :], [ysn], [("yT", tile * NQ + qb)], sk=ysn)
    return _fin(P, nc, io, ("yT",))


def l2_inputs(u, prm, gh, L):
    f = lambda a: np.ascontiguousarray(a, dtype=np.float32)
    d = dict(uT=f(u[:, 256 * gh:256 * gh + 256].T))
    bpr = np.zeros((128, 8, 128), np.float32); bpi = np.zeros_like(bpr)
    cpr = np.zeros((128, 8, 128), np.float32); cpi = np.zeros_like(cpr)
    pr = np.zeros((128, 3, 8), np.float32)
    for gp in range(8):
        gl = gp % 4
        for gpar in range(2):
            g = 16 * gh + 2 * gp + gpar
            chs = slice(16 * (2 * gl + gpar), 16 * (2 * gl + gpar) + 16)
            ps = slice(64 * gpar, 64 * gpar + 64)
            bpr[chs, gp, ps] = prm["b_re"][g].T
            bpi[chs, gp, ps] = prm["b_im"][g].T
            cpr[ps, gp, chs] = prm["c_re"][g].T
            cpi[ps, gp, chs] = prm["c_im"][g].T
            pr[ps, 0, gp] = prm["lam_re"][g]
            pr[ps, 1, gp] = prm["lam_im"][g]
            pr[ps, 2, gp] = prm["log_dt"][g]
    d.update(bpr=bpr, bpi=bpi, cpr=cpr, cpi=cpi, prm=pr)
    d["dcol"] = f(prm["d"][256 * gh:256 * gh + 256].reshape(2, 128).T)
    return d


DFF = 2816


def build_L5a(TOK=2048, HT=1024, ctx=None):
    nc, P, io = _ctx(ctx)
    NH = TOK // HT
    NTT = HT // 128
    NB5 = HT // 512
    I = _mkI(nc, io)
    x_tok = I("x_tok", [TOK, D])
    ypreT, oT, ycT = I("ypreT", [512, TOK]), I("oT", [512, TOK]), I("ycT", [512, TOK])
    glT = I("glT", [3072, TOK])
    w_glu, w_br, w_out = I("w_glu", [512, 512]), I("w_br", [3, 512, 1024]), I("w_out", [1024, 1024])
    x_out = _mkO(nc, io, "x_out", [TOK, D])

    wst = [P.sb("wst%d" % i, [128, 1024]) for i in range(3)]
    wk = [0]

    def load_w(dst, dn, sub, src_ap, ncol):
        i = wk[0] % 3
        wk[0] += 1
        P.dma(wst[i][:, :ncol], src_ap, [], ["wst%d" % i], eng=("sp" if i % 2 == 0 else "pool"))
        P.cp("pool" if i % 2 else "dve", dst, wst[i][:, :ncol], ["wst%d" % i], [(dn, sub)])

    wglu = P.sb("wglu", [128, 4, 512], BF16)
    wbr = P.sb("wbr", [128, 12, 1024], BF16)
    wout = P.sb("wout", [128, 8, 1024], BF16)
    for c in range(4):
        load_w(wglu[:, c, :], "wglu", c, w_glu[c * 128:(c + 1) * 128, :], 512)
    for n in range(3):
        for c in range(4):
            load_w(wbr[:, n * 4 + c, :], "wbr", n * 4 + c, w_br[n, c * 128:(c + 1) * 128, :], 1024)
    for c in range(8):
        load_w(wout[:, c, :], "wout", c, w_out[c * 128:(c + 1) * 128, :], 1024)

    fin = P.sb("fin", [128, 4, HT])
    t1 = P.sb("t1", [128, 4, HT])
    yb = [P.sb("yb%d" % n, [128, 4, HT], BF16) for n in range(3)]
    ya2 = P.sb("ya2", [128, 4, HT], BF16)
    gst = [P.sb("gst%d" % i, [128, 512]) for i in range(2)]
    sg = [P.sb("sg%d" % i, [128, 512]) for i in range(2)]
    pr = [P.sb("pr%d" % i, [128, 512]) for i in range(2)]
    macc = P.sb("macc", [128, 512])
    mT = P.sb("mT", [128, 8, HT], BF16)
    x1 = [P.sb("x1_%d" % i, [128, D]) for i in range(2)]
    xin = [P.sb("xin%d" % i, [128, D]) for i in range(2)]
    pss = [P.ps("ps%d" % i, [128, 512]) for i in range(6)]
    pk = [0]

    def nextps():
        k = pk[0] % 6
        pk[0] += 1
        return pss[k], "ps%d" % k

    for half in range(NH):
        t0 = half * HT
        tsl = slice(t0, t0 + HT)
        fmv = lambda a: a.ap().rearrange("(c p) t -> p c t", p=128)
        P.dma(fin[:, :, :], fmv(ypreT)[:, :, tsl], [], ["fin"])
        P.tt("pool", t1[:, :, :], fin[:, :, :], fin[:, :, :], ALU.mult, ["fin"], ["t1"])
        P.ts("dve", t1[:, :, :], t1[:, :, :], 0.044715, 1.0, ALU.mult, ALU.add, ["t1"], ["t1"])
        P.tt("pool", t1[:, :, :], t1[:, :, :], fin[:, :, :], ALU.mult, ["t1", "fin"], ["t1"])
        P.act(t1[:, :, :], t1[:, :, :], AF.Sigmoid, ["t1"], ["t1"], scale=1.5957691216057308)
        P.tt("dve", t1[:, :, :], t1[:, :, :], fin[:, :, :], ALU.mult, ["t1", "fin"], ["t1"])
        P.cp("pool", ya2[:, :, :], t1[:, :, :], ["t1"], ["ya2"])
        for jc in range(4):
            for b5 in range(NB5):
                bs = slice(b5 * 512, b5 * 512 + 512)
                ps, psn = nextps()
                for c in range(4):
                    P.mm(ps[:, :], wglu[:, c, jc * 128:(jc + 1) * 128], ya2[:, c, bs], [("wglu", c), "ya2"], [psn], start=(c == 0), stop=(c == 3))
                s_, sn = sg[(jc * NB5 + b5) % 2], "sg%d" % ((jc * NB5 + b5) % 2)
                P.act(s_[:, :], ps[:, :], AF.Sigmoid, [psn], [sn])
                P.tt("dve", yb[0][:, jc, bs], t1[:, jc, bs], s_[:, :], ALU.mult, ["t1", sn], [("yb0", jc)])
        P.dma(fin[:, :, :], fmv(oT)[:, :, tsl], ["fin"], ["fin"])
        P.cp("dve", yb[1][:, :, :], fin[:, :, :], ["fin"], ["yb1"])
        P.dma(fin[:, :, :], fmv(ycT)[:, :, tsl], ["fin"], ["fin"])
        P.cp("dve", yb[2][:, :, :], fin[:, :, :], ["fin"], ["yb2"])
        kk = 0
        for dc in range(8):
            for b5 in range(NB5):
                bs = slice(b5 * 512, b5 * 512 + 512)
                for n in range(3):
                    g_, gn = gst[kk % 2], "gst%d" % (kk % 2)
                    s_, sn = sg[kk % 2], "sg%d" % (kk % 2)
                    p_, pn = pr[kk % 2], "pr%d" % (kk % 2)
                    kk += 1
                    r0 = n * 1024 + dc * 128
                    P.dma(g_[:, :], glT[r0:r0 + 128, t0 + b5 * 512:t0 + b5 * 512 + 512], [], [gn], eng=("sp" if kk % 2 else "pool"))
                    P.act(s_[:, :], g_[:, :], AF.Sigmoid, [gn], [sn])
                    ps, psn = nextps()
                    for c in range(4):
                        P.mm(ps[:, :], wbr[:, n * 4 + c, dc * 128:(dc + 1) * 128], yb[n][:, c, bs], [("wbr", n * 4 + c), "yb%d" % n], [psn],
                             start=(c == 0), stop=(c == 3))
                    if n == 0:
                        P.tt("dve", macc[:, :], ps[:, :], s_[:, :], ALU.mult, [psn, sn], ["macc"])
                    else:
                        P.tt("dve", p_[:, :], ps[:, :], s_[:, :], ALU.mult, [psn, sn], [pn])
                        if n == 1:
                            P.tt("pool", macc[:, :], macc[:, :], p_[:, :], ALU.add, ["macc", pn], ["macc"])
                        else:
                            P.tt("pool", mT[:, dc, bs], macc[:, :], p_[:, :], ALU.add, ["macc", pn], [("mT", dc)])
        for tt_ in range(NTT):
            xi, xn = xin[tt_ % 2], "xin%d" % (tt_ % 2)
            xo_, xon = x1[tt_ % 2], "x1_%d" % (tt_ % 2)
            P.dma(xi[:, :], x_tok[t0 + tt_ * 128:t0 + (tt_ + 1) * 128, :], [], [xn])
            for ch in range(2):
                ps, psn = nextps()
                for c in range(8):
                    P.mm(ps[:, :], mT[:, c, tt_ * 128:(tt_ + 1) * 128], wout[:, c, ch * 512:(ch + 1) * 512], [("mT", c), ("wout", c)], [psn],
                         start=(c == 0), stop=(c == 7))
                P.tt("dve", xo_[:, ch * 512:(ch + 1) * 512], ps[:, :], xi[:, ch * 512:(ch + 1) * 512], ALU.add, [psn, xn], [(xon, ch)])
            P.dma(x_out[t0 + tt_ * 128:t0 + (tt_ + 1) * 128, :], xo_[:, :], [xon], [("x_out", half * NTT + tt_)], sk=xon)
    return _fin(P, nc, io, ("x_out",))


def build_L5b(TOK=2048, HT=1024, ctx=None):
    nc, P, io = _ctx(ctx)
    NH = TOK // HT
    NTT = HT // 128
    NB5 = HT // 512
    I = _mkI(nc, io)
    x_tok = I("x_tok", [TOK, D])
    g2 = I("g2", [128, 8])
    w_fi, w_fo = I("w_fi", [1024, 2 * DFF]), I("w_fo", [DFF, 1024])
    c_ident = I("c_ident", [128, 128])
    x_out = _mkO(nc, io, "x_out", [TOK, D])
    g2s = P.sb("g2s", [128, 8])
    P.dma(g2s[:, :], g2[:, :], [], ["g2s"])
    identf = P.sb("identf", [128, 128])
    P.dma(identf[:, :], c_ident[:, :], [], ["identf"])
    identb = P.sb("identb", [128, 128], BF16)
    P.cp("dve", identb[:, :], identf[:, :], ["identf"], ["identb"])
    wst = [P.sb("wst%d" % i, [128, 1024]) for i in range(3)]
    wk = [0]

    def load_w(dst, dn, sub, src_ap, ncol):
        i = wk[0] % 3
        wk[0] += 1
        P.dma(wst[i][:, :ncol], src_ap, [], ["wst%d" % i], eng=("sp" if i % 2 == 0 else "pool"))
        P.cp("pool" if i % 2 else "dve", dst, wst[i][:, :ncol], ["wst%d" % i], [(dn, sub)])
    wfo = P.sb("wfo", [128, 22, 1024], BF16)
    for c in range(22):
        load_w(wfo[:, c, :], "wfo", c, w_fo[c * 128:(c + 1) * 128, :], 1024)
    x1 = P.sb("x1", [128, NTT, D])
    junk = P.sb("junk", [128, D], BF16)
    ss = P.sb("ss5", [128, NTT])
    hb = [P.sb("hb%d" % i, [128, D], BF16) for i in range(2)]
    h2T = P.sb("h2T", [128, 8, HT], BF16)
    wfi_f = [P.sb("wfif%d" % i, [128, 8, 256]) for i in range(2)]
    wfi_b = [P.sb("wfib%d" % i, [128, 8, 256], BF16) for i in range(2)]
    actT = P.sb("actT", [128, 22, HT], BF16)
    sil = [P.sb("sil%d" % i, [128, 512]) for i in range(2)]
    xo = [P.sb("xo%d" % i, [128, 512]) for i in range(2)]
    pss = [P.ps("ps%d" % i, [128, 512]) for i in range(6)]
    psb = P.ps("psb", [128, 1024], BF16)
    pk = [0]

    def nextps():
        k = pk[0] % 6
        pk[0] += 1
        return pss[k], "ps%d" % k

    for half in range(NH):
        t0 = half * HT
        for tt_ in range(NTT):
            P.dma(x1[:, tt_, :], x_tok[t0 + tt_ * 128:t0 + (tt_ + 1) * 128, :], [], [("x1", tt_)], eng=("sp" if tt_ % 2 else "pool"))
            P.act(junk[:, :], x1[:, tt_, :], AF.Square, [("x1", tt_)], ["junk", ("ss5", tt_)], accum_out=ss[:, tt_:tt_ + 1])
        P.ts("dve", ss[:, :], ss[:, :], 1.0 / D, EPS, ALU.mult, ALU.add, ["ss5"], ["ss5"])
        P.act(ss[:, :], ss[:, :], AF.Sqrt, ["ss5"], ["ss5"])
        P.op("dve", lambda e: e.reciprocal(ss[:, :], ss[:, :]), ["ss5"], ["ss5"])
        for tt_ in range(NTT):
            h_, hn = hb[tt_ % 2], "hb%d" % (tt_ % 2)
            P.act(h_[:, :], x1[:, tt_, :], AF.Copy, [("x1", tt_), "ss5"], [hn], scale=ss[:, tt_:tt_ + 1])
            for c in range(8):
                P.tr(psb[:, c * 128:(c + 1) * 128], h_[:, c * 128:(c + 1) * 128], identb[:, :], [hn, "identb"], ["psb"])
            for c in range(8):
                P.ts("dve", h2T[:, c, tt_ * 128:(tt_ + 1) * 128], psb[:, c * 128:(c + 1) * 128], g2s[:, c:c + 1], None, ALU.mult, None,
                     ["psb", "g2s"], [("h2T", c)])
        for fc in range(22):
            wf, wfn = wfi_f[fc % 2], "wfif%d" % (fc % 2)
            wb_, wbn = wfi_b[fc % 2], "wfib%d" % (fc % 2)
            wv = w_fi.ap().rearrange("(c p) n -> p c n", p=128)
            P.dma(wf[:, :, 0:128], wv[:, :, fc * 128:(fc + 1) * 128], [], [(wfn, 0)], eng="sp", sk=wfn + "a")
            P.dma(wf[:, :, 128:256], wv[:, :, DFF + fc * 128:DFF + (fc + 1) * 128], [], [(wfn, 1)], eng="pool", sk=wfn + "b")
            P.cp("pool", wb_[:, :, :], wf[:, :, :], [wfn], [wbn])
            for b5 in range(NB5):
                bs = slice(b5 * 512, b5 * 512 + 512)
                pg, pgn = nextps()
                pu, pun = nextps()
                for c in range(8):
                    P.mm(pg[:, :], wb_[:, c, 0:128], h2T[:, c, bs], [wbn, ("h2T", c)], [pgn], start=(c == 0), stop=(c == 7))
                for c in range(8):
                    P.mm(pu[:, :], wb_[:, c, 128:256], h2T[:, c, bs], [wbn, ("h2T", c)], [pun], start=(c == 0), stop=(c == 7))
                s_, sn = sil[(fc * NB5 + b5) % 2], "sil%d" % ((fc * NB5 + b5) % 2)
                P.act(s_[:, :], pg[:, :], AF.Silu, [pgn], [sn])
                P.tt("dve", actT[:, fc, bs], pu[:, :], s_[:, :], ALU.mult, [pun, sn], [("actT", fc)])
        for tt_ in range(NTT):
            for ch in range(2):
                ps, psn = nextps()
                for fc in range(22):
                    P.mm(ps[:, :], actT[:, fc, tt_ * 128:(tt_ + 1) * 128], wfo[:, fc, ch * 512:(ch + 1) * 512], [("actT", fc), ("wfo", fc)], [psn],
                         start=(fc == 0), stop=(fc == 21))
                o_, on = xo[(tt_ * 2 + ch) % 2], "xo%d" % ((tt_ * 2 + ch) % 2)
                P.tt("dve", o_[:, :], ps[:, :], x1[:, tt_, ch * 512:(ch + 1) * 512], ALU.add, [psn, ("x1", tt_)], [on])
                P.dma(x_out[t0 + tt_ * 128:t0 + (tt_ + 1) * 128, ch * 512:(ch + 1) * 512], o_[:, :], [on],
                      [("x_out", (half * NTT + tt_) * 2 + ch)], sk=on)
    return _fin(P, nc, io, ("x_out",))


def _hw_runner(nc, in_maps):
    n = len(in_maps)
    maps = list(in_maps) + [in_maps[-1]] * (NCORES - n)
    res = run_bass_kernel_spmd(nc, maps, core_ids=list(range(NCORES)))
    return res.results[:n]


def kernel_impl(inp, runner=_hw_runner, TOKL=2048, HT=512):
    f = lambda a: np.ascontiguousarray(a, dtype=np.float32)
    x = np.asarray(inp["x"], dtype=np.float32)
    B, L, _ = x.shape
    NTOK = B * L
    ntc = NTOK // TOKL
    xcur = x.reshape(NTOK, D)
    depth = inp["w_in"].shape[0]
    progs = {}

    def prog(name, fn):
        if name not in progs:
            progs[name] = fn()
        return progs[name]
    ident = np.eye(128, dtype=np.float32)
    vfirst = None
    for i in range(depth):
        lam_init = 0.8 - 0.6 * float(np.exp(-0.3 * i))
        nc = prog("L1", lambda: build_L1(TOKL))
        g1 = f(np.asarray(inp["norm1_g"][i]).reshape(8, 128).T)
        w_in = f(inp["w_in"][i])
        maps = []
        for c in range(ntc):
            xs = xcur[c * TOKL:(c + 1) * TOKL]
            maps.append(dict(x_tok=f(xs), xT=f(xs.T), w=w_in, g=g1))
        res = runner(nc, maps)
        proj = np.concatenate([r["proj"] for r in res], 0).reshape(B, L, NIN)
        u, q, k, v = proj[..., 0:512], proj[..., 512:1024], proj[..., 1024:1536], proj[..., 1536:2048]
        z, gl = proj[..., 2048:3744], proj[..., 3744:]
        nc = prog("L2", lambda: build_L2(L))
        p2 = dict(lam_re=np.asarray(inp["s5_lambda_re"][i]), lam_im=np.asarray(inp["s5_lambda_im"][i]), log_dt=np.asarray(inp["s5_log_dt"][i]),
                  b_re=np.asarray(inp["s5_b_re"][i]), b_im=np.asarray(inp["s5_b_im"][i]), c_re=np.asarray(inp["s5_c_re"][i]),
                  c_im=np.asarray(inp["s5_c_im"][i]), d=np.asarray(inp["s5_d"][i]))
        res = runner(nc, [l2_inputs(u[c // 2], p2, c % 2, L) for c in range(2 * B)])
        ypre = np.stack([np.concatenate([res[2 * b]["yT"], res[2 * b + 1]["yT"]], 0) for b in range(B)])
        nc = prog("L3", lambda: build_L3(L))
        p3 = dict(q_gain=np.asarray(inp["da_q_gain"][i]), k_gain=np.asarray(inp["da_k_gain"][i]), lam=np.asarray(inp["da_lambda"][i]),
                  subln_g=np.asarray(inp["da_subln_g"][i]))
        res = runner(nc, [l3_inputs(q[c // 2], k[c // 2], v[c // 2], p3, c % 2, L, lam_init) for c in range(2 * B)])
        oat = np.stack([np.concatenate([res[2 * b]["oT"].reshape(256, L), res[2 * b + 1]["oT"].reshape(256, L)], 0) for b in range(B)])
        vg = i > 0
        nc = prog("L4v" if vg else "L4", lambda: build_L4(L, has_vgate=vg))
        p4 = dict(mu=np.asarray(inp["rw_mu"][i]), w0=np.asarray(inp["rw_w0"][i]), w2=np.asarray(inp["rw_w2"][i]), a0=np.asarray(inp["rw_a0"][i]),
                  a2=np.asarray(inp["rw_a2"][i]), g2=np.asarray(inp["rw_g2"][i]), k_k=np.asarray(inp["rw_k_k"][i]), k_a=np.asarray(inp["rw_k_a"][i]),
                  r_k=np.asarray(inp["rw_r_k"][i]), ln_g=np.asarray(inp["rw_ln_g"][i]), ln_b=np.asarray(inp["rw_ln_b"][i]))
        if vg:
            p4.update(v0=np.asarray(inp["rw_v0"][i - 1]), v1=np.asarray(inp["rw_v1"][i - 1]), v2=np.asarray(inp["rw_v2"][i - 1]))
        maps = []
        for c in range(2 * B):
            b, hg = c // 2, c % 2
            maps.append(l4_inputs(z[b], p4, hg, L, vfirst=(vfirst[b][256 * hg:256 * hg + 256] if vg else None)))
        res = runner(nc, maps)
        yc = np.stack([np.concatenate([res[2 * b]["y_out"], res[2 * b + 1]["y_out"]], 1) for b in range(B)])
        if not vg:
            vfirst = np.stack([np.concatenate([res[2 * b]["vfirst_out"], res[2 * b + 1]["vfirst_out"]], 0) for b in range(B)])
        nc = prog("L5a", lambda: build_L5a(TOKL, HT))
        wa = dict(w_glu=f(inp["s5_w_glu"][i]), w_br=f(inp["w_branch"][i]), w_out=f(inp["w_out"][i]))
        maps = []
        for c in range(ntc):
            t0 = c * TOKL
            b, s0 = t0 // L, t0 % L
            sl = slice(s0, s0 + TOKL)
            maps.append(dict(wa, x_tok=f(xcur[t0:t0 + TOKL]), ypreT=f(ypre[b][:, sl]), oT=f(oat[b][:, sl]), ycT=f(yc[b][sl].T), glT=f(gl[b][sl].T)))
        res = runner(nc, maps)
        x1 = np.concatenate([r["x_out"] for r in res], 0)
        nc = prog("L5b", lambda: build_L5b(TOKL, HT))
        wb_ = dict(g2=f(np.asarray(inp["norm2_g"][i]).reshape(8, 128).T), w_fi=f(inp["w_ffn_in"][i]), w_fo=f(inp["w_ffn_out"][i]), c_ident=ident)
        res = runner(nc, [dict(wb_, x_tok=f(x1[c * TOKL:(c + 1) * TOKL])) for c in range(ntc)])
        xcur = np.concatenate([r["x_out"] for r in res], 0)
    return xcur.reshape(B, L, D).astype(np.float32)


def kernel(**inputs):
    return kernel_fused(inputs)


def build_fused(L=4096, depth=2):
    nc = bass.Bass("TRN2", target_bir_lowering=False)
    sh = Shared(nc)
    E = lambda n, s: nc.dram_tensor(n, list(s), F32, kind="ExternalInput")
    x_in = E("x_in", [L, D])
    x_out = nc.dram_tensor("x_final", [L, D], F32, kind="ExternalOutput")
    FT = nc.dram_tensor("FT", [6304, L], F32)
    VT = nc.dram_tensor("VT", [L, 512], F32)
    YB = nc.dram_tensor("YB", [1536, L], F32)
    X1 = nc.dram_tensor("X1", [L, D], F32)
    XS = nc.dram_tensor("XS", [L, D], F32)
    VF = nc.dram_tensor("VF", [512, L], F32)
    VF2 = nc.dram_tensor("VF2", [512, L], F32)
    cs = {}
    LHr = min(L, 1024)
    for n, shp in (("c_ident", [128, 128]), ("c_cmask", [128, 4, 512]), ("c_b64", [128, 128]), ("c_ones", [128, 128]),
                   ("c_mask4", [128, 512]), ("c_sl", [128, 128]), ("c_reset", [128, LHr]), ("c_bones", [128, 128]), ("c_hsel", [128, 2])):
        cs[n] = E(n, shp)
    TP = min(2048, L)
    for i in range(depth):
        sfx = "_%d" % i
        xin = x_in if i == 0 else XS
        xo = x_out if i == depth - 1 else XS
        io = dict(x_tok=xin, w=E("w_in" + sfx, [D, NIN]), g=E("g1" + sfx, [128, 8]), c_ident=cs["c_ident"], FT=FT, VT=VT)
        build_L1F(L, ctx=(nc, sh, io))
        s5 = {n: E("s5" + n + sfx, [2, 128, 8, 128]) for n in ("bpr", "bpi", "cpr", "cpi")}
        s5prm = E("s5prm" + sfx, [2, 128, 3, 8])
        s5d = E("s5d" + sfx, [2, 128, 2])
        for gh in range(2):
            io = dict(uT=V(FT.ap()[256 * gh:256 * gh + 256, :]), prm=V(s5prm.ap()[gh]), dcol=V(s5d.ap()[gh]),
                      yT=V(YB.ap()[256 * gh:256 * gh + 256, :]))
            for n in s5:
                io[n] = V(s5[n].ap()[gh])
            build_L2(L, ctx=(nc, sh, io))
        gains = E("da_gains" + sfx, [128, 4])
        lamv = E("da_lamv" + sfx, [128, 4, 64])
        laminit = E("da_laminit" + sfx, [128, 1])
        for hp in range(2):
            hv = lambda a, r0: V(a.ap()[r0:r0 + 256, :].rearrange("(h p) l -> h p l", h=2))
            io = dict(qT=hv(FT, 512 + 256 * hp), kT=hv(FT, 1024 + 256 * hp),
                      vtok=V(VT.ap()[:, 256 * hp:256 * hp + 256].rearrange("l (h d) -> h l d", h=2)),
                      gains=gains, lamv=lamv, laminit=laminit, c_cmask=cs["c_cmask"], c_b64=cs["c_b64"], c_ones=cs["c_ones"],
                      oT=hv(YB, 512 + 256 * hp))
            build_L3(L, ctx=(nc, sh, io))
        vg = i > 0
        r4 = dict(mu3=E("rw_mu3" + sfx, [2, 128, 6]), mulo=E("rw_mulo" + sfx, [96, 3]), w2=E("rw_w2p" + sfx, [2, 32, 256]),
                  a2=E("rw_a2p" + sfx, [2, 32, 256]), g2=E("rw_g2p" + sfx, [2, 96, 256]), pc=E("rw_pc" + sfx, [2, 128, 10]),
                  lng=E("rw_lng" + sfx, [2, 128, 256]), lnb=E("rw_lnb" + sfx, [2, 128, 256]))
        if vg:
            r4.update(muvf=E("rw_muvf" + sfx, [128, 4]), v1=E("rw_v1p" + sfx, [128, 4, 32]), v2=E("rw_v2p" + sfx, [2, 32, 256]),
                      v0=E("rw_v0p" + sfx, [2, 128, 2]))
        for hg in range(2):
            R = lambda r0, n: V(FT.ap()[r0:r0 + n, :])
            io = dict(zr=R(1536 + 256 * hg, 256), zk=R(2048 + 256 * hg, 256), zv=R(2560 + 256 * hg, 256),
                      zw=R(3072, 32), za=R(3104, 32), zg=R(3136, 96), mulo=r4["mulo"],
                      y_out=V(YB.ap()[1024 + 256 * hg:1024 + 256 * hg + 256, :]),
                      vfirst_out=V((VF2 if vg else VF).ap()[256 * hg:256 * hg + 256, :]))
            for n in ("mu3", "w2", "a2", "g2", "pc", "lng", "lnb"):
                io[n] = V(r4[n].ap()[hg])
            for n in ("c_mask4", "c_sl", "c_reset", "c_bones", "c_hsel", "c_ident"):
                io[n] = cs[n]
            if vg:
                io.update(zvfull=R(2560, 512), muvf=r4["muvf"], v1=r4["v1"], v2=V(r4["v2"].ap()[hg]), v0=V(r4["v0"].ap()[hg]),
                          vfirst_in=V(VF.ap()[256 * hg:256 * hg + 256, :]))
            build_L4(L, has_vgate=vg, ctx=(nc, sh, io), fm_out=True)
        wa = dict(w_glu=E("w_glu" + sfx, [512, 512]), w_br=E("w_br" + sfx, [3, 512, 1024]), w_out=E("w_out" + sfx, [1024, 1024]))
        for th in range(L // TP):
            tsl = slice(th * TP, th * TP + TP)
            io = dict(wa, x_tok=V(xin.ap()[tsl, :]), ypreT=V(YB.ap()[0:512, tsl]), oT=V(YB.ap()[512:1024, tsl]), ycT=V(YB.ap()[1024:1536, tsl]),
                      glT=V(FT.ap()[3232:6304, tsl]), x_out=V(X1.ap()[tsl, :]))
            build_L5a(TP, 512, ctx=(nc, sh, io))
        wb_ = dict(g2=E("g2n" + sfx, [128, 8]), w_fi=E("w_fi" + sfx, [1024, 2 * DFF]), w_fo=E("w_fo" + sfx, [DFF, 1024]), c_ident=cs["c_ident"])
        for th in range(L // TP):
            tsl = slice(th * TP, th * TP + TP)
            io = dict(wb_, x_tok=V(X1.ap()[tsl, :]), x_out=V(xo.ap()[tsl, :]))
            if i == depth - 1:
                io["_final"] = True
            build_L5b(TP, 512, ctx=(nc, sh, io))
    sh.close()
    return nc


def fused_inputs(inp, L):
    f = lambda a: np.ascontiguousarray(a, dtype=np.float32)
    d = {}
    d.update(at_consts())
    rc = rw_consts(L)
    d.update(rc)
    depth = inp["w_in"].shape[0]
    zL = np.zeros((L, 512), np.float32)
    zz = np.zeros((L, 1696), np.float32)
    for i in range(depth):
        sfx = "_%d" % i
        lam_init = 0.8 - 0.6 * float(np.exp(-0.3 * i))
        d["w_in" + sfx] = f(inp["w_in"][i])
        d["g1" + sfx] = f(np.asarray(inp["norm1_g"][i]).reshape(8, 128).T)
        p2 = dict(lam_re=np.asarray(inp["s5_lambda_re"][i]), lam_im=np.asarray(inp["s5_lambda_im"][i]), log_dt=np.asarray(inp["s5_log_dt"][i]),
                  b_re=np.asarray(inp["s5_b_re"][i]), b_im=np.asarray(inp["s5_b_im"][i]), c_re=np.asarray(inp["s5_c_re"][i]),
                  c_im=np.asarray(inp["s5_c_im"][i]), d=np.asarray(inp["s5_d"][i]))
        l2 = [l2_inputs(zL, p2, gh, L) for gh in range(2)]
        for n in ("bpr", "bpi", "cpr", "cpi"):
            d["s5" + n + sfx] = f(np.stack([l2[gh][n] for gh in range(2)]))
        d["s5prm" + sfx] = f(np.stack([l2[gh]["prm"] for gh in range(2)]))
        d["s5d" + sfx] = f(np.stack([l2[gh]["dcol"] for gh in range(2)]))
        p3 = dict(q_gain=np.asarray(inp["da_q_gain"][i]), k_gain=np.asarray(inp["da_k_gain"][i]), lam=np.asarray(inp["da_lambda"][i]),
                  subln_g=np.asarray(inp["da_subln_g"][i]))
        l3 = l3_inputs(zL, zL, zL, p3, 0, L, lam_init)
        d["da_gains" + sfx], d["da_lamv" + sfx], d["da_laminit" + sfx] = l3["gains"], l3["lamv"], l3["laminit"]
        p4 = dict(mu=np.asarray(inp["rw_mu"][i]), w0=np.asarray(inp["rw_w0"][i]), w2=np.asarray(inp["rw_w2"][i]), a0=np.asarray(inp["rw_a0"][i]),
                  a2=np.asarray(inp["rw_a2"][i]), g2=np.asarray(inp["rw_g2"][i]), k_k=np.asarray(inp["rw_k_k"][i]), k_a=np.asarray(inp["rw_k_a"][i]),
                  r_k=np.asarray(inp["rw_r_k"][i]), ln_g=np.asarray(inp["rw_ln_g"][i]), ln_b=np.asarray(inp["rw_ln_b"][i]))
        vg = i > 0
        if vg:
            p4.update(v0=np.asarray(inp["rw_v0"][i - 1]), v1=np.asarray(inp["rw_v1"][i - 1]), v2=np.asarray(inp["rw_v2"][i - 1]))
        l4 = [l4_inputs(zz, p4, hg, L, vfirst=(np.zeros((256, L), np.float32) if vg else None)) for hg in range(2)]
        for n, m in (("mu3", "rw_mu3"), ("w2", "rw_w2p"), ("a2", "rw_a2p"), ("g2", "rw_g2p"), ("pc", "rw_pc"), ("lng", "rw_lng"), ("lnb", "rw_lnb")):
            d[m + sfx] = f(np.stack([l4[hg][n] for hg in range(2)]))
        d["rw_mulo" + sfx] = l4[0]["mulo"]
        if vg:
            d["rw_muvf" + sfx] = l4[0]["muvf"]
            d["rw_v1p" + sfx] = l4[0]["v1"]
            d["rw_v2p" + sfx] = f(np.stack([l4[hg]["v2"] for hg in range(2)]))
            d["rw_v0p" + sfx] = f(np.stack([l4[hg]["v0"] for hg in range(2)]))
        d["w_glu" + sfx] = f(inp["s5_w_glu"][i])
        d["w_br" + sfx] = f(inp["w_branch"][i])
        d["w_out" + sfx] = f(inp["w_out"][i])
        d["g2n" + sfx] = f(np.asarray(inp["norm2_g"][i]).reshape(8, 128).T)
        d["w_fi" + sfx] = f(inp["w_ffn_in"][i])
        d["w_fo" + sfx] = f(inp["w_ffn_out"][i])
    return d


def kernel_fused(inp, runner=None):
    x = np.asarray(inp["x"], dtype=np.float32)
    B, L, _ = x.shape
    depth = inp["w_in"].shape[0]
    nc = build_fused(L, depth)
    common = fused_inputs(inp, L)
    ncore = NCORES if runner is None else B
    maps = [dict(common, x_in=np.ascontiguousarray(x[c % B])) for c in range(ncore)]
    if runner is None:
        res = run_bass_kernel_spmd(nc, maps, core_ids=list(range(NCORES))).results
    else:
        res = runner(nc, maps)
    return np.stack([res[b]["x_final"] for b in range(B)]).astype(np.float32)
```

```python
from contextlib import ExitStack
import numpy as np
import concourse.bass as bass
import concourse.mybir as mybir

F32 = mybir.dt.float32
BF16 = mybir.dt.bfloat16
ALU = mybir.AluOpType
AF = mybir.ActivationFunctionType

ENGS = ["pe", "act", "dve", "pool", "sp"]


class Shared:
    NDQ = 92

    def __init__(self, nc):
        self.nc = nc
        self.stack = ExitStack()
        names = ["e_" + e for e in ENGS] + ["dq%d" % i for i in range(self.NDQ)] + ["cc0", "cc1"]
        self.sems = {n: self.stack.enter_context(nc.semaphore(n)) for n in names}
        self.cnt = {n: 0 for n in names}

    def close(self):
        self.stack.close()


class V:
    def __init__(self, ap):
        self._ap = ap

    def ap(self):
        return self._ap

    def __getitem__(self, idx):
        return self._ap[idx]


class VTok2:
    def __init__(self, aps, tokh):
        self.aps, self.tokh = aps, tokh

    def __getitem__(self, idx):
        idx = list(idx)
        sl = idx[-1]
        j = sl.start // self.tokh
        assert (sl.stop - 1) // self.tokh == j
        idx[-1] = slice(sl.start - j * self.tokh, sl.stop - j * self.tokh)
        return self.aps[j][tuple(idx)]


class Prog:
    _pid = [0]

    def __init__(self, nc, shared=None):
        self.shared = shared
        Prog._pid[0] += 1
        self.pfx = "p%d_" % Prog._pid[0]
        self.nc = nc
        self.ins = []
        self.byeng = {e: [] for e in ENGS}
        self.lastw = {}
        self.rds = {}
        self.stack = ExitStack()
        self.n_ps = 0

    def sb(self, name, shape, dt=F32):
        return self.stack.enter_context(self.nc.sbuf_tensor(self.pfx + name, list(shape), dt))

    def ps(self, name, shape, dt=F32):
        return self.stack.enter_context(self.nc.psum_tensor(self.pfx + name, list(shape), dt))

    def dram(self, name, shape, dt=F32, kind="Internal"):
        return self.nc.dram_tensor(name, list(shape), dt, kind=kind)

    @staticmethod
    def _ov(a, b):
        return a is None or b is None or a == b

    def _norm(self, keys):
        out = []
        for k in keys:
            if isinstance(k, tuple):
                out.append((k[0], None if k[0].startswith("ps") else k[1]))
            else:
                out.append((k, None))
        return out

    def op(self, eng, fn, reads=(), writes=(), dma=False, sk=None, cc=None):
        reads = self._norm(reads)
        writes = self._norm(writes)
        i = len(self.ins)
        deps = set()
        for (t, s) in reads:
            for s2, w in self.lastw.get(t, {}).items():
                if self._ov(s, s2):
                    deps.add(w)
        for (t, s) in writes:
            for s2, w in self.lastw.get(t, {}).items():
                if self._ov(s, s2):
                    deps.add(w)
            for s2, rl in self.rds.get(t, {}).items():
                if self._ov(s, s2):
                    deps.update(rl)
        deps.discard(i)
        for (t, s) in reads:
            self.rds.setdefault(t, {}).setdefault(s, []).append(i)
        for (t, s) in writes:
            d = self.lastw.setdefault(t, {})
            r = self.rds.setdefault(t, {})
            if s is None:
                d.clear()
                r.clear()
            else:
                r[s] = []
            d[s] = i
        if eng == "pe":
            deps = {d for d in deps if self.ins[d]["eng"] != "pe" or self.ins[d]["dma"]}
        self.ins.append(dict(eng=eng, fn=fn, deps=deps, dma=dma, cc=cc, key=(writes[0] if writes else None),
                             sk=(sk if sk is not None else (writes[0] if writes else None))))
        self.byeng[eng].append(i)
        return i

    def mm(self, out, lhsT, rhs, reads, writes, start=True, stop=True):
        return self.op("pe", lambda e: e.matmul(out, lhsT, rhs, start=start, stop=stop), reads, writes)

    def tr(self, out, in_, ident, reads, writes):
        return self.op("pe", lambda e: e.transpose(out, in_, ident), reads, writes)

    def act(self, out, in_, func, reads, writes, eng="act", **kw):
        return self.op(eng, lambda e: e.activation(out, in_, func, **kw), reads, writes)

    def tt(self, eng, out, in0, in1, op, reads, writes):
        return self.op(eng, lambda e: e.tensor_tensor(out, in0, in1, op), reads, writes)

    def ts(self, eng, out, in0, s1, s2, op0, op1, reads, writes, **kw):
        if op1 is None:
            return self.op(eng, lambda e: e.tensor_scalar(out, in0, s1, None, op0, **kw), reads, writes)
        return self.op(eng, lambda e: e.tensor_scalar(out, in0, s1, s2, op0, op1, **kw), reads, writes)

    def stt(self, out, in0, scalar, in1, op0, op1, reads, writes):
        return self.op("dve", lambda e: e.scalar_tensor_tensor(out, in0, scalar, in1, op0, op1), reads, writes)

    def cp(self, eng, out, in_, reads, writes):
        if eng == "act":
            return self.op(eng, lambda e: e.copy(out, in_), reads, writes)
        return self.op(eng, lambda e: e.tensor_copy(out, in_), reads, writes)

    def dma(self, out, in_, reads, writes, eng="sp", sk=None, **kw):
        return self.op(eng, lambda e: e.dma_start(out, in_, **kw), reads, writes, dma=True, sk=sk)

    def memset(self, eng, ap, val, writes):
        return self.op(eng, lambda e: e.memset(ap, val), (), writes)

    def emit(self, final_wait_keys=()):
        nc = self.nc
        ins = self.ins
        sh = self.shared
        own = sh is None
        if own:
            sh = Shared(nc)
        has_dep = [False] * len(ins)
        for i, d in enumerate(ins):
            for j in d["deps"]:
                has_dep[j] = True
        final_ids = [i for i, d in enumerate(ins) if d["dma"] and d["key"] is not None and d["key"][0] in final_wait_keys]
        for i, d in enumerate(ins):
            if d["dma"]:
                has_dep[i] = True
        for e in ENGS:
            for i in reversed(self.byeng[e]):
                if not ins[i]["dma"]:
                    has_dep[i] = True
                    break
        start = dict(sh.cnt)
        dma_sem_of = {}
        dma_eng = {}
        nsp, npool = [0], [0]
        sig = [None] * len(ins)
        for e in ENGS:
            for i in self.byeng[e]:
                d = ins[i]
                if not has_dep[i]:
                    continue
                if d["cc"] is not None:
                    sn = "cc%d" % d["cc"]
                    sh.cnt[sn] += 1
                    sig[i] = (sn, sh.cnt[sn])
                elif d["dma"]:
                    k = d["sk"]
                    if k not in dma_sem_of:
                        if e == "pool":
                            npool[0] += 1
                            assert npool[0] <= 34, "too many pool dma sem keys"
                            dma_sem_of[k] = "dq%d" % (Shared.NDQ - npool[0])
                        else:
                            nsp[0] += 1
                            assert nsp[0] <= Shared.NDQ - 34, "too many sp dma sem keys"
                            dma_sem_of[k] = "dq%d" % (nsp[0] - 1)
                        dma_eng[k] = e
                    assert dma_eng[k] == e, ("dma sem key used from two queues", k)
                    sn = dma_sem_of[k]
                    sh.cnt[sn] += 16
                    sig[i] = (sn, sh.cnt[sn])
                else:
                    sh.cnt["e_" + e] += 1
                    sig[i] = ("e_" + e, sh.cnt["e_" + e])
        sems = sh.sems
        block = self.stack.enter_context(nc.Block())
        engobj = dict(pe="tensor", act="scalar", dve="vector", pool="gpsimd", sp="sync")
        last_phase_final = [(sig[i][0], sig[i][1]) for i in final_ids]

        def make(e):
            def body(eng):
                waited = {}
                for sn, v in start.items():
                    if v > 0 and sn != "e_" + e:
                        eng.wait_ge(sems[sn], v)
                    waited[sn] = v
                for i in self.byeng[e]:
                    d = ins[i]
                    need = {}
                    for j in d["deps"]:
                        s_ = sig[j]
                        assert s_ is not None
                        if need.get(s_[0], 0) < s_[1]:
                            need[s_[0]] = s_[1]
                    for sn, v in need.items():
                        if waited.get(sn, 0) < v:
                            eng.wait_ge(sems[sn], v)
                            waited[sn] = v
                    inst = d["fn"](eng)
                    if sig[i] is not None:
                        inst.then_inc(sems[sig[i][0]], 16 if (d["dma"] and d["cc"] is None) else 1)
                if e == "sp":
                    for (sn, v) in last_phase_final:
                        if waited.get(sn, 0) < v:
                            eng.wait_ge(sems[sn], v)
                            waited[sn] = v
            return body

        for e in ENGS:
            getattr(block, engobj[e])(make(e))
        self.stack.close()
        if own:
            sh.close()
        return nc


from concourse.bass_utils import run_bass_kernel_spmd

NCORES = 8
D = 1024
NIN = 6816
EPS = 1e-6


def _ctx(ctx):
    if ctx is None:
        nc = bass.Bass("TRN2", target_bir_lowering=False)
        return nc, Prog(nc), None
    nc, shared, io = ctx
    return nc, Prog(nc, shared), io


def _mkI(nc, io):
    if io is None:
        return lambda n, s: nc.dram_tensor(n, list(s), F32, kind="ExternalInput")
    return lambda n, s: io[n]


def _mkO(nc, io, n, s):
    if io is None:
        return nc.dram_tensor(n, list(s), F32, kind="ExternalOutput")
    return io[n]


def _fin(P, nc, io, outs):
    if io is None:
        P.emit(final_wait_keys=outs)
    else:
        P.emit(final_wait_keys=(outs if io.get("_final") else ()))
    return nc


def ap3(t, off, dims):
    return bass.AP(t, off, [list(d) for d in dims])


def build_L1(TOK=2048):
    nc = bass.Bass("TRN2", target_bir_lowering=False)
    P = Prog(nc)
    NT = TOK // 128
    x_tok = nc.dram_tensor("x_tok", [TOK, D], F32, kind="ExternalInput")
    xT = nc.dram_tensor("xT", [D, TOK], F32, kind="ExternalInput")
    w = nc.dram_tensor("w", [D, NIN], F32, kind="ExternalInput")
    g = nc.dram_tensor("g", [128, 8], F32, kind="ExternalInput")
    proj = nc.dram_tensor("proj", [TOK, NIN], F32, kind="ExternalOutput")

    g_sb = P.sb("g_sb", [128, 8])
    rstd = P.sb("rstd", [128, NT])
    ss = P.sb("ss", [128, NT])
    xt_in = [P.sb("xt_in%d" % i, [128, D]) for i in range(2)]
    junk = P.sb("junk", [128, D])
    xTs = P.sb("xTs", [128, 8, TOK])
    xg = P.sb("xg", [128, 8, TOK], BF16)
    wf = [P.sb("wf%d" % i, [128, 8, 512]) for i in range(2)]
    wb = [P.sb("wb%d" % i, [128, 8, 512], BF16) for i in range(2)]
    ost = [P.sb("ost%d" % i, [128, 512]) for i in range(3)]
    pss = [P.ps("ps%d" % i, [128, 512]) for i in range(4)]

    P.dma(g_sb[:, :], g[:, :], [], ["g_sb"])
    for t in range(NT):
        b = xt_in[t % 2]
        bn = "xt_in%d" % (t % 2)
        P.dma(b[:, :], x_tok[t * 128:(t + 1) * 128, :], [], [bn], eng="sp")
        P.act(junk[:, :], b[:, :], AF.Square, [bn], ["junk", ("ss", t)], accum_out=ss[:, t:t + 1])
    P.ts("dve", rstd[:, :], ss[:, :], 1.0 / D, EPS, ALU.mult, ALU.add, ["ss"], ["rstd"])
    P.act(rstd[:, :], rstd[:, :], AF.Sqrt, ["rstd"], ["rstd"])
    P.op("dve", lambda e: e.reciprocal(rstd[:, :], rstd[:, :]), ["rstd"], ["rstd"])
    xTv = xT.ap().rearrange("(c p) t -> p c t", p=128)
    for c in range(8):
        P.dma(xTs[:, c, :], xTv[:, c, :], [], [("xTs", c)], eng="pool")
        P.ts("dve", xg[:, c, :], xTs[:, c, :], g_sb[:, c:c + 1], None, ALU.mult, None,
             [("xTs", c), "g_sb"], [("xg", c)])
    wv = w.ap().rearrange("(c p) n -> p c n", p=128)
    nblk = (NIN + 511) // 512
    k = 0
    for cb in range(nblk):
        c0 = cb * 512
        cw = min(512, NIN - c0)
        wfb, wbb = wf[cb % 2], wb[cb % 2]
        wfn, wbn = "wf%d" % (cb % 2), "wb%d" % (cb % 2)
        P.dma(wfb[:, :, :cw], wv[:, :, c0:c0 + cw], [], [wfn], eng="sp")
        for c in range(8):
            P.cp("pool" if c % 2 else "dve", wbb[:, c, :cw], wfb[:, c, :cw], [wfn], [(wbn, c)])
        for t in range(NT):
            ps = pss[k % 4]
            psn = "ps%d" % (k % 4)
            for c in range(8):
                P.mm(ps[:, :cw], xg[:, c, t * 128:(t + 1) * 128], wbb[:, c, :cw],
                     [("xg", c), (wbn, c)], [psn], start=(c == 0), stop=(c == 7))
            o = ost[k % 3]
            on = "ost%d" % (k % 3)
            P.act(o[:, :cw], ps[:, :cw], AF.Copy, [psn, "rstd"], [on], scale=rstd[:, t:t + 1])
            P.dma(proj[t * 128:(t + 1) * 128, c0:c0 + cw], o[:, :cw], [on], [("proj", k)], eng="sp", sk=on)
            k += 1
    P.emit(final_wait_keys=("proj",))
    return nc


def build_L1F(L=4096, ctx=None):
    nc, P, io = _ctx(ctx)
    I = _mkI(nc, io)
    x_tok = I("x_tok", [L, D])
    w = I("w", [D, NIN])
    g = I("g", [128, 8])
    c_ident = I("c_ident", [128, 128])
    FT = _mkO(nc, io, "FT", [6304, L])
    VT = _mkO(nc, io, "VT", [L, 512])
    HT = min(L, 1024)
    NH = L // HT
    NTT = HT // 128
    NB = L // 512
    g_sb = P.sb("g_sb", [128, 8])
    P.dma(g_sb[:, :], g[:, :], [], ["g_sb"])
    identf = P.sb("identf", [128, 128])
    P.dma(identf[:, :], c_ident[:, :], [], ["identf"])
    identb = P.sb("identb", [128, 128], BF16)
    P.cp("dve", identb[:, :], identf[:, :], ["identf"], ["identb"])
    hT = P.sb("hT", [128, 8, L], BF16)
    x1 = [P.sb("x1_%d" % i, [128, D]) for i in range(3)]
    junk = P.sb("junk", [128, D], BF16)
    ss = P.sb("ss", [128, NTT])
    hb = [P.sb("hb%d" % i, [128, D], BF16) for i in range(2)]
    wf = [P.sb("wf%d" % i, [128, 8, 512]) for i in range(2)]
    wb = [P.sb("wb%d" % i, [128, 8, 512], BF16) for i in range(2)]
    ost = [P.sb("ost%d" % i, [128, 512]) for i in range(4)]
    pss = [P.ps("ps%d" % i, [128, 512]) for i in range(6)]
    psb = P.ps("psb", [128, 1024], BF16)
    for half in range(NH):
        t0 = half * HT
        xs_ = []
        for tt_ in range(NTT):
            xi, xn = x1[tt_ % 3], "x1_%d" % (tt_ % 3)
            P.dma(xi[:, :], x_tok[t0 + tt_ * 128:t0 + (tt_ + 1) * 128, :], [], [xn], eng="sp")
            P.act(junk[:, :], xi[:, :], AF.Square, [xn], ["junk", ("ss", tt_)], accum_out=ss[:, tt_:tt_ + 1])
        P.ts("dve", ss[:, :], ss[:, :], 1.0 / D, EPS, ALU.mult, ALU.add, ["ss"], ["ss"])
        P.act(ss[:, :], ss[:, :], AF.Sqrt, ["ss"], ["ss"])
        P.op("dve", lambda e: e.reciprocal(ss[:, :], ss[:, :]), ["ss"], ["ss"])
        for tt_ in range(NTT):
            xi, xn = x1[tt_ % 3], "x1_%d" % (tt_ % 3)
            P.dma(xi[:, :], x_tok[t0 + tt_ * 128:t0 + (tt_ + 1) * 128, :], [], [xn], eng="sp")
            h_, hn = hb[tt_ % 2], "hb%d" % (tt_ % 2)
            P.act(h_[:, :], xi[:, :], AF.Copy, [xn, "ss"], [hn], scale=ss[:, tt_:tt_ + 1])
            for c in range(8):
                P.tr(psb[:, c * 128:(c + 1) * 128], h_[:, c * 128:(c + 1) * 128], identb[:, :], [hn, "identb"], ["psb"])
            for c in range(8):
                P.ts("dve", hT[:, c, t0 + tt_ * 128:t0 + (tt_ + 1) * 128], psb[:, c * 128:(c + 1) * 128], g_sb[:, c:c + 1], None, ALU.mult, None,
                     ["psb", "g_sb"], [("hT", c)])
    wv = w.ap().rearrange("(c p) n -> p c n", p=128)
    k = 0
    nblk = (NIN + 511) // 512
    for cb in range(nblk):
        c0 = cb * 512
        cw = min(512, NIN - c0)
        wfb, wbb = wf[cb % 2], wb[cb % 2]
        wfn, wbn = "wf%d" % (cb % 2), "wb%d" % (cb % 2)
        P.dma(wfb[:, :, :cw], wv[:, :, c0:c0 + cw], [], [wfn], eng=("sp" if cb % 2 else "pool"))
        for c in range(8):
            P.cp("pool" if c % 2 else "dve", wbb[:, c, :cw], wfb[:, c, :cw], [wfn], [(wbn, c)])
        if c0 == 1536:
            for t in range(L // 128):
                ps, psn = pss[k % 6], "ps%d" % (k % 6)
                for c in range(8):
                    P.mm(ps[:, :], hT[:, c, t * 128:(t + 1) * 128], wbb[:, c, :], [("hT", c), (wbn, c)], [psn], start=(c == 0), stop=(c == 7))
                o, on = ost[k % 4], "ost%d" % (k % 4)
                P.cp("act" if k % 2 else "dve", o[:, :], ps[:, :], [psn], [on])
                P.dma(VT[t * 128:(t + 1) * 128, :], o[:, :], [on], [("VT", t)], sk=on, eng=("sp" if k % 2 else "pool"))
                k += 1
            continue
        r0 = c0 if c0 < 1536 else c0 - 512
        for ct in range((cw + 127) // 128):
            mw = min(128, cw - ct * 128)
            for b5 in range(NB):
                bs = slice(b5 * 512, b5 * 512 + 512)
                ps, psn = pss[k % 6], "ps%d" % (k % 6)
                for c in range(8):
                    P.mm(ps[:mw, :], wbb[:, c, ct * 128:ct * 128 + mw], hT[:, c, bs], [("hT", c), (wbn, c)], [psn], start=(c == 0), stop=(c == 7))
                o, on = ost[k % 4], "ost%d" % (k % 4)
                P.cp("act" if k % 2 else "dve", o[:mw, :], ps[:mw, :], [psn], [on])
                P.dma(FT[r0 + ct * 128:r0 + ct * 128 + mw, bs], o[:mw, :], [on], [("FT", k)], sk=on, eng=("sp" if k % 2 else "pool"))
                k += 1
    return _fin(P, nc, io, ("FT", "VT"))


def build_L1G(L=2048, ctx=None):
    nc, P, io = _ctx(ctx)
    I = _mkI(nc, io)
    x_tok = I("x_tok", [L, D])
    w = I("w", [D, NIN])
    g = I("g", [128, 8])
    c_ident = I("c_ident", [128, 128])
    G = _mkO(nc, io, "G", [3744, L])
    GL = _mkO(nc, io, "GL", [3072, L])
    VT = V(G.ap()[3232:3744, :].rearrange("a (b c) -> (a b) c", c=512))
    HT = min(L, 1024)
    NH = L // HT
    NTT = HT // 128
    NB = L // 512
    g_sb = P.sb("g_sb", [128, 8])
    P.dma(g_sb[:, :], g[:, :], [], ["g_sb"])
    identf = P.sb("identf", [128, 128])
    P.dma(identf[:, :], c_ident[:, :], [], ["identf"])
    identb = P.sb("identb", [128, 128], BF16)
    P.cp("dve", identb[:, :], identf[:, :], ["identf"], ["identb"])
    hT = P.sb("hT", [128, 8, L], BF16)
    x1 = [P.sb("x1_%d" % i, [128, D]) for i in range(3)]
    junk = P.sb("junk", [128, D], BF16)
    ss = P.sb("ss", [128, NTT])
    hb = [P.sb("hb%d" % i, [128, D], BF16) for i in range(2)]
    wf = [P.sb("wf%d" % i, [128, 8, 512]) for i in range(2)]
    wb = [P.sb("wb%d" % i, [128, 8, 512], BF16) for i in range(2)]
    ost = [P.sb("ost%d" % i, [128, 512]) for i in range(4)]
    pss = [P.ps("ps%d" % i, [128, 512]) for i in range(6)]
    psb = P.ps("psb", [128, 1024], BF16)
    for half in range(NH):
        t0 = half * HT
        xs_ = []
        for tt_ in range(NTT):
            xi, xn = x1[tt_ % 3], "x1_%d" % (tt_ % 3)
            P.dma(xi[:, :], x_tok[t0 + tt_ * 128:t0 + (tt_ + 1) * 128, :], [], [xn], eng="sp")
            P.act(junk[:, :], xi[:, :], AF.Square, [xn], ["junk", ("ss", tt_)], accum_out=ss[:, tt_:tt_ + 1])
        P.ts("dve", ss[:, :], ss[:, :], 1.0 / D, EPS, ALU.mult, ALU.add, ["ss"], ["ss"])
        P.act(ss[:, :], ss[:, :], AF.Sqrt, ["ss"], ["ss"])
        P.op("dve", lambda e: e.reciprocal(ss[:, :], ss[:, :]), ["ss"], ["ss"])
        for tt_ in range(NTT):
            xi, xn = x1[tt_ % 3], "x1_%d" % (tt_ % 3)
            P.dma(xi[:, :], x_tok[t0 + tt_ * 128:t0 + (tt_ + 1) * 128, :], [], [xn], eng="sp")
            h_, hn = hb[tt_ % 2], "hb%d" % (tt_ % 2)
            P.act(h_[:, :], xi[:, :], AF.Copy, [xn, "ss"], [hn], scale=ss[:, tt_:tt_ + 1])
            for c in range(8):
                P.tr(psb[:, c * 128:(c + 1) * 128], h_[:, c * 128:(c + 1) * 128], identb[:, :], [hn, "identb"], ["psb"])
            for c in range(8):
                P.ts("dve", hT[:, c, t0 + tt_ * 128:t0 + (tt_ + 1) * 128], psb[:, c * 128:(c + 1) * 128], g_sb[:, c:c + 1], None, ALU.mult, None,
                     ["psb", "g_sb"], [("hT", c)])
    wv = w.ap().rearrange("(c p) n -> p c n", p=128)
    k = 0
    nblk = (NIN + 511) // 512
    for cb in range(nblk):
        c0 = cb * 512
        cw = min(512, NIN - c0)
        wfb, wbb = wf[cb % 2], wb[cb % 2]
        wfn, wbn = "wf%d" % (cb % 2), "wb%d" % (cb % 2)
        P.dma(wfb[:, :, :cw], wv[:, :, c0:c0 + cw], [], [wfn], eng=("sp" if cb % 2 else "pool"))
        for c in range(8):
            P.cp("pool" if c % 2 else "dve", wbb[:, c, :cw], wfb[:, c, :cw], [wfn], [(wbn, c)])
        if c0 == 1536:
            for t in range(L // 128):
                ps, psn = pss[k % 6], "ps%d" % (k % 6)
                for c in range(8):
                    P.mm(ps[:, :], hT[:, c, t * 128:(t + 1) * 128], wbb[:, c, :], [("hT", c), (wbn, c)], [psn], start=(c == 0), stop=(c == 7))
                o, on = ost[k % 4], "ost%d" % (k % 4)
                P.cp("act" if k % 2 else "dve", o[:, :], ps[:, :], [psn], [on])
                P.dma(VT[t * 128:(t + 1) * 128, :], o[:, :], [on], [("VT", t)], sk=on, eng=("sp" if k % 2 else "pool"))
                k += 1
            continue
        r0 = c0 if c0 < 1536 else c0 - 512
        for ct in range((cw + 127) // 128):
            mw = min(128, cw - ct * 128)
            for b5 in range(NB):
                bs = slice(b5 * 512, b5 * 512 + 512)
                ps, psn = pss[k % 6], "ps%d" % (k % 6)
                for c in range(8):
                    P.mm(ps[:mw, :], wbb[:, c, ct * 128:ct * 128 + mw], hT[:, c, bs], [("hT", c), (wbn, c)], [psn], start=(c == 0), stop=(c == 7))
                o, on = ost[k % 4], "ost%d" % (k % 4)
                P.cp("act" if k % 2 else "dve", o[:mw, :], ps[:mw, :], [psn], [on])
                ra, rb = r0 + ct * 128, r0 + ct * 128 + mw
                if ra < 3232:
                    n_ = min(rb, 3232) - ra
                    P.dma(G[ra:ra + n_, bs], o[0:n_, :], [on], [("G", k)], sk=on, eng=("sp" if k % 2 else "pool"))
                if rb > 3232:
                    p0 = max(ra, 3232) - ra
                    P.dma(GL[max(ra, 3232) - 3232:rb - 3232, bs], o[p0:mw, :], [on], [("GL", k)], sk=on, eng=("sp" if k % 2 else "pool"))
                k += 1
    return _fin(P, nc, io, ("G", "GL", "VT"))


def rw_consts(L):
    T = 64
    i = np.arange(128)
    same = (i[:, None] // T) == (i[None, :] // T)
    su = (same & (i[:, None] < i[None, :])).astype(np.float32)
    iu = (same & (i[:, None] <= i[None, :])).astype(np.float32)
    sl = (same & (i[:, None] > i[None, :])).astype(np.float32)
    mask4 = np.concatenate([su, iu, su, iu], 1)
    reset = np.ones((128, min(L, 1024)), np.float32)
    reset[:, ::T] = 0.0
    bones = ((i[:, None] // 64) == (i[None, :] // 64)).astype(np.float32)
    hsel = np.zeros((128, 2), np.float32)
    hsel[:64, 0] = 1
    hsel[64:, 1] = 1
    return dict(c_mask4=mask4, c_sl=sl, c_reset=reset, c_bones=bones, c_hsel=hsel,
                c_ident=np.eye(128, dtype=np.float32))


def build_L4(L=4096, has_vgate=False, LH=None, ctx=None, fm_out=False):
    nc, P, io = _ctx(ctx)
    NB = L // 128
    NCH = L // 64
    NCB = (L + 511) // 512
    I = _mkI(nc, io)
    zr, zk, zv = I("zr", [256, L]), I("zk", [256, L]), I("zv", [256, L])
    zw, za, zg = I("zw", [32, L]), I("za", [32, L]), I("zg", [96, L])
    mu3 = I("mu3", [128, 6])
    mulo = I("mulo", [96, 3])
    w2, a2, g2 = I("w2", [32, 256]), I("a2", [32, 256]), I("g2", [96, 256])
    pc = I("pc", [128, 10])
    lng, lnb = I("lng", [128, 256]), I("lnb", [128, 256])
    c_mask4, c_sl, c_reset = I("c_mask4", [128, 512]), I("c_sl", [128, 128]), I("c_reset", [128, min(L, 1024)])
    c_bones, c_hsel, c_ident = I("c_bones", [128, 128]), I("c_hsel", [128, 2]), I("c_ident", [128, 128])
    if has_vgate:
        zvfull = I("zvfull", [512, L])
        muvf = I("muvf", [128, 4])
        v1 = I("v1", [128, 4, 32])
        v2 = I("v2", [32, 256])
        v0 = I("v0", [128, 2])
        vfirst_in = I("vfirst_in", [256, L])
    y_out = _mkO(nc, io, "y_out", ([256, L] if fm_out else [L, 256]))
    vfirst_out = _mkO(nc, io, "vfirst_out", [256, L])

    def ld(name, src, shape, dt=F32, eng="sp"):
        t = P.sb(name, shape, dt)
        if len(shape) == 2:
            P.dma(t[:, :], src[:, :], [], [name], eng=eng)
        else:
            P.dma(t[:, :, :], src[:, :, :], [], [name], eng=eng)
        return t
    mu3s, mulos = ld("mu3s", mu3, [128, 6]), ld("mulos", mulo, [96, 3])
    pcs = ld("pcs", pc, [128, 10])
    lngs, lnbs = ld("lngs", lng, [128, 256]), ld("lnbs", lnb, [128, 256])
    mask4, msl, reset = ld("mask4", c_mask4, [128, 512]), ld("msl", c_sl, [128, 128]), ld("reset", c_reset, [128, min(L, 1024)])
    bones, hself, identf = ld("bones", c_bones, [128, 128]), ld("hself", c_hsel, [128, 2]), ld("identf", c_ident, [128, 128])
    w2f, a2f, g2f = ld("w2f", w2, [32, 256]), ld("a2f", a2, [32, 256]), ld("g2f", g2, [96, 256])
    identb = P.sb("identb", [128, 128], BF16)
    hselb = P.sb("hselb", [128, 2], BF16)
    w2b, a2b, g2b = P.sb("w2b", [32, 256], BF16), P.sb("a2b", [32, 256], BF16), P.sb("g2b", [96, 256], BF16)
    P.cp("dve", identb[:, :], identf[:, :], ["identf"], ["identb"])
    P.cp("dve", hselb[:, :], hself[:, :], ["hself"], ["hselb"])
    P.cp("dve", w2b[:, :], w2f[:, :], ["w2f"], ["w2b"])
    P.cp("dve", a2b[:, :], a2f[:, :], ["a2f"], ["a2b"])
    P.cp("dve", g2b[:, :], g2f[:, :], ["g2f"], ["g2b"])

    LH = LH or min(L, 1024)
    NQH = L // LH
    NCBH = (LH + 511) // 512
    S = [P.sb("S%d" % i, [128, LH + 1]) for i in range(5)]
    SN = ["S%d" % i for i in range(5)]
    AT, BT, KT, RT, VB, RK = [P.sb(n, [128, 2, L], BF16) for n in ("AT", "BT", "KT", "RT", "VB", "RK")]
    gam = P.sb("gam", [128, 2, NCH])
    lo_w, lo_a, lo_g = P.sb("lo_w", [32, LH], BF16), P.sb("lo_a", [32, LH], BF16), P.sb("lo_g", [96, L], BF16)
    resetb = P.sb("resetb", [128, LH], BF16)
    P.cp("dve", resetb[:, :], reset[:, 0:LH], ["reset"], ["resetb"])
    pss = [P.ps("ps%d" % i, [128, 512]) for i in range(7)]
    psb = P.ps("psb", [128, 1024], BF16)
    pk = [0]

    def nextps():
        k = pk[0] % 4
        pk[0] += 1
        return pss[k], "ps%d" % k

    def shift_mix(dst, dn, src_rows, rows, mucol, scratch, scn, raw, rawn, t0, eng="sp"):
        if t0 == 0:
            P.memset("pool", raw[:rows, 0:1], 0.0, [(rawn, 0)])
            P.dma(raw[:rows, 1:LH + 1], src_rows[:, 0:LH], [], [(rawn, 1)], eng=eng)
        else:
            P.dma(raw[:rows, 0:LH + 1], src_rows[:, t0 - 1:t0 + LH], [], [rawn], eng=eng)
        P.tt("dve", scratch[:rows, 0:LH], raw[:rows, 0:LH], raw[:rows, 1:LH + 1], ALU.subtract, [rawn], [scn])
        P.stt(dst[:rows, 0:LH], scratch[:rows, 0:LH], mucol, raw[:rows, 1:LH + 1], ALU.mult, ALU.add, [scn, rawn, "mu3s", "mulos", "muvfs"], [dn])

    if has_vgate:
        v1f = ld("v1f", v1, [128, 4, 32])
        v1b = P.sb("v1b", [128, 4, 32], BF16)
        P.cp("dve", v1b[:, :, :], v1f[:, :, :], ["v1f"], ["v1b"])
        v2f = ld("v2f", v2, [32, 256])
        v2b = P.sb("v2b", [32, 256], BF16)
        P.cp("dve", v2b[:, :], v2f[:, :], ["v2f"], ["v2b"])
        v0s = ld("v0s", v0, [128, 2])
        muvfs = ld("muvfs", muvf, [128, 4])
        vfb = P.sb("vfb", [128, 4, LH], BF16)
        t1b = P.sb("t1b", [32, LH], BF16)

    W = slice(0, LH)
    for hq in range(NQH):
        t0 = hq * LH
        TS = slice(t0, t0 + LH)
        shift_mix(S[2], SN[2], zw, 32, mulos[:32, 0:1], S[1], SN[1], S[0], SN[0], t0)
        P.act(lo_w[:, :], S[2][:32, W], AF.Tanh, [SN[2]], ["lo_w"])
        shift_mix(S[2], SN[2], za, 32, mulos[:32, 1:2], S[1], SN[1], S[0], SN[0], t0)
        P.cp("dve", lo_a[:, :], S[2][:32, W], [SN[2]], ["lo_a"])
        shift_mix(S[2], SN[2], zg, 96, mulos[:96, 2:3], S[1], SN[1], S[0], SN[0], t0)
        P.act(lo_g[:, TS], S[2][:96, W], AF.Sigmoid, [SN[2]], [("lo_g", hq)])
        if has_vgate:
            for c in range(4):
                shift_mix(S[2], SN[2], zvfull[c * 128:(c + 1) * 128, :], 128, muvfs[:, c:c + 1], S[1], SN[1], S[0], SN[0], t0)
                P.cp("pool", vfb[:, c, :], S[2][:, W], [SN[2]], [("vfb", c)])
            for cb in range(NCBH):
                c0 = cb * 512
                cw = min(512, LH - c0)
                ps, psn = nextps()
                for c in range(4):
                    P.mm(ps[:32, :cw], v1b[:, c, :], vfb[:, c, c0:c0 + cw], ["v1b", ("vfb", c)], [psn], start=(c == 0), stop=(c == 3))
                P.cp("act", t1b[:, c0:c0 + cw], ps[:32, :cw], [psn], [("t1b", cb)])
        for p in range(2):
            rows = slice(p * 128, (p + 1) * 128)
            col = lambda j: pcs[:, 2 * j + p:2 * j + p + 1]
            shift_mix(S[2], SN[2], zv[rows, :], 128, mu3s[:, 4 + p:5 + p], S[1], SN[1], S[0], SN[0], t0)
            if has_vgate:
                for cb in range(NCBH):
                    c0 = cb * 512
                    cw = min(512, LH - c0)
                    ps, psn = nextps()
                    P.mm(ps[:, :cw], v2b[:, p * 128:(p + 1) * 128], t1b[:, c0:c0 + cw], ["v2b", ("t1b", cb)], [psn])
                    P.act(S[3][:, c0:c0 + cw], ps[:, :cw], AF.Sigmoid, [psn, "v0s"], [(SN[3], cb)], bias=v0s[:, p:p + 1])
                P.dma(S[0][:, W], vfirst_in[rows, TS], [], [SN[0]])
                P.tt("dve", S[1][:, W], S[0][:, W], S[2][:, W], ALU.subtract, [SN[0], SN[2]], [SN[1]])
                P.tt("dve", S[1][:, W], S[1][:, W], S[3][:, W], ALU.mult, [SN[1], SN[3]], [SN[1]])
                P.tt("dve", S[2][:, W], S[2][:, W], S[1][:, W], ALU.add, [SN[2], SN[1]], [SN[2]])
                P.dma(vfirst_out[rows, TS], S[0][:, W], [SN[0]], [("vfirst_out", hq * 2 + p)], sk="vfo")
            else:
                P.dma(vfirst_out[rows, TS], S[2][:, W], [SN[2]], [("vfirst_out", hq * 2 + p)], sk="vfo")
            P.cp("pool", VB[:, p, TS], S[2][:, W], [SN[2]], [("VB", p)])
            for cb in range(NCBH):
                c0 = cb * 512
                cw = min(512, LH - c0)
                ps, psn = nextps()
                P.mm(ps[:, :cw], a2b[:, p * 128:(p + 1) * 128], lo_a[:, c0:c0 + cw], ["a2b", "lo_a"], [psn])
                P.act(S[3][:, c0:c0 + cw], ps[:, :cw], AF.Sigmoid, [psn, "pcs"], [(SN[3], cb)], bias=col(1))
                ps, psn = nextps()
                P.mm(ps[:, :cw], w2b[:, p * 128:(p + 1) * 128], lo_w[:, c0:c0 + cw], ["w2b", "lo_w"], [psn])
                P.act(S[4][:, c0:c0 + cw], ps[:, :cw], AF.Sigmoid, [psn, "pcs"], [(SN[4], cb)], bias=col(0))
            P.ts("dve", S[4][:, W], S[4][:, W], -float(np.exp(-0.5)), None, ALU.mult, None, [SN[4]], [SN[4]])
            shift_mix(S[2], SN[2], zk[rows, :], 128, mu3s[:, 2 + p:3 + p], S[1], SN[1], S[0], SN[0], t0)
            P.ts("dve", S[0][:, W], S[2][:, W], col(2), None, ALU.mult, None, [SN[2], "pcs"], [SN[0]])
            P.tt("pool", S[1][:, W], S[0][:, W], S[0][:, W], ALU.mult, [SN[0]], [SN[1]])
            for cb in range(NCBH):
                c0 = cb * 512
                cw = min(512, LH - c0)
                ps, psn = nextps()
                P.mm(ps[:, :cw], bones[:, :], S[1][:, c0:c0 + cw], ["bones", SN[1]], [psn])
                P.ts("dve", S[1][:, c0:c0 + cw], ps[:, :cw], 1e-24, None, ALU.max, None, [psn], [(SN[1], cb)])
            P.act(S[1][:, W], S[1][:, W], AF.Sqrt, [SN[1]], [SN[1]])
            P.op("dve", lambda e: e.reciprocal(S[1][:, W], S[1][:, W]), [SN[1]], [SN[1]])
            P.tt("dve", S[0][:, W], S[0][:, W], S[1][:, W], ALU.mult, [SN[0], SN[1]], [SN[0]])
            P.ts("dve", S[1][:, W], S[3][:, W], -1.0, col(3), ALU.add, ALU.mult, [SN[3], "pcs"], [SN[1]])
            P.stt(S[2][:, W], S[1][:, W], 1.0, S[2][:, W], ALU.add, ALU.mult, [SN[1], SN[2]], [SN[2]])
            P.op("dve", lambda e: e.tensor_tensor_scan(S[1][:, W], resetb[:, :], S[4][:, W], 0.0, ALU.mult, ALU.add),
                 ["resetb", SN[4]], [SN[1]])
            P.tt("pool", S[4][:, W], S[1][:, W], S[4][:, W], ALU.subtract, [SN[1], SN[4]], [SN[4]])
            P.act(S[4][:, W], S[4][:, W], AF.Exp, [SN[4]], [SN[4]])
            P.stt(AT[:, p, TS], S[0][:, W], -1.0, S[4][:, W], ALU.mult, ALU.mult, [SN[0], SN[4]], [("AT", p)])
            P.act(S[4][:, W], S[1][:, W], AF.Exp, [SN[1]], [SN[4]], scale=-1.0)
            P.tt("dve", S[0][:, W], S[0][:, W], S[3][:, W], ALU.mult, [SN[0], SN[3]], [SN[0]])
            P.tt("dve", BT[:, p, TS], S[0][:, W], S[4][:, W], ALU.mult, [SN[0], SN[4]], [("BT", p)])
            P.tt("pool", KT[:, p, TS], S[2][:, W], S[4][:, W], ALU.mult, [SN[2], SN[4]], [("KT", p)])
            P.act(S[4][:, W], S[1][:, W], AF.Exp, [SN[1]], [SN[4]])
            gv = ap3(S[4], 63, [[LH + 1, 128], [64, LH // 64]])
            P.cp("dve", gam[:, p, t0 // 64:(t0 + LH) // 64], gv, [SN[4]], [("gam", p)])
            shift_mix(S[0], SN[0], zr[rows, :], 128, mu3s[:, p:p + 1], S[1], SN[1], S[3], SN[3], t0)
            P.tt("dve", RT[:, p, TS], S[0][:, W], S[4][:, W], ALU.mult, [SN[0], SN[4]], [("RT", p)])
            P.stt(RK[:, p, TS], S[0][:, W], col(4), S[2][:, W], ALU.mult, ALU.mult, [SN[0], SN[2], "pcs"], [("RK", p)])

    A4 = [P.sb("A4_%d" % i, [128, 512], BF16) for i in range(2)]
    N0s = [P.sb("N0s_%d" % i, [128, 128], BF16) for i in range(2)]
    NL = [[P.sb("NL_%d_%d" % (i, j), [128, 256], BF16) for j in range(2)] for i in range(2)]
    TOK = [P.sb("TOK%d" % i, [128, 4, 128], BF16) for i in range(2)]
    Zb = [[P.sb("Zb_%d_%d" % (i, j), [128, 128], BF16) for j in range(2)] for i in range(2)]
    W1P = [P.sb("W1P%d" % h, [128, 128], BF16) for h in range(2)]
    XTP = [[P.sb("XTP_%d_%d" % (h, c), [128, 128]) for c in range(2)] for h in range(2)]
    Qg = [P.sb("Qg%d" % i, [128, 64]) for i in range(2)]
    RM = [[P.sb("RM_%d_%d" % (h, c), [128, 128], BF16) for c in range(2)] for h in range(2)]
    Hf = [[P.sb("Hf_%d_%d" % (p, h), [128, 64]) for h in range(2)] for p in range(2)]
    Hb = [[[P.sb("Hb_%d_%d_%d" % (p, h, c), [128, 64], BF16) for c in range(2)] for h in range(2)] for p in range(2)]
    Yall = [P.sb("Yall%d" % i, [128, 256]) for i in range(2)]
    rkc = [P.sb("rkc%d" % i, [128, 4]) for i in range(2)]
    gtk = [P.sb("gtk%d" % i, [128, 256]) for i in range(2)]
    st = P.sb("st", [128, 16])
    sq = P.sb("sq", [128, 256])
    yo = [P.sb("yo%d" % i, [128, 256]) for i in range(2)]
    yfm = [P.sb("yfm%d" % i, [128, 2, 128]) for i in range(2)]
    for h in range(2):
        P.memset("pool", W1P[h][:, :], 0.0, ["W1P%d" % h])
        for c in range(2):
            P.memset("pool", XTP[h][c][:, :], 0.0, ["XTP_%d_%d" % (h, c)])
            P.memset("pool", RM[h][c][:, :], 0.0, ["RM_%d_%d" % (h, c)])
    for p in range(2):
        for h in range(2):
            P.memset("pool", Hf[p][h][:, :], 0.0, ["Hf_%d_%d" % (p, h)])
            P.memset("pool", Hb[p][h][0][:, :], 0.0, ["Hb_%d_%d_0" % (p, h)])

    u = 0
    for blk in range(NB):
        cols = slice(blk * 128, (blk + 1) * 128)
        Y, Yn = Yall[blk % 2], "Yall%d" % (blk % 2)
        pg, pgn = pss[4], "ps4"
        P.mm(pg[:, 0:256], lo_g[:, cols], g2b[:, :], ["lo_g", "g2b"], [(pgn, 0)])
        for p in range(2):
            P.mm(pg[:, 256 + 2 * p:258 + 2 * p], RK[:, p, cols], hselb[:, :], [("RK", p), "hselb"], [(pgn, 1 + p)])
        gt, gtn = gtk[blk % 2], "gtk%d" % (blk % 2)
        rk_, rkn = rkc[blk % 2], "rkc%d" % (blk % 2)
        P.cp("act", gt[:, :], pg[:, 0:256], [(pgn, 0)], [gtn])
        P.cp("act", rk_[:, :], pg[:, 256:260], [(pgn, 1), (pgn, 2)], [rkn])
        for p in range(2):
            tk, tkn = TOK[p], "TOK%d" % p
            for j, (X, xn) in enumerate(((BT, "BT"), (KT, "KT"), (VB, "VB"), (AT, "AT"))):
                P.tr(psb[:, j * 128:(j + 1) * 128], X[:, p, cols], identb[:, :], [(xn, p), "identb"], [("psb", j)])
            P.cp("act", tk[:, :, :], psb[:, 0:512].rearrange("p (j c) -> p j c", j=4), ["psb"], [tkn])
            for h in range(2):
                o = 64 * h
                hs = slice(o, o + 64)
                bt, kt_, at, rt = BT[hs, p, cols], KT[hs, p, cols], AT[hs, p, cols], RT[hs, p, cols]
                a4, a4n = A4[u % 2], "A4_%d" % (u % 2)
                n0, n0n = N0s[u % 2], "N0s_%d" % (u % 2)
                p1, p1n = pss[5], "ps5"
                P.mm(p1[:, 0:128], bt, at, [("BT", p), ("AT", p)], [(p1n, 0)])
                P.mm(p1[:, 128:256], bt, rt, [("BT", p), ("RT", p)], [(p1n, 1)])
                P.mm(p1[:, 256:384], kt_, at, [("KT", p), ("AT", p)], [(p1n, 2)])
                P.mm(p1[:, 384:512], kt_, rt, [("KT", p), ("RT", p)], [(p1n, 3)])
                P.tt("dve", a4[:, :], p1[:, :], mask4[:, :], ALU.mult, [p1n, "mask4"], [a4n])
                p2, p2n = nextps()
                P.mm(p2[:, 0:128], at, bt, [("BT", p), ("AT", p)], [p2n])
                P.tt("dve", n0[:, :], p2[:, 0:128], msl[:, :], ALU.mult, [p2n, "msl"], [n0n])
                L0, ArbT, AakT, ArkT = a4[:, 0:128], a4[:, 128:256], a4[:, 256:384], a4[:, 384:512]
                zc, zcn = Zb[u % 2][0], "Zb_%d_0" % (u % 2)
                p3, p3n = nextps()
                P.mm(p3[:, 0:64], AakT, tk[:, 2, hs], [a4n, tkn], [p3n])
                P.cp("dve", zc[:, 0:64], tk[:, 3, hs], [tkn], [(zcn, 0)])
                P.cp("act", zc[:, 64:128], p3[:, 0:64], [p3n], [(zcn, 1)])
                Nj, Lj, njn, ljn = n0[:, :], L0, n0n, a4n
                zi = 0
                for j in range(6):
                    pz, pzn = nextps()
                    zc, zcn = Zb[u % 2][zi], "Zb_%d_%d" % (u % 2, zi)
                    zn_, znn = Zb[u % 2][1 - zi], "Zb_%d_%d" % (u % 2, 1 - zi)
                    P.mm(pz[:, 0:128], Lj, zc[:, :], [ljn, zcn], [pzn])
                    if j == 5:
                        P.tt("dve", zn_[:, :], pz[:, 0:128], zc[:, :], ALU.add, [pzn, zcn], [znn])
                        P.cp("pool", W1P[h][:, hs], zn_[:, 0:64], [znn], ["W1P%d" % h])
                    else:
                        P.tt("dve", zn_[:, :], pz[:, 0:128], zc[:, :], ALU.add, [pzn, zcn], [znn])
                        pq, pqn = nextps()
                        nl, nln = NL[u % 2][j % 2], "NL_%d_%d" % (u % 2, j % 2)
                        if j < 4:
                            P.mm(pq[:, 0:128], Lj, Nj, [ljn, njn], [(pqn, 0)])
                        P.mm(pq[:, 128:256], Nj, Lj, [ljn, njn], [(pqn, 1)])
                        if j < 4:
                            P.cp("act", nl[:, :], pq[:, 0:256], [pqn], [nln])
                        else:
                            P.cp("act", nl[:, 128:256], pq[:, 128:256], [pqn], [nln])
                        Nj, Lj, njn, ljn = nl[:, 0:128], nl[:, 128:256], nln, nln
                    zi = 1 - zi
                Zf, Zfn = Zb[u % 2][zi], "Zb_%d_%d" % (u % 2, zi)
                U0 = Zf[:, 64:128]
                w1p, w1n = W1P[h], "W1P%d" % h
                pr, prn = nextps()
                P.mm(pr[:, 0:128], w1p[:, :], ArbT, [w1n, a4n], [prn])
                for c in range(2):
                    cs = slice(64 * c, 64 * c + 64)
                    P.tt("dve", RM[h][c][hs, cs], pr[hs, cs], RT[hs, p, blk * 128 + 64 * c:blk * 128 + 64 * c + 64], ALU.add,
                         [prn, ("RT", p)], ["RM_%d_%d" % (h, c)])
                py, pyn = pss[6], "ps6"
                P.mm(py[:, 0:64], ArbT, U0, [a4n, Zfn], [pyn], start=True, stop=False)
                P.mm(py[:, 0:64], ArkT, tk[:, 2, hs], [a4n, tkn], [pyn], start=False, stop=False)
                hf, hfn = Hf[p][h], "Hf_%d_%d" % (p, h)
                for c in range(2):
                    ch = blk * 2 + c
                    rs = slice(64 * c, 64 * c + 64)
                    hb, hbn = Hb[p][h][c], "Hb_%d_%d_%d" % (p, h, c)
                    P.mm(py[:, 0:64], RM[h][c][hs, :], hb[hs, :], ["RM_%d_%d" % (h, c), hbn], [pyn], start=False, stop=(c == 1))
                    px, pxn = nextps()
                    P.mm(px[:, 0:64], w1p[rs, :], tk[rs, 0, hs], [w1n, tkn], [(pxn, 0)])
                    xt, xtn = XTP[h][c], "XTP_%d_%d" % (h, c)
                    P.tt("dve", xt[hs, hs], px[hs, 0:64], identf[hs, hs], ALU.add, [(pxn, 0), "identf"], [xtn])
                    P.mm(px[:, 64:128], tk[rs, 0, :], U0[rs, :], [tkn, Zfn], [(pxn, 1)], start=True, stop=False)
                    P.mm(px[:, 64:128], tk[rs, 1, :], tk[rs, 2, hs], [tkn], [(pxn, 1)], start=False, stop=True)
                    qg, qgn = Qg[c], "Qg%d" % c
                    P.ts("dve", qg[hs, :], px[hs, 64:128], gam[hs, p, ch:ch + 1], None, ALU.mult, None, [(pxn, 1), ("gam", p)], [qgn])
                    ph, phn = nextps()
                    P.mm(ph[:, 0:64], xt[hs, :], hf[hs, :], [xtn, hfn], [phn])
                    P.stt(hf[hs, :], ph[hs, 0:64], gam[hs, p, ch:ch + 1], qg[hs, :], ALU.mult, ALU.add, [phn, ("gam", p), qgn], [hfn])
                    nb_ = Hb[p][h][1 - c]
                    P.cp("pool", nb_[hs, :], hf[hs, :], [hfn], ["Hb_%d_%d_%d" % (p, h, 1 - c)])
                P.cp("act", Y[:, (2 * p + h) * 64:(2 * p + h + 1) * 64], py[:, 0:64], [pyn], [(Yn, 2 * p + h)])
                u += 1
        Y3 = Y[:, :].rearrange("p (h v) -> p h v", h=4)
        P.op("dve", lambda e, Y3=Y3: e.tensor_reduce(st[:, 0:4], Y3, mybir.AxisListType.X, ALU.add), [Yn], [("st", 0)])
        P.tt("pool", sq[:, :], Y[:, :], Y[:, :], ALU.mult, [Yn], ["sq"])
        sq3 = sq[:, :].rearrange("p (h v) -> p h v", h=4)
        P.op("dve", lambda e, sq3=sq3: e.tensor_reduce(st[:, 4:8], sq3, mybir.AxisListType.X, ALU.add), ["sq"], [("st", 1)])
        P.ts("dve", st[:, 0:4], st[:, 0:4], 1.0 / 64, None, ALU.mult, None, [("st", 0)], [("st", 0)])
        P.tt("dve", st[:, 8:12], st[:, 0:4], st[:, 0:4], ALU.mult, [("st", 0)], [("st", 2)])
        P.stt(st[:, 4:8], st[:, 4:8], 1.0 / 64, st[:, 8:12], ALU.mult, ALU.subtract, [("st", 1), ("st", 2)], [("st", 1)])
        P.ts("dve", st[:, 4:8], st[:, 4:8], 64e-5, None, ALU.add, None, [("st", 1)], [("st", 1)])
        P.act(st[:, 4:8], st[:, 4:8], AF.Sqrt, [("st", 1)], [("st", 1)])
        P.op("dve", lambda e: e.reciprocal(st[:, 4:8], st[:, 4:8]), [("st", 1)], [("st", 1)])
        o_, on = yo[blk % 2], "yo%d" % (blk % 2)
        for hh in range(4):
            cs = slice(hh * 64, hh * 64 + 64)
            P.ts("dve", o_[:, cs], Y[:, cs], st[:, hh:hh + 1], st[:, 4 + hh:5 + hh], ALU.subtract, ALU.mult, [Yn, "st"], [(on, hh)])
        P.tt("dve", o_[:, :], o_[:, :], lngs[:, :], ALU.mult, [on, "lngs"], [on])
        P.tt("pool", o_[:, :], o_[:, :], lnbs[:, :], ALU.add, [on, "lnbs"], [on])
        for hh in range(4):
            cs = slice(hh * 64, hh * 64 + 64)
            pp, h2 = hh // 2, hh % 2
            P.stt(o_[:, cs], TOK[pp][:, 2, 64 * h2:64 * h2 + 64], rk_[:, hh:hh + 1], o_[:, cs], ALU.mult, ALU.add,
                  ["TOK%d" % pp, rkn, on], [on])
        P.tt("dve", o_[:, :], o_[:, :], gt[:, :], ALU.mult, [on, gtn], [on])
        if fm_out:
            pt_, ptn = nextps()
            for c in range(2):
                P.tr(pt_[:, c * 128:(c + 1) * 128], o_[:, c * 128:(c + 1) * 128], identf[:, :], [on, "identf"], [ptn])
            yf, yfn = yfm[blk % 2], "yfm%d" % (blk % 2)
            P.cp("act", yf[:, :, :], pt_[:, 0:256].rearrange("p (c t) -> p c t", c=2), [ptn], [yfn])
            P.dma(y_out[0:256, blk * 128:(blk + 1) * 128].rearrange("(c p) t -> p c t", p=128), yf[:, :, :], [yfn], [("y_out", blk)], sk=yfn)
        else:
            P.dma(y_out[blk * 128:(blk + 1) * 128, :], o_[:, :], [on], [("y_out", blk)], sk=on)
    return _fin(P, nc, io, ("y_out", "vfirst_out"))


def l4_inputs(z, prm, hg, L, vfirst=None):
    c0 = 256 * hg
    f = lambda a: np.ascontiguousarray(a, dtype=np.float32)
    mu = prm["mu"]
    d = dict(zr=f(z[:, c0:c0 + 256].T), zk=f(z[:, 512 + c0:512 + c0 + 256].T), zv=f(z[:, 1024 + c0:1024 + c0 + 256].T),
             zw=f(z[:, 1536:1568].T), za=f(z[:, 1568:1600].T), zg=f(z[:, 1600:1696].T))
    mu3 = np.zeros((128, 6), np.float32)
    for j in range(3):
        for p in range(2):
            mu3[:, 2 * j + p] = mu[512 * j + c0 + 128 * p:512 * j + c0 + 128 * p + 128]
    d["mu3"] = mu3
    mulo = np.zeros((96, 3), np.float32)
    mulo[:32, 0] = mu[1536:1568]
    mulo[:32, 1] = mu[1568:1600]
    mulo[:, 2] = mu[1600:1696]
    d["mulo"] = mulo
    d["w2"] = f(prm["w2"][:, c0:c0 + 256]); d["a2"] = f(prm["a2"][:, c0:c0 + 256]); d["g2"] = f(prm["g2"][:, c0:c0 + 256])
    pc = np.zeros((128, 10), np.float32)
    for j, nm in enumerate(("w0", "a0", "k_k", "k_a", "r_k")):
        v = prm[nm].reshape(-1)
        for p in range(2):
            pc[:, 2 * j + p] = v[c0 + 128 * p:c0 + 128 * p + 128]
    d["pc"] = pc
    d["lng"] = f(np.broadcast_to(prm["ln_g"][c0:c0 + 256][None, :], (128, 256)))
    d["lnb"] = f(np.broadcast_to(prm["ln_b"][c0:c0 + 256][None, :], (128, 256)))
    d.update(rw_consts(L))
    if vfirst is not None:
        d["zvfull"] = f(z[:, 1024:1536].T)
        d["muvf"] = f(mu[1024:1536].reshape(4, 128).T)
        d["v1"] = f(prm["v1"].reshape(4, 128, 32).transpose(1, 0, 2))
        d["v2"] = f(prm["v2"][:, c0:c0 + 256])
        v0 = np.zeros((128, 2), np.float32)
        for p in range(2):
            v0[:, p] = prm["v0"][c0 + 128 * p:c0 + 128 * p + 128]
        d["v0"] = v0
        d["vfirst_in"] = f(vfirst)
    return d


def at_consts():
    tk = np.arange(128)[:, None]
    tq = np.arange(512)[None, :]
    m = np.stack([((128 * j + tk) <= tq).astype(np.float32) for j in range(4)], 1)
    i = np.arange(128)
    b64 = ((i[:, None] // 64) == (i[None, :] // 64)).astype(np.float32)
    return dict(c_cmask=m, c_b64=b64, c_ones=np.ones((128, 128), np.float32))


def build_L3(L=4096, ctx=None):
    nc, P, io = _ctx(ctx)
    NT = L // 128
    NQ = L // 512
    I = _mkI(nc, io)
    qT, kT = I("qT", [2, 128, L]), I("kT", [2, 128, L])
    vtok = I("vtok", [2, L, 128])
    gains = I("gains", [128, 4])
    lamv = I("lamv", [128, 4, 64])
    laminit = I("laminit", [128, 1])
    c_cmask, c_b64, c_ones = I("c_cmask", [128, 4, 512]), I("c_b64", [128, 128]), I("c_ones", [128, 128])
    oT = _mkO(nc, io, "oT", [2, 128, L])

    def ld(name, src, shape):
        t = P.sb(name, shape)
        if len(shape) == 2:
            P.dma(t[:, :], src[:, :], [], [name])
        else:
            P.dma(t[:, :, :], src[:, :, :], [], [name])
        return t
    gs = ld("gs", gains, [128, 4])
    lv = ld("lv", lamv, [128, 4, 64])
    li = ld("li", laminit, [128, 1])
    cmf = ld("cmf", c_cmask, [128, 4, 512])
    b64 = ld("b64", c_b64, [128, 128])
    onesf = ld("onesf", c_ones, [128, 128])
    cmb = P.sb("cmb", [128, 4, 512], BF16)
    onesb = P.sb("onesb", [128, 128], BF16)
    P.cp("dve", cmb[:, :, :], cmf[:, :, :], ["cmf"], ["cmb"])
    P.cp("dve", onesb[:, :], onesf[:, :], ["onesf"], ["onesb"])
    lt = P.sb("lt", [128, 2, 64])
    ls = P.sb("ls", [128, 4])
    P.tt("dve", lt[:, 0, :], lv[:, 0, :], lv[:, 1, :], ALU.mult, ["lv"], [("lt", 0)])
    P.tt("dve", lt[:, 1, :], lv[:, 2, :], lv[:, 3, :], ALU.mult, ["lv"], [("lt", 1)])
    P.op("dve", lambda e: e.tensor_reduce(ls[:, 0:2], lt[:, :, :], mybir.AxisListType.X, ALU.add), ["lt"], ["ls"])
    P.act(ls[:, 0:2], ls[:, 0:2], AF.Exp, ["ls"], ["ls"])
    P.tt("dve", ls[:, 2:3], ls[:, 1:2], ls[:, 0:1], ALU.subtract, ["ls"], ["ls"])
    P.tt("dve", ls[:, 3:4], ls[:, 2:3], li[:, :], ALU.subtract, ["ls", "li"], ["ls"])
    gq = P.sb("gq", [128, 1])
    P.ts("dve", gq[:, :], gs[:, 0:1], 0.125, None, ALU.mult, None, ["gs"], ["gq"])
    gsub = P.sb("gsub", [128, 1])
    P.tt("dve", gsub[:, :], gs[:, 2:3], gs[:, 3:4], ALU.mult, ["gs"], ["gsub"])

    raw = P.sb("raw", [128, L])
    sqt = P.sb("sqt", [128, L])
    qn = P.sb("qn", [128, L], BF16)
    kn = P.sb("kn", [128, L], BF16)
    vf = P.sb("vf", [128, NT, 128])
    vb = P.sb("vb", [128, NT, 128], BF16)
    pt = [P.sb("pt%d" % i, [128, 512], BF16) for i in range(4)]
    ob = P.sb("ob", [128, 512])
    o1 = P.sb("o1", [128, 512])
    rz = P.sb("rz", [128, 512])
    osq = P.sb("osq", [128, 512])
    oo = [P.sb("oo%d" % i, [128, 512]) for i in range(2)]
    psS = [P.ps("psS%d" % i, [128, 512]) for i in range(2)]
    psO = [P.ps("psO%d" % i, [128, 512]) for i in range(2)]
    psZ = [P.ps("psZ%d" % i, [128, 512]) for i in range(2)]
    psM = P.ps("psM", [128, 512])

    def qknorm(src, dst, dn, gcol):
        P.dma(raw[:, :], src, [], ["raw"])
        P.tt("pool", sqt[:, :], raw[:, :], raw[:, :], ALU.mult, ["raw"], ["sqt"])
        for cb in range(NQ):
            cs = slice(cb * 512, cb * 512 + 512)
            P.mm(psM[:, :], b64[:, :], sqt[:, cs], ["b64", "sqt"], ["psM"])
            P.ts("dve", sqt[:, cs], psM[:, :], 1.0 / 64, EPS, ALU.mult, ALU.add, ["psM"], [("sqt", cb)])
        P.act(sqt[:, :], sqt[:, :], AF.Sqrt, ["sqt"], ["sqt"])
        P.op("dve", lambda e: e.reciprocal(sqt[:, :], sqt[:, :]), ["sqt"], ["sqt"])
        P.stt(dst[:, :], raw[:, :], gcol, sqt[:, :], ALU.mult, ALU.mult, ["raw", "sqt", "gs", "gq"], [dn])

    qn2 = [qn, P.sb("qn_b", [128, L], BF16)]
    kn2 = [kn, P.sb("kn_b", [128, L], BF16)]
    vb2 = [vb, P.sb("vb_b", [128, NT, 128], BF16)]
    qnn, knn, vbn = ["qn", "qn_b"], ["kn", "kn_b"], ["vb", "vb_b"]
    sO = [P.sb("sO%d" % i, [128, 512]) for i in range(2)]
    sZ = [P.sb("sZ%d" % i, [128, 512]) for i in range(2)]
    for h in range(2):
        qknorm(qT[h, :, :], qn2[h], qnn[h], gq[:, 0:1])
        qknorm(kT[h, :, :], kn2[h], knn[h], gs[:, 1:2])
        P.dma(vf[:, :, :], vtok[h, :, :].rearrange("(n p) d -> p n d", p=128), [], ["vf"])
        P.cp("pool", vb2[h][:, :, :], vf[:, :, :], ["vf"], [vbn[h]])
    kcount = 0
    for h in range(2):
        qn_, kn_, vb_ = qn2[h], kn2[h], vb2[h]
        for qb in range(NQ):
            qs = slice(qb * 512, qb * 512 + 512)
            ntk = 4 * qb + 4
            its = [(j, c2) for j in range(ntk) for c2 in range(2)]
            bufs = []

            def emit_S(idx):
                nonlocal kcount
                j, c2 = its[idx]
                hs = slice(64 * c2, 64 * c2 + 64)
                s_, sn = psS[kcount % 2], "psS%d" % (kcount % 2)
                p_, pn = pt[kcount % 4], "pt%d" % (kcount % 4)
                kcount += 1
                P.mm(s_[:, :], kn_[hs, j * 128:j * 128 + 128], qn_[hs, qs], [knn[h], qnn[h]], [sn])
                bufs.append((s_, sn, p_, pn))
            emit_S(0)
            for idx, (j, c2) in enumerate(its):
                if idx + 1 < len(its):
                    emit_S(idx + 1)
                s_, sn, p_, pn = bufs[idx]
                P.act(p_[:, :], s_[:, :], AF.Exp, [sn], [pn])
                if j >= 4 * qb:
                    P.tt("dve", p_[:, :], p_[:, :], cmb[:, j - 4 * qb, :], ALU.mult, [pn, "cmb"], [pn])
                P.mm(psO[c2][:, :], vb_[:, j, :], p_[:, :], [vbn[h], pn], ["psO%d" % c2], start=(j == 0), stop=(j == ntk - 1))
                P.mm(psZ[c2][:, :], onesb[:, :], p_[:, :], ["onesb", pn], ["psZ%d" % c2], start=(j == 0), stop=(j == ntk - 1))
            for c2 in range(2):
                P.cp("act", sO[c2][:, :], psO[c2][:, :], ["psO%d" % c2], ["sO%d" % c2])
                P.cp("act", sZ[c2][:, :], psZ[c2][:, :], ["psZ%d" % c2], ["sZ%d" % c2])
            P.op("dve", lambda e: e.reciprocal(rz[:, :], sZ[0][:, :]), ["sZ0"], ["rz"])
            P.tt("dve", ob[:, :], sO[0][:, :], rz[:, :], ALU.mult, ["sO0", "rz"], ["ob"])
            P.op("dve", lambda e: e.reciprocal(rz[:, :], sZ[1][:, :]), ["sZ1"], ["rz"])
            P.tt("pool", o1[:, :], sO[1][:, :], rz[:, :], ALU.mult, ["sO1", "rz"], ["o1"])
            P.stt(ob[:, :], o1[:, :], ls[:, 3:4], ob[:, :], ALU.mult, ALU.add, ["o1", "ls", "ob"], ["ob"])
            P.tt("pool", osq[:, :], ob[:, :], ob[:, :], ALU.mult, ["ob"], ["osq"])
            P.mm(psM[:, :], onesf[:, :], osq[:, :], ["onesf", "osq"], ["psM"])
            P.ts("dve", osq[:, :], psM[:, :], 1.0 / 128, 1e-5, ALU.mult, ALU.add, ["psM"], ["osq"])
            P.act(osq[:, :], osq[:, :], AF.Sqrt, ["osq"], ["osq"])
            P.op("dve", lambda e: e.reciprocal(osq[:, :], osq[:, :]), ["osq"], ["osq"])
            o_, on = oo[qb % 2], "oo%d" % (qb % 2)
            P.stt(o_[:, :], ob[:, :], gsub[:, 0:1], osq[:, :], ALU.mult, ALU.mult, ["ob", "gsub", "osq"], [on])
            P.dma(oT[h, :, qs], o_[:, :], [on], [("oT", h * NQ + qb)], sk=on)
    return _fin(P, nc, io, ("oT",))


def l3_inputs(q, k, v, prm, hp, L, lam_init):
    f = lambda a: np.ascontiguousarray(a, dtype=np.float32)
    hs = [2 * hp, 2 * hp + 1]
    d = dict(qT=f(np.stack([q[:, h * 128:(h + 1) * 128].T for h in hs])),
             kT=f(np.stack([k[:, h * 128:(h + 1) * 128].T for h in hs])),
             vtok=f(np.stack([v[:, h * 128:(h + 1) * 128] for h in hs])))
    g = np.zeros((128, 4), np.float32)
    g[:, 0] = prm["q_gain"].reshape(-1)
    g[:, 1] = prm["k_gain"].reshape(-1)
    g[:, 2] = prm["subln_g"]
    g[:, 3] = 1.0 - lam_init
    d["gains"] = g
    d["lamv"] = f(np.broadcast_to(prm["lam"][None], (128, 4, 64)))
    d["laminit"] = np.full((128, 1), lam_init, np.float32)
    d.update(at_consts())
    return d


PI = float(np.pi)


def build_L2(L=4096, ctx=None):
    nc, P, io = _ctx(ctx)
    NQ = L // 512
    I = _mkI(nc, io)
    uT = I("uT", [256, L])
    bpr, bpi = I("bpr", [128, 8, 128]), I("bpi", [128, 8, 128])
    cpr, cpi = I("cpr", [128, 8, 128]), I("cpi", [128, 8, 128])
    prm = I("prm", [128, 3, 8])
    dcol = I("dcol", [128, 2])
    yT = _mkO(nc, io, "yT", [256, L])

    def ld(name, src, shape):
        t = P.sb(name, shape)
        if len(shape) == 2:
            P.dma(t[:, :], src[:, :], [], [name])
        else:
            P.dma(t[:, :, :], src[:, :, :], [], [name])
        return t
    prs = ld("prs", prm, [128, 3, 8])
    ds = ld("ds", dcol, [128, 2])
    wts = {}
    for nm, src in (("bpr", bpr), ("bpi", bpi), ("cpr", cpr), ("cpi", cpi)):
        f = ld(nm + "f", src, [128, 8, 128])
        b = P.sb(nm + "b", [128, 8, 128], BF16)
        P.cp("dve", b[:, :, :], f[:, :, :], [nm + "f"], [nm + "b"])
        wts[nm] = b
    iaf = P.sb("iaf", [128, L])
    ub = P.sb("ub", [128, 2, L], BF16)
    for t in range(2):
        P.dma(iaf[:, :], uT[t * 128:(t + 1) * 128, :], [], ["iaf"])
        P.cp("dve", ub[:, t, :], iaf[:, :], ["iaf"], [("ub", t)])
    q = P.sb("q", [128, 16, 8])
    Q = lambda i: q[:, i, :]
    lr, li_, ldt = prs[:, 0, :], prs[:, 1, :], prs[:, 2, :]
    negpi = P.sb("negpi", [128, 1])
    P.memset("pool", negpi[:, :], -PI, ["negpi"])
    P.act(Q(0), ldt, AF.Exp, ["prs"], [("q", 0)])
    P.tt("dve", Q(1), lr, Q(0), ALU.mult, ["prs", ("q", 0)], [("q", 1)])
    P.tt("dve", Q(2), li_, Q(0), ALU.mult, ["prs", ("q", 0)], [("q", 2)])
    P.act(Q(3), Q(1), AF.Exp, [("q", 1)], [("q", 3)])
    NA = L // 64
    pw = P.sb("pw", [128, 24, 2, 8])
    wt = P.sb("wt", [128, 6, 8])
    W = lambda i: wt[:, i, :]

    def square_norm(cin, sin_, cout, sout, rd, wr):
        P.tt("dve", W(0), cin, cin, ALU.mult, rd, [("wt", 0)])
        P.tt("dve", W(1), sin_, sin_, ALU.mult, rd, [("wt", 1)])
        P.tt("dve", W(2), cin, sin_, ALU.mult, rd, [("wt", 2)])
        P.tt("dve", W(0), W(0), W(1), ALU.subtract, [("wt", 0), ("wt", 1)], [("wt", 0)])
        P.ts("dve", W(2), W(2), 2.0, None, ALU.mult, None, [("wt", 2)], [("wt", 2)])
        P.tt("dve", W(3), W(0), W(0), ALU.mult, [("wt", 0)], [("wt", 3)])
        P.tt("dve", W(4), W(2), W(2), ALU.mult, [("wt", 2)], [("wt", 4)])
        P.tt("dve", W(3), W(3), W(4), ALU.add, [("wt", 3), ("wt", 4)], [("wt", 3)])
        P.act(W(3), W(3), AF.Sqrt, [("wt", 3)], [("wt", 3)])
        P.op("dve", lambda e: e.reciprocal(W(3), W(3)), [("wt", 3)], [("wt", 3)])
        P.tt("dve", cout, W(0), W(3), ALU.mult, [("wt", 0), ("wt", 3)], wr)
        P.tt("dve", sout, W(2), W(3), ALU.mult, [("wt", 2), ("wt", 3)], wr)
    P.act(pw[:, 0, 1, :], Q(2), AF.Sin, [("q", 2)], [("pw", 0)], scale=1.0 / 16)
    P.act(Q(5), Q(2), AF.Sin, [("q", 2)], [("q", 5)], scale=1.0 / 32)
    P.tt("dve", Q(5), Q(5), Q(5), ALU.mult, [("q", 5)], [("q", 5)])
    P.ts("dve", pw[:, 0, 0, :], Q(5), -2.0, 1.0, ALU.mult, ALU.add, [("q", 5)], [("pw", 0)])
    NPW = 4 + 6 + max(1, int(np.log2(NA)))
    for k_ in range(1, NPW):
        square_norm(pw[:, k_ - 1, 0, :], pw[:, k_ - 1, 1, :], pw[:, k_, 0, :], pw[:, k_, 1, :], [("pw", k_ - 1)], [("pw", k_)])
    P.cp("dve", Q(5), pw[:, 4, 0, :], [("pw", 4)], [("q", 5)])
    P.cp("dve", Q(4), pw[:, 4, 1, :], [("pw", 4)], [("q", 4)])
    EBc, EBs = P.sb("EBc", [128, 8, 64]), P.sb("EBs", [128, 8, 64])
    EAc, EAs = P.sb("EAc", [128, 8, NA]), P.sb("EAs", [128, 8, NA])
    tbl = P.sb("tbl", [128, 2, 8, 64])

    def build_table(Ec, Es, ecn, esn, n_tot, k0, rowlen):
        P.memset("pool", Ec[:, :, 0:1], 1.0, [ecn])
        P.memset("pool", Es[:, :, 0:1], 0.0, [esn])
        n = 1
        k_ = k0
        while n < n_tot:
            wc = ap3(pw, (k_ * 2 + 0) * 8, [[24 * 2 * 8, 128], [1, 8], [0, n]])
            ws = ap3(pw, (k_ * 2 + 1) * 8, [[24 * 2 * 8, 128], [1, 8], [0, n]])
            t0_, t1_ = tbl[:, 0, :, 0:n], tbl[:, 1, :, 0:n]
            rd = [ecn, esn, ("pw", k_)]
            P.tt("dve", t0_, Es[:, :, 0:n], ws, ALU.mult, rd, [("tbl", 0)])
            P.tt("dve", t1_, Ec[:, :, 0:n], ws, ALU.mult, rd, [("tbl", 1)])
            P.tt("dve", Ec[:, :, n:2 * n], Ec[:, :, 0:n], wc, ALU.mult, rd, [ecn])
            P.tt("dve", Es[:, :, n:2 * n], Es[:, :, 0:n], wc, ALU.mult, rd, [esn])
            P.tt("dve", Ec[:, :, n:2 * n], Ec[:, :, n:2 * n], t0_, ALU.subtract, [ecn, ("tbl", 0)], [ecn])
            P.tt("dve", Es[:, :, n:2 * n], Es[:, :, n:2 * n], t1_, ALU.add, [esn, ("tbl", 1)], [esn])
            n *= 2
            k_ += 1
    build_table(EBc, EBs, "EBc", "EBs", 64, 4, 64)
    build_table(EAc, EAs, "EAc", "EAs", NA, 10, NA)
    P.tt("dve", Q(6), Q(3), Q(5), ALU.mult, [("q", 3), ("q", 5)], [("q", 6)])
    P.tt("dve", Q(7), Q(3), Q(4), ALU.mult, [("q", 3), ("q", 4)], [("q", 7)])
    P.tt("dve", Q(8), lr, lr, ALU.mult, ["prs"], [("q", 8)])
    P.tt("dve", Q(9), li_, li_, ALU.mult, ["prs"], [("q", 9)])
    P.tt("dve", Q(8), Q(8), Q(9), ALU.add, [("q", 8), ("q", 9)], [("q", 8)])
    P.op("dve", lambda e: e.reciprocal(Q(8), Q(8)), [("q", 8)], [("q", 8)])
    P.ts("dve", Q(9), Q(6), -1.0, None, ALU.add, None, [("q", 6)], [("q", 9)])
    P.tt("dve", Q(10), Q(9), lr, ALU.mult, [("q", 9), "prs"], [("q", 10)])
    P.tt("dve", Q(11), Q(7), li_, ALU.mult, [("q", 7), "prs"], [("q", 11)])
    P.tt("dve", Q(10), Q(10), Q(11), ALU.add, [("q", 10), ("q", 11)], [("q", 10)])
    P.tt("dve", Q(10), Q(10), Q(8), ALU.mult, [("q", 10), ("q", 8)], [("q", 10)])
    P.tt("dve", Q(11), Q(7), lr, ALU.mult, [("q", 7), "prs"], [("q", 11)])
    P.tt("dve", Q(12), Q(9), li_, ALU.mult, [("q", 9), "prs"], [("q", 12)])
    P.tt("dve", Q(11), Q(11), Q(12), ALU.subtract, [("q", 11), ("q", 12)], [("q", 11)])
    P.tt("dve", Q(11), Q(11), Q(8), ALU.mult, [("q", 11), ("q", 8)], [("q", 11)])
    P.ts("dve", Q(14), Q(10), -1.0, None, ALU.mult, None, [("q", 10)], [("q", 14)])
    MAG, TH, KR, KI, NKR = 3, 2, 10, 11, 14

    xb = P.sb("xb", [128, 4, 2, L], BF16)
    NS = 2
    def mk(n, dt=F32):
        return [P.sb("%s%d" % (n, i), [128, 512], dt) for i in range(NS)]
    ang, cc, ss_, T1, T2, mr, mi, tmp, tmp2 = [mk(n) for n in ("ang", "cc", "ss", "T1", "T2", "mr", "mi", "tmp", "tmp2")]
    wr = [P.sb("wr%d" % i, [128, 512]) for i in range(2)]
    wi = [P.sb("wi%d" % i, [128, 512]) for i in range(2)]
    magt = P.sb("magt", [128, 512])
    onest = P.sb("onest", [128, 512])
    P.memset("pool", onest[:, :], 1.0, ["onest"])
    ust = [P.sb("ust%d" % i, [128, 512]) for i in range(2)]
    yst = [P.sb("yst%d" % i, [128, 512]) for i in range(2)]
    psr = [P.ps("psr%d" % i, [128, 512]) for i in range(2)]
    psi = [P.ps("psi%d" % i, [128, 512]) for i in range(2)]
    psy = [P.ps("psy%d" % i, [128, 512]) for i in range(2)]
    k = 0
    for tile in range(2):
        for gl in range(4):
            gp = 4 * tile + gl
            sc = lambda idx: q[:, idx, gp:gp + 1]
            P.ts("dve", magt[:, :], onest[:, :], sc(MAG), None, ALU.mult, None, ["onest", "q"], ["magt"])
            for qb in range(NQ):
                i = k % NS
                k += 1
                n = lambda s: "%s%d" % (s, i)
                cs = slice(qb * 512, qb * 512 + 512)
                P.mm(psr[i][:, :], wts["bpr"][:, gp, :], ub[:, tile, cs], ["bprb", ("ub", tile)], ["psr%d" % i])
                P.mm(psi[i][:, :], wts["bpi"][:, gp, :], ub[:, tile, cs], ["bpib", ("ub", tile)], ["psi%d" % i])
                a0 = 8 * qb
                bro = lambda T_, n_: ap3(T_, gp * n_ + a0, [[8 * n_, 128], [1, 8], [0, 64]])
                brb = lambda T_: ap3(T_, gp * 64, [[8 * 64, 128], [0, 8], [1, 64]])
                v3 = lambda t_: t_[:, :].rearrange("p (a b) -> p a b", a=8)
                P.tt("pool", v3(tmp[i]), bro(EAs, NA), brb(EBs), ALU.mult, ["EAs", "EBs"], [n("tmp")])
                P.tt("dve", v3(cc[i]), bro(EAc, NA), brb(EBc), ALU.mult, ["EAc", "EBc"], [n("cc")])
                P.tt("pool", cc[i][:, :], cc[i][:, :], tmp[i][:, :], ALU.subtract, [n("cc"), n("tmp")], [n("cc")])
                P.tt("pool", v3(tmp2[i]), bro(EAc, NA), brb(EBs), ALU.mult, ["EAc", "EBs"], [n("tmp2")])
                P.tt("dve", v3(ss_[i]), bro(EAs, NA), brb(EBc), ALU.mult, ["EAs", "EBc"], [n("ss")])
                P.tt("pool", ss_[i][:, :], ss_[i][:, :], tmp2[i][:, :], ALU.add, [n("ss"), n("tmp2")], [n("ss")])
                P.ts("dve", T1[i][:, :], cc[i][:, :], sc(KR), None, ALU.mult, None, [n("cc"), "q"], [n("T1")])
                P.stt(T1[i][:, :], ss_[i][:, :], sc(KI), T1[i][:, :], ALU.mult, ALU.add, [n("ss"), "q", n("T1")], [n("T1")])
                P.ts("dve", T2[i][:, :], cc[i][:, :], sc(KI), None, ALU.mult, None, [n("cc"), "q"], [n("T2")])
                P.stt(T2[i][:, :], ss_[i][:, :], sc(NKR), T2[i][:, :], ALU.mult, ALU.add, [n("ss"), "q", n("T2")], [n("T2")])
                P.tt("dve", tmp[i][:, :], psi[i][:, :], T2[i][:, :], ALU.mult, ["psi%d" % i, n("T2")], [n("tmp")])
                P.tt("dve", mr[i][:, :], psr[i][:, :], T1[i][:, :], ALU.mult, ["psr%d" % i, n("T1")], [n("mr")])
                P.tt("pool", mr[i][:, :], mr[i][:, :], tmp[i][:, :], ALU.subtract, [n("mr"), n("tmp")], [n("mr")])
                P.tt("dve", tmp2[i][:, :], psr[i][:, :], T2[i][:, :], ALU.mult, ["psr%d" % i, n("T2")], [n("tmp2")])
                P.tt("dve", mi[i][:, :], psi[i][:, :], T1[i][:, :], ALU.mult, ["psi%d" % i, n("T1")], [n("mi")])
                P.tt("pool", mi[i][:, :], mi[i][:, :], tmp2[i][:, :], ALU.add, [n("mi"), n("tmp2")], [n("mi")])
                w_r, w_i = wr[qb % 2], wi[qb % 2]
                wrn, win = "wr%d" % (qb % 2), "wi%d" % (qb % 2)
                if qb == 0:
                    ini_r, ini_i, extra = 0.0, 0.0, []
                else:
                    ini_r, ini_i = wr[1 - qb % 2][:, 511:512], wi[1 - qb % 2][:, 511:512]
                    extra = ["wr%d" % (1 - qb % 2), "wi%d" % (1 - qb % 2)]
                P.op("dve", lambda e, w_r=w_r, m=mr[i], ini=ini_r: e.tensor_tensor_scan(w_r[:, :], magt[:, :], m[:, :], ini, ALU.mult, ALU.add),
                     ["magt", n("mr")] + extra, [wrn])
                P.op("dve", lambda e, w_i=w_i, m=mi[i], ini=ini_i: e.tensor_tensor_scan(w_i[:, :], magt[:, :], m[:, :], ini, ALU.mult, ALU.add),
                     ["magt", n("mi")] + extra, [win])
                P.tt("pool", tmp[i][:, :], ss_[i][:, :], w_i[:, :], ALU.mult, [n("ss"), win], [n("tmp")])
                P.tt("pool", tmp2[i][:, :], cc[i][:, :], w_r[:, :], ALU.mult, [n("cc"), wrn], [n("tmp2")])
                P.tt("pool", xb[:, gl, 0, cs], tmp2[i][:, :], tmp[i][:, :], ALU.subtract, [n("tmp2"), n("tmp")], [("xb", gl * 2)])
                P.tt("pool", tmp[i][:, :], ss_[i][:, :], w_r[:, :], ALU.mult, [n("ss"), wrn], [n("tmp")])
                P.tt("pool", tmp2[i][:, :], cc[i][:, :], w_i[:, :], ALU.mult, [n("cc"), win], [n("tmp2")])
                P.stt(xb[:, gl, 1, cs], tmp[i][:, :], -1.0, tmp2[i][:, :], ALU.mult, ALU.subtract, [n("tmp"), n("tmp2")], [("xb", gl * 2 + 1)])
        for qb in range(NQ):
            cs = slice(qb * 512, qb * 512 + 512)
            py, pyn = psy[qb % 2], "psy%d" % (qb % 2)
            for gl in range(4):
                gp = 4 * tile + gl
                P.mm(py[:, :], wts["cpr"][:, gp, :], xb[:, gl, 0, cs], ["cprb", ("xb", gl * 2)], [pyn], start=(gl == 0), stop=False)
                P.mm(py[:, :], wts["cpi"][:, gp, :], xb[:, gl, 1, cs], ["cpib", ("xb", gl * 2 + 1)], [pyn], start=False, stop=(gl == 3))
            us, usn = ust[qb % 2], "ust%d" % (qb % 2)
            ys, ysn = yst[qb % 2], "yst%d" % (qb % 2)
            P.dma(us[:, :], uT[tile * 128:(tile + 1) * 128, cs], [], [usn])
            P.stt(ys[:, :], us[:, :], ds[:, tile:tile + 1], py[:, :], ALU.mult, ALU.add, [usn, "ds", pyn], [ysn])
            P.dma(yT[tile * 128:(tile + 1) * 128, cs], ys[:, :], [ysn], [("yT", tile * NQ + qb)], sk=ysn)
    return _fin(P, nc, io, ("yT",))


def l2_inputs(u, prm, gh, L):
    f = lambda a: np.ascontiguousarray(a, dtype=np.float32)
    d = dict(uT=f(u[:, 256 * gh:256 * gh + 256].T))
    bpr = np.zeros((128, 8, 128), np.float32); bpi = np.zeros_like(bpr)
    cpr = np.zeros((128, 8, 128), np.float32); cpi = np.zeros_like(cpr)
    pr = np.zeros((128, 3, 8), np.float32)
    for gp in range(8):
        gl = gp % 4
        for gpar in range(2):
            g = 16 * gh + 2 * gp + gpar
            chs = slice(16 * (2 * gl + gpar), 16 * (2 * gl + gpar) + 16)
            ps = slice(64 * gpar, 64 * gpar + 64)
            bpr[chs, gp, ps] = prm["b_re"][g].T
            bpi[chs, gp, ps] = prm["b_im"][g].T
            cpr[ps, gp, chs] = prm["c_re"][g].T
            cpi[ps, gp, chs] = prm["c_im"][g].T
            pr[ps, 0, gp] = prm["lam_re"][g]
            pr[ps, 1, gp] = prm["lam_im"][g]
            pr[ps, 2, gp] = prm["log_dt"][g]
    d.update(bpr=bpr, bpi=bpi, cpr=cpr, cpi=cpi, prm=pr)
    d["dcol"] = f(prm["d"][256 * gh:256 * gh + 256].reshape(2, 128).T)
    return d


DFF = 2816


def build_L5a(TOK=2048, HT=1024, ctx=None):
    nc, P, io = _ctx(ctx)
    NH = TOK // HT
    NTT = HT // 128
    NB5 = HT // 512
    I = _mkI(nc, io)
    x_tok = I("x_tok", [TOK, D])
    ypreT, oT, ycT = I("ypreT", [512, TOK]), I("oT", [512, TOK]), I("ycT", [512, TOK])
    glT = I("glT", [3072, TOK])
    w_glu, w_br, w_out = I("w_glu", [512, 512]), I("w_br", [3, 512, 1024]), I("w_out", [1024, 1024])
    x_out = _mkO(nc, io, "x_out", [TOK, D])

    wst = [P.sb("wst%d" % i, [128, 1024]) for i in range(3)]
    wk = [0]

    def load_w(dst, dn, sub, src_ap, ncol):
        i = wk[0] % 3
        wk[0] += 1
        P.dma(wst[i][:, :ncol], src_ap, [], ["wst%d" % i], eng=("sp" if i % 2 == 0 else "pool"))
        P.cp("pool" if i % 2 else "dve", dst, wst[i][:, :ncol], ["wst%d" % i], [(dn, sub)])

    wglu = P.sb("wglu", [128, 4, 512], BF16)
    wbr = P.sb("wbr", [128, 12, 1024], BF16)
    wout = P.sb("wout", [128, 8, 1024], BF16)
    for c in range(4):
        load_w(wglu[:, c, :], "wglu", c, w_glu[c * 128:(c + 1) * 128, :], 512)
    for n in range(3):
        for c in range(4):
            load_w(wbr[:, n * 4 + c, :], "wbr", n * 4 + c, w_br[n, c * 128:(c + 1) * 128, :], 1024)
    for c in range(8):
        load_w(wout[:, c, :], "wout", c, w_out[c * 128:(c + 1) * 128, :], 1024)

    fin = P.sb("fin", [128, 4, HT])
    t1 = P.sb("t1", [128, 4, HT])
    yb = [P.sb("yb%d" % n, [128, 4, HT], BF16) for n in range(3)]
    ya2 = P.sb("ya2", [128, 4, HT], BF16)
    gst = [P.sb("gst%d" % i, [128, 512]) for i in range(2)]
    sg = [P.sb("sg%d" % i, [128, 512]) for i in range(2)]
    pr = [P.sb("pr%d" % i, [128, 512]) for i in range(2)]
    macc = P.sb("macc", [128, 512])
    mT = P.sb("mT", [128, 8, HT], BF16)
    x1 = [P.sb("x1_%d" % i, [128, D]) for i in range(2)]
    xin = [P.sb("xin%d" % i, [128, D]) for i in range(2)]
    pss = [P.ps("ps%d" % i, [128, 512]) for i in range(6)]
    pk = [0]

    def nextps():
        k = pk[0] % 6
        pk[0] += 1
        return pss[k], "ps%d" % k

    for half in range(NH):
        t0 = half * HT
        tsl = slice(t0, t0 + HT)
        fmv = lambda a: a.ap().rearrange("(c p) t -> p c t", p=128)
        P.dma(fin[:, :, :], fmv(ypreT)[:, :, tsl], [], ["fin"])
        P.tt("pool", t1[:, :, :], fin[:, :, :], fin[:, :, :], ALU.mult, ["fin"], ["t1"])
        P.ts("dve", t1[:, :, :], t1[:, :, :], 0.044715, 1.0, ALU.mult, ALU.add, ["t1"], ["t1"])
        P.tt("pool", t1[:, :, :], t1[:, :, :], fin[:, :, :], ALU.mult, ["t1", "fin"], ["t1"])
        P.act(t1[:, :, :], t1[:, :, :], AF.Sigmoid, ["t1"], ["t1"], scale=1.5957691216057308)
        P.tt("dve", t1[:, :, :], t1[:, :, :], fin[:, :, :], ALU.mult, ["t1", "fin"], ["t1"])
        P.cp("pool", ya2[:, :, :], t1[:, :, :], ["t1"], ["ya2"])
        for jc in range(4):
            for b5 in range(NB5):
                bs = slice(b5 * 512, b5 * 512 + 512)
                ps, psn = nextps()
                for c in range(4):
                    P.mm(ps[:, :], wglu[:, c, jc * 128:(jc + 1) * 128], ya2[:, c, bs], [("wglu", c), "ya2"], [psn], start=(c == 0), stop=(c == 3))
                s_, sn = sg[(jc * NB5 + b5) % 2], "sg%d" % ((jc * NB5 + b5) % 2)
                P.act(s_[:, :], ps[:, :], AF.Sigmoid, [psn], [sn])
                P.tt("dve", yb[0][:, jc, bs], t1[:, jc, bs], s_[:, :], ALU.mult, ["t1", sn], [("yb0", jc)])
        P.dma(fin[:, :, :], fmv(oT)[:, :, tsl], ["fin"], ["fin"])
        P.cp("dve", yb[1][:, :, :], fin[:, :, :], ["fin"], ["yb1"])
        P.dma(fin[:, :, :], fmv(ycT)[:, :, tsl], ["fin"], ["fin"])
        P.cp("dve", yb[2][:, :, :], fin[:, :, :], ["fin"], ["yb2"])
        kk = 0
        for dc in range(8):
            for b5 in range(NB5):
                bs = slice(b5 * 512, b5 * 512 + 512)
                for n in range(3):
                    g_, gn = gst[kk % 2], "gst%d" % (kk % 2)
                    s_, sn = sg[kk % 2], "sg%d" % (kk % 2)
                    p_, pn = pr[kk % 2], "pr%d" % (kk % 2)
                    kk += 1
                    r0 = n * 1024 + dc * 128
                    P.dma(g_[:, :], glT[r0:r0 + 128, t0 + b5 * 512:t0 + b5 * 512 + 512], [], [gn], eng=("sp" if kk % 2 else "pool"))
                    P.act(s_[:, :], g_[:, :], AF.Sigmoid, [gn], [sn])
                    ps, psn = nextps()
                    for c in range(4):
                        P.mm(ps[:, :], wbr[:, n * 4 + c, dc * 128:(dc + 1) * 128], yb[n][:, c, bs], [("wbr", n * 4 + c), "yb%d" % n], [psn],
                             start=(c == 0), stop=(c == 3))
                    if n == 0:
                        P.tt("dve", macc[:, :], ps[:, :], s_[:, :], ALU.mult, [psn, sn], ["macc"])
                    else:
                        P.tt("dve", p_[:, :], ps[:, :], s_[:, :], ALU.mult, [psn, sn], [pn])
                        if n == 1:
                            P.tt("pool", macc[:, :], macc[:, :], p_[:, :], ALU.add, ["macc", pn], ["macc"])
                        else:
                            P.tt("pool", mT[:, dc, bs], macc[:, :], p_[:, :], ALU.add, ["macc", pn], [("mT", dc)])
        for tt_ in range(NTT):
            xi, xn = xin[tt_ % 2], "xin%d" % (tt_ % 2)
            xo_, xon = x1[tt_ % 2], "x1_%d" % (tt_ % 2)
            P.dma(xi[:, :], x_tok[t0 + tt_ * 128:t0 + (tt_ + 1) * 128, :], [], [xn])
            for ch in range(2):
                ps, psn = nextps()
                for c in range(8):
                    P.mm(ps[:, :], mT[:, c, tt_ * 128:(tt_ + 1) * 128], wout[:, c, ch * 512:(ch + 1) * 512], [("mT", c), ("wout", c)], [psn],
                         start=(c == 0), stop=(c == 7))
                P.tt("dve", xo_[:, ch * 512:(ch + 1) * 512], ps[:, :], xi[:, ch * 512:(ch + 1) * 512], ALU.add, [psn, xn], [(xon, ch)])
            P.dma(x_out[t0 + tt_ * 128:t0 + (tt_ + 1) * 128, :], xo_[:, :], [xon], [("x_out", half * NTT + tt_)], sk=xon)
    return _fin(P, nc, io, ("x_out",))


def build_L5b(TOK=2048, HT=1024, ctx=None):
    nc, P, io = _ctx(ctx)
    NH = TOK // HT
    NTT = HT // 128
    NB5 = HT // 512
    I = _mkI(nc, io)
    x_tok = I("x_tok", [TOK, D])
    g2 = I("g2", [128, 8])
    w_fi, w_fo = I("w_fi", [1024, 2 * DFF]), I("w_fo", [DFF, 1024])
    c_ident = I("c_ident", [128, 128])
    x_out = _mkO(nc, io, "x_out", [TOK, D])
    g2s = P.sb("g2s", [128, 8])
    P.dma(g2s[:, :], g2[:, :], [], ["g2s"])
    identf = P.sb("identf", [128, 128])
    P.dma(identf[:, :], c_ident[:, :], [], ["identf"])
    identb = P.sb("identb", [128, 128], BF16)
    P.cp("dve", identb[:, :], identf[:, :], ["identf"], ["identb"])
    wst = [P.sb("wst%d" % i, [128, 1024]) for i in range(3)]
    wk = [0]

    def load_w(dst, dn, sub, src_ap, ncol):
        i = wk[0] % 3
        wk[0] += 1
        P.dma(wst[i][:, :ncol], src_ap, [], ["wst%d" % i], eng=("sp" if i % 2 == 0 else "pool"))
        P.cp("pool" if i % 2 else "dve", dst, wst[i][:, :ncol], ["wst%d" % i], [(dn, sub)])
    wfo = P.sb("wfo", [128, 22, 1024], BF16)
    for c in range(22):
        load_w(wfo[:, c, :], "wfo", c, w_fo[c * 128:(c + 1) * 128, :], 1024)
    x1 = P.sb("x1", [128, NTT, D])
    junk = P.sb("junk", [128, D], BF16)
    ss = P.sb("ss5", [128, NTT])
    hb = [P.sb("hb%d" % i, [128, D], BF16) for i in range(2)]
    h2T = P.sb("h2T", [128, 8, HT], BF16)
    wfi_f = [P.sb("wfif%d" % i, [128, 8, 256]) for i in range(2)]
    wfi_b = [P.sb("wfib%d" % i, [128, 8, 256], BF16) for i in range(2)]
    actT = P.sb("actT", [128, 22, HT], BF16)
    sil = [P.sb("sil%d" % i, [128, 512]) for i in range(2)]
    xo = [P.sb("xo%d" % i, [128, 512]) for i in range(2)]
    pss = [P.ps("ps%d" % i, [128, 512]) for i in range(6)]
    psb = P.ps("psb", [128, 1024], BF16)
    pk = [0]

    def nextps():
        k = pk[0] % 6
        pk[0] += 1
        return pss[k], "ps%d" % k

    for half in range(NH):
        t0 = half * HT
        for tt_ in range(NTT):
            P.dma(x1[:, tt_, :], x_tok[t0 + tt_ * 128:t0 + (tt_ + 1) * 128, :], [], [("x1", tt_)], eng=("sp" if tt_ % 2 else "pool"))
            P.act(junk[:, :], x1[:, tt_, :], AF.Square, [("x1", tt_)], ["junk", ("ss5", tt_)], accum_out=ss[:, tt_:tt_ + 1])
        P.ts("dve", ss[:, :], ss[:, :], 1.0 / D, EPS, ALU.mult, ALU.add, ["ss5"], ["ss5"])
        P.act(ss[:, :], ss[:, :], AF.Sqrt, ["ss5"], ["ss5"])
        P.op("dve", lambda e: e.reciprocal(ss[:, :], ss[:, :]), ["ss5"], ["ss5"])
        for tt_ in range(NTT):
            h_, hn = hb[tt_ % 2], "hb%d" % (tt_ % 2)
            P.act(h_[:, :], x1[:, tt_, :], AF.Copy, [("x1", tt_), "ss5"], [hn], scale=ss[:, tt_:tt_ + 1])
            for c in range(8):
                P.tr(psb[:, c * 128:(c + 1) * 128], h_[:, c * 128:(c + 1) * 128], identb[:, :], [hn, "identb"], ["psb"])
            for c in range(8):
                P.ts("dve", h2T[:, c, tt_ * 128:(tt_ + 1) * 128], psb[:, c * 128:(c + 1) * 128], g2s[:, c:c + 1], None, ALU.mult, None,
                     ["psb", "g2s"], [("h2T", c)])
        for fc in range(22):
            wf, wfn = wfi_f[fc % 2], "wfif%d" % (fc % 2)
            wb_, wbn = wfi_b[fc % 2], "wfib%d" % (fc % 2)
            wv = w_fi.ap().rearrange("(c p) n -> p c n", p=128)
            P.dma(wf[:, :, 0:128], wv[:, :, fc * 128:(fc + 1) * 128], [], [(wfn, 0)], eng="sp", sk=wfn + "a")
            P.dma(wf[:, :, 128:256], wv[:, :, DFF + fc * 128:DFF + (fc + 1) * 128], [], [(wfn, 1)], eng="pool", sk=wfn + "b")
            P.cp("pool", wb_[:, :, :], wf[:, :, :], [wfn], [wbn])
            for b5 in range(NB5):
                bs = slice(b5 * 512, b5 * 512 + 512)
                pg, pgn = nextps()
                pu, pun = nextps()
                for c in range(8):
                    P.mm(pg[:, :], wb_[:, c, 0:128], h2T[:, c, bs], [wbn, ("h2T", c)], [pgn], start=(c == 0), stop=(c == 7))
                for c in range(8):
                    P.mm(pu[:, :], wb_[:, c, 128:256], h2T[:, c, bs], [wbn, ("h2T", c)], [pun], start=(c == 0), stop=(c == 7))
                s_, sn = sil[(fc * NB5 + b5) % 2], "sil%d" % ((fc * NB5 + b5) % 2)
                P.act(s_[:, :], pg[:, :], AF.Silu, [pgn], [sn])
                P.tt("dve", actT[:, fc, bs], pu[:, :], s_[:, :], ALU.mult, [pun, sn], [("actT", fc)])
        for tt_ in range(NTT):
            for ch in range(2):
                ps, psn = nextps()
                for fc in range(22):
                    P.mm(ps[:, :], actT[:, fc, tt_ * 128:(tt_ + 1) * 128], wfo[:, fc, ch * 512:(ch + 1) * 512], [("actT", fc), ("wfo", fc)], [psn],
                         start=(fc == 0), stop=(fc == 21))
                o_, on = xo[(tt_ * 2 + ch) % 2], "xo%d" % ((tt_ * 2 + ch) % 2)
                P.tt("dve", o_[:, :], ps[:, :], x1[:, tt_, ch * 512:(ch + 1) * 512], ALU.add, [psn, ("x1", tt_)], [on])
                P.dma(x_out[t0 + tt_ * 128:t0 + (tt_ + 1) * 128, ch * 512:(ch + 1) * 512], o_[:, :], [on],
                      [("x_out", (half * NTT + tt_) * 2 + ch)], sk=on)
    return _fin(P, nc, io, ("x_out",))


def _hw_runner(nc, in_maps):
    n = len(in_maps)
    maps = list(in_maps) + [in_maps[-1]] * (NCORES - n)
    res = run_bass_kernel_spmd(nc, maps, core_ids=list(range(NCORES)))
    return res.results[:n]


def kernel_impl(inp, runner=_hw_runner, TOKL=2048, HT=512):
    f = lambda a: np.ascontiguousarray(a, dtype=np.float32)
    x = np.asarray(inp["x"], dtype=np.float32)
    B, L, _ = x.shape
    NTOK = B * L
    ntc = NTOK // TOKL
    xcur = x.reshape(NTOK, D)
    depth = inp["w_in"].shape[0]
    progs = {}

    def prog(name, fn):
        if name not in progs:
            progs[name] = fn()
        return progs[name]
    ident = np.eye(128, dtype=np.float32)
    vfirst = None
    for i in range(depth):
        lam_init = 0.8 - 0.6 * float(np.exp(-0.3 * i))
        nc = prog("L1", lambda: build_L1(TOKL))
        g1 = f(np.asarray(inp["norm1_g"][i]).reshape(8, 128).T)
        w_in = f(inp["w_in"][i])
        maps = []
        for c in range(ntc):
            xs = xcur[c * TOKL:(c + 1) * TOKL]
            maps.append(dict(x_tok=f(xs), xT=f(xs.T), w=w_in, g=g1))
        res = runner(nc, maps)
        proj = np.concatenate([r["proj"] for r in res], 0).reshape(B, L, NIN)
        u, q, k, v = proj[..., 0:512], proj[..., 512:1024], proj[..., 1024:1536], proj[..., 1536:2048]
        z, gl = proj[..., 2048:3744], proj[..., 3744:]
        nc = prog("L2", lambda: build_L2(L))
        p2 = dict(lam_re=np.asarray(inp["s5_lambda_re"][i]), lam_im=np.asarray(inp["s5_lambda_im"][i]), log_dt=np.asarray(inp["s5_log_dt"][i]),
                  b_re=np.asarray(inp["s5_b_re"][i]), b_im=np.asarray(inp["s5_b_im"][i]), c_re=np.asarray(inp["s5_c_re"][i]),
                  c_im=np.asarray(inp["s5_c_im"][i]), d=np.asarray(inp["s5_d"][i]))
        res = runner(nc, [l2_inputs(u[c // 2], p2, c % 2, L) for c in range(2 * B)])
        ypre = np.stack([np.concatenate([res[2 * b]["yT"], res[2 * b + 1]["yT"]], 0) for b in range(B)])
        nc = prog("L3", lambda: build_L3(L))
        p3 = dict(q_gain=np.asarray(inp["da_q_gain"][i]), k_gain=np.asarray(inp["da_k_gain"][i]), lam=np.asarray(inp["da_lambda"][i]),
                  subln_g=np.asarray(inp["da_subln_g"][i]))
        res = runner(nc, [l3_inputs(q[c // 2], k[c // 2], v[c // 2], p3, c % 2, L, lam_init) for c in range(2 * B)])
        oat = np.stack([np.concatenate([res[2 * b]["oT"].reshape(256, L), res[2 * b + 1]["oT"].reshape(256, L)], 0) for b in range(B)])
        vg = i > 0
        nc = prog("L4v" if vg else "L4", lambda: build_L4(L, has_vgate=vg))
        p4 = dict(mu=np.asarray(inp["rw_mu"][i]), w0=np.asarray(inp["rw_w0"][i]), w2=np.asarray(inp["rw_w2"][i]), a0=np.asarray(inp["rw_a0"][i]),
                  a2=np.asarray(inp["rw_a2"][i]), g2=np.asarray(inp["rw_g2"][i]), k_k=np.asarray(inp["rw_k_k"][i]), k_a=np.asarray(inp["rw_k_a"][i]),
                  r_k=np.asarray(inp["rw_r_k"][i]), ln_g=np.asarray(inp["rw_ln_g"][i]), ln_b=np.asarray(inp["rw_ln_b"][i]))
        if vg:
            p4.update(v0=np.asarray(inp["rw_v0"][i - 1]), v1=np.asarray(inp["rw_v1"][i - 1]), v2=np.asarray(inp["rw_v2"][i - 1]))
        maps = []
        for c in range(2 * B):
            b, hg = c // 2, c % 2
            maps.append(l4_inputs(z[b], p4, hg, L, vfirst=(vfirst[b][256 * hg:256 * hg + 256] if vg else None)))
        res = runner(nc, maps)
        yc = np.stack([np.concatenate([res[2 * b]["y_out"], res[2 * b + 1]["y_out"]], 1) for b in range(B)])
        if not vg:
            vfirst = np.stack([np.concatenate([res[2 * b]["vfirst_out"], res[2 * b + 1]["vfirst_out"]], 0) for b in range(B)])
        nc = prog("L5a", lambda: build_L5a(TOKL, HT))
        wa = dict(w_glu=f(inp["s5_w_glu"][i]), w_br=f(inp["w_branch"][i]), w_out=f(inp["w_out"][i]))
        maps = []
        for c in range(ntc):
            t0 = c * TOKL
            b, s0 = t0 // L, t0 % L
            sl = slice(s0, s0 + TOKL)
            maps.append(dict(wa, x_tok=f(xcur[t0:t0 + TOKL]), ypreT=f(ypre[b][:, sl]), oT=f(oat[b][:, sl]), ycT=f(yc[b][sl].T), glT=f(gl[b][sl].T)))
        res = runner(nc, maps)
        x1 = np.concatenate([r["x_out"] for r in res], 0)
        nc = prog("L5b", lambda: build_L5b(TOKL, HT))
        wb_ = dict(g2=f(np.asarray(inp["norm2_g"][i]).reshape(8, 128).T), w_fi=f(inp["w_ffn_in"][i]), w_fo=f(inp["w_ffn_out"][i]), c_ident=ident)
        res = runner(nc, [dict(wb_, x_tok=f(x1[c * TOKL:(c + 1) * TOKL])) for c in range(ntc)])
        xcur = np.concatenate([r["x_out"] for r in res], 0)
    return xcur.reshape(B, L, D).astype(np.float32)


def build_fused(L=4096, depth=2):
    nc = bass.Bass("TRN2", target_bir_lowering=False)
    sh = Shared(nc)
    E = lambda n, s: nc.dram_tensor(n, list(s), F32, kind="ExternalInput")
    x_in = E("x_in", [L, D])
    x_out = nc.dram_tensor("x_final", [L, D], F32, kind="ExternalOutput")
    FT = nc.dram_tensor("FT", [6304, L], F32)
    VT = nc.dram_tensor("VT", [L, 512], F32)
    YB = nc.dram_tensor("YB", [1536, L], F32)
    X1 = nc.dram_tensor("X1", [L, D], F32)
    XS = nc.dram_tensor("XS", [L, D], F32)
    VF = nc.dram_tensor("VF", [512, L], F32)
    VF2 = nc.dram_tensor("VF2", [512, L], F32)
    cs = {}
    LHr = min(L, 1024)
    for n, shp in (("c_ident", [128, 128]), ("c_cmask", [128, 4, 512]), ("c_b64", [128, 128]), ("c_ones", [128, 128]),
                   ("c_mask4", [128, 512]), ("c_sl", [128, 128]), ("c_reset", [128, LHr]), ("c_bones", [128, 128]), ("c_hsel", [128, 2])):
        cs[n] = E(n, shp)
    TP = min(2048, L)
    for i in range(depth):
        sfx = "_%d" % i
        xin = x_in if i == 0 else XS
        xo = x_out if i == depth - 1 else XS
        io = dict(x_tok=xin, w=E("w_in" + sfx, [D, NIN]), g=E("g1" + sfx, [128, 8]), c_ident=cs["c_ident"], FT=FT, VT=VT)
        build_L1F(L, ctx=(nc, sh, io))
        s5 = {n: E("s5" + n + sfx, [2, 128, 8, 128]) for n in ("bpr", "bpi", "cpr", "cpi")}
        s5prm = E("s5prm" + sfx, [2, 128, 3, 8])
        s5d = E("s5d" + sfx, [2, 128, 2])
        for gh in range(2):
            io = dict(uT=V(FT.ap()[256 * gh:256 * gh + 256, :]), prm=V(s5prm.ap()[gh]), dcol=V(s5d.ap()[gh]),
                      yT=V(YB.ap()[256 * gh:256 * gh + 256, :]))
            for n in s5:
                io[n] = V(s5[n].ap()[gh])
            build_L2(L, ctx=(nc, sh, io))
        gains = E("da_gains" + sfx, [128, 4])
        lamv = E("da_lamv" + sfx, [128, 4, 64])
        laminit = E("da_laminit" + sfx, [128, 1])
        for hp in range(2):
            hv = lambda a, r0: V(a.ap()[r0:r0 + 256, :].rearrange("(h p) l -> h p l", h=2))
            io = dict(qT=hv(FT, 512 + 256 * hp), kT=hv(FT, 1024 + 256 * hp),
                      vtok=V(VT.ap()[:, 256 * hp:256 * hp + 256].rearrange("l (h d) -> h l d", h=2)),
                      gains=gains, lamv=lamv, laminit=laminit, c_cmask=cs["c_cmask"], c_b64=cs["c_b64"], c_ones=cs["c_ones"],
                      oT=hv(YB, 512 + 256 * hp))
            build_L3(L, ctx=(nc, sh, io))
        vg = i > 0
        r4 = dict(mu3=E("rw_mu3" + sfx, [2, 128, 6]), mulo=E("rw_mulo" + sfx, [96, 3]), w2=E("rw_w2p" + sfx, [2, 32, 256]),
                  a2=E("rw_a2p" + sfx, [2, 32, 256]), g2=E("rw_g2p" + sfx, [2, 96, 256]), pc=E("rw_pc" + sfx, [2, 128, 10]),
                  lng=E("rw_lng" + sfx, [2, 128, 256]), lnb=E("rw_lnb" + sfx, [2, 128, 256]))
        if vg:
            r4.update(muvf=E("rw_muvf" + sfx, [128, 4]), v1=E("rw_v1p" + sfx, [128, 4, 32]), v2=E("rw_v2p" + sfx, [2, 32, 256]),
                      v0=E("rw_v0p" + sfx, [2, 128, 2]))
        for hg in range(2):
            R = lambda r0, n: V(FT.ap()[r0:r0 + n, :])
            io = dict(zr=R(1536 + 256 * hg, 256), zk=R(2048 + 256 * hg, 256), zv=R(2560 + 256 * hg, 256),
                      zw=R(3072, 32), za=R(3104, 32), zg=R(3136, 96), mulo=r4["mulo"],
                      y_out=V(YB.ap()[1024 + 256 * hg:1024 + 256 * hg + 256, :]),
                      vfirst_out=V((VF2 if vg else VF).ap()[256 * hg:256 * hg + 256, :]))
            for n in ("mu3", "w2", "a2", "g2", "pc", "lng", "lnb"):
                io[n] = V(r4[n].ap()[hg])
            for n in ("c_mask4", "c_sl", "c_reset", "c_bones", "c_hsel", "c_ident"):
                io[n] = cs[n]
            if vg:
                io.update(zvfull=R(2560, 512), muvf=r4["muvf"], v1=r4["v1"], v2=V(r4["v2"].ap()[hg]), v0=V(r4["v0"].ap()[hg]),
                          vfirst_in=V(VF.ap()[256 * hg:256 * hg + 256, :]))
            build_L4(L, has_vgate=vg, ctx=(nc, sh, io), fm_out=True)
        wa = dict(w_glu=E("w_glu" + sfx, [512, 512]), w_br=E("w_br" + sfx, [3, 512, 1024]), w_out=E("w_out" + sfx, [1024, 1024]))
        for th in range(L // TP):
            tsl = slice(th * TP, th * TP + TP)
            io = dict(wa, x_tok=V(xin.ap()[tsl, :]), ypreT=V(YB.ap()[0:512, tsl]), oT=V(YB.ap()[512:1024, tsl]), ycT=V(YB.ap()[1024:1536, tsl]),
                      glT=V(FT.ap()[3232:6304, tsl]), x_out=V(X1.ap()[tsl, :]))
            build_L5a(TP, 512, ctx=(nc, sh, io))
        wb_ = dict(g2=E("g2n" + sfx, [128, 8]), w_fi=E("w_fi" + sfx, [1024, 2 * DFF]), w_fo=E("w_fo" + sfx, [DFF, 1024]), c_ident=cs["c_ident"])
        for th in range(L // TP):
            tsl = slice(th * TP, th * TP + TP)
            io = dict(wb_, x_tok=V(X1.ap()[tsl, :]), x_out=V(xo.ap()[tsl, :]))
            if i == depth - 1:
                io["_final"] = True
            build_L5b(TP, 512, ctx=(nc, sh, io))
    sh.close()
    return nc


def fused_inputs(inp, L):
    f = lambda a: np.ascontiguousarray(a, dtype=np.float32)
    d = {}
    d.update(at_consts())
    rc = rw_consts(L)
    d.update(rc)
    depth = inp["w_in"].shape[0]
    zL = np.zeros((L, 512), np.float32)
    zz = np.zeros((L, 1696), np.float32)
    for i in range(depth):
        sfx = "_%d" % i
        lam_init = 0.8 - 0.6 * float(np.exp(-0.3 * i))
        d["w_in" + sfx] = f(inp["w_in"][i])
        d["g1" + sfx] = f(np.asarray(inp["norm1_g"][i]).reshape(8, 128).T)
        p2 = dict(lam_re=np.asarray(inp["s5_lambda_re"][i]), lam_im=np.asarray(inp["s5_lambda_im"][i]), log_dt=np.asarray(inp["s5_log_dt"][i]),
                  b_re=np.asarray(inp["s5_b_re"][i]), b_im=np.asarray(inp["s5_b_im"][i]), c_re=np.asarray(inp["s5_c_re"][i]),
                  c_im=np.asarray(inp["s5_c_im"][i]), d=np.asarray(inp["s5_d"][i]))
        l2 = [l2_inputs(zL, p2, gh, L) for gh in range(2)]
        for n in ("bpr", "bpi", "cpr", "cpi"):
            d["s5" + n + sfx] = f(np.stack([l2[gh][n] for gh in range(2)]))
        d["s5prm" + sfx] = f(np.stack([l2[gh]["prm"] for gh in range(2)]))
        d["s5d" + sfx] = f(np.stack([l2[gh]["dcol"] for gh in range(2)]))
        p3 = dict(q_gain=np.asarray(inp["da_q_gain"][i]), k_gain=np.asarray(inp["da_k_gain"][i]), lam=np.asarray(inp["da_lambda"][i]),
                  subln_g=np.asarray(inp["da_subln_g"][i]))
        l3 = l3_inputs(zL, zL, zL, p3, 0, L, lam_init)
        d["da_gains" + sfx], d["da_lamv" + sfx], d["da_laminit" + sfx] = l3["gains"], l3["lamv"], l3["laminit"]
        p4 = dict(mu=np.asarray(inp["rw_mu"][i]), w0=np.asarray(inp["rw_w0"][i]), w2=np.asarray(inp["rw_w2"][i]), a0=np.asarray(inp["rw_a0"][i]),
                  a2=np.asarray(inp["rw_a2"][i]), g2=np.asarray(inp["rw_g2"][i]), k_k=np.asarray(inp["rw_k_k"][i]), k_a=np.asarray(inp["rw_k_a"][i]),
                  r_k=np.asarray(inp["rw_r_k"][i]), ln_g=np.asarray(inp["rw_ln_g"][i]), ln_b=np.asarray(inp["rw_ln_b"][i]))
        vg = i > 0
        if vg:
            p4.update(v0=np.asarray(inp["rw_v0"][i - 1]), v1=np.asarray(inp["rw_v1"][i - 1]), v2=np.asarray(inp["rw_v2"][i - 1]))
        l4 = [l4_inputs(zz, p4, hg, L, vfirst=(np.zeros((256, L), np.float32) if vg else None)) for hg in range(2)]
        for n, m in (("mu3", "rw_mu3"), ("w2", "rw_w2p"), ("a2", "rw_a2p"), ("g2", "rw_g2p"), ("pc", "rw_pc"), ("lng", "rw_lng"), ("lnb", "rw_lnb")):
            d[m + sfx] = f(np.stack([l4[hg][n] for hg in range(2)]))
        d["rw_mulo" + sfx] = l4[0]["mulo"]
        if vg:
            d["rw_muvf" + sfx] = l4[0]["muvf"]
            d["rw_v1p" + sfx] = l4[0]["v1"]
            d["rw_v2p" + sfx] = f(np.stack([l4[hg]["v2"] for hg in range(2)]))
            d["rw_v0p" + sfx] = f(np.stack([l4[hg]["v0"] for hg in range(2)]))
        d["w_glu" + sfx] = f(inp["s5_w_glu"][i])
        d["w_br" + sfx] = f(inp["w_branch"][i])
        d["w_out" + sfx] = f(inp["w_out"][i])
        d["g2n" + sfx] = f(np.asarray(inp["norm2_g"][i]).reshape(8, 128).T)
        d["w_fi" + sfx] = f(inp["w_ffn_in"][i])
        d["w_fo" + sfx] = f(inp["w_ffn_out"][i])
    return d


def kernel_fused(inp, runner=None):
    x = np.asarray(inp["x"], dtype=np.float32)
    B, L, _ = x.shape
    depth = inp["w_in"].shape[0]
    nc = build_fused(L, depth)
    common = fused_inputs(inp, L)
    ncore = NCORES if runner is None else B
    maps = [dict(common, x_in=np.ascontiguousarray(x[c % B])) for c in range(ncore)]
    if runner is None:
        res = run_bass_kernel_spmd(nc, maps, core_ids=list(range(NCORES))).results
    else:
        res = runner(nc, maps)
    return np.stack([res[b]["x_final"] for b in range(B)]).astype(np.float32)


I32 = mybir.dt.int32


def _dyn_dma(P, hreg, tmpregs, ctr, dst, tensor, base, hscale, dims, reads, writes, sk):
    def fn(e):
        r = tmpregs[ctr[0] % len(tmpregs)]
        ctr[0] += 1
        e.reg_mul(r, hreg, hscale)
        e.reg_add(r, r, base)
        return e.dma_start(out=dst, in_=bass.AP(tensor, r, [list(d) for d in dims]))
    return P.op("pool", fn, reads, writes, dma=True, sk=sk)


def phase_load_h(nc, sh, regs, hid):
    P = Prog(nc, sh)
    hreg, _ = regs
    hs = P.stack.enter_context(nc.sbuf_tensor(P.pfx + "hs", [1, 1], I32))
    P.dma(hs[:, :], hid[:, :], [], ["hs"], eng="pool")
    P.op("pool", lambda e: e.reg_load(hreg, hs[:1, :1]), ["hs"], ["hreg"])
    P.emit()


def _chunked_allgather(P, cc, src, dst, chunks, groups):
    for ci, (r0, n) in enumerate(chunks):
        P.op("pool", lambda e, r0=r0, n=n: e.collective_compute("AllGather", ALU.bypass, replica_groups=groups,
                                                                ins=[src.ap()[r0:r0 + n, :]], outs=[dst.ap()[2 * r0:2 * r0 + 2 * n, :]]),
             [], [("gat", ci)], dma=True, cc=cc)


def phase_xchg_G(nc, sh, regs, hid, G_in, G_out, FTm, VTm, TOKH, groups):
    P = Prog(nc, sh)
    hreg, tmpregs = regs
    ctr = [0]
    chunks = [(256 * c, 256) for c in range(12)] + [(3072, 160), (3232, 256), (3488, 256)]
    _chunked_allgather(P, 0, G_in, G_out, chunks, groups)
    HT2 = TOKH // 2
    for t in range(2):
        tc = slice(t * TOKH, (t + 1) * TOKH)
        dstA = FTm.ap()[0:1536, tc].rearrange("(s r) c -> s r c", s=6)
        _dyn_dma(P, hreg, tmpregs, ctr, dstA, G_out, (256 * t) * TOKH, 512 * TOKH, [[1024 * TOKH, 6], [TOKH, 256], [1, TOKH]],
                 ["gat"], [("FTm", 3 * t)], "rlA%d" % t)
        _dyn_dma(P, hreg, tmpregs, ctr, FTm.ap()[1536:1792, tc], G_out, (5632 + 256 * t) * TOKH, -512 * TOKH, [[TOKH, 256], [1, TOKH]],
                 ["gat"], [("FTm", 3 * t + 1)], "rlB%d" % t)
        P.dma(FTm.ap()[1792:1952, tc], G_out.ap()[6144 + 160 * t:6144 + 160 * t + 160, :], ["gat"], [("FTm", 3 * t + 2)], eng="pool", sk="rlC%d" % t)
        for q_ in range(2):
            _dyn_dma(P, hreg, tmpregs, ctr, VTm.ap()[t * TOKH + q_ * HT2:t * TOKH + (q_ + 1) * HT2, :], G_out,
                     (6464 + 512 * q_ + 256 * t) * TOKH, 256, [[512, HT2], [1, 256]], ["gat"], [("VTm", 2 * t + q_)], "rlv%d" % (2 * t + q_))
    P.emit()


def phase_xchg_Y(nc, sh, regs, hid, Y_in, Y_out, YBh, L, TOKH, groups):
    P = Prog(nc, sh)
    hreg, tmpregs = regs
    ctr = [0]
    _chunked_allgather(P, 1, Y_in, Y_out, [(256 * c, 256) for c in range(6)], groups)
    for r in range(2):
        dst = YBh.ap().rearrange("(n q) c -> n q c", n=3)[:, r * 256:(r + 1) * 256, :]
        _dyn_dma(P, hreg, tmpregs, ctr, dst, Y_out, (256 * r) * TOKH, 1536 * TOKH, [[512 * TOKH, 3], [TOKH, 256], [1, TOKH]],
                 ["gat"], [("YBh", r)], "rly%d" % r)
    P.emit()


def build_fused8(L=4096, depth=2, ncores=8):
    nc = bass.Bass("TRN2", target_bir_lowering=False)
    sh = Shared(nc)
    TOKH = L // 2
    groups = [[2 * i, 2 * i + 1] for i in range(ncores // 2)]
    hreg = sh.stack.enter_context(nc.gpsimd.register("hreg"))
    tmpregs = [sh.stack.enter_context(nc.gpsimd.register("tmpr%d" % i)) for i in range(2)]
    regs = (hreg, tmpregs)
    E = lambda n, s: nc.dram_tensor(n, list(s), F32, kind="ExternalInput")
    x_in = E("x_in", [TOKH, D])
    hid = nc.dram_tensor("hid", [1, 1], I32, kind="ExternalInput")
    x_out = nc.dram_tensor("x_final", [TOKH, D], F32, kind="ExternalOutput")
    G_in = nc.dram_tensor("G_in", [3744, TOKH], F32)
    G_out = nc.dram_tensor("G_out", [2 * 3744, TOKH], F32)
    GL = nc.dram_tensor("GL", [3072, TOKH], F32)
    FTm = nc.dram_tensor("FTm", [1952, L], F32)
    VTm = nc.dram_tensor("VTm", [L, 256], F32)
    Y_in = nc.dram_tensor("Y_in", [2 * 768, TOKH], F32)
    Y_out = nc.dram_tensor("Y_out", [4 * 768, TOKH], F32)
    ysl = lambda r0, n: [Y_in.ap()[j * 768 + r0:j * 768 + r0 + n, :] for j in range(2)]
    YBh = nc.dram_tensor("YBh", [1536, TOKH], F32)
    X1 = nc.dram_tensor("X1", [TOKH, D], F32)
    XS = nc.dram_tensor("XS", [TOKH, D], F32)
    VF = nc.dram_tensor("VF", [256, L], F32)
    VF2 = nc.dram_tensor("VF2", [256, L], F32)
    cs = {}
    LHr = min(L, 1024)
    for n, shp in (("c_ident", [128, 128]), ("c_cmask", [128, 4, 512]), ("c_b64", [128, 128]), ("c_ones", [128, 128]),
                   ("c_mask4", [128, 512]), ("c_sl", [128, 128]), ("c_reset", [128, LHr]), ("c_bones", [128, 128]), ("c_hsel", [128, 2])):
        cs[n] = E(n, shp)
    R = lambda r0, n: V(FTm.ap()[r0:r0 + n, :])
    phase_load_h(nc, sh, regs, hid)
    for i in range(depth):
        sfx = "_%d" % i
        xin = x_in if i == 0 else XS
        xo = x_out if i == depth - 1 else XS
        io = dict(x_tok=xin, w=E("w_in" + sfx, [D, NIN]), g=E("g1" + sfx, [128, 8]), c_ident=cs["c_ident"], G=G_in, GL=GL)
        build_L1G(TOKH, ctx=(nc, sh, io))
        phase_xchg_G(nc, sh, regs, hid, G_in, G_out, FTm, VTm, TOKH, groups)
        io = dict(uT=R(0, 256), prm=E("s5prm" + sfx, [128, 3, 8]), dcol=E("s5d" + sfx, [128, 2]), yT=VTok2(ysl(0, 256), TOKH))
        for n in ("bpr", "bpi", "cpr", "cpi"):
            io[n] = E("s5" + n + sfx, [128, 8, 128])
        build_L2(L, ctx=(nc, sh, io))
        hv = lambda ap_: V(ap_.rearrange("(h p) l -> h p l", h=2))
        io = dict(qT=hv(FTm.ap()[256:512, :]), kT=hv(FTm.ap()[512:768, :]), vtok=V(VTm.ap().rearrange("l (h d) -> h l d", h=2)),
                  gains=E("da_gains" + sfx, [128, 4]), lamv=E("da_lamv" + sfx, [128, 4, 64]), laminit=E("da_laminit" + sfx, [128, 1]),
                  c_cmask=cs["c_cmask"], c_b64=cs["c_b64"], c_ones=cs["c_ones"],
                  oT=VTok2([a_.rearrange("(h p) l -> h p l", h=2) for a_ in ysl(256, 256)], TOKH))
        build_L3(L, ctx=(nc, sh, io))
        vg = i > 0
        io = dict(zr=R(768, 256), zk=R(1024, 256), zv=R(1280, 256), zw=R(1792, 32), za=R(1824, 32), zg=R(1856, 96),
                  mu3=E("rw_mu3" + sfx, [128, 6]), mulo=E("rw_mulo" + sfx, [96, 3]), w2=E("rw_w2p" + sfx, [32, 256]),
                  a2=E("rw_a2p" + sfx, [32, 256]), g2=E("rw_g2p" + sfx, [96, 256]), pc=E("rw_pc" + sfx, [128, 10]),
                  lng=E("rw_lng" + sfx, [128, 256]), lnb=E("rw_lnb" + sfx, [128, 256]),
                  y_out=VTok2(ysl(512, 256), TOKH), vfirst_out=(VF2 if vg else VF))
        for n in ("c_mask4", "c_sl", "c_reset", "c_bones", "c_hsel", "c_ident"):
            io[n] = cs[n]
        if vg:
            io.update(zvfull=R(1280, 512), muvf=E("rw_muvf" + sfx, [128, 4]), v1=E("rw_v1p" + sfx, [128, 4, 32]),
                      v2=E("rw_v2p" + sfx, [32, 256]), v0=E("rw_v0p" + sfx, [128, 2]), vfirst_in=VF)
        build_L4(L, has_vgate=vg, ctx=(nc, sh, io), fm_out=True)
        phase_xchg_Y(nc, sh, regs, hid, Y_in, Y_out, YBh, L, TOKH, groups)
        io = dict(w_glu=E("w_glu" + sfx, [512, 512]), w_br=E("w_br" + sfx, [3, 512, 1024]), w_out=E("w_out" + sfx, [1024, 1024]),
                  x_tok=xin, ypreT=V(YBh.ap()[0:512, :]), oT=V(YBh.ap()[512:1024, :]), ycT=V(YBh.ap()[1024:1536, :]), glT=GL, x_out=X1)
        build_L5a(TOKH, 512, ctx=(nc, sh, io))
        io = dict(g2=E("g2n" + sfx, [128, 8]), w_fi=E("w_fi" + sfx, [1024, 2 * DFF]), w_fo=E("w_fo" + sfx, [DFF, 1024]), c_ident=cs["c_ident"],
                  x_tok=X1, x_out=xo)
        if i == depth - 1:
            io["_final"] = True
        build_L5b(TOKH, 512, ctx=(nc, sh, io))
    sh.close()
    return nc


def fused8_inputs(inp, L, h):
    f = lambda a: np.ascontiguousarray(a, dtype=np.float32)
    d = {}
    d.update(at_consts())
    d.update(rw_consts(L))
    d["hid"] = np.array([[h]], np.int32)
    depth = inp["w_in"].shape[0]
    zL = np.zeros((L, 512), np.float32)
    zz = np.zeros((L, 1696), np.float32)
    for i in range(depth):
        sfx = "_%d" % i
        lam_init = 0.8 - 0.6 * float(np.exp(-0.3 * i))
        d["w_in" + sfx] = f(inp["w_in"][i])
        d["g1" + sfx] = f(np.asarray(inp["norm1_g"][i]).reshape(8, 128).T)
        p2 = dict(lam_re=np.asarray(inp["s5_lambda_re"][i]), lam_im=np.asarray(inp["s5_lambda_im"][i]), log_dt=np.asarray(inp["s5_log_dt"][i]),
                  b_re=np.asarray(inp["s5_b_re"][i]), b_im=np.asarray(inp["s5_b_im"][i]), c_re=np.asarray(inp["s5_c_re"][i]),
                  c_im=np.asarray(inp["s5_c_im"][i]), d=np.asarray(inp["s5_d"][i]))
        l2 = l2_inputs(zL, p2, h, L)
        for n in ("bpr", "bpi", "cpr", "cpi"):
            d["s5" + n + sfx] = l2[n]
        d["s5prm" + sfx], d["s5d" + sfx] = l2["prm"], l2["dcol"]
        p3 = dict(q_gain=np.asarray(inp["da_q_gain"][i]), k_gain=np.asarray(inp["da_k_gain"][i]), lam=np.asarray(inp["da_lambda"][i]),
                  subln_g=np.asarray(inp["da_subln_g"][i]))
        l3 = l3_inputs(zL, zL, zL, p3, 0, L, lam_init)
        d["da_gains" + sfx], d["da_lamv" + sfx], d["da_laminit" + sfx] = l3["gains"], l3["lamv"], l3["laminit"]
        p4 = dict(mu=np.asarray(inp["rw_mu"][i]), w0=np.asarray(inp["rw_w0"][i]), w2=np.asarray(inp["rw_w2"][i]), a0=np.asarray(inp["rw_a0"][i]),
                  a2=np.asarray(inp["rw_a2"][i]), g2=np.asarray(inp["rw_g2"][i]), k_k=np.asarray(inp["rw_k_k"][i]), k_a=np.asarray(inp["rw_k_a"][i]),
                  r_k=np.asarray(inp["rw_r_k"][i]), ln_g=np.asarray(inp["rw_ln_g"][i]), ln_b=np.asarray(inp["rw_ln_b"][i]))
        vg = i > 0
        if vg:
            p4.update(v0=np.asarray(inp["rw_v0"][i - 1]), v1=np.asarray(inp["rw_v1"][i - 1]), v2=np.asarray(inp["rw_v2"][i - 1]))
        l4 = l4_inputs(zz, p4, h, L, vfirst=(np.zeros((256, L), np.float32) if vg else None))
        for n, m in (("mu3", "rw_mu3"), ("mulo", "rw_mulo"), ("w2", "rw_w2p"), ("a2", "rw_a2p"), ("g2", "rw_g2p"), ("pc", "rw_pc"),
                     ("lng", "rw_lng"), ("lnb", "rw_lnb")):
            d[m + sfx] = l4[n]
        if vg:
            perm = np.concatenate([np.arange(256 * h, 256 * h + 256), np.arange(256 * (1 - h), 256 * (1 - h) + 256)])
            muv = np.asarray(inp["rw_mu"][i])[1024:1536][perm]
            d["rw_muvf" + sfx] = f(muv.reshape(4, 128).T)
            d["rw_v1p" + sfx] = f(np.asarray(inp["rw_v1"][i - 1])[perm].reshape(4, 128, 32).transpose(1, 0, 2))
            d["rw_v2p" + sfx] = l4["v2"]
            d["rw_v0p" + sfx] = l4["v0"]
        d["w_glu" + sfx] = f(inp["s5_w_glu"][i])
        d["w_br" + sfx] = f(inp["w_branch"][i])
        d["w_out" + sfx] = f(inp["w_out"][i])
        d["g2n" + sfx] = f(np.asarray(inp["norm2_g"][i]).reshape(8, 128).T)
        d["w_fi" + sfx] = f(inp["w_ffn_in"][i])
        d["w_fo" + sfx] = f(inp["w_ffn_out"][i])
    return d


def kernel_fused8(inp, runner=None):
    x = np.asarray(inp["x"], dtype=np.float32)
    B, L, _ = x.shape
    depth = inp["w_in"].shape[0]
    ncores = 2 * B
    nc = build_fused8(L, depth, ncores)
    TOKH = L // 2
    common = [fused8_inputs(inp, L, h) for h in range(2)]
    maps = [dict(common[c % 2], x_in=np.ascontiguousarray(x[c // 2, (c % 2) * TOKH:(c % 2 + 1) * TOKH])) for c in range(ncores)]
    if runner is None:
        res = run_bass_kernel_spmd(nc, maps, core_ids=list(range(ncores))).results
    else:
        res = runner(nc, maps)
    out = np.stack([np.concatenate([res[2 * b]["x_final"], res[2 * b + 1]["x_final"]], 0) for b in range(B)])
    return out.astype(np.float32)


def kernel(**inputs):
    return kernel_fused8(inputs)
```

```python
from contextlib import ExitStack
import numpy as np
import concourse.bass as bass
import concourse.mybir as mybir

F32 = mybir.dt.float32
BF16 = mybir.dt.bfloat16
ALU = mybir.AluOpType
AF = mybir.ActivationFunctionType

ENGS = ["pe", "act", "dve", "pool", "sp"]


class Shared:
    NDQ = 92

    def __init__(self, nc):
        self.nc = nc
        self.stack = ExitStack()
        names = ["e_" + e for e in ENGS] + ["dq%d" % i for i in range(self.NDQ)] + ["cc0", "cc1"]
        self.sems = {n: self.stack.enter_context(nc.semaphore(n)) for n in names}
        self.cnt = {n: 0 for n in names}

    def close(self):
        self.stack.close()


class V:
    def __init__(self, ap):
        self._ap = ap

    def ap(self):
        return self._ap

    def __getitem__(self, idx):
        return self._ap[idx]


class VTok2:
    def __init__(self, aps, tokh):
        self.aps, self.tokh = aps, tokh

    def __getitem__(self, idx):
        idx = list(idx)
        sl = idx[-1]
        j = sl.start // self.tokh
        assert (sl.stop - 1) // self.tokh == j
        idx[-1] = slice(sl.start - j * self.tokh, sl.stop - j * self.tokh)
        return self.aps[j][tuple(idx)]


class Prog:
    _pid = [0]

    def __init__(self, nc, shared=None):
        self.shared = shared
        Prog._pid[0] += 1
        self.pfx = "p%d_" % Prog._pid[0]
        self.nc = nc
        self.ins = []
        self.byeng = {e: [] for e in ENGS}
        self.lastw = {}
        self.rds = {}
        self.stack = ExitStack()
        self.n_ps = 0

    def sb(self, name, shape, dt=F32):
        return self.stack.enter_context(self.nc.sbuf_tensor(self.pfx + name, list(shape), dt))

    def ps(self, name, shape, dt=F32):
        return self.stack.enter_context(self.nc.psum_tensor(self.pfx + name, list(shape), dt))

    def dram(self, name, shape, dt=F32, kind="Internal"):
        return self.nc.dram_tensor(name, list(shape), dt, kind=kind)

    @staticmethod
    def _ov(a, b):
        return a is None or b is None or a == b

    def _norm(self, keys):
        out = []
        for k in keys:
            if isinstance(k, tuple):
                out.append((k[0], None if k[0].startswith("ps") else k[1]))
            else:
                out.append((k, None))
        return out

    def op(self, eng, fn, reads=(), writes=(), dma=False, sk=None, cc=None):
        reads = self._norm(reads)
        writes = self._norm(writes)
        i = len(self.ins)
        deps = set()
        for (t, s) in reads:
            for s2, w in self.lastw.get(t, {}).items():
                if self._ov(s, s2):
                    deps.add(w)
        for (t, s) in writes:
            for s2, w in self.lastw.get(t, {}).items():
                if self._ov(s, s2):
                    deps.add(w)
            for s2, rl in self.rds.get(t, {}).items():
                if self._ov(s, s2):
                    deps.update(rl)
        deps.discard(i)
        for (t, s) in reads:
            self.rds.setdefault(t, {}).setdefault(s, []).append(i)
        for (t, s) in writes:
            d = self.lastw.setdefault(t, {})
            r = self.rds.setdefault(t, {})
            if s is None:
                d.clear()
                r.clear()
            else:
                r[s] = []
            d[s] = i
        if eng == "pe":
            deps = {d for d in deps if self.ins[d]["eng"] != "pe" or self.ins[d]["dma"]}
        self.ins.append(dict(eng=eng, fn=fn, deps=deps, dma=dma, cc=cc, key=(writes[0] if writes else None),
                             sk=(sk if sk is not None else (writes[0] if writes else None))))
        self.byeng[eng].append(i)
        return i

    def mm(self, out, lhsT, rhs, reads, writes, start=True, stop=True):
        return self.op("pe", lambda e: e.matmul(out, lhsT, rhs, start=start, stop=stop), reads, writes)

    def tr(self, out, in_, ident, reads, writes):
        return self.op("pe", lambda e: e.transpose(out, in_, ident), reads, writes)

    def act(self, out, in_, func, reads, writes, eng="act", **kw):
        return self.op(eng, lambda e: e.activation(out, in_, func, **kw), reads, writes)

    def tt(self, eng, out, in0, in1, op, reads, writes):
        return self.op(eng, lambda e: e.tensor_tensor(out, in0, in1, op), reads, writes)

    def ts(self, eng, out, in0, s1, s2, op0, op1, reads, writes, **kw):
        if op1 is None:
            return self.op(eng, lambda e: e.tensor_scalar(out, in0, s1, None, op0, **kw), reads, writes)
        return self.op(eng, lambda e: e.tensor_scalar(out, in0, s1, s2, op0, op1, **kw), reads, writes)

    def stt(self, out, in0, scalar, in1, op0, op1, reads, writes):
        return self.op("dve", lambda e: e.scalar_tensor_tensor(out, in0, scalar, in1, op0, op1), reads, writes)

    def cp(self, eng, out, in_, reads, writes):
        if eng == "act":
            return self.op(eng, lambda e: e.copy(out, in_), reads, writes)
        return self.op(eng, lambda e: e.tensor_copy(out, in_), reads, writes)

    def dma(self, out, in_, reads, writes, eng="sp", sk=None, **kw):
        return self.op(eng, lambda e: e.dma_start(out, in_, **kw), reads, writes, dma=True, sk=sk)

    def memset(self, eng, ap, val, writes):
        return self.op(eng, lambda e: e.memset(ap, val), (), writes)

    def emit(self, final_wait_keys=()):
        nc = self.nc
        ins = self.ins
        sh = self.shared
        own = sh is None
        if own:
            sh = Shared(nc)
        has_dep = [False] * len(ins)
        for i, d in enumerate(ins):
            for j in d["deps"]:
                has_dep[j] = True
        final_ids = [i for i, d in enumerate(ins) if d["dma"] and d["key"] is not None and d["key"][0] in final_wait_keys]
        for i, d in enumerate(ins):
            if d["dma"]:
                has_dep[i] = True
        for e in ENGS:
            for i in reversed(self.byeng[e]):
                if not ins[i]["dma"]:
                    has_dep[i] = True
                    break
        start = dict(sh.cnt)
        dma_sem_of = {}
        dma_eng = {}
        nsp, npool = [0], [0]
        sig = [None] * len(ins)
        for e in ENGS:
            for i in self.byeng[e]:
                d = ins[i]
                if not has_dep[i]:
                    continue
                if d["cc"] is not None:
                    sn = "cc%d" % d["cc"]
                    sh.cnt[sn] += 1
                    sig[i] = (sn, sh.cnt[sn])
                elif d["dma"]:
                    k = d["sk"]
                    if k not in dma_sem_of:
                        if e == "pool":
                            npool[0] += 1
                            assert npool[0] <= 34, "too many pool dma sem keys"
                            dma_sem_of[k] = "dq%d" % (Shared.NDQ - npool[0])
                        else:
                            nsp[0] += 1
                            assert nsp[0] <= Shared.NDQ - 34, "too many sp dma sem keys"
                            dma_sem_of[k] = "dq%d" % (nsp[0] - 1)
                        dma_eng[k] = e
                    assert dma_eng[k] == e, ("dma sem key used from two queues", k)
                    sn = dma_sem_of[k]
                    sh.cnt[sn] += 16
                    sig[i] = (sn, sh.cnt[sn])
                else:
                    sh.cnt["e_" + e] += 1
                    sig[i] = ("e_" + e, sh.cnt["e_" + e])
        sems = sh.sems
        block = self.stack.enter_context(nc.Block())
        engobj = dict(pe="tensor", act="scalar", dve="vector", pool="gpsimd", sp="sync")
        last_phase_final = [(sig[i][0], sig[i][1]) for i in final_ids]

        def make(e):
            def body(eng):
                waited = {}
                for sn, v in start.items():
                    if v > 0 and sn != "e_" + e:
                        eng.wait_ge(sems[sn], v)
                    waited[sn] = v
                for i in self.byeng[e]:
                    d = ins[i]
                    need = {}
                    for j in d["deps"]:
                        s_ = sig[j]
                        assert s_ is not None
                        if need.get(s_[0], 0) < s_[1]:
                            need[s_[0]] = s_[1]
                    for sn, v in need.items():
                        if waited.get(sn, 0) < v:
                            eng.wait_ge(sems[sn], v)
                            waited[sn] = v
                    inst = d["fn"](eng)
                    if sig[i] is not None:
                        inst.then_inc(sems[sig[i][0]], 16 if (d["dma"] and d["cc"] is None) else 1)
                if e == "sp":
                    for (sn, v) in last_phase_final:
                        if waited.get(sn, 0) < v:
                            eng.wait_ge(sems[sn], v)
                            waited[sn] = v
            return body

        for e in ENGS:
            getattr(block, engobj[e])(make(e))
        self.stack.close()
        if own:
            sh.close()
        return nc


from concourse.bass_utils import run_bass_kernel_spmd

NCORES = 8
D = 1024
NIN = 6816
EPS = 1e-6


def _ctx(ctx):
    if ctx is None:
        nc = bass.Bass("TRN2", target_bir_lowering=False)
        return nc, Prog(nc), None
    nc, shared, io = ctx
    return nc, Prog(nc, shared), io


def _mkI(nc, io):
    if io is None:
        return lambda n, s: nc.dram_tensor(n, list(s), F32, kind="ExternalInput")
    return lambda n, s: io[n]


def _mkO(nc, io, n, s):
    if io is None:
        return nc.dram_tensor(n, list(s), F32, kind="ExternalOutput")
    return io[n]


def _fin(P, nc, io, outs):
    if io is None:
        P.emit(final_wait_keys=outs)
    else:
        P.emit(final_wait_keys=(outs if io.get("_final") else ()))
    return nc


def ap3(t, off, dims):
    return bass.AP(t, off, [list(d) for d in dims])


def build_L1(TOK=2048):
    nc = bass.Bass("TRN2", target_bir_lowering=False)
    P = Prog(nc)
    NT = TOK // 128
    x_tok = nc.dram_tensor("x_tok", [TOK, D], F32, kind="ExternalInput")
    xT = nc.dram_tensor("xT", [D, TOK], F32, kind="ExternalInput")
    w = nc.dram_tensor("w", [D, NIN], F32, kind="ExternalInput")
    g = nc.dram_tensor("g", [128, 8], F32, kind="ExternalInput")
    proj = nc.dram_tensor("proj", [TOK, NIN], F32, kind="ExternalOutput")

    g_sb = P.sb("g_sb", [128, 8])
    rstd = P.sb("rstd", [128, NT])
    ss = P.sb("ss", [128, NT])
    xt_in = [P.sb("xt_in%d" % i, [128, D]) for i in range(2)]
    junk = P.sb("junk", [128, D])
    xTs = P.sb("xTs", [128, 8, TOK])
    xg = P.sb("xg", [128, 8, TOK], BF16)
    wf = [P.sb("wf%d" % i, [128, 8, 512]) for i in range(2)]
    wb = [P.sb("wb%d" % i, [128, 8, 512], BF16) for i in range(2)]
    ost = [P.sb("ost%d" % i, [128, 512]) for i in range(3)]
    pss = [P.ps("ps%d" % i, [128, 512]) for i in range(4)]

    P.dma(g_sb[:, :], g[:, :], [], ["g_sb"])
    for t in range(NT):
        b = xt_in[t % 2]
        bn = "xt_in%d" % (t % 2)
        P.dma(b[:, :], x_tok[t * 128:(t + 1) * 128, :], [], [bn], eng="sp")
        P.act(junk[:, :], b[:, :], AF.Square, [bn], ["junk", ("ss", t)], accum_out=ss[:, t:t + 1])
    P.ts("dve", rstd[:, :], ss[:, :], 1.0 / D, EPS, ALU.mult, ALU.add, ["ss"], ["rstd"])
    P.act(rstd[:, :], rstd[:, :], AF.Sqrt, ["rstd"], ["rstd"])
    P.op("dve", lambda e: e.reciprocal(rstd[:, :], rstd[:, :]), ["rstd"], ["rstd"])
    xTv = xT.ap().rearrange("(c p) t -> p c t", p=128)
    for c in range(8):
        P.dma(xTs[:, c, :], xTv[:, c, :], [], [("xTs", c)], eng="pool")
        P.ts("dve", xg[:, c, :], xTs[:, c, :], g_sb[:, c:c + 1], None, ALU.mult, None,
             [("xTs", c), "g_sb"], [("xg", c)])
    wv = w.ap().rearrange("(c p) n -> p c n", p=128)
    nblk = (NIN + 511) // 512
    k = 0
    for cb in range(nblk):
        c0 = cb * 512
        cw = min(512, NIN - c0)
        wfb, wbb = wf[cb % 2], wb[cb % 2]
        wfn, wbn = "wf%d" % (cb % 2), "wb%d" % (cb % 2)
        P.dma(wfb[:, :, :cw], wv[:, :, c0:c0 + cw], [], [wfn], eng="sp")
        for c in range(8):
            P.cp("pool" if c % 2 else "dve", wbb[:, c, :cw], wfb[:, c, :cw], [wfn], [(wbn, c)])
        for t in range(NT):
            ps = pss[k % 4]
            psn = "ps%d" % (k % 4)
            for c in range(8):
                P.mm(ps[:, :cw], xg[:, c, t * 128:(t + 1) * 128], wbb[:, c, :cw],
                     [("xg", c), (wbn, c)], [psn], start=(c == 0), stop=(c == 7))
            o = ost[k % 3]
            on = "ost%d" % (k % 3)
            P.act(o[:, :cw], ps[:, :cw], AF.Copy, [psn, "rstd"], [on], scale=rstd[:, t:t + 1])
            P.dma(proj[t * 128:(t + 1) * 128, c0:c0 + cw], o[:, :cw], [on], [("proj", k)], eng="sp", sk=on)
            k += 1
    P.emit(final_wait_keys=("proj",))
    return nc


def build_L1F(L=4096, ctx=None):
    nc, P, io = _ctx(ctx)
    I = _mkI(nc, io)
    x_tok = I("x_tok", [L, D])
    w = I("w", [D, NIN])
    g = I("g", [128, 8])
    c_ident = I("c_ident", [128, 128])
    FT = _mkO(nc, io, "FT", [6304, L])
    VT = _mkO(nc, io, "VT", [L, 512])
    HT = min(L, 1024)
    NH = L // HT
    NTT = HT // 128
    NB = L // 512
    g_sb = P.sb("g_sb", [128, 8])
    P.dma(g_sb[:, :], g[:, :], [], ["g_sb"])
    identf = P.sb("identf", [128, 128])
    P.dma(identf[:, :], c_ident[:, :], [], ["identf"])
    identb = P.sb("identb", [128, 128], BF16)
    P.cp("dve", identb[:, :], identf[:, :], ["identf"], ["identb"])
    hT = P.sb("hT", [128, 8, L], BF16)
    x1 = [P.sb("x1_%d" % i, [128, D]) for i in range(3)]
    junk = P.sb("junk", [128, D], BF16)
    ss = P.sb("ss", [128, NTT])
    hb = [P.sb("hb%d" % i, [128, D], BF16) for i in range(2)]
    wf = [P.sb("wf%d" % i, [128, 8, 512]) for i in range(2)]
    wb = [P.sb("wb%d" % i, [128, 8, 512], BF16) for i in range(2)]
    ost = [P.sb("ost%d" % i, [128, 512]) for i in range(4)]
    pss = [P.ps("ps%d" % i, [128, 512]) for i in range(6)]
    psb = P.ps("psb", [128, 1024], BF16)
    for half in range(NH):
        t0 = half * HT
        xs_ = []
        for tt_ in range(NTT):
            xi, xn = x1[tt_ % 3], "x1_%d" % (tt_ % 3)
            P.dma(xi[:, :], x_tok[t0 + tt_ * 128:t0 + (tt_ + 1) * 128, :], [], [xn], eng="sp")
            P.act(junk[:, :], xi[:, :], AF.Square, [xn], ["junk", ("ss", tt_)], accum_out=ss[:, tt_:tt_ + 1])
        P.ts("dve", ss[:, :], ss[:, :], 1.0 / D, EPS, ALU.mult, ALU.add, ["ss"], ["ss"])
        P.act(ss[:, :], ss[:, :], AF.Sqrt, ["ss"], ["ss"])
        P.op("dve", lambda e: e.reciprocal(ss[:, :], ss[:, :]), ["ss"], ["ss"])
        for tt_ in range(NTT):
            xi, xn = x1[tt_ % 3], "x1_%d" % (tt_ % 3)
            P.dma(xi[:, :], x_tok[t0 + tt_ * 128:t0 + (tt_ + 1) * 128, :], [], [xn], eng="sp")
            h_, hn = hb[tt_ % 2], "hb%d" % (tt_ % 2)
            P.act(h_[:, :], xi[:, :], AF.Copy, [xn, "ss"], [hn], scale=ss[:, tt_:tt_ + 1])
            for c in range(8):
                P.tr(psb[:, c * 128:(c + 1) * 128], h_[:, c * 128:(c + 1) * 128], identb[:, :], [hn, "identb"], ["psb"])
            for c in range(8):
                P.ts("dve", hT[:, c, t0 + tt_ * 128:t0 + (tt_ + 1) * 128], psb[:, c * 128:(c + 1) * 128], g_sb[:, c:c + 1], None, ALU.mult, None,
                     ["psb", "g_sb"], [("hT", c)])
    wv = w.ap().rearrange("(c p) n -> p c n", p=128)
    k = 0
    nblk = (NIN + 511) // 512
    for cb in range(nblk):
        c0 = cb * 512
        cw = min(512, NIN - c0)
        wfb, wbb = wf[cb % 2], wb[cb % 2]
        wfn, wbn = "wf%d" % (cb % 2), "wb%d" % (cb % 2)
        P.dma(wfb[:, :, :cw], wv[:, :, c0:c0 + cw], [], [wfn], eng=("sp" if cb % 2 else "pool"))
        for c in range(8):
            P.cp("pool" if c % 2 else "dve", wbb[:, c, :cw], wfb[:, c, :cw], [wfn], [(wbn, c)])
        if c0 == 1536:
            for t in range(L // 128):
                ps, psn = pss[k % 6], "ps%d" % (k % 6)
                for c in range(8):
                    P.mm(ps[:, :], hT[:, c, t * 128:(t + 1) * 128], wbb[:, c, :], [("hT", c), (wbn, c)], [psn], start=(c == 0), stop=(c == 7))
                o, on = ost[k % 4], "ost%d" % (k % 4)
                P.cp("act" if k % 2 else "dve", o[:, :], ps[:, :], [psn], [on])
                P.dma(VT[t * 128:(t + 1) * 128, :], o[:, :], [on], [("VT", t)], sk=on, eng=("sp" if k % 2 else "pool"))
                k += 1
            continue
        r0 = c0 if c0 < 1536 else c0 - 512
        for ct in range((cw + 127) // 128):
            mw = min(128, cw - ct * 128)
            for b5 in range(NB):
                bs = slice(b5 * 512, b5 * 512 + 512)
                ps, psn = pss[k % 6], "ps%d" % (k % 6)
                for c in range(8):
                    P.mm(ps[:mw, :], wbb[:, c, ct * 128:ct * 128 + mw], hT[:, c, bs], [("hT", c), (wbn, c)], [psn], start=(c == 0), stop=(c == 7))
                o, on = ost[k % 4], "ost%d" % (k % 4)
                P.cp("act" if k % 2 else "dve", o[:mw, :], ps[:mw, :], [psn], [on])
                P.dma(FT[r0 + ct * 128:r0 + ct * 128 + mw, bs], o[:mw, :], [on], [("FT", k)], sk=on, eng=("sp" if k % 2 else "pool"))
                k += 1
    return _fin(P, nc, io, ("FT", "VT"))


def build_L1G(L=2048, ctx=None):
    nc, P, io = _ctx(ctx)
    I = _mkI(nc, io)
    x_tok = I("x_tok", [L, D])
    w = I("w", [D, NIN])
    g = I("g", [128, 8])
    c_ident = I("c_ident", [128, 128])
    G = _mkO(nc, io, "G", [3744, L])
    GL = _mkO(nc, io, "GL", [3072, L])
    VT = V(G.ap()[3232:3744, :].rearrange("a (b c) -> (a b) c", c=512))
    HT = min(L, 1024)
    NH = L // HT
    NTT = HT // 128
    NB = L // 512
    g_sb = P.sb("g_sb", [128, 8])
    P.dma(g_sb[:, :], g[:, :], [], ["g_sb"])
    identf = P.sb("identf", [128, 128])
    P.dma(identf[:, :], c_ident[:, :], [], ["identf"])
    identb = P.sb("identb", [128, 128], BF16)
    P.cp("dve", identb[:, :], identf[:, :], ["identf"], ["identb"])
    hT = P.sb("hT", [128, 8, L], BF16)
    x1 = [P.sb("x1_%d" % i, [128, D]) for i in range(3)]
    junk = P.sb("junk", [128, D], BF16)
    ss = P.sb("ss", [128, NTT])
    hb = [P.sb("hb%d" % i, [128, D], BF16) for i in range(2)]
    wf = [P.sb("wf%d" % i, [128, 8, 512]) for i in range(2)]
    wb = [P.sb("wb%d" % i, [128, 8, 512], BF16) for i in range(2)]
    ost = [P.sb("ost%d" % i, [128, 512]) for i in range(4)]
    pss = [P.ps("ps%d" % i, [128, 512]) for i in range(6)]
    psb = P.ps("psb", [128, 1024], BF16)
    for half in range(NH):
        t0 = half * HT
        xs_ = []
        for tt_ in range(NTT):
            xi, xn = x1[tt_ % 3], "x1_%d" % (tt_ % 3)
            P.dma(xi[:, :], x_tok[t0 + tt_ * 128:t0 + (tt_ + 1) * 128, :], [], [xn], eng="sp")
            P.act(junk[:, :], xi[:, :], AF.Square, [xn], ["junk", ("ss", tt_)], accum_out=ss[:, tt_:tt_ + 1])
        P.ts("dve", ss[:, :], ss[:, :], 1.0 / D, EPS, ALU.mult, ALU.add, ["ss"], ["ss"])
        P.act(ss[:, :], ss[:, :], AF.Sqrt, ["ss"], ["ss"])
        P.op("dve", lambda e: e.reciprocal(ss[:, :], ss[:, :]), ["ss"], ["ss"])
        for tt_ in range(NTT):
            xi, xn = x1[tt_ % 3], "x1_%d" % (tt_ % 3)
            P.dma(xi[:, :], x_tok[t0 + tt_ * 128:t0 + (tt_ + 1) * 128, :], [], [xn], eng="sp")
            h_, hn = hb[tt_ % 2], "hb%d" % (tt_ % 2)
            P.act(h_[:, :], xi[:, :], AF.Copy, [xn, "ss"], [hn], scale=ss[:, tt_:tt_ + 1])
            for c in range(8):
                P.tr(psb[:, c * 128:(c + 1) * 128], h_[:, c * 128:(c + 1) * 128], identb[:, :], [hn, "identb"], ["psb"])
            for c in range(8):
                P.ts("dve", hT[:, c, t0 + tt_ * 128:t0 + (tt_ + 1) * 128], psb[:, c * 128:(c + 1) * 128], g_sb[:, c:c + 1], None, ALU.mult, None,
                     ["psb", "g_sb"], [("hT", c)])
    wv = w.ap().rearrange("(c p) n -> p c n", p=128)
    k = 0
    nblk = (NIN + 511) // 512
    for cb in range(nblk):
        c0 = cb * 512
        cw = min(512, NIN - c0)
        wfb, wbb = wf[cb % 2], wb[cb % 2]
        wfn, wbn = "wf%d" % (cb % 2), "wb%d" % (cb % 2)
        P.dma(wfb[:, :, :cw], wv[:, :, c0:c0 + cw], [], [wfn], eng=("sp" if cb % 2 else "pool"))
        for c in range(8):
            P.cp("pool" if c % 2 else "dve", wbb[:, c, :cw], wfb[:, c, :cw], [wfn], [(wbn, c)])
        if c0 == 1536:
            for t in range(L // 128):
                ps, psn = pss[k % 6], "ps%d" % (k % 6)
                for c in range(8):
                    P.mm(ps[:, :], hT[:, c, t * 128:(t + 1) * 128], wbb[:, c, :], [("hT", c), (wbn, c)], [psn], start=(c == 0), stop=(c == 7))
                o, on = ost[k % 4], "ost%d" % (k % 4)
                P.cp("act" if k % 2 else "dve", o[:, :], ps[:, :], [psn], [on])
                P.dma(VT[t * 128:(t + 1) * 128, :], o[:, :], [on], [("VT", t)], sk=on, eng=("sp" if k % 2 else "pool"))
                k += 1
            continue
        r0 = c0 if c0 < 1536 else c0 - 512
        for ct in range((cw + 127) // 128):
            mw = min(128, cw - ct * 128)
            for b5 in range(NB):
                bs = slice(b5 * 512, b5 * 512 + 512)
                ps, psn = pss[k % 6], "ps%d" % (k % 6)
                for c in range(8):
                    P.mm(ps[:mw, :], wbb[:, c, ct * 128:ct * 128 + mw], hT[:, c, bs], [("hT", c), (wbn, c)], [psn], start=(c == 0), stop=(c == 7))
                o, on = ost[k % 4], "ost%d" % (k % 4)
                P.cp("act" if k % 2 else "dve", o[:mw, :], ps[:mw, :], [psn], [on])
                ra, rb = r0 + ct * 128, r0 + ct * 128 + mw
                if ra < 3232:
                    n_ = min(rb, 3232) - ra
                    P.dma(G[ra:ra + n_, bs], o[0:n_, :], [on], [("G", k)], sk=on, eng=("sp" if k % 2 else "pool"))
                if rb > 3232:
                    p0 = max(ra, 3232) - ra
                    P.dma(GL[max(ra, 3232) - 3232:rb - 3232, bs], o[p0:mw, :], [on], [("GL", k)], sk=on, eng=("sp" if k % 2 else "pool"))
                k += 1
    return _fin(P, nc, io, ("G", "GL", "VT"))


def rw_consts(L):
    T = 64
    i = np.arange(128)
    same = (i[:, None] // T) == (i[None, :] // T)
    su = (same & (i[:, None] < i[None, :])).astype(np.float32)
    iu = (same & (i[:, None] <= i[None, :])).astype(np.float32)
    sl = (same & (i[:, None] > i[None, :])).astype(np.float32)
    mask4 = np.concatenate([su, iu, su, iu], 1)
    reset = np.ones((128, min(L, 1024)), np.float32)
    reset[:, ::T] = 0.0
    bones = ((i[:, None] // 64) == (i[None, :] // 64)).astype(np.float32)
    hsel = np.zeros((128, 2), np.float32)
    hsel[:64, 0] = 1
    hsel[64:, 1] = 1
    return dict(c_mask4=mask4, c_sl=sl, c_reset=reset, c_bones=bones, c_hsel=hsel,
                c_ident=np.eye(128, dtype=np.float32))


def build_L4(L=4096, has_vgate=False, LH=None, ctx=None, fm_out=False):
    nc, P, io = _ctx(ctx)
    NB = L // 128
    NCH = L // 64
    NCB = (L + 511) // 512
    I = _mkI(nc, io)
    zr, zk, zv = I("zr", [256, L]), I("zk", [256, L]), I("zv", [256, L])
    zw, za, zg = I("zw", [32, L]), I("za", [32, L]), I("zg", [96, L])
    mu3 = I("mu3", [128, 6])
    mulo = I("mulo", [96, 3])
    w2, a2, g2 = I("w2", [32, 256]), I("a2", [32, 256]), I("g2", [96, 256])
    pc = I("pc", [128, 10])
    lng, lnb = I("lng", [128, 256]), I("lnb", [128, 256])
    c_mask4, c_sl, c_reset = I("c_mask4", [128, 512]), I("c_sl", [128, 128]), I("c_reset", [128, min(L, 1024)])
    c_bones, c_hsel, c_ident = I("c_bones", [128, 128]), I("c_hsel", [128, 2]), I("c_ident", [128, 128])
    if has_vgate:
        zvfull = I("zvfull", [512, L])
        muvf = I("muvf", [128, 4])
        v1 = I("v1", [128, 4, 32])
        v2 = I("v2", [32, 256])
        v0 = I("v0", [128, 2])
        vfirst_in = I("vfirst_in", [256, L])
    y_out = _mkO(nc, io, "y_out", ([256, L] if fm_out else [L, 256]))
    vfirst_out = _mkO(nc, io, "vfirst_out", [256, L])

    def ld(name, src, shape, dt=F32, eng="sp"):
        t = P.sb(name, shape, dt)
        if len(shape) == 2:
            P.dma(t[:, :], src[:, :], [], [name], eng=eng)
        else:
            P.dma(t[:, :, :], src[:, :, :], [], [name], eng=eng)
        return t
    mu3s, mulos = ld("mu3s", mu3, [128, 6]), ld("mulos", mulo, [96, 3])
    pcs = ld("pcs", pc, [128, 10])
    lngs, lnbs = ld("lngs", lng, [128, 256]), ld("lnbs", lnb, [128, 256])
    mask4, msl, reset = ld("mask4", c_mask4, [128, 512]), ld("msl", c_sl, [128, 128]), ld("reset", c_reset, [128, min(L, 1024)])
    bones, hself, identf = ld("bones", c_bones, [128, 128]), ld("hself", c_hsel, [128, 2]), ld("identf", c_ident, [128, 128])
    w2f, a2f, g2f = ld("w2f", w2, [32, 256]), ld("a2f", a2, [32, 256]), ld("g2f", g2, [96, 256])
    identb = P.sb("identb", [128, 128], BF16)
    hselb = P.sb("hselb", [128, 2], BF16)
    w2b, a2b, g2b = P.sb("w2b", [32, 256], BF16), P.sb("a2b", [32, 256], BF16), P.sb("g2b", [96, 256], BF16)
    P.cp("dve", identb[:, :], identf[:, :], ["identf"], ["identb"])
    P.cp("dve", hselb[:, :], hself[:, :], ["hself"], ["hselb"])
    P.cp("dve", w2b[:, :], w2f[:, :], ["w2f"], ["w2b"])
    P.cp("dve", a2b[:, :], a2f[:, :], ["a2f"], ["a2b"])
    P.cp("dve", g2b[:, :], g2f[:, :], ["g2f"], ["g2b"])

    LH = LH or min(L, 1024)
    NQH = L // LH
    NCBH = (LH + 511) // 512
    S = [P.sb("S%d" % i, [128, LH + 1]) for i in range(5)]
    SN = ["S%d" % i for i in range(5)]
    AT, BT, KT, RT, VB, RK = [P.sb(n, [128, 2, L], BF16) for n in ("AT", "BT", "KT", "RT", "VB", "RK")]
    gam = P.sb("gam", [128, 2, NCH])
    lo_w, lo_a, lo_g = P.sb("lo_w", [32, LH], BF16), P.sb("lo_a", [32, LH], BF16), P.sb("lo_g", [96, L], BF16)
    resetb = P.sb("resetb", [128, LH], BF16)
    P.cp("dve", resetb[:, :], reset[:, 0:LH], ["reset"], ["resetb"])
    pss = [P.ps("ps%d" % i, [128, 512]) for i in range(7)]
    psb = P.ps("psb", [128, 1024], BF16)
    pk = [0]

    def nextps():
        k = pk[0] % 4
        pk[0] += 1
        return pss[k], "ps%d" % k

    def shift_mix(dst, dn, src_rows, rows, mucol, scratch, scn, raw, rawn, t0, eng="sp"):
        if t0 == 0:
            P.memset("pool", raw[:rows, 0:1], 0.0, [(rawn, 0)])
            P.dma(raw[:rows, 1:LH + 1], src_rows[:, 0:LH], [], [(rawn, 1)], eng=eng)
        else:
            P.dma(raw[:rows, 0:LH + 1], src_rows[:, t0 - 1:t0 + LH], [], [rawn], eng=eng)
        P.tt("dve", scratch[:rows, 0:LH], raw[:rows, 0:LH], raw[:rows, 1:LH + 1], ALU.subtract, [rawn], [scn])
        P.stt(dst[:rows, 0:LH], scratch[:rows, 0:LH], mucol, raw[:rows, 1:LH + 1], ALU.mult, ALU.add, [scn, rawn, "mu3s", "mulos", "muvfs"], [dn])

    if has_vgate:
        v1f = ld("v1f", v1, [128, 4, 32])
        v1b = P.sb("v1b", [128, 4, 32], BF16)
        P.cp("dve", v1b[:, :, :], v1f[:, :, :], ["v1f"], ["v1b"])
        v2f = ld("v2f", v2, [32, 256])
        v2b = P.sb("v2b", [32, 256], BF16)
        P.cp("dve", v2b[:, :], v2f[:, :], ["v2f"], ["v2b"])
        v0s = ld("v0s", v0, [128, 2])
        muvfs = ld("muvfs", muvf, [128, 4])
        vfb = P.sb("vfb", [128, 4, LH], BF16)
        t1b = P.sb("t1b", [32, LH], BF16)

    W = slice(0, LH)
    for hq in range(NQH):
        t0 = hq * LH
        TS = slice(t0, t0 + LH)
        shift_mix(S[2], SN[2], zw, 32, mulos[:32, 0:1], S[1], SN[1], S[0], SN[0], t0)
        P.act(lo_w[:, :], S[2][:32, W], AF.Tanh, [SN[2]], ["lo_w"])
        shift_mix(S[2], SN[2], za, 32, mulos[:32, 1:2], S[1], SN[1], S[0], SN[0], t0)
        P.cp("dve", lo_a[:, :], S[2][:32, W], [SN[2]], ["lo_a"])
        shift_mix(S[2], SN[2], zg, 96, mulos[:96, 2:3], S[1], SN[1], S[0], SN[0], t0)
        P.act(lo_g[:, TS], S[2][:96, W], AF.Sigmoid, [SN[2]], [("lo_g", hq)])
        if has_vgate:
            for c in range(4):
                shift_mix(S[2], SN[2], zvfull[c * 128:(c + 1) * 128, :], 128, muvfs[:, c:c + 1], S[1], SN[1], S[0], SN[0], t0)
                P.cp("pool", vfb[:, c, :], S[2][:, W], [SN[2]], [("vfb", c)])
            for cb in range(NCBH):
                c0 = cb * 512
                cw = min(512, LH - c0)
                ps, psn = nextps()
                for c in range(4):
                    P.mm(ps[:32, :cw], v1b[:, c, :], vfb[:, c, c0:c0 + cw], ["v1b", ("vfb", c)], [psn], start=(c == 0), stop=(c == 3))
                P.cp("act", t1b[:, c0:c0 + cw], ps[:32, :cw], [psn], [("t1b", cb)])
        for p in range(2):
            rows = slice(p * 128, (p + 1) * 128)
            col = lambda j: pcs[:, 2 * j + p:2 * j + p + 1]
            shift_mix(S[2], SN[2], zv[rows, :], 128, mu3s[:, 4 + p:5 + p], S[1], SN[1], S[0], SN[0], t0)
            if has_vgate:
                for cb in range(NCBH):
                    c0 = cb * 512
                    cw = min(512, LH - c0)
                    ps, psn = nextps()
                    P.mm(ps[:, :cw], v2b[:, p * 128:(p + 1) * 128], t1b[:, c0:c0 + cw], ["v2b", ("t1b", cb)], [psn])
                    P.act(S[3][:, c0:c0 + cw], ps[:, :cw], AF.Sigmoid, [psn, "v0s"], [(SN[3], cb)], bias=v0s[:, p:p + 1])
                P.dma(S[0][:, W], vfirst_in[rows, TS], [], [SN[0]])
                P.tt("dve", S[1][:, W], S[0][:, W], S[2][:, W], ALU.subtract, [SN[0], SN[2]], [SN[1]])
                P.tt("dve", S[1][:, W], S[1][:, W], S[3][:, W], ALU.mult, [SN[1], SN[3]], [SN[1]])
                P.tt("dve", S[2][:, W], S[2][:, W], S[1][:, W], ALU.add, [SN[2], SN[1]], [SN[2]])
                P.dma(vfirst_out[rows, TS], S[0][:, W], [SN[0]], [("vfirst_out", hq * 2 + p)], sk="vfo")
            else:
                P.dma(vfirst_out[rows, TS], S[2][:, W], [SN[2]], [("vfirst_out", hq * 2 + p)], sk="vfo")
            P.cp("pool", VB[:, p, TS], S[2][:, W], [SN[2]], [("VB", p)])
            for cb in range(NCBH):
                c0 = cb * 512
                cw = min(512, LH - c0)
                ps, psn = nextps()
                P.mm(ps[:, :cw], a2b[:, p * 128:(p + 1) * 128], lo_a[:, c0:c0 + cw], ["a2b", "lo_a"], [psn])
                P.act(S[3][:, c0:c0 + cw], ps[:, :cw], AF.Sigmoid, [psn, "pcs"], [(SN[3], cb)], bias=col(1))
                ps, psn = nextps()
                P.mm(ps[:, :cw], w2b[:, p * 128:(p + 1) * 128], lo_w[:, c0:c0 + cw], ["w2b", "lo_w"], [psn])
                P.act(S[4][:, c0:c0 + cw], ps[:, :cw], AF.Sigmoid, [psn, "pcs"], [(SN[4], cb)], bias=col(0))
            P.ts("dve", S[4][:, W], S[4][:, W], -float(np.exp(-0.5)), None, ALU.mult, None, [SN[4]], [SN[4]])
            shift_mix(S[2], SN[2], zk[rows, :], 128, mu3s[:, 2 + p:3 + p], S[1], SN[1], S[0], SN[0], t0)
            P.ts("dve", S[0][:, W], S[2][:, W], col(2), None, ALU.mult, None, [SN[2], "pcs"], [SN[0]])
            P.tt("pool", S[1][:, W], S[0][:, W], S[0][:, W], ALU.mult, [SN[0]], [SN[1]])
            for cb in range(NCBH):
                c0 = cb * 512
                cw = min(512, LH - c0)
                ps, psn = nextps()
                P.mm(ps[:, :cw], bones[:, :], S[1][:, c0:c0 + cw], ["bones", SN[1]], [psn])
                P.ts("dve", S[1][:, c0:c0 + cw], ps[:, :cw], 1e-24, None, ALU.max, None, [psn], [(SN[1], cb)])
            P.act(S[1][:, W], S[1][:, W], AF.Sqrt, [SN[1]], [SN[1]])
            P.op("dve", lambda e: e.reciprocal(S[1][:, W], S[1][:, W]), [SN[1]], [SN[1]])
            P.tt("dve", S[0][:, W], S[0][:, W], S[1][:, W], ALU.mult, [SN[0], SN[1]], [SN[0]])
            P.ts("dve", S[1][:, W], S[3][:, W], -1.0, col(3), ALU.add, ALU.mult, [SN[3], "pcs"], [SN[1]])
            P.stt(S[2][:, W], S[1][:, W], 1.0, S[2][:, W], ALU.add, ALU.mult, [SN[1], SN[2]], [SN[2]])
            P.op("dve", lambda e: e.tensor_tensor_scan(S[1][:, W], resetb[:, :], S[4][:, W], 0.0, ALU.mult, ALU.add),
                 ["resetb", SN[4]], [SN[1]])
            P.tt("pool", S[4][:, W], S[1][:, W], S[4][:, W], ALU.subtract, [SN[1], SN[4]], [SN[4]])
            P.act(S[4][:, W], S[4][:, W], AF.Exp, [SN[4]], [SN[4]])
            P.stt(AT[:, p, TS], S[0][:, W], -1.0, S[4][:, W], ALU.mult, ALU.mult, [SN[0], SN[4]], [("AT", p)])
            P.act(S[4][:, W], S[1][:, W], AF.Exp, [SN[1]], [SN[4]], scale=-1.0)
            P.tt("dve", S[0][:, W], S[0][:, W], S[3][:, W], ALU.mult, [SN[0], SN[3]], [SN[0]])
            P.tt("dve", BT[:, p, TS], S[0][:, W], S[4][:, W], ALU.mult, [SN[0], SN[4]], [("BT", p)])
            P.tt("pool", KT[:, p, TS], S[2][:, W], S[4][:, W], ALU.mult, [SN[2], SN[4]], [("KT", p)])
            P.act(S[4][:, W], S[1][:, W], AF.Exp, [SN[1]], [SN[4]])
            gv = ap3(S[4], 63, [[LH + 1, 128], [64, LH // 64]])
            P.cp("dve", gam[:, p, t0 // 64:(t0 + LH) // 64], gv, [SN[4]], [("gam", p)])
            shift_mix(S[0], SN[0], zr[rows, :], 128, mu3s[:, p:p + 1], S[1], SN[1], S[3], SN[3], t0)
            P.tt("dve", RT[:, p, TS], S[0][:, W], S[4][:, W], ALU.mult, [SN[0], SN[4]], [("RT", p)])
            P.stt(RK[:, p, TS], S[0][:, W], col(4), S[2][:, W], ALU.mult, ALU.mult, [SN[0], SN[2], "pcs"], [("RK", p)])

    A4 = [P.sb("A4_%d" % i, [128, 512], BF16) for i in range(4)]
    N0s = [P.sb("N0s_%d" % i, [128, 128], BF16) for i in range(4)]
    NL = [[P.sb("NL_%d_%d" % (i, j), [128, 256], BF16) for j in range(2)] for i in range(4)]
    TOK = [P.sb("TOK%d" % i, [128, 4, 128], BF16) for i in range(2)]
    Zb = [[P.sb("Zb_%d_%d" % (i, j), [128, 128], BF16) for j in range(2)] for i in range(4)]
    W1P = [P.sb("W1P%d" % i, [128, 128], BF16) for i in range(4)]
    XTP = [[P.sb("XTP_%d_%d" % (i, c), [128, 128]) for c in range(2)] for i in range(4)]
    Qg = [[P.sb("Qg_%d_%d" % (i, c), [128, 64]) for c in range(2)] for i in range(4)]
    RM = [[P.sb("RM_%d_%d" % (i, c), [128, 128], BF16) for c in range(2)] for i in range(4)]
    Hf = [[P.sb("Hf_%d_%d" % (p, h), [128, 64]) for h in range(2)] for p in range(2)]
    Hb = [[[P.sb("Hb_%d_%d_%d" % (p, h, c), [128, 64], BF16) for c in range(2)] for h in range(2)] for p in range(2)]
    Yall = [P.sb("Yall%d" % i, [128, 256]) for i in range(2)]
    rkc = [P.sb("rkc%d" % i, [128, 4]) for i in range(2)]
    gtk = [P.sb("gtk%d" % i, [128, 256]) for i in range(2)]
    st = P.sb("st", [128, 16])
    sq = P.sb("sq", [128, 256])
    yo = [P.sb("yo%d" % i, [128, 256]) for i in range(2)]
    yfm = [P.sb("yfm%d" % i, [128, 2, 128]) for i in range(2)]
    for i in range(4):
        P.memset("pool", W1P[i][:, :], 0.0, ["W1P%d" % i])
        for c in range(2):
            P.memset("pool", XTP[i][c][:, :], 0.0, ["XTP_%d_%d" % (i, c)])
            P.memset("pool", RM[i][c][:, :], 0.0, ["RM_%d_%d" % (i, c)])
    for p in range(2):
        for h in range(2):
            P.memset("pool", Hf[p][h][:, :], 0.0, ["Hf_%d_%d" % (p, h)])
            P.memset("pool", Hb[p][h][0][:, :], 0.0, ["Hb_%d_%d_0" % (p, h)])
    pk6 = [0]

    def nps():
        k = pk6[0] % 6
        pk6[0] += 1
        return pss[k], "ps%d" % k

    def unit(p, h, blk, Y, Yn):
        us = 2 * p + h
        cols = slice(blk * 128, (blk + 1) * 128)
        tk, tkn = TOK[p], "TOK%d" % p
        o = 64 * h
        hs = slice(o, o + 64)
        bt, kt_, at, rt = BT[hs, p, cols], KT[hs, p, cols], AT[hs, p, cols], RT[hs, p, cols]
        a4, a4n = A4[us], "A4_%d" % us
        n0, n0n = N0s[us], "N0s_%d" % us
        p1, p1n = nps()
        P.mm(p1[:, 0:128], bt, at, [("BT", p), ("AT", p)], [p1n])
        P.mm(p1[:, 128:256], bt, rt, [("BT", p), ("RT", p)], [p1n])
        P.mm(p1[:, 256:384], kt_, at, [("KT", p), ("AT", p)], [p1n])
        P.mm(p1[:, 384:512], kt_, rt, [("KT", p), ("RT", p)], [p1n])
        P.tt("dve", a4[:, :], p1[:, :], mask4[:, :], ALU.mult, [p1n, "mask4"], [a4n])
        p2, p2n = nps()
        P.mm(p2[:, 0:128], at, bt, [("BT", p), ("AT", p)], [p2n])
        P.tt("dve", n0[:, :], p2[:, 0:128], msl[:, :], ALU.mult, [p2n, "msl"], [n0n])
        yield
        L0, ArbT, AakT, ArkT = a4[:, 0:128], a4[:, 128:256], a4[:, 256:384], a4[:, 384:512]
        zc, zcn = Zb[us][0], "Zb_%d_0" % us
        p3, p3n = nps()
        P.mm(p3[:, 0:64], AakT, tk[:, 2, hs], [a4n, tkn], [p3n])
        P.cp("dve", zc[:, 0:64], tk[:, 3, hs], [tkn], [(zcn, 0)])
        P.cp("act", zc[:, 64:128], p3[:, 0:64], [p3n], [(zcn, 1)])
        yield
        Nj, Lj, njn, ljn = n0[:, :], L0, n0n, a4n
        zi = 0
        for j in range(6):
            pz, pzn = nps()
            zc, zcn = Zb[us][zi], "Zb_%d_%d" % (us, zi)
            zn_, znn = Zb[us][1 - zi], "Zb_%d_%d" % (us, 1 - zi)
            P.mm(pz[:, 0:128], Lj, zc[:, :], [ljn, zcn], [pzn])
            if j < 5:
                pq, pqn = nps()
                nl, nln = NL[us][j % 2], "NL_%d_%d" % (us, j % 2)
                if j < 4:
                    P.mm(pq[:, 0:128], Lj, Nj, [ljn, njn], [pqn])
                P.mm(pq[:, 128:256], Nj, Lj, [ljn, njn], [pqn])
            P.tt("dve", zn_[:, :], pz[:, 0:128], zc[:, :], ALU.add, [pzn, zcn], [znn])
            if j == 5:
                P.cp("pool", W1P[us][:, hs], zn_[:, 0:64], [znn], ["W1P%d" % us])
            else:
                if j < 4:
                    P.cp("act", nl[:, :], pq[:, 0:256], [pqn], [nln])
                else:
                    P.cp("act", nl[:, 128:256], pq[:, 128:256], [pqn], [nln])
                Nj, Lj, njn, ljn = nl[:, 0:128], nl[:, 128:256], nln, nln
            zi = 1 - zi
            yield
        Zf, Zfn = Zb[us][zi], "Zb_%d_%d" % (us, zi)
        U0 = Zf[:, 64:128]
        w1p, w1n = W1P[us], "W1P%d" % us
        pr, prn = nps()
        P.mm(pr[:, 0:128], w1p[:, :], ArbT, [w1n, a4n], [prn])
        for c in range(2):
            cs = slice(64 * c, 64 * c + 64)
            P.tt("dve", RM[us][c][hs, cs], pr[hs, cs], RT[hs, p, blk * 128 + 64 * c:blk * 128 + 64 * c + 64], ALU.add,
                 [prn, ("RT", p)], ["RM_%d_%d" % (us, c)])
        yield
        hf, hfn = Hf[p][h], "Hf_%d_%d" % (p, h)
        for c in range(2):
            ch = blk * 2 + c
            rs = slice(64 * c, 64 * c + 64)
            px, pxn = nps()
            P.mm(px[:, 0:64], w1p[rs, :], tk[rs, 0, hs], [w1n, tkn], [pxn])
            P.mm(px[:, 64:128], tk[rs, 0, :], U0[rs, :], [tkn, Zfn], [pxn], start=True, stop=False)
            P.mm(px[:, 64:128], tk[rs, 1, :], tk[rs, 2, hs], [tkn], [pxn], start=False, stop=True)
            xt, xtn = XTP[us][c], "XTP_%d_%d" % (us, c)
            P.tt("dve", xt[hs, hs], px[hs, 0:64], identf[hs, hs], ALU.add, [pxn, "identf"], [xtn])
            qg, qgn = Qg[us][c], "Qg_%d_%d" % (us, c)
            P.ts("dve", qg[hs, :], px[hs, 64:128], gam[hs, p, ch:ch + 1], None, ALU.mult, None, [pxn, ("gam", p)], [qgn])
            yield
        ph, phn = nps()
        P.mm(ph[:, 0:64], XTP[us][0][hs, :], hf[hs, :], ["XTP_%d_0" % us, hfn], [phn])
        P.stt(hf[hs, :], ph[hs, 0:64], gam[hs, p, 2 * blk:2 * blk + 1], Qg[us][0][hs, :], ALU.mult, ALU.add,
              [phn, ("gam", p), "Qg_%d_0" % us], [hfn])
        P.cp("pool", Hb[p][h][1][hs, :], hf[hs, :], [hfn], ["Hb_%d_%d_1" % (p, h)])
        yield
        ph, phn = nps()
        P.mm(ph[:, 0:64], XTP[us][1][hs, :], hf[hs, :], ["XTP_%d_1" % us, hfn], [phn])
        P.stt(hf[hs, :], ph[hs, 0:64], gam[hs, p, 2 * blk + 1:2 * blk + 2], Qg[us][1][hs, :], ALU.mult, ALU.add,
              [phn, ("gam", p), "Qg_%d_1" % us], [hfn])
        py, pyn = nps()
        P.mm(py[:, 0:64], ArbT, U0, [a4n, Zfn], [pyn], start=True, stop=False)
        P.mm(py[:, 0:64], ArkT, tk[:, 2, hs], [a4n, tkn], [pyn], start=False, stop=False)
        P.mm(py[:, 0:64], RM[us][0][hs, :], Hb[p][h][0][hs, :], ["RM_%d_0" % us, "Hb_%d_%d_0" % (p, h)], [pyn], start=False, stop=False)
        P.mm(py[:, 0:64], RM[us][1][hs, :], Hb[p][h][1][hs, :], ["RM_%d_1" % us, "Hb_%d_%d_1" % (p, h)], [pyn], start=False, stop=True)
        P.cp("act", Y[:, us * 64:(us + 1) * 64], py[:, 0:64], [pyn], [(Yn, us)])
        P.cp("pool", Hb[p][h][0][hs, :], hf[hs, :], [hfn], ["Hb_%d_%d_0" % (p, h)])
        yield

    for blk in range(NB):
        cols = slice(blk * 128, (blk + 1) * 128)
        Y, Yn = Yall[blk % 2], "Yall%d" % (blk % 2)
        pg, pgn = pss[6], "ps6"
        P.mm(pg[:, 0:256], lo_g[:, cols], g2b[:, :], ["lo_g", "g2b"], [pgn])
        for p in range(2):
            P.mm(pg[:, 256 + 2 * p:258 + 2 * p], RK[:, p, cols], hselb[:, :], [("RK", p), "hselb"], [pgn])
        gt, gtn = gtk[blk % 2], "gtk%d" % (blk % 2)
        rk_, rkn = rkc[blk % 2], "rkc%d" % (blk % 2)
        P.cp("act", gt[:, :], pg[:, 0:256], [pgn], [gtn])
        P.cp("act", rk_[:, :], pg[:, 256:260], [pgn], [rkn])
        for p in range(2):
            tk, tkn = TOK[p], "TOK%d" % p
            for j, (X, xn) in enumerate(((BT, "BT"), (KT, "KT"), (VB, "VB"), (AT, "AT"))):
                P.tr(psb[:, j * 128:(j + 1) * 128], X[:, p, cols], identb[:, :], [(xn, p), "identb"], ["psb"])
            P.cp("act", tk[:, :, :], psb[:, 0:512].rearrange("p (j c) -> p j c", j=4), ["psb"], [tkn])
        gens = [unit(p, h, blk, Y, Yn) for p in range(2) for h in range(2)]
        live = list(gens)
        while live:
            nxt = []
            for g_ in live:
                try:
                    next(g_)
                    nxt.append(g_)
                except StopIteration:
                    pass
            live = nxt
        Y3 = Y[:, :].rearrange("p (h v) -> p h v", h=4)
        P.op("dve", lambda e, Y3=Y3: e.tensor_reduce(st[:, 0:4], Y3, mybir.AxisListType.X, ALU.add), [Yn], [("st", 0)])
        P.tt("pool", sq[:, :], Y[:, :], Y[:, :], ALU.mult, [Yn], ["sq"])
        sq3 = sq[:, :].rearrange("p (h v) -> p h v", h=4)
        P.op("dve", lambda e, sq3=sq3: e.tensor_reduce(st[:, 4:8], sq3, mybir.AxisListType.X, ALU.add), ["sq"], [("st", 1)])
        P.ts("dve", st[:, 0:4], st[:, 0:4], 1.0 / 64, None, ALU.mult, None, [("st", 0)], [("st", 0)])
        P.tt("dve", st[:, 8:12], st[:, 0:4], st[:, 0:4], ALU.mult, [("st", 0)], [("st", 2)])
        P.stt(st[:, 4:8], st[:, 4:8], 1.0 / 64, st[:, 8:12], ALU.mult, ALU.subtract, [("st", 1), ("st", 2)], [("st", 1)])
        P.ts("dve", st[:, 4:8], st[:, 4:8], 64e-5, None, ALU.add, None, [("st", 1)], [("st", 1)])
        P.act(st[:, 4:8], st[:, 4:8], AF.Sqrt, [("st", 1)], [("st", 1)])
        P.op("dve", lambda e: e.reciprocal(st[:, 4:8], st[:, 4:8]), [("st", 1)], [("st", 1)])
        o_, on = yo[blk % 2], "yo%d" % (blk % 2)
        for hh in range(4):
            cs = slice(hh * 64, hh * 64 + 64)
            P.ts("dve", o_[:, cs], Y[:, cs], st[:, hh:hh + 1], st[:, 4 + hh:5 + hh], ALU.subtract, ALU.mult, [Yn, "st"], [(on, hh)])
        P.tt("dve", o_[:, :], o_[:, :], lngs[:, :], ALU.mult, [on, "lngs"], [on])
        P.tt("pool", o_[:, :], o_[:, :], lnbs[:, :], ALU.add, [on, "lnbs"], [on])
        for hh in range(4):
            cs = slice(hh * 64, hh * 64 + 64)
            pp, h2 = hh // 2, hh % 2
            P.stt(o_[:, cs], TOK[pp][:, 2, 64 * h2:64 * h2 + 64], rk_[:, hh:hh + 1], o_[:, cs], ALU.mult, ALU.add,
                  ["TOK%d" % pp, rkn, on], [on])
        P.tt("dve", o_[:, :], o_[:, :], gt[:, :], ALU.mult, [on, gtn], [on])
        if fm_out:
            pt_, ptn = nextps()
            for c in range(2):
                P.tr(pt_[:, c * 128:(c + 1) * 128], o_[:, c * 128:(c + 1) * 128], identf[:, :], [on, "identf"], [ptn])
            yf, yfn = yfm[blk % 2], "yfm%d" % (blk % 2)
            P.cp("act", yf[:, :, :], pt_[:, 0:256].rearrange("p (c t) -> p c t", c=2), [ptn], [yfn])
            P.dma(y_out[0:256, blk * 128:(blk + 1) * 128].rearrange("(c p) t -> p c t", p=128), yf[:, :, :], [yfn], [("y_out", blk)], sk=yfn)
        else:
            P.dma(y_out[blk * 128:(blk + 1) * 128, :], o_[:, :], [on], [("y_out", blk)], sk=on)
    return _fin(P, nc, io, ("y_out", "vfirst_out"))


def l4_inputs(z, prm, hg, L, vfirst=None):
    c0 = 256 * hg
    f = lambda a: np.ascontiguousarray(a, dtype=np.float32)
    mu = prm["mu"]
    d = dict(zr=f(z[:, c0:c0 + 256].T), zk=f(z[:, 512 + c0:512 + c0 + 256].T), zv=f(z[:, 1024 + c0:1024 + c0 + 256].T),
             zw=f(z[:, 1536:1568].T), za=f(z[:, 1568:1600].T), zg=f(z[:, 1600:1696].T))
    mu3 = np.zeros((128, 6), np.float32)
    for j in range(3):
        for p in range(2):
            mu3[:, 2 * j + p] = mu[512 * j + c0 + 128 * p:512 * j + c0 + 128 * p + 128]
    d["mu3"] = mu3
    mulo = np.zeros((96, 3), np.float32)
    mulo[:32, 0] = mu[1536:1568]
    mulo[:32, 1] = mu[1568:1600]
    mulo[:, 2] = mu[1600:1696]
    d["mulo"] = mulo
    d["w2"] = f(prm["w2"][:, c0:c0 + 256]); d["a2"] = f(prm["a2"][:, c0:c0 + 256]); d["g2"] = f(prm["g2"][:, c0:c0 + 256])
    pc = np.zeros((128, 10), np.float32)
    for j, nm in enumerate(("w0", "a0", "k_k", "k_a", "r_k")):
        v = prm[nm].reshape(-1)
        for p in range(2):
            pc[:, 2 * j + p] = v[c0 + 128 * p:c0 + 128 * p + 128]
    d["pc"] = pc
    d["lng"] = f(np.broadcast_to(prm["ln_g"][c0:c0 + 256][None, :], (128, 256)))
    d["lnb"] = f(np.broadcast_to(prm["ln_b"][c0:c0 + 256][None, :], (128, 256)))
    d.update(rw_consts(L))
    if vfirst is not None:
        d["zvfull"] = f(z[:, 1024:1536].T)
        d["muvf"] = f(mu[1024:1536].reshape(4, 128).T)
        d["v1"] = f(prm["v1"].reshape(4, 128, 32).transpose(1, 0, 2))
        d["v2"] = f(prm["v2"][:, c0:c0 + 256])
        v0 = np.zeros((128, 2), np.float32)
        for p in range(2):
            v0[:, p] = prm["v0"][c0 + 128 * p:c0 + 128 * p + 128]
        d["v0"] = v0
        d["vfirst_in"] = f(vfirst)
    return d


def at_consts():
    tk = np.arange(128)[:, None]
    tq = np.arange(512)[None, :]
    m = np.stack([((128 * j + tk) <= tq).astype(np.float32) for j in range(4)], 1)
    i = np.arange(128)
    b64 = ((i[:, None] // 64) == (i[None, :] // 64)).astype(np.float32)
    return dict(c_cmask=m, c_b64=b64, c_ones=np.ones((128, 128), np.float32))


def build_L3(L=4096, ctx=None):
    nc, P, io = _ctx(ctx)
    NT = L // 128
    NQ = L // 512
    I = _mkI(nc, io)
    qT, kT = I("qT", [2, 128, L]), I("kT", [2, 128, L])
    vtok = I("vtok", [2, L, 128])
    gains = I("gains", [128, 4])
    lamv = I("lamv", [128, 4, 64])
    laminit = I("laminit", [128, 1])
    c_cmask, c_b64, c_ones = I("c_cmask", [128, 4, 512]), I("c_b64", [128, 128]), I("c_ones", [128, 128])
    oT = _mkO(nc, io, "oT", [2, 128, L])

    def ld(name, src, shape):
        t = P.sb(name, shape)
        if len(shape) == 2:
            P.dma(t[:, :], src[:, :], [], [name])
        else:
            P.dma(t[:, :, :], src[:, :, :], [], [name])
        return t
    gs = ld("gs", gains, [128, 4])
    lv = ld("lv", lamv, [128, 4, 64])
    li = ld("li", laminit, [128, 1])
    cmf = ld("cmf", c_cmask, [128, 4, 512])
    b64 = ld("b64", c_b64, [128, 128])
    onesf = ld("onesf", c_ones, [128, 128])
    cmb = P.sb("cmb", [128, 4, 512], BF16)
    onesb = P.sb("onesb", [128, 128], BF16)
    P.cp("dve", cmb[:, :, :], cmf[:, :, :], ["cmf"], ["cmb"])
    P.cp("dve", onesb[:, :], onesf[:, :], ["onesf"], ["onesb"])
    lt = P.sb("lt", [128, 2, 64])
    ls = P.sb("ls", [128, 4])
    P.tt("dve", lt[:, 0, :], lv[:, 0, :], lv[:, 1, :], ALU.mult, ["lv"], [("lt", 0)])
    P.tt("dve", lt[:, 1, :], lv[:, 2, :], lv[:, 3, :], ALU.mult, ["lv"], [("lt", 1)])
    P.op("dve", lambda e: e.tensor_reduce(ls[:, 0:2], lt[:, :, :], mybir.AxisListType.X, ALU.add), ["lt"], ["ls"])
    P.act(ls[:, 0:2], ls[:, 0:2], AF.Exp, ["ls"], ["ls"])
    P.tt("dve", ls[:, 2:3], ls[:, 1:2], ls[:, 0:1], ALU.subtract, ["ls"], ["ls"])
    P.tt("dve", ls[:, 3:4], ls[:, 2:3], li[:, :], ALU.subtract, ["ls", "li"], ["ls"])
    gq = P.sb("gq", [128, 1])
    P.ts("dve", gq[:, :], gs[:, 0:1], 0.125, None, ALU.mult, None, ["gs"], ["gq"])
    gsub = P.sb("gsub", [128, 1])
    P.tt("dve", gsub[:, :], gs[:, 2:3], gs[:, 3:4], ALU.mult, ["gs"], ["gsub"])

    raw = P.sb("raw", [128, L])
    sqt = P.sb("sqt", [128, L])
    qn = P.sb("qn", [128, L], BF16)
    kn = P.sb("kn", [128, L], BF16)
    vf = P.sb("vf", [128, NT, 128])
    vb = P.sb("vb", [128, NT, 128], BF16)
    pt = [P.sb("pt%d" % i, [128, 512], BF16) for i in range(4)]
    ob = P.sb("ob", [128, 512])
    o1 = P.sb("o1", [128, 512])
    rz = P.sb("rz", [128, 512])
    osq = P.sb("osq", [128, 512])
    oo = [P.sb("oo%d" % i, [128, 512]) for i in range(2)]
    psS = [P.ps("psS%d" % i, [128, 512]) for i in range(2)]
    psO = [P.ps("psO%d" % i, [128, 512]) for i in range(2)]
    psZ = [P.ps("psZ%d" % i, [128, 512]) for i in range(2)]
    psM = P.ps("psM", [128, 512])

    def qknorm(src, dst, dn, gcol):
        P.dma(raw[:, :], src, [], ["raw"])
        P.tt("pool", sqt[:, :], raw[:, :], raw[:, :], ALU.mult, ["raw"], ["sqt"])
        for cb in range(NQ):
            cs = slice(cb * 512, cb * 512 + 512)
            P.mm(psM[:, :], b64[:, :], sqt[:, cs], ["b64", "sqt"], ["psM"])
            P.ts("dve", sqt[:, cs], psM[:, :], 1.0 / 64, EPS, ALU.mult, ALU.add, ["psM"], [("sqt", cb)])
        P.act(sqt[:, :], sqt[:, :], AF.Sqrt, ["sqt"], ["sqt"])
        P.op("dve", lambda e: e.reciprocal(sqt[:, :], sqt[:, :]), ["sqt"], ["sqt"])
        P.stt(dst[:, :], raw[:, :], gcol, sqt[:, :], ALU.mult, ALU.mult, ["raw", "sqt", "gs", "gq"], [dn])

    qn2 = [qn, P.sb("qn_b", [128, L], BF16)]
    kn2 = [kn, P.sb("kn_b", [128, L], BF16)]
    vb2 = [vb, P.sb("vb_b", [128, NT, 128], BF16)]
    qnn, knn, vbn = ["qn", "qn_b"], ["kn", "kn_b"], ["vb", "vb_b"]
    sO = [P.sb("sO%d" % i, [128, 512]) for i in range(2)]
    sZ = [P.sb("sZ%d" % i, [128, 512]) for i in range(2)]
    for h in range(2):
        qknorm(qT[h, :, :], qn2[h], qnn[h], gq[:, 0:1])
        qknorm(kT[h, :, :], kn2[h], knn[h], gs[:, 1:2])
        P.dma(vf[:, :, :], vtok[h, :, :].rearrange("(n p) d -> p n d", p=128), [], ["vf"])
        P.cp("pool", vb2[h][:, :, :], vf[:, :, :], ["vf"], [vbn[h]])
    kcount = 0
    for h in range(2):
        qn_, kn_, vb_ = qn2[h], kn2[h], vb2[h]
        for qb in range(NQ):
            qs = slice(qb * 512, qb * 512 + 512)
            ntk = 4 * qb + 4
            its = [(j, c2) for j in range(ntk) for c2 in range(2)]
            bufs = []

            def emit_S(idx):
                nonlocal kcount
                j, c2 = its[idx]
                hs = slice(64 * c2, 64 * c2 + 64)
                s_, sn = psS[kcount % 2], "psS%d" % (kcount % 2)
                p_, pn = pt[kcount % 4], "pt%d" % (kcount % 4)
                kcount += 1
                P.mm(s_[:, :], kn_[hs, j * 128:j * 128 + 128], qn_[hs, qs], [knn[h], qnn[h]], [sn])
                bufs.append((s_, sn, p_, pn))
            emit_S(0)
            for idx, (j, c2) in enumerate(its):
                if idx + 1 < len(its):
                    emit_S(idx + 1)
                s_, sn, p_, pn = bufs[idx]
                P.act(p_[:, :], s_[:, :], AF.Exp, [sn], [pn])
                if j >= 4 * qb:
                    P.tt("dve", p_[:, :], p_[:, :], cmb[:, j - 4 * qb, :], ALU.mult, [pn, "cmb"], [pn])
                P.mm(psO[c2][:, :], vb_[:, j, :], p_[:, :], [vbn[h], pn], ["psO%d" % c2], start=(j == 0), stop=(j == ntk - 1))
                P.mm(psZ[c2][:, :], onesb[:, :], p_[:, :], ["onesb", pn], ["psZ%d" % c2], start=(j == 0), stop=(j == ntk - 1))
            for c2 in range(2):
                P.cp("act", sO[c2][:, :], psO[c2][:, :], ["psO%d" % c2], ["sO%d" % c2])
                P.cp("act", sZ[c2][:, :], psZ[c2][:, :], ["psZ%d" % c2], ["sZ%d" % c2])
            P.op("dve", lambda e: e.reciprocal(rz[:, :], sZ[0][:, :]), ["sZ0"], ["rz"])
            P.tt("dve", ob[:, :], sO[0][:, :], rz[:, :], ALU.mult, ["sO0", "rz"], ["ob"])
            P.op("dve", lambda e: e.reciprocal(rz[:, :], sZ[1][:, :]), ["sZ1"], ["rz"])
            P.tt("pool", o1[:, :], sO[1][:, :], rz[:, :], ALU.mult, ["sO1", "rz"], ["o1"])
            P.stt(ob[:, :], o1[:, :], ls[:, 3:4], ob[:, :], ALU.mult, ALU.add, ["o1", "ls", "ob"], ["ob"])
            P.tt("pool", osq[:, :], ob[:, :], ob[:, :], ALU.mult, ["ob"], ["osq"])
            P.mm(psM[:, :], onesf[:, :], osq[:, :], ["onesf", "osq"], ["psM"])
            P.ts("dve", osq[:, :], psM[:, :], 1.0 / 128, 1e-5, ALU.mult, ALU.add, ["psM"], ["osq"])
            P.act(osq[:, :], osq[:, :], AF.Sqrt, ["osq"], ["osq"])
            P.op("dve", lambda e: e.reciprocal(osq[:, :], osq[:, :]), ["osq"], ["osq"])
            o_, on = oo[qb % 2], "oo%d" % (qb % 2)
            P.stt(o_[:, :], ob[:, :], gsub[:, 0:1], osq[:, :], ALU.mult, ALU.mult, ["ob", "gsub", "osq"], [on])
            P.dma(oT[h, :, qs], o_[:, :], [on], [("oT", h * NQ + qb)], sk=on)
    return _fin(P, nc, io, ("oT",))


def l3_inputs(q, k, v, prm, hp, L, lam_init):
    f = lambda a: np.ascontiguousarray(a, dtype=np.float32)
    hs = [2 * hp, 2 * hp + 1]
    d = dict(qT=f(np.stack([q[:, h * 128:(h + 1) * 128].T for h in hs])),
             kT=f(np.stack([k[:, h * 128:(h + 1) * 128].T for h in hs])),
             vtok=f(np.stack([v[:, h * 128:(h + 1) * 128] for h in hs])))
    g = np.zeros((128, 4), np.float32)
    g[:, 0] = prm["q_gain"].reshape(-1)
    g[:, 1] = prm["k_gain"].reshape(-1)
    g[:, 2] = prm["subln_g"]
    g[:, 3] = 1.0 - lam_init
    d["gains"] = g
    d["lamv"] = f(np.broadcast_to(prm["lam"][None], (128, 4, 64)))
    d["laminit"] = np.full((128, 1), lam_init, np.float32)
    d.update(at_consts())
    return d


PI = float(np.pi)


def build_L2(L=4096, ctx=None):
    nc, P, io = _ctx(ctx)
    NQ = L // 512
    I = _mkI(nc, io)
    uT = I("uT", [256, L])
    bpr, bpi = I("bpr", [128, 8, 128]), I("bpi", [128, 8, 128])
    cpr, cpi = I("cpr", [128, 8, 128]), I("cpi", [128, 8, 128])
    prm = I("prm", [128, 3, 8])
    dcol = I("dcol", [128, 2])
    yT = _mkO(nc, io, "yT", [256, L])

    def ld(name, src, shape):
        t = P.sb(name, shape)
        if len(shape) == 2:
            P.dma(t[:, :], src[:, :], [], [name])
        else:
            P.dma(t[:, :, :], src[:, :, :], [], [name])
        return t
    prs = ld("prs", prm, [128, 3, 8])
    ds = ld("ds", dcol, [128, 2])
    wts = {}
    for nm, src in (("bpr", bpr), ("bpi", bpi), ("cpr", cpr), ("cpi", cpi)):
        f = ld(nm + "f", src, [128, 8, 128])
        b = P.sb(nm + "b", [128, 8, 128], BF16)
        P.cp("dve", b[:, :, :], f[:, :, :], [nm + "f"], [nm + "b"])
        wts[nm] = b
    iaf = P.sb("iaf", [128, L])
    ub = P.sb("ub", [128, 2, L], BF16)
    for t in range(2):
        P.dma(iaf[:, :], uT[t * 128:(t + 1) * 128, :], [], ["iaf"])
        P.cp("dve", ub[:, t, :], iaf[:, :], ["iaf"], [("ub", t)])
    q = P.sb("q", [128, 16, 8])
    Q = lambda i: q[:, i, :]
    lr, li_, ldt = prs[:, 0, :], prs[:, 1, :], prs[:, 2, :]
    negpi = P.sb("negpi", [128, 1])
    P.memset("pool", negpi[:, :], -PI, ["negpi"])
    P.act(Q(0), ldt, AF.Exp, ["prs"], [("q", 0)])
    P.tt("dve", Q(1), lr, Q(0), ALU.mult, ["prs", ("q", 0)], [("q", 1)])
    P.tt("dve", Q(2), li_, Q(0), ALU.mult, ["prs", ("q", 0)], [("q", 2)])
    P.act(Q(3), Q(1), AF.Exp, [("q", 1)], [("q", 3)])
    NA = L // 64
    pw = P.sb("pw", [128, 24, 2, 8])
    wt = P.sb("wt", [128, 6, 8])
    W = lambda i: wt[:, i, :]

    def square_norm(cin, sin_, cout, sout, rd, wr):
        P.tt("dve", W(0), cin, cin, ALU.mult, rd, [("wt", 0)])
        P.tt("dve", W(1), sin_, sin_, ALU.mult, rd, [("wt", 1)])
        P.tt("dve", W(2), cin, sin_, ALU.mult, rd, [("wt", 2)])
        P.tt("dve", W(0), W(0), W(1), ALU.subtract, [("wt", 0), ("wt", 1)], [("wt", 0)])
        P.ts("dve", W(2), W(2), 2.0, None, ALU.mult, None, [("wt", 2)], [("wt", 2)])
        P.tt("dve", W(3), W(0), W(0), ALU.mult, [("wt", 0)], [("wt", 3)])
        P.tt("dve", W(4), W(2), W(2), ALU.mult, [("wt", 2)], [("wt", 4)])
        P.tt("dve", W(3), W(3), W(4), ALU.add, [("wt", 3), ("wt", 4)], [("wt", 3)])
        P.act(W(3), W(3), AF.Sqrt, [("wt", 3)], [("wt", 3)])
        P.op("dve", lambda e: e.reciprocal(W(3), W(3)), [("wt", 3)], [("wt", 3)])
        P.tt("dve", cout, W(0), W(3), ALU.mult, [("wt", 0), ("wt", 3)], wr)
        P.tt("dve", sout, W(2), W(3), ALU.mult, [("wt", 2), ("wt", 3)], wr)
    P.act(pw[:, 0, 1, :], Q(2), AF.Sin, [("q", 2)], [("pw", 0)], scale=1.0 / 16)
    P.act(Q(5), Q(2), AF.Sin, [("q", 2)], [("q", 5)], scale=1.0 / 32)
    P.tt("dve", Q(5), Q(5), Q(5), ALU.mult, [("q", 5)], [("q", 5)])
    P.ts("dve", pw[:, 0, 0, :], Q(5), -2.0, 1.0, ALU.mult, ALU.add, [("q", 5)], [("pw", 0)])
    NPW = 4 + 6 + max(1, int(np.log2(NA)))
    for k_ in range(1, NPW):
        square_norm(pw[:, k_ - 1, 0, :], pw[:, k_ - 1, 1, :], pw[:, k_, 0, :], pw[:, k_, 1, :], [("pw", k_ - 1)], [("pw", k_)])
    P.cp("dve", Q(5), pw[:, 4, 0, :], [("pw", 4)], [("q", 5)])
    P.cp("dve", Q(4), pw[:, 4, 1, :], [("pw", 4)], [("q", 4)])
    EBc, EBs = P.sb("EBc", [128, 8, 64]), P.sb("EBs", [128, 8, 64])
    EAc, EAs = P.sb("EAc", [128, 8, NA]), P.sb("EAs", [128, 8, NA])
    tbl = P.sb("tbl", [128, 2, 8, 64])

    def build_table(Ec, Es, ecn, esn, n_tot, k0, rowlen):
        P.memset("pool", Ec[:, :, 0:1], 1.0, [ecn])
        P.memset("pool", Es[:, :, 0:1], 0.0, [esn])
        n = 1
        k_ = k0
        while n < n_tot:
            wc = ap3(pw, (k_ * 2 + 0) * 8, [[24 * 2 * 8, 128], [1, 8], [0, n]])
            ws = ap3(pw, (k_ * 2 + 1) * 8, [[24 * 2 * 8, 128], [1, 8], [0, n]])
            t0_, t1_ = tbl[:, 0, :, 0:n], tbl[:, 1, :, 0:n]
            rd = [ecn, esn, ("pw", k_)]
            P.tt("dve", t0_, Es[:, :, 0:n], ws, ALU.mult, rd, [("tbl", 0)])
            P.tt("dve", t1_, Ec[:, :, 0:n], ws, ALU.mult, rd, [("tbl", 1)])
            P.tt("dve", Ec[:, :, n:2 * n], Ec[:, :, 0:n], wc, ALU.mult, rd, [ecn])
            P.tt("dve", Es[:, :, n:2 * n], Es[:, :, 0:n], wc, ALU.mult, rd, [esn])
            P.tt("dve", Ec[:, :, n:2 * n], Ec[:, :, n:2 * n], t0_, ALU.subtract, [ecn, ("tbl", 0)], [ecn])
            P.tt("dve", Es[:, :, n:2 * n], Es[:, :, n:2 * n], t1_, ALU.add, [esn, ("tbl", 1)], [esn])
            n *= 2
            k_ += 1
    build_table(EBc, EBs, "EBc", "EBs", 64, 4, 64)
    build_table(EAc, EAs, "EAc", "EAs", NA, 10, NA)
    P.tt("dve", Q(6), Q(3), Q(5), ALU.mult, [("q", 3), ("q", 5)], [("q", 6)])
    P.tt("dve", Q(7), Q(3), Q(4), ALU.mult, [("q", 3), ("q", 4)], [("q", 7)])
    P.tt("dve", Q(8), lr, lr, ALU.mult, ["prs"], [("q", 8)])
    P.tt("dve", Q(9), li_, li_, ALU.mult, ["prs"], [("q", 9)])
    P.tt("dve", Q(8), Q(8), Q(9), ALU.add, [("q", 8), ("q", 9)], [("q", 8)])
    P.op("dve", lambda e: e.reciprocal(Q(8), Q(8)), [("q", 8)], [("q", 8)])
    P.ts("dve", Q(9), Q(6), -1.0, None, ALU.add, None, [("q", 6)], [("q", 9)])
    P.tt("dve", Q(10), Q(9), lr, ALU.mult, [("q", 9), "prs"], [("q", 10)])
    P.tt("dve", Q(11), Q(7), li_, ALU.mult, [("q", 7), "prs"], [("q", 11)])
    P.tt("dve", Q(10), Q(10), Q(11), ALU.add, [("q", 10), ("q", 11)], [("q", 10)])
    P.tt("dve", Q(10), Q(10), Q(8), ALU.mult, [("q", 10), ("q", 8)], [("q", 10)])
    P.tt("dve", Q(11), Q(7), lr, ALU.mult, [("q", 7), "prs"], [("q", 11)])
    P.tt("dve", Q(12), Q(9), li_, ALU.mult, [("q", 9), "prs"], [("q", 12)])
    P.tt("dve", Q(11), Q(11), Q(12), ALU.subtract, [("q", 11), ("q", 12)], [("q", 11)])
    P.tt("dve", Q(11), Q(11), Q(8), ALU.mult, [("q", 11), ("q", 8)], [("q", 11)])
    P.ts("dve", Q(14), Q(10), -1.0, None, ALU.mult, None, [("q", 10)], [("q", 14)])
    MAG, TH, KR, KI, NKR = 3, 2, 10, 11, 14

    xb = P.sb("xb", [128, 4, 2, L], BF16)
    NS = 2
    def mk(n, dt=F32):
        return [P.sb("%s%d" % (n, i), [128, 512], dt) for i in range(NS)]
    ang, cc, ss_, T1, T2, mr, mi, tmp, tmp2 = [mk(n) for n in ("ang", "cc", "ss", "T1", "T2", "mr", "mi", "tmp", "tmp2")]
    wr = [P.sb("wr%d" % i, [128, 512]) for i in range(2)]
    wi = [P.sb("wi%d" % i, [128, 512]) for i in range(2)]
    magt = P.sb("magt", [128, 512])
    onest = P.sb("onest", [128, 512])
    P.memset("pool", onest[:, :], 1.0, ["onest"])
    ust = [P.sb("ust%d" % i, [128, 512]) for i in range(2)]
    yst = [P.sb("yst%d" % i, [128, 512]) for i in range(2)]
    psr = [P.ps("psr%d" % i, [128, 512]) for i in range(2)]
    psi = [P.ps("psi%d" % i, [128, 512]) for i in range(2)]
    psy = [P.ps("psy%d" % i, [128, 512]) for i in range(2)]
    k = 0
    for tile in range(2):
        for gl in range(4):
            gp = 4 * tile + gl
            sc = lambda idx: q[:, idx, gp:gp + 1]
            P.ts("dve", magt[:, :], onest[:, :], sc(MAG), None, ALU.mult, None, ["onest", "q"], ["magt"])
            for qb in range(NQ):
                i = k % NS
                k += 1
                n = lambda s: "%s%d" % (s, i)
                cs = slice(qb * 512, qb * 512 + 512)
                P.mm(psr[i][:, :], wts["bpr"][:, gp, :], ub[:, tile, cs], ["bprb", ("ub", tile)], ["psr%d" % i])
                P.mm(psi[i][:, :], wts["bpi"][:, gp, :], ub[:, tile, cs], ["bpib", ("ub", tile)], ["psi%d" % i])
                a0 = 8 * qb
                bro = lambda T_, n_: ap3(T_, gp * n_ + a0, [[8 * n_, 128], [1, 8], [0, 64]])
                brb = lambda T_: ap3(T_, gp * 64, [[8 * 64, 128], [0, 8], [1, 64]])
                v3 = lambda t_: t_[:, :].rearrange("p (a b) -> p a b", a=8)
                P.tt("pool", v3(tmp[i]), bro(EAs, NA), brb(EBs), ALU.mult, ["EAs", "EBs"], [n("tmp")])
                P.tt("dve", v3(cc[i]), bro(EAc, NA), brb(EBc), ALU.mult, ["EAc", "EBc"], [n("cc")])
                P.tt("pool", cc[i][:, :], cc[i][:, :], tmp[i][:, :], ALU.subtract, [n("cc"), n("tmp")], [n("cc")])
                P.tt("pool", v3(tmp2[i]), bro(EAc, NA), brb(EBs), ALU.mult, ["EAc", "EBs"], [n("tmp2")])
                P.tt("dve", v3(ss_[i]), bro(EAs, NA), brb(EBc), ALU.mult, ["EAs", "EBc"], [n("ss")])
                P.tt("pool", ss_[i][:, :], ss_[i][:, :], tmp2[i][:, :], ALU.add, [n("ss"), n("tmp2")], [n("ss")])
                P.ts("dve", T1[i][:, :], cc[i][:, :], sc(KR), None, ALU.mult, None, [n("cc"), "q"], [n("T1")])
                P.stt(T1[i][:, :], ss_[i][:, :], sc(KI), T1[i][:, :], ALU.mult, ALU.add, [n("ss"), "q", n("T1")], [n("T1")])
                P.ts("dve", T2[i][:, :], cc[i][:, :], sc(KI), None, ALU.mult, None, [n("cc"), "q"], [n("T2")])
                P.stt(T2[i][:, :], ss_[i][:, :], sc(NKR), T2[i][:, :], ALU.mult, ALU.add, [n("ss"), "q", n("T2")], [n("T2")])
                P.tt("dve", tmp[i][:, :], psi[i][:, :], T2[i][:, :], ALU.mult, ["psi%d" % i, n("T2")], [n("tmp")])
                P.tt("dve", mr[i][:, :], psr[i][:, :], T1[i][:, :], ALU.mult, ["psr%d" % i, n("T1")], [n("mr")])
                P.tt("pool", mr[i][:, :], mr[i][:, :], tmp[i][:, :], ALU.subtract, [n("mr"), n("tmp")], [n("mr")])
                P.tt("dve", tmp2[i][:, :], psr[i][:, :], T2[i][:, :], ALU.mult, ["psr%d" % i, n("T2")], [n("tmp2")])
                P.tt("dve", mi[i][:, :], psi[i][:, :], T1[i][:, :], ALU.mult, ["psi%d" % i, n("T1")], [n("mi")])
                P.tt("pool", mi[i][:, :], mi[i][:, :], tmp2[i][:, :], ALU.add, [n("mi"), n("tmp2")], [n("mi")])
                w_r, w_i = wr[qb % 2], wi[qb % 2]
                wrn, win = "wr%d" % (qb % 2), "wi%d" % (qb % 2)
                if qb == 0:
                    ini_r, ini_i, extra = 0.0, 0.0, []
                else:
                    ini_r, ini_i = wr[1 - qb % 2][:, 511:512], wi[1 - qb % 2][:, 511:512]
                    extra = ["wr%d" % (1 - qb % 2), "wi%d" % (1 - qb % 2)]
                P.op("dve", lambda e, w_r=w_r, m=mr[i], ini=ini_r: e.tensor_tensor_scan(w_r[:, :], magt[:, :], m[:, :], ini, ALU.mult, ALU.add),
                     ["magt", n("mr")] + extra, [wrn])
                P.op("dve", lambda e, w_i=w_i, m=mi[i], ini=ini_i: e.tensor_tensor_scan(w_i[:, :], magt[:, :], m[:, :], ini, ALU.mult, ALU.add),
                     ["magt", n("mi")] + extra, [win])
                P.tt("pool", tmp[i][:, :], ss_[i][:, :], w_i[:, :], ALU.mult, [n("ss"), win], [n("tmp")])
                P.tt("pool", tmp2[i][:, :], cc[i][:, :], w_r[:, :], ALU.mult, [n("cc"), wrn], [n("tmp2")])
                P.tt("pool", xb[:, gl, 0, cs], tmp2[i][:, :], tmp[i][:, :], ALU.subtract, [n("tmp2"), n("tmp")], [("xb", gl * 2)])
                P.tt("pool", tmp[i][:, :], ss_[i][:, :], w_r[:, :], ALU.mult, [n("ss"), wrn], [n("tmp")])
                P.tt("pool", tmp2[i][:, :], cc[i][:, :], w_i[:, :], ALU.mult, [n("cc"), win], [n("tmp2")])
                P.stt(xb[:, gl, 1, cs], tmp[i][:, :], -1.0, tmp2[i][:, :], ALU.mult, ALU.subtract, [n("tmp"), n("tmp2")], [("xb", gl * 2 + 1)])
        for qb in range(NQ):
            cs = slice(qb * 512, qb * 512 + 512)
            py, pyn = psy[qb % 2], "psy%d" % (qb % 2)
            for gl in range(4):
                gp = 4 * tile + gl
                P.mm(py[:, :], wts["cpr"][:, gp, :], xb[:, gl, 0, cs], ["cprb", ("xb", gl * 2)], [pyn], start=(gl == 0), stop=False)
                P.mm(py[:, :], wts["cpi"][:, gp, :], xb[:, gl, 1, cs], ["cpib", ("xb", gl * 2 + 1)], [pyn], start=False, stop=(gl == 3))
            us, usn = ust[qb % 2], "ust%d" % (qb % 2)
            ys, ysn = yst[qb % 2], "yst%d" % (qb % 2)
            P.dma(us[:, :], uT[tile * 128:(tile + 1) * 128, cs], [], [usn])
            P.stt(ys[:, :], us[:, :], ds[:, tile:tile + 1], py[:, :], ALU.mult, ALU.add, [usn, "ds", pyn], [ysn])
            P.dma(yT[tile * 128:(tile + 1) * 128, cs], ys[:, :], [ysn], [("yT", tile * NQ + qb)], sk=ysn)
    return _fin(P, nc, io, ("yT",))


def l2_inputs(u, prm, gh, L):
    f = lambda a: np.ascontiguousarray(a, dtype=np.float32)
    d = dict(uT=f(u[:, 256 * gh:256 * gh + 256].T))
    bpr = np.zeros((128, 8, 128), np.float32); bpi = np.zeros_like(bpr)
    cpr = np.zeros((128, 8, 128), np.float32); cpi = np.zeros_like(cpr)
    pr = np.zeros((128, 3, 8), np.float32)
    for gp in range(8):
        gl = gp % 4
        for gpar in range(2):
            g = 16 * gh + 2 * gp + gpar
            chs = slice(16 * (2 * gl + gpar), 16 * (2 * gl + gpar) + 16)
            ps = slice(64 * gpar, 64 * gpar + 64)
            bpr[chs, gp, ps] = prm["b_re"][g].T
            bpi[chs, gp, ps] = prm["b_im"][g].T
            cpr[ps, gp, chs] = prm["c_re"][g].T
            cpi[ps, gp, chs] = prm["c_im"][g].T
            pr[ps, 0, gp] = prm["lam_re"][g]
            pr[ps, 1, gp] = prm["lam_im"][g]
            pr[ps, 2, gp] = prm["log_dt"][g]
    d.update(bpr=bpr, bpi=bpi, cpr=cpr, cpi=cpi, prm=pr)
    d["dcol"] = f(prm["d"][256 * gh:256 * gh + 256].reshape(2, 128).T)
    return d


DFF = 2816


def build_L5a(TOK=2048, HT=1024, ctx=None):
    nc, P, io = _ctx(ctx)
    NH = TOK // HT
    NTT = HT // 128
    NB5 = HT // 512
    I = _mkI(nc, io)
    x_tok = I("x_tok", [TOK, D])
    ypreT, oT, ycT = I("ypreT", [512, TOK]), I("oT", [512, TOK]), I("ycT", [512, TOK])
    glT = I("glT", [3072, TOK])
    w_glu, w_br, w_out = I("w_glu", [512, 512]), I("w_br", [3, 512, 1024]), I("w_out", [1024, 1024])
    x_out = _mkO(nc, io, "x_out", [TOK, D])

    wst = [P.sb("wst%d" % i, [128, 1024]) for i in range(3)]
    wk = [0]

    def load_w(dst, dn, sub, src_ap, ncol):
        i = wk[0] % 3
        wk[0] += 1
        P.dma(wst[i][:, :ncol], src_ap, [], ["wst%d" % i], eng=("sp" if i % 2 == 0 else "pool"))
        P.cp("pool" if i % 2 else "dve", dst, wst[i][:, :ncol], ["wst%d" % i], [(dn, sub)])

    wglu = P.sb("wglu", [128, 4, 512], BF16)
    wbr = P.sb("wbr", [128, 12, 1024], BF16)
    wout = P.sb("wout", [128, 8, 1024], BF16)
    for c in range(4):
        load_w(wglu[:, c, :], "wglu", c, w_glu[c * 128:(c + 1) * 128, :], 512)
    for n in range(3):
        for c in range(4):
            load_w(wbr[:, n * 4 + c, :], "wbr", n * 4 + c, w_br[n, c * 128:(c + 1) * 128, :], 1024)
    for c in range(8):
        load_w(wout[:, c, :], "wout", c, w_out[c * 128:(c + 1) * 128, :], 1024)

    fin = P.sb("fin", [128, 4, HT])
    t1 = P.sb("t1", [128, 4, HT])
    yb = [P.sb("yb%d" % n, [128, 4, HT], BF16) for n in range(3)]
    ya2 = P.sb("ya2", [128, 4, HT], BF16)
    gst = [P.sb("gst%d" % i, [128, 512]) for i in range(2)]
    sg = [P.sb("sg%d" % i, [128, 512]) for i in range(2)]
    pr = [P.sb("pr%d" % i, [128, 512]) for i in range(2)]
    macc = P.sb("macc", [128, 512])
    mT = P.sb("mT", [128, 8, HT], BF16)
    x1 = [P.sb("x1_%d" % i, [128, D]) for i in range(2)]
    xin = [P.sb("xin%d" % i, [128, D]) for i in range(2)]
    pss = [P.ps("ps%d" % i, [128, 512]) for i in range(6)]
    pk = [0]

    def nextps():
        k = pk[0] % 6
        pk[0] += 1
        return pss[k], "ps%d" % k

    for half in range(NH):
        t0 = half * HT
        tsl = slice(t0, t0 + HT)
        fmv = lambda a: a.ap().rearrange("(c p) t -> p c t", p=128)
        P.dma(fin[:, :, :], fmv(ypreT)[:, :, tsl], [], ["fin"])
        P.tt("pool", t1[:, :, :], fin[:, :, :], fin[:, :, :], ALU.mult, ["fin"], ["t1"])
        P.ts("dve", t1[:, :, :], t1[:, :, :], 0.044715, 1.0, ALU.mult, ALU.add, ["t1"], ["t1"])
        P.tt("pool", t1[:, :, :], t1[:, :, :], fin[:, :, :], ALU.mult, ["t1", "fin"], ["t1"])
        P.act(t1[:, :, :], t1[:, :, :], AF.Sigmoid, ["t1"], ["t1"], scale=1.5957691216057308)
        P.tt("dve", t1[:, :, :], t1[:, :, :], fin[:, :, :], ALU.mult, ["t1", "fin"], ["t1"])
        P.cp("pool", ya2[:, :, :], t1[:, :, :], ["t1"], ["ya2"])
        for jc in range(4):
            for b5 in range(NB5):
                bs = slice(b5 * 512, b5 * 512 + 512)
                ps, psn = nextps()
                for c in range(4):
                    P.mm(ps[:, :], wglu[:, c, jc * 128:(jc + 1) * 128], ya2[:, c, bs], [("wglu", c), "ya2"], [psn], start=(c == 0), stop=(c == 3))
                s_, sn = sg[(jc * NB5 + b5) % 2], "sg%d" % ((jc * NB5 + b5) % 2)
                P.act(s_[:, :], ps[:, :], AF.Sigmoid, [psn], [sn])
                P.tt("dve", yb[0][:, jc, bs], t1[:, jc, bs], s_[:, :], ALU.mult, ["t1", sn], [("yb0", jc)])
        P.dma(fin[:, :, :], fmv(oT)[:, :, tsl], ["fin"], ["fin"])
        P.cp("dve", yb[1][:, :, :], fin[:, :, :], ["fin"], ["yb1"])
        P.dma(fin[:, :, :], fmv(ycT)[:, :, tsl], ["fin"], ["fin"])
        P.cp("dve", yb[2][:, :, :], fin[:, :, :], ["fin"], ["yb2"])
        kk = 0
        for dc in range(8):
            for b5 in range(NB5):
                bs = slice(b5 * 512, b5 * 512 + 512)
                for n in range(3):
                    g_, gn = gst[kk % 2], "gst%d" % (kk % 2)
                    s_, sn = sg[kk % 2], "sg%d" % (kk % 2)
                    p_, pn = pr[kk % 2], "pr%d" % (kk % 2)
                    kk += 1
                    r0 = n * 1024 + dc * 128
                    P.dma(g_[:, :], glT[r0:r0 + 128, t0 + b5 * 512:t0 + b5 * 512 + 512], [], [gn], eng=("sp" if kk % 2 else "pool"))
                    P.act(s_[:, :], g_[:, :], AF.Sigmoid, [gn], [sn])
                    ps, psn = nextps()
                    for c in range(4):
                        P.mm(ps[:, :], wbr[:, n * 4 + c, dc * 128:(dc + 1) * 128], yb[n][:, c, bs], [("wbr", n * 4 + c), "yb%d" % n], [psn],
                             start=(c == 0), stop=(c == 3))
                    if n == 0:
                        P.tt("dve", macc[:, :], ps[:, :], s_[:, :], ALU.mult, [psn, sn], ["macc"])
                    else:
                        P.tt("dve", p_[:, :], ps[:, :], s_[:, :], ALU.mult, [psn, sn], [pn])
                        if n == 1:
                            P.tt("pool", macc[:, :], macc[:, :], p_[:, :], ALU.add, ["macc", pn], ["macc"])
                        else:
                            P.tt("pool", mT[:, dc, bs], macc[:, :], p_[:, :], ALU.add, ["macc", pn], [("mT", dc)])
        for tt_ in range(NTT):
            xi, xn = xin[tt_ % 2], "xin%d" % (tt_ % 2)
            xo_, xon = x1[tt_ % 2], "x1_%d" % (tt_ % 2)
            P.dma(xi[:, :], x_tok[t0 + tt_ * 128:t0 + (tt_ + 1) * 128, :], [], [xn])
            for ch in range(2):
                ps, psn = nextps()
                for c in range(8):
                    P.mm(ps[:, :], mT[:, c, tt_ * 128:(tt_ + 1) * 128], wout[:, c, ch * 512:(ch + 1) * 512], [("mT", c), ("wout", c)], [psn],
                         start=(c == 0), stop=(c == 7))
                P.tt("dve", xo_[:, ch * 512:(ch + 1) * 512], ps[:, :], xi[:, ch * 512:(ch + 1) * 512], ALU.add, [psn, xn], [(xon, ch)])
            P.dma(x_out[t0 + tt_ * 128:t0 + (tt_ + 1) * 128, :], xo_[:, :], [xon], [("x_out", half * NTT + tt_)], sk=xon)
    return _fin(P, nc, io, ("x_out",))


def build_L5b(TOK=2048, HT=1024, ctx=None):
    nc, P, io = _ctx(ctx)
    NH = TOK // HT
    NTT = HT // 128
    NB5 = HT // 512
    I = _mkI(nc, io)
    x_tok = I("x_tok", [TOK, D])
    g2 = I("g2", [128, 8])
    w_fi, w_fo = I("w_fi", [1024, 2 * DFF]), I("w_fo", [DFF, 1024])
    c_ident = I("c_ident", [128, 128])
    x_out = _mkO(nc, io, "x_out", [TOK, D])
    g2s = P.sb("g2s", [128, 8])
    P.dma(g2s[:, :], g2[:, :], [], ["g2s"])
    identf = P.sb("identf", [128, 128])
    P.dma(identf[:, :], c_ident[:, :], [], ["identf"])
    identb = P.sb("identb", [128, 128], BF16)
    P.cp("dve", identb[:, :], identf[:, :], ["identf"], ["identb"])
    wst = [P.sb("wst%d" % i, [128, 1024]) for i in range(3)]
    wk = [0]

    def load_w(dst, dn, sub, src_ap, ncol):
        i = wk[0] % 3
        wk[0] += 1
        P.dma(wst[i][:, :ncol], src_ap, [], ["wst%d" % i], eng=("sp" if i % 2 == 0 else "pool"))
        P.cp("pool" if i % 2 else "dve", dst, wst[i][:, :ncol], ["wst%d" % i], [(dn, sub)])
    wfo = P.sb("wfo", [128, 22, 1024], BF16)
    for c in range(22):
        load_w(wfo[:, c, :], "wfo", c, w_fo[c * 128:(c + 1) * 128, :], 1024)
    x1 = P.sb("x1", [128, NTT, D])
    junk = P.sb("junk", [128, D], BF16)
    ss = P.sb("ss5", [128, NTT])
    hb = [P.sb("hb%d" % i, [128, D], BF16) for i in range(2)]
    h2T = P.sb("h2T", [128, 8, HT], BF16)
    wfi_f = [P.sb("wfif%d" % i, [128, 8, 256]) for i in range(2)]
    wfi_b = [P.sb("wfib%d" % i, [128, 8, 256], BF16) for i in range(2)]
    actT = P.sb("actT", [128, 22, HT], BF16)
    sil = [P.sb("sil%d" % i, [128, 512]) for i in range(2)]
    xo = [P.sb("xo%d" % i, [128, 512]) for i in range(2)]
    pss = [P.ps("ps%d" % i, [128, 512]) for i in range(6)]
    psb = P.ps("psb", [128, 1024], BF16)
    pk = [0]

    def nextps():
        k = pk[0] % 6
        pk[0] += 1
        return pss[k], "ps%d" % k

    for half in range(NH):
        t0 = half * HT
        for tt_ in range(NTT):
            P.dma(x1[:, tt_, :], x_tok[t0 + tt_ * 128:t0 + (tt_ + 1) * 128, :], [], [("x1", tt_)], eng=("sp" if tt_ % 2 else "pool"))
            P.act(junk[:, :], x1[:, tt_, :], AF.Square, [("x1", tt_)], ["junk", ("ss5", tt_)], accum_out=ss[:, tt_:tt_ + 1])
        P.ts("dve", ss[:, :], ss[:, :], 1.0 / D, EPS, ALU.mult, ALU.add, ["ss5"], ["ss5"])
        P.act(ss[:, :], ss[:, :], AF.Sqrt, ["ss5"], ["ss5"])
        P.op("dve", lambda e: e.reciprocal(ss[:, :], ss[:, :]), ["ss5"], ["ss5"])
        for tt_ in range(NTT):
            h_, hn = hb[tt_ % 2], "hb%d" % (tt_ % 2)
            P.act(h_[:, :], x1[:, tt_, :], AF.Copy, [("x1", tt_), "ss5"], [hn], scale=ss[:, tt_:tt_ + 1])
            for c in range(8):
                P.tr(psb[:, c * 128:(c + 1) * 128], h_[:, c * 128:(c + 1) * 128], identb[:, :], [hn, "identb"], ["psb"])
            for c in range(8):
                P.ts("dve", h2T[:, c, tt_ * 128:(tt_ + 1) * 128], psb[:, c * 128:(c + 1) * 128], g2s[:, c:c + 1], None, ALU.mult, None,
                     ["psb", "g2s"], [("h2T", c)])
        for fc in range(22):
            wf, wfn = wfi_f[fc % 2], "wfif%d" % (fc % 2)
            wb_, wbn = wfi_b[fc % 2], "wfib%d" % (fc % 2)
            wv = w_fi.ap().rearrange("(c p) n -> p c n", p=128)
            P.dma(wf[:, :, 0:128], wv[:, :, fc * 128:(fc + 1) * 128], [], [(wfn, 0)], eng="sp", sk=wfn + "a")
            P.dma(wf[:, :, 128:256], wv[:, :, DFF + fc * 128:DFF + (fc + 1) * 128], [], [(wfn, 1)], eng="pool", sk=wfn + "b")
            P.cp("pool", wb_[:, :, :], wf[:, :, :], [wfn], [wbn])
            for b5 in range(NB5):
                bs = slice(b5 * 512, b5 * 512 + 512)
                pg, pgn = nextps()
                pu, pun = nextps()
                for c in range(8):
                    P.mm(pg[:, :], wb_[:, c, 0:128], h2T[:, c, bs], [wbn, ("h2T", c)], [pgn], start=(c == 0), stop=(c == 7))
                for c in range(8):
                    P.mm(pu[:, :], wb_[:, c, 128:256], h2T[:, c, bs], [wbn, ("h2T", c)], [pun], start=(c == 0), stop=(c == 7))
                s_, sn = sil[(fc * NB5 + b5) % 2], "sil%d" % ((fc * NB5 + b5) % 2)
                P.act(s_[:, :], pg[:, :], AF.Silu, [pgn], [sn])
                P.tt("dve", actT[:, fc, bs], pu[:, :], s_[:, :], ALU.mult, [pun, sn], [("actT", fc)])
        for tt_ in range(NTT):
            for ch in range(2):
                ps, psn = nextps()
                for fc in range(22):
                    P.mm(ps[:, :], actT[:, fc, tt_ * 128:(tt_ + 1) * 128], wfo[:, fc, ch * 512:(ch + 1) * 512], [("actT", fc), ("wfo", fc)], [psn],
                         start=(fc == 0), stop=(fc == 21))
                o_, on = xo[(tt_ * 2 + ch) % 2], "xo%d" % ((tt_ * 2 + ch) % 2)
                P.tt("dve", o_[:, :], ps[:, :], x1[:, tt_, ch * 512:(ch + 1) * 512], ALU.add, [psn, ("x1", tt_)], [on])
                P.dma(x_out[t0 + tt_ * 128:t0 + (tt_ + 1) * 128, ch * 512:(ch + 1) * 512], o_[:, :], [on],
                      [("x_out", (half * NTT + tt_) * 2 + ch)], sk=on)
    return _fin(P, nc, io, ("x_out",))


def _hw_runner(nc, in_maps):
    n = len(in_maps)
    maps = list(in_maps) + [in_maps[-1]] * (NCORES - n)
    res = run_bass_kernel_spmd(nc, maps, core_ids=list(range(NCORES)))
    return res.results[:n]


def kernel_impl(inp, runner=_hw_runner, TOKL=2048, HT=512):
    f = lambda a: np.ascontiguousarray(a, dtype=np.float32)
    x = np.asarray(inp["x"], dtype=np.float32)
    B, L, _ = x.shape
    NTOK = B * L
    ntc = NTOK // TOKL
    xcur = x.reshape(NTOK, D)
    depth = inp["w_in"].shape[0]
    progs = {}

    def prog(name, fn):
        if name not in progs:
            progs[name] = fn()
        return progs[name]
    ident = np.eye(128, dtype=np.float32)
    vfirst = None
    for i in range(depth):
        lam_init = 0.8 - 0.6 * float(np.exp(-0.3 * i))
        nc = prog("L1", lambda: build_L1(TOKL))
        g1 = f(np.asarray(inp["norm1_g"][i]).reshape(8, 128).T)
        w_in = f(inp["w_in"][i])
        maps = []
        for c in range(ntc):
            xs = xcur[c * TOKL:(c + 1) * TOKL]
            maps.append(dict(x_tok=f(xs), xT=f(xs.T), w=w_in, g=g1))
        res = runner(nc, maps)
        proj = np.concatenate([r["proj"] for r in res], 0).reshape(B, L, NIN)
        u, q, k, v = proj[..., 0:512], proj[..., 512:1024], proj[..., 1024:1536], proj[..., 1536:2048]
        z, gl = proj[..., 2048:3744], proj[..., 3744:]
        nc = prog("L2", lambda: build_L2(L))
        p2 = dict(lam_re=np.asarray(inp["s5_lambda_re"][i]), lam_im=np.asarray(inp["s5_lambda_im"][i]), log_dt=np.asarray(inp["s5_log_dt"][i]),
                  b_re=np.asarray(inp["s5_b_re"][i]), b_im=np.asarray(inp["s5_b_im"][i]), c_re=np.asarray(inp["s5_c_re"][i]),
                  c_im=np.asarray(inp["s5_c_im"][i]), d=np.asarray(inp["s5_d"][i]))
        res = runner(nc, [l2_inputs(u[c // 2], p2, c % 2, L) for c in range(2 * B)])
        ypre = np.stack([np.concatenate([res[2 * b]["yT"], res[2 * b + 1]["yT"]], 0) for b in range(B)])
        nc = prog("L3", lambda: build_L3(L))
        p3 = dict(q_gain=np.asarray(inp["da_q_gain"][i]), k_gain=np.asarray(inp["da_k_gain"][i]), lam=np.asarray(inp["da_lambda"][i]),
                  subln_g=np.asarray(inp["da_subln_g"][i]))
        res = runner(nc, [l3_inputs(q[c // 2], k[c // 2], v[c // 2], p3, c % 2, L, lam_init) for c in range(2 * B)])
        oat = np.stack([np.concatenate([res[2 * b]["oT"].reshape(256, L), res[2 * b + 1]["oT"].reshape(256, L)], 0) for b in range(B)])
        vg = i > 0
        nc = prog("L4v" if vg else "L4", lambda: build_L4(L, has_vgate=vg))
        p4 = dict(mu=np.asarray(inp["rw_mu"][i]), w0=np.asarray(inp["rw_w0"][i]), w2=np.asarray(inp["rw_w2"][i]), a0=np.asarray(inp["rw_a0"][i]),
                  a2=np.asarray(inp["rw_a2"][i]), g2=np.asarray(inp["rw_g2"][i]), k_k=np.asarray(inp["rw_k_k"][i]), k_a=np.asarray(inp["rw_k_a"][i]),
                  r_k=np.asarray(inp["rw_r_k"][i]), ln_g=np.asarray(inp["rw_ln_g"][i]), ln_b=np.asarray(inp["rw_ln_b"][i]))
        if vg:
            p4.update(v0=np.asarray(inp["rw_v0"][i - 1]), v1=np.asarray(inp["rw_v1"][i - 1]), v2=np.asarray(inp["rw_v2"][i - 1]))
        maps = []
        for c in range(2 * B):
            b, hg = c // 2, c % 2
            maps.append(l4_inputs(z[b], p4, hg, L, vfirst=(vfirst[b][256 * hg:256 * hg + 256] if vg else None)))
        res = runner(nc, maps)
        yc = np.stack([np.concatenate([res[2 * b]["y_out"], res[2 * b + 1]["y_out"]], 1) for b in range(B)])
        if not vg:
            vfirst = np.stack([np.concatenate([res[2 * b]["vfirst_out"], res[2 * b + 1]["vfirst_out"]], 0) for b in range(B)])
        nc = prog("L5a", lambda: build_L5a(TOKL, HT))
        wa = dict(w_glu=f(inp["s5_w_glu"][i]), w_br=f(inp["w_branch"][i]), w_out=f(inp["w_out"][i]))
        maps = []
        for c in range(ntc):
            t0 = c * TOKL
            b, s0 = t0 // L, t0 % L
            sl = slice(s0, s0 + TOKL)
            maps.append(dict(wa, x_tok=f(xcur[t0:t0 + TOKL]), ypreT=f(ypre[b][:, sl]), oT=f(oat[b][:, sl]), ycT=f(yc[b][sl].T), glT=f(gl[b][sl].T)))
        res = runner(nc, maps)
        x1 = np.concatenate([r["x_out"] for r in res], 0)
        nc = prog("L5b", lambda: build_L5b(TOKL, HT))
        wb_ = dict(g2=f(np.asarray(inp["norm2_g"][i]).reshape(8, 128).T), w_fi=f(inp["w_ffn_in"][i]), w_fo=f(inp["w_ffn_out"][i]), c_ident=ident)
        res = runner(nc, [dict(wb_, x_tok=f(x1[c * TOKL:(c + 1) * TOKL])) for c in range(ntc)])
        xcur = np.concatenate([r["x_out"] for r in res], 0)
    return xcur.reshape(B, L, D).astype(np.float32)


def build_fused(L=4096, depth=2):
    nc = bass.Bass("TRN2", target_bir_lowering=False)
    sh = Shared(nc)
    E = lambda n, s: nc.dram_tensor(n, list(s), F32, kind="ExternalInput")
    x_in = E("x_in", [L, D])
    x_out = nc.dram_tensor("x_final", [L, D], F32, kind="ExternalOutput")
    FT = nc.dram_tensor("FT", [6304, L], F32)
    VT = nc.dram_tensor("VT", [L, 512], F32)
    YB = nc.dram_tensor("YB", [1536, L], F32)
    X1 = nc.dram_tensor("X1", [L, D], F32)
    XS = nc.dram_tensor("XS", [L, D], F32)
    VF = nc.dram_tensor("VF", [512, L], F32)
    VF2 = nc.dram_tensor("VF2", [512, L], F32)
    cs = {}
    LHr = min(L, 1024)
    for n, shp in (("c_ident", [128, 128]), ("c_cmask", [128, 4, 512]), ("c_b64", [128, 128]), ("c_ones", [128, 128]),
                   ("c_mask4", [128, 512]), ("c_sl", [128, 128]), ("c_reset", [128, LHr]), ("c_bones", [128, 128]), ("c_hsel", [128, 2])):
        cs[n] = E(n, shp)
    TP = min(2048, L)
    for i in range(depth):
        sfx = "_%d" % i
        xin = x_in if i == 0 else XS
        xo = x_out if i == depth - 1 else XS
        io = dict(x_tok=xin, w=E("w_in" + sfx, [D, NIN]), g=E("g1" + sfx, [128, 8]), c_ident=cs["c_ident"], FT=FT, VT=VT)
        build_L1F(L, ctx=(nc, sh, io))
        s5 = {n: E("s5" + n + sfx, [2, 128, 8, 128]) for n in ("bpr", "bpi", "cpr", "cpi")}
        s5prm = E("s5prm" + sfx, [2, 128, 3, 8])
        s5d = E("s5d" + sfx, [2, 128, 2])
        for gh in range(2):
            io = dict(uT=V(FT.ap()[256 * gh:256 * gh + 256, :]), prm=V(s5prm.ap()[gh]), dcol=V(s5d.ap()[gh]),
                      yT=V(YB.ap()[256 * gh:256 * gh + 256, :]))
            for n in s5:
                io[n] = V(s5[n].ap()[gh])
            build_L2(L, ctx=(nc, sh, io))
        gains = E("da_gains" + sfx, [128, 4])
        lamv = E("da_lamv" + sfx, [128, 4, 64])
        laminit = E("da_laminit" + sfx, [128, 1])
        for hp in range(2):
            hv = lambda a, r0: V(a.ap()[r0:r0 + 256, :].rearrange("(h p) l -> h p l", h=2))
            io = dict(qT=hv(FT, 512 + 256 * hp), kT=hv(FT, 1024 + 256 * hp),
                      vtok=V(VT.ap()[:, 256 * hp:256 * hp + 256].rearrange("l (h d) -> h l d", h=2)),
                      gains=gains, lamv=lamv, laminit=laminit, c_cmask=cs["c_cmask"], c_b64=cs["c_b64"], c_ones=cs["c_ones"],
                      oT=hv(YB, 512 + 256 * hp))
            build_L3(L, ctx=(nc, sh, io))
        vg = i > 0
        r4 = dict(mu3=E("rw_mu3" + sfx, [2, 128, 6]), mulo=E("rw_mulo" + sfx, [96, 3]), w2=E("rw_w2p" + sfx, [2, 32, 256]),
                  a2=E("rw_a2p" + sfx, [2, 32, 256]), g2=E("rw_g2p" + sfx, [2, 96, 256]), pc=E("rw_pc" + sfx, [2, 128, 10]),
                  lng=E("rw_lng" + sfx, [2, 128, 256]), lnb=E("rw_lnb" + sfx, [2, 128, 256]))
        if vg:
            r4.update(muvf=E("rw_muvf" + sfx, [128, 4]), v1=E("rw_v1p" + sfx, [128, 4, 32]), v2=E("rw_v2p" + sfx, [2, 32, 256]),
                      v0=E("rw_v0p" + sfx, [2, 128, 2]))
        for hg in range(2):
            R = lambda r0, n: V(FT.ap()[r0:r0 + n, :])
            io = dict(zr=R(1536 + 256 * hg, 256), zk=R(2048 + 256 * hg, 256), zv=R(2560 + 256 * hg, 256),
                      zw=R(3072, 32), za=R(3104, 32), zg=R(3136, 96), mulo=r4["mulo"],
                      y_out=V(YB.ap()[1024 + 256 * hg:1024 + 256 * hg + 256, :]),
                      vfirst_out=V((VF2 if vg else VF).ap()[256 * hg:256 * hg + 256, :]))
            for n in ("mu3", "w2", "a2", "g2", "pc", "lng", "lnb"):
                io[n] = V(r4[n].ap()[hg])
            for n in ("c_mask4", "c_sl", "c_reset", "c_bones", "c_hsel", "c_ident"):
                io[n] = cs[n]
            if vg:
                io.update(zvfull=R(2560, 512), muvf=r4["muvf"], v1=r4["v1"], v2=V(r4["v2"].ap()[hg]), v0=V(r4["v0"].ap()[hg]),
                          vfirst_in=V(VF.ap()[256 * hg:256 * hg + 256, :]))
            build_L4(L, has_vgate=vg, ctx=(nc, sh, io), fm_out=True)
        wa = dict(w_glu=E("w_glu" + sfx, [512, 512]), w_br=E("w_br" + sfx, [3, 512, 1024]), w_out=E("w_out" + sfx, [1024, 1024]))
        for th in range(L // TP):
            tsl = slice(th * TP, th * TP + TP)
            io = dict(wa, x_tok=V(xin.ap()[tsl, :]), ypreT=V(YB.ap()[0:512, tsl]), oT=V(YB.ap()[512:1024, tsl]), ycT=V(YB.ap()[1024:1536, tsl]),
                      glT=V(FT.ap()[3232:6304, tsl]), x_out=V(X1.ap()[tsl, :]))
            build_L5a(TP, 512, ctx=(nc, sh, io))
        wb_ = dict(g2=E("g2n" + sfx, [128, 8]), w_fi=E("w_fi" + sfx, [1024, 2 * DFF]), w_fo=E("w_fo" + sfx, [DFF, 1024]), c_ident=cs["c_ident"])
        for th in range(L // TP):
            tsl = slice(th * TP, th * TP + TP)
            io = dict(wb_, x_tok=V(X1.ap()[tsl, :]), x_out=V(xo.ap()[tsl, :]))
            if i == depth - 1:
                io["_final"] = True
            build_L5b(TP, 512, ctx=(nc, sh, io))
    sh.close()
    return nc


def fused_inputs(inp, L):
    f = lambda a: np.ascontiguousarray(a, dtype=np.float32)
    d = {}
    d.update(at_consts())
    rc = rw_consts(L)
    d.update(rc)
    depth = inp["w_in"].shape[0]
    zL = np.zeros((L, 512), np.float32)
    zz = np.zeros((L, 1696), np.float32)
    for i in range(depth):
        sfx = "_%d" % i
        lam_init = 0.8 - 0.6 * float(np.exp(-0.3 * i))
        d["w_in" + sfx] = f(inp["w_in"][i])
        d["g1" + sfx] = f(np.asarray(inp["norm1_g"][i]).reshape(8, 128).T)
        p2 = dict(lam_re=np.asarray(inp["s5_lambda_re"][i]), lam_im=np.asarray(inp["s5_lambda_im"][i]), log_dt=np.asarray(inp["s5_log_dt"][i]),
                  b_re=np.asarray(inp["s5_b_re"][i]), b_im=np.asarray(inp["s5_b_im"][i]), c_re=np.asarray(inp["s5_c_re"][i]),
                  c_im=np.asarray(inp["s5_c_im"][i]), d=np.asarray(inp["s5_d"][i]))
        l2 = [l2_inputs(zL, p2, gh, L) for gh in range(2)]
        for n in ("bpr", "bpi", "cpr", "cpi"):
            d["s5" + n + sfx] = f(np.stack([l2[gh][n] for gh in range(2)]))
        d["s5prm" + sfx] = f(np.stack([l2[gh]["prm"] for gh in range(2)]))
        d["s5d" + sfx] = f(np.stack([l2[gh]["dcol"] for gh in range(2)]))
        p3 = dict(q_gain=np.asarray(inp["da_q_gain"][i]), k_gain=np.asarray(inp["da_k_gain"][i]), lam=np.asarray(inp["da_lambda"][i]),
                  subln_g=np.asarray(inp["da_subln_g"][i]))
        l3 = l3_inputs(zL, zL, zL, p3, 0, L, lam_init)
        d["da_gains" + sfx], d["da_lamv" + sfx], d["da_laminit" + sfx] = l3["gains"], l3["lamv"], l3["laminit"]
        p4 = dict(mu=np.asarray(inp["rw_mu"][i]), w0=np.asarray(inp["rw_w0"][i]), w2=np.asarray(inp["rw_w2"][i]), a0=np.asarray(inp["rw_a0"][i]),
                  a2=np.asarray(inp["rw_a2"][i]), g2=np.asarray(inp["rw_g2"][i]), k_k=np.asarray(inp["rw_k_k"][i]), k_a=np.asarray(inp["rw_k_a"][i]),
                  r_k=np.asarray(inp["rw_r_k"][i]), ln_g=np.asarray(inp["rw_ln_g"][i]), ln_b=np.asarray(inp["rw_ln_b"][i]))
        vg = i > 0
        if vg:
            p4.update(v0=np.asarray(inp["rw_v0"][i - 1]), v1=np.asarray(inp["rw_v1"][i - 1]), v2=np.asarray(inp["rw_v2"][i - 1]))
        l4 = [l4_inputs(zz, p4, hg, L, vfirst=(np.zeros((256, L), np.float32) if vg else None)) for hg in range(2)]
        for n, m in (("mu3", "rw_mu3"), ("w2", "rw_w2p"), ("a2", "rw_a2p"), ("g2", "rw_g2p"), ("pc", "rw_pc"), ("lng", "rw_lng"), ("lnb", "rw_lnb")):
            d[m + sfx] = f(np.stack([l4[hg][n] for hg in range(2)]))
        d["rw_mulo" + sfx] = l4[0]["mulo"]
        if vg:
            d["rw_muvf" + sfx] = l4[0]["muvf"]
            d["rw_v1p" + sfx] = l4[0]["v1"]
            d["rw_v2p" + sfx] = f(np.stack([l4[hg]["v2"] for hg in range(2)]))
            d["rw_v0p" + sfx] = f(np.stack([l4[hg]["v0"] for hg in range(2)]))
        d["w_glu" + sfx] = f(inp["s5_w_glu"][i])
        d["w_br" + sfx] = f(inp["w_branch"][i])
        d["w_out" + sfx] = f(inp["w_out"][i])
        d["g2n" + sfx] = f(np.asarray(inp["norm2_g"][i]).reshape(8, 128).T)
        d["w_fi" + sfx] = f(inp["w_ffn_in"][i])
        d["w_fo" + sfx] = f(inp["w_ffn_out"][i])
    return d


def kernel_fused(inp, runner=None):
    x = np.asarray(inp["x"], dtype=np.float32)
    B, L, _ = x.shape
    depth = inp["w_in"].shape[0]
    nc = build_fused(L, depth)
    common = fused_inputs(inp, L)
    ncore = NCORES if runner is None else B
    maps = [dict(common, x_in=np.ascontiguousarray(x[c % B])) for c in range(ncore)]
    if runner is None:
        res = run_bass_kernel_spmd(nc, maps, core_ids=list(range(NCORES))).results
    else:
        res = runner(nc, maps)
    return np.stack([res[b]["x_final"] for b in range(B)]).astype(np.float32)


I32 = mybir.dt.int32


def _dyn_dma(P, hreg, tmpregs, ctr, dst, tensor, base, hscale, dims, reads, writes, sk):
    def fn(e):
        r = tmpregs[ctr[0] % len(tmpregs)]
        ctr[0] += 1
        e.reg_mul(r, hreg, hscale)
        e.reg_add(r, r, base)
        return e.dma_start(out=dst, in_=bass.AP(tensor, r, [list(d) for d in dims]))
    return P.op("pool", fn, reads, writes, dma=True, sk=sk)


def phase_load_h(nc, sh, regs, hid):
    P = Prog(nc, sh)
    hreg, _ = regs
    hs = P.stack.enter_context(nc.sbuf_tensor(P.pfx + "hs", [1, 1], I32))
    P.dma(hs[:, :], hid[:, :], [], ["hs"], eng="pool")
    P.op("pool", lambda e: e.reg_load(hreg, hs[:1, :1]), ["hs"], ["hreg"])
    P.emit()


def _chunked_allgather(P, cc, src, dst, chunks, groups):
    for ci, (r0, n) in enumerate(chunks):
        P.op("pool", lambda e, r0=r0, n=n: e.collective_compute("AllGather", ALU.bypass, replica_groups=groups,
                                                                ins=[src.ap()[r0:r0 + n, :]], outs=[dst.ap()[2 * r0:2 * r0 + 2 * n, :]]),
             [], [("gat", ci)], dma=True, cc=cc)


def phase_xchg_G(nc, sh, regs, hid, G_in, G_out, FTm, VTm, TOKH, groups):
    P = Prog(nc, sh)
    hreg, tmpregs = regs
    ctr = [0]
    chunks = [(256 * c, 256) for c in range(12)] + [(3072, 160), (3232, 256), (3488, 256)]
    _chunked_allgather(P, 0, G_in, G_out, chunks, groups)
    HT2 = TOKH // 2
    for t in range(2):
        tc = slice(t * TOKH, (t + 1) * TOKH)
        dstA = FTm.ap()[0:1536, tc].rearrange("(s r) c -> s r c", s=6)
        _dyn_dma(P, hreg, tmpregs, ctr, dstA, G_out, (256 * t) * TOKH, 512 * TOKH, [[1024 * TOKH, 6], [TOKH, 256], [1, TOKH]],
                 ["gat"], [("FTm", 3 * t)], "rlA%d" % t)
        _dyn_dma(P, hreg, tmpregs, ctr, FTm.ap()[1536:1792, tc], G_out, (5632 + 256 * t) * TOKH, -512 * TOKH, [[TOKH, 256], [1, TOKH]],
                 ["gat"], [("FTm", 3 * t + 1)], "rlB%d" % t)
        P.dma(FTm.ap()[1792:1952, tc], G_out.ap()[6144 + 160 * t:6144 + 160 * t + 160, :], ["gat"], [("FTm", 3 * t + 2)], eng="pool", sk="rlC%d" % t)
        for q_ in range(2):
            _dyn_dma(P, hreg, tmpregs, ctr, VTm.ap()[t * TOKH + q_ * HT2:t * TOKH + (q_ + 1) * HT2, :], G_out,
                     (6464 + 512 * q_ + 256 * t) * TOKH, 256, [[512, HT2], [1, 256]], ["gat"], [("VTm", 2 * t + q_)], "rlv%d" % (2 * t + q_))
    P.emit()


def phase_xchg_Y(nc, sh, regs, hid, Y_in, Y_out, YBh, L, TOKH, groups):
    P = Prog(nc, sh)
    hreg, tmpregs = regs
    ctr = [0]
    _chunked_allgather(P, 1, Y_in, Y_out, [(256 * c, 256) for c in range(6)], groups)
    for r in range(2):
        dst = YBh.ap().rearrange("(n q) c -> n q c", n=3)[:, r * 256:(r + 1) * 256, :]
        _dyn_dma(P, hreg, tmpregs, ctr, dst, Y_out, (256 * r) * TOKH, 1536 * TOKH, [[512 * TOKH, 3], [TOKH, 256], [1, TOKH]],
                 ["gat"], [("YBh", r)], "rly%d" % r)
    P.emit()


def build_fused8(L=4096, depth=2, ncores=8):
    nc = bass.Bass("TRN2", target_bir_lowering=False)
    sh = Shared(nc)
    TOKH = L // 2
    groups = [[2 * i, 2 * i + 1] for i in range(ncores // 2)]
    hreg = sh.stack.enter_context(nc.gpsimd.register("hreg"))
    tmpregs = [sh.stack.enter_context(nc.gpsimd.register("tmpr%d" % i)) for i in range(2)]
    regs = (hreg, tmpregs)
    E = lambda n, s: nc.dram_tensor(n, list(s), F32, kind="ExternalInput")
    x_in = E("x_in", [TOKH, D])
    hid = nc.dram_tensor("hid", [1, 1], I32, kind="ExternalInput")
    x_out = nc.dram_tensor("x_final", [TOKH, D], F32, kind="ExternalOutput")
    G_in = nc.dram_tensor("G_in", [3744, TOKH], F32)
    G_out = nc.dram_tensor("G_out", [2 * 3744, TOKH], F32)
    GL = nc.dram_tensor("GL", [3072, TOKH], F32)
    FTm = nc.dram_tensor("FTm", [1952, L], F32)
    VTm = nc.dram_tensor("VTm", [L, 256], F32)
    Y_in = nc.dram_tensor("Y_in", [2 * 768, TOKH], F32)
    Y_out = nc.dram_tensor("Y_out", [4 * 768, TOKH], F32)
    ysl = lambda r0, n: [Y_in.ap()[j * 768 + r0:j * 768 + r0 + n, :] for j in range(2)]
    YBh = nc.dram_tensor("YBh", [1536, TOKH], F32)
    X1 = nc.dram_tensor("X1", [TOKH, D], F32)
    XS = nc.dram_tensor("XS", [TOKH, D], F32)
    VF = nc.dram_tensor("VF", [256, L], F32)
    VF2 = nc.dram_tensor("VF2", [256, L], F32)
    cs = {}
    LHr = min(L, 1024)
    for n, shp in (("c_ident", [128, 128]), ("c_cmask", [128, 4, 512]), ("c_b64", [128, 128]), ("c_ones", [128, 128]),
                   ("c_mask4", [128, 512]), ("c_sl", [128, 128]), ("c_reset", [128, LHr]), ("c_bones", [128, 128]), ("c_hsel", [128, 2])):
        cs[n] = E(n, shp)
    R = lambda r0, n: V(FTm.ap()[r0:r0 + n, :])
    phase_load_h(nc, sh, regs, hid)
    for i in range(depth):
        sfx = "_%d" % i
        xin = x_in if i == 0 else XS
        xo = x_out if i == depth - 1 else XS
        io = dict(x_tok=xin, w=E("w_in" + sfx, [D, NIN]), g=E("g1" + sfx, [128, 8]), c_ident=cs["c_ident"], G=G_in, GL=GL)
        build_L1G(TOKH, ctx=(nc, sh, io))
        phase_xchg_G(nc, sh, regs, hid, G_in, G_out, FTm, VTm, TOKH, groups)
        io = dict(uT=R(0, 256), prm=E("s5prm" + sfx, [128, 3, 8]), dcol=E("s5d" + sfx, [128, 2]), yT=VTok2(ysl(0, 256), TOKH))
        for n in ("bpr", "bpi", "cpr", "cpi"):
            io[n] = E("s5" + n + sfx, [128, 8, 128])
        build_L2(L, ctx=(nc, sh, io))
        hv = lambda ap_: V(ap_.rearrange("(h p) l -> h p l", h=2))
        io = dict(qT=hv(FTm.ap()[256:512, :]), kT=hv(FTm.ap()[512:768, :]), vtok=V(VTm.ap().rearrange("l (h d) -> h l d", h=2)),
                  gains=E("da_gains" + sfx, [128, 4]), lamv=E("da_lamv" + sfx, [128, 4, 64]), laminit=E("da_laminit" + sfx, [128, 1]),
                  c_cmask=cs["c_cmask"], c_b64=cs["c_b64"], c_ones=cs["c_ones"],
                  oT=VTok2([a_.rearrange("(h p) l -> h p l", h=2) for a_ in ysl(256, 256)], TOKH))
        build_L3(L, ctx=(nc, sh, io))
        vg = i > 0
        io = dict(zr=R(768, 256), zk=R(1024, 256), zv=R(1280, 256), zw=R(1792, 32), za=R(1824, 32), zg=R(1856, 96),
                  mu3=E("rw_mu3" + sfx, [128, 6]), mulo=E("rw_mulo" + sfx, [96, 3]), w2=E("rw_w2p" + sfx, [32, 256]),
                  a2=E("rw_a2p" + sfx, [32, 256]), g2=E("rw_g2p" + sfx, [96, 256]), pc=E("rw_pc" + sfx, [128, 10]),
                  lng=E("rw_lng" + sfx, [128, 256]), lnb=E("rw_lnb" + sfx, [128, 256]),
                  y_out=VTok2(ysl(512, 256), TOKH), vfirst_out=(VF2 if vg else VF))
        for n in ("c_mask4", "c_sl", "c_reset", "c_bones", "c_hsel", "c_ident"):
            io[n] = cs[n]
        if vg:
            io.update(zvfull=R(1280, 512), muvf=E("rw_muvf" + sfx, [128, 4]), v1=E("rw_v1p" + sfx, [128, 4, 32]),
                      v2=E("rw_v2p" + sfx, [32, 256]), v0=E("rw_v0p" + sfx, [128, 2]), vfirst_in=VF)
        build_L4(L, has_vgate=vg, ctx=(nc, sh, io), fm_out=True)
        phase_xchg_Y(nc, sh, regs, hid, Y_in, Y_out, YBh, L, TOKH, groups)
        io = dict(w_glu=E("w_glu" + sfx, [512, 512]), w_br=E("w_br" + sfx, [3, 512, 1024]), w_out=E("w_out" + sfx, [1024, 1024]),
                  x_tok=xin, ypreT=V(YBh.ap()[0:512, :]), oT=V(YBh.ap()[512:1024, :]), ycT=V(YBh.ap()[1024:1536, :]), glT=GL, x_out=X1)
        build_L5a(TOKH, 512, ctx=(nc, sh, io))
        io = dict(g2=E("g2n" + sfx, [128, 8]), w_fi=E("w_fi" + sfx, [1024, 2 * DFF]), w_fo=E("w_fo" + sfx, [DFF, 1024]), c_ident=cs["c_ident"],
                  x_tok=X1, x_out=xo)
        if i == depth - 1:
            io["_final"] = True
        build_L5b(TOKH, min(1024, TOKH), ctx=(nc, sh, io))
    sh.close()
    return nc


def fused8_inputs(inp, L, h):
    f = lambda a: np.ascontiguousarray(a, dtype=np.float32)
    d = {}
    d.update(at_consts())
    d.update(rw_consts(L))
    d["hid"] = np.array([[h]], np.int32)
    depth = inp["w_in"].shape[0]
    zL = np.zeros((L, 512), np.float32)
    zz = np.zeros((L, 1696), np.float32)
    for i in range(depth):
        sfx = "_%d" % i
        lam_init = 0.8 - 0.6 * float(np.exp(-0.3 * i))
        d["w_in" + sfx] = f(inp["w_in"][i])
        d["g1" + sfx] = f(np.asarray(inp["norm1_g"][i]).reshape(8, 128).T)
        p2 = dict(lam_re=np.asarray(inp["s5_lambda_re"][i]), lam_im=np.asarray(inp["s5_lambda_im"][i]), log_dt=np.asarray(inp["s5_log_dt"][i]),
                  b_re=np.asarray(inp["s5_b_re"][i]), b_im=np.asarray(inp["s5_b_im"][i]), c_re=np.asarray(inp["s5_c_re"][i]),
                  c_im=np.asarray(inp["s5_c_im"][i]), d=np.asarray(inp["s5_d"][i]))
        l2 = l2_inputs(zL, p2, h, L)
        for n in ("bpr", "bpi", "cpr", "cpi"):
            d["s5" + n + sfx] = l2[n]
        d["s5prm" + sfx], d["s5d" + sfx] = l2["prm"], l2["dcol"]
        p3 = dict(q_gain=np.asarray(inp["da_q_gain"][i]), k_gain=np.asarray(inp["da_k_gain"][i]), lam=np.asarray(inp["da_lambda"][i]),
                  subln_g=np.asarray(inp["da_subln_g"][i]))
        l3 = l3_inputs(zL, zL, zL, p3, 0, L, lam_init)
        d["da_gains" + sfx], d["da_lamv" + sfx], d["da_laminit" + sfx] = l3["gains"], l3["lamv"], l3["laminit"]
        p4 = dict(mu=np.asarray(inp["rw_mu"][i]), w0=np.asarray(inp["rw_w0"][i]), w2=np.asarray(inp["rw_w2"][i]), a0=np.asarray(inp["rw_a0"][i]),
                  a2=np.asarray(inp["rw_a2"][i]), g2=np.asarray(inp["rw_g2"][i]), k_k=np.asarray(inp["rw_k_k"][i]), k_a=np.asarray(inp["rw_k_a"][i]),
                  r_k=np.asarray(inp["rw_r_k"][i]), ln_g=np.asarray(inp["rw_ln_g"][i]), ln_b=np.asarray(inp["rw_ln_b"][i]))
        vg = i > 0
        if vg:
            p4.update(v0=np.asarray(inp["rw_v0"][i - 1]), v1=np.asarray(inp["rw_v1"][i - 1]), v2=np.asarray(inp["rw_v2"][i - 1]))
        l4 = l4_inputs(zz, p4, h, L, vfirst=(np.zeros((256, L), np.float32) if vg else None))
        for n, m in (("mu3", "rw_mu3"), ("mulo", "rw_mulo"), ("w2", "rw_w2p"), ("a2", "rw_a2p"), ("g2", "rw_g2p"), ("pc", "rw_pc"),
                     ("lng", "rw_lng"), ("lnb", "rw_lnb")):
            d[m + sfx] = l4[n]
        if vg:
            perm = np.concatenate([np.arange(256 * h, 256 * h + 256), np.arange(256 * (1 - h), 256 * (1 - h) + 256)])
            muv = np.asarray(inp["rw_mu"][i])[1024:1536][perm]
            d["rw_muvf" + sfx] = f(muv.reshape(4, 128).T)
            d["rw_v1p" + sfx] = f(np.asarray(inp["rw_v1"][i - 1])[perm].reshape(4, 128, 32).transpose(1, 0, 2))
            d["rw_v2p" + sfx] = l4["v2"]
            d["rw_v0p" + sfx] = l4["v0"]
        d["w_glu" + sfx] = f(inp["s5_w_glu"][i])
        d["w_br" + sfx] = f(inp["w_branch"][i])
        d["w_out" + sfx] = f(inp["w_out"][i])
        d["g2n" + sfx] = f(np.asarray(inp["norm2_g"][i]).reshape(8, 128).T)
        d["w_fi" + sfx] = f(inp["w_ffn_in"][i])
        d["w_fo" + sfx] = f(inp["w_ffn_out"][i])
    return d


def kernel_fused8(inp, runner=None):
    x = np.asarray(inp["x"], dtype=np.float32)
    B, L, _ = x.shape
    depth = inp["w_in"].shape[0]
    ncores = 2 * B
    nc = build_fused8(L, depth, ncores)
    TOKH = L // 2
    common = [fused8_inputs(inp, L, h) for h in range(2)]
    maps = [dict(common[c % 2], x_in=np.ascontiguousarray(x[c // 2, (c % 2) * TOKH:(c % 2 + 1) * TOKH])) for c in range(ncores)]
    if runner is None:
        res = run_bass_kernel_spmd(nc, maps, core_ids=list(range(ncores))).results
    else:
        res = runner(nc, maps)
    out = np.stack([np.concatenate([res[2 * b]["x_final"], res[2 * b + 1]["x_final"]], 0) for b in range(B)])
    return out.astype(np.float32)


def kernel(**inputs):
    return kernel_fused8(inputs)
```

```python
from contextlib import ExitStack
import numpy as np
import concourse.bass as bass
import concourse.mybir as mybir

F32 = mybir.dt.float32
BF16 = mybir.dt.bfloat16
ALU = mybir.AluOpType
AF = mybir.ActivationFunctionType

ENGS = ["pe", "act", "dve", "pool", "sp"]
SAME_ENG_SYNC = True


class Shared:
    NDQ = 92

    def __init__(self, nc):
        self.nc = nc
        self.stack = ExitStack()
        names = ["e_" + e for e in ENGS] + ["dq%d" % i for i in range(self.NDQ)] + ["cc0", "cc1"]
        self.sems = {n: self.stack.enter_context(nc.semaphore(n)) for n in names}
        self.cnt = {n: 0 for n in names}

    def close(self):
        self.stack.close()


class V:
    def __init__(self, ap):
        self._ap = ap

    def ap(self):
        return self._ap

    def __getitem__(self, idx):
        return self._ap[idx]


class VTok2:
    def __init__(self, aps, tokh):
        self.aps, self.tokh = aps, tokh

    def __getitem__(self, idx):
        idx = list(idx)
        sl = idx[-1]
        j = sl.start // self.tokh
        assert (sl.stop - 1) // self.tokh == j
        idx[-1] = slice(sl.start - j * self.tokh, sl.stop - j * self.tokh)
        return self.aps[j][tuple(idx)]


class Prog:
    _pid = [0]

    def __init__(self, nc, shared=None):
        self.shared = shared
        Prog._pid[0] += 1
        self.pfx = "p%d_" % Prog._pid[0]
        self.nc = nc
        self.ins = []
        self.byeng = {e: [] for e in ENGS}
        self.lastw = {}
        self.rds = {}
        self.stack = ExitStack()
        self.n_ps = 0

    def sb(self, name, shape, dt=F32):
        return self.stack.enter_context(self.nc.sbuf_tensor(self.pfx + name, list(shape), dt))

    def ps(self, name, shape, dt=F32):
        return self.stack.enter_context(self.nc.psum_tensor(self.pfx + name, list(shape), dt))

    def dram(self, name, shape, dt=F32, kind="Internal"):
        return self.nc.dram_tensor(name, list(shape), dt, kind=kind)

    @staticmethod
    def _ov(a, b):
        return a is None or b is None or a == b

    def _norm(self, keys):
        out = []
        for k in keys:
            if isinstance(k, tuple):
                out.append((k[0], None if k[0].startswith("ps") else k[1]))
            else:
                out.append((k, None))
        return out

    def op(self, eng, fn, reads=(), writes=(), dma=False, sk=None, cc=None):
        reads = self._norm(reads)
        writes = self._norm(writes)
        i = len(self.ins)
        deps = set()
        for (t, s) in reads:
            for s2, w in self.lastw.get(t, {}).items():
                if self._ov(s, s2):
                    deps.add(w)
        for (t, s) in writes:
            for s2, w in self.lastw.get(t, {}).items():
                if self._ov(s, s2):
                    deps.add(w)
            for s2, rl in self.rds.get(t, {}).items():
                if self._ov(s, s2):
                    deps.update(rl)
        deps.discard(i)
        for (t, s) in reads:
            self.rds.setdefault(t, {}).setdefault(s, []).append(i)
        for (t, s) in writes:
            d = self.lastw.setdefault(t, {})
            r = self.rds.setdefault(t, {})
            if s is None:
                d.clear()
                r.clear()
            else:
                r[s] = []
            d[s] = i
        if eng == "pe" or not SAME_ENG_SYNC:
            deps = {d for d in deps if self.ins[d]["eng"] != eng or self.ins[d]["dma"] or dma}
        self.ins.append(dict(eng=eng, fn=fn, deps=deps, dma=dma, cc=cc, key=(writes[0] if writes else None),
                             sk=(sk if sk is not None else (writes[0] if writes else None))))
        self.byeng[eng].append(i)
        return i

    def mm(self, out, lhsT, rhs, reads, writes, start=True, stop=True):
        return self.op("pe", lambda e: e.matmul(out, lhsT, rhs, start=start, stop=stop), reads, writes)

    def tr(self, out, in_, ident, reads, writes):
        return self.op("pe", lambda e: e.transpose(out, in_, ident), reads, writes)

    def act(self, out, in_, func, reads, writes, eng="act", **kw):
        return self.op(eng, lambda e: e.activation(out, in_, func, **kw), reads, writes)

    def tt(self, eng, out, in0, in1, op, reads, writes):
        return self.op(eng, lambda e: e.tensor_tensor(out, in0, in1, op), reads, writes)

    def ts(self, eng, out, in0, s1, s2, op0, op1, reads, writes, **kw):
        if op1 is None:
            return self.op(eng, lambda e: e.tensor_scalar(out, in0, s1, None, op0, **kw), reads, writes)
        return self.op(eng, lambda e: e.tensor_scalar(out, in0, s1, s2, op0, op1, **kw), reads, writes)

    def stt(self, out, in0, scalar, in1, op0, op1, reads, writes):
        return self.op("dve", lambda e: e.scalar_tensor_tensor(out, in0, scalar, in1, op0, op1), reads, writes)

    def cp(self, eng, out, in_, reads, writes):
        if eng == "act":
            return self.op(eng, lambda e: e.copy(out, in_), reads, writes)
        return self.op(eng, lambda e: e.tensor_copy(out, in_), reads, writes)

    def dma(self, out, in_, reads, writes, eng="sp", sk=None, **kw):
        return self.op(eng, lambda e: e.dma_start(out, in_, **kw), reads, writes, dma=True, sk=sk)

    def memset(self, eng, ap, val, writes):
        return self.op(eng, lambda e: e.memset(ap, val), (), writes)

    def emit(self, final_wait_keys=()):
        nc = self.nc
        ins = self.ins
        sh = self.shared
        own = sh is None
        if own:
            sh = Shared(nc)
        has_dep = [False] * len(ins)
        for i, d in enumerate(ins):
            for j in d["deps"]:
                has_dep[j] = True
        final_ids = [i for i, d in enumerate(ins) if d["dma"] and d["key"] is not None and d["key"][0] in final_wait_keys]
        for i, d in enumerate(ins):
            if d["dma"]:
                has_dep[i] = True
        for e in ENGS:
            for i in reversed(self.byeng[e]):
                if not ins[i]["dma"]:
                    has_dep[i] = True
                    break
        start = dict(sh.cnt)
        dma_sem_of = {}
        dma_eng = {}
        nsp, npool = [0], [0]
        sig = [None] * len(ins)
        for e in ENGS:
            for i in self.byeng[e]:
                d = ins[i]
                if not has_dep[i]:
                    continue
                if d["cc"] is not None:
                    sn = "cc%d" % d["cc"]
                    sh.cnt[sn] += 1
                    sig[i] = (sn, sh.cnt[sn])
                elif d["dma"]:
                    k = d["sk"]
                    if k not in dma_sem_of:
                        if e == "pool":
                            npool[0] += 1
                            assert npool[0] <= 34, "too many pool dma sem keys"
                            dma_sem_of[k] = "dq%d" % (Shared.NDQ - npool[0])
                        else:
                            nsp[0] += 1
                            assert nsp[0] <= Shared.NDQ - 34, "too many sp dma sem keys"
                            dma_sem_of[k] = "dq%d" % (nsp[0] - 1)
                        dma_eng[k] = e
                    assert dma_eng[k] == e, ("dma sem key used from two queues", k)
                    sn = dma_sem_of[k]
                    sh.cnt[sn] += 16
                    sig[i] = (sn, sh.cnt[sn])
                else:
                    sh.cnt["e_" + e] += 1
                    sig[i] = ("e_" + e, sh.cnt["e_" + e])
        sems = sh.sems
        block = self.stack.enter_context(nc.Block())
        engobj = dict(pe="tensor", act="scalar", dve="vector", pool="gpsimd", sp="sync")
        last_phase_final = [(sig[i][0], sig[i][1]) for i in final_ids]

        def make(e):
            def body(eng):
                waited = {}
                for sn, v in start.items():
                    if v > 0 and sn != "e_" + e:
                        eng.wait_ge(sems[sn], v)
                    waited[sn] = v
                for i in self.byeng[e]:
                    d = ins[i]
                    need = {}
                    for j in d["deps"]:
                        s_ = sig[j]
                        assert s_ is not None
                        if need.get(s_[0], 0) < s_[1]:
                            need[s_[0]] = s_[1]
                    for sn, v in need.items():
                        if waited.get(sn, 0) < v:
                            eng.wait_ge(sems[sn], v)
                            waited[sn] = v
                    inst = d["fn"](eng)
                    if sig[i] is not None:
                        inst.then_inc(sems[sig[i][0]], 16 if (d["dma"] and d["cc"] is None) else 1)
                if e == "sp":
                    for (sn, v) in last_phase_final:
                        if waited.get(sn, 0) < v:
                            eng.wait_ge(sems[sn], v)
                            waited[sn] = v
            return body

        for e in ENGS:
            getattr(block, engobj[e])(make(e))
        self.stack.close()
        if own:
            sh.close()
        return nc


from concourse.bass_utils import run_bass_kernel_spmd

NCORES = 8
D = 1024
NIN = 6816
EPS = 1e-6


def _ctx(ctx):
    if ctx is None:
        nc = bass.Bass("TRN2", target_bir_lowering=False)
        return nc, Prog(nc), None
    nc, shared, io = ctx
    return nc, Prog(nc, shared), io


def _mkI(nc, io):
    if io is None:
        return lambda n, s: nc.dram_tensor(n, list(s), F32, kind="ExternalInput")
    return lambda n, s: io[n]


def _mkO(nc, io, n, s):
    if io is None:
        return nc.dram_tensor(n, list(s), F32, kind="ExternalOutput")
    return io[n]


def _fin(P, nc, io, outs):
    if io is None:
        P.emit(final_wait_keys=outs)
    else:
        P.emit(final_wait_keys=(outs if io.get("_final") else ()))
    return nc


def ap3(t, off, dims):
    return bass.AP(t, off, [list(d) for d in dims])


def build_L1(TOK=2048):
    nc = bass.Bass("TRN2", target_bir_lowering=False)
    P = Prog(nc)
    NT = TOK // 128
    x_tok = nc.dram_tensor("x_tok", [TOK, D], F32, kind="ExternalInput")
    xT = nc.dram_tensor("xT", [D, TOK], F32, kind="ExternalInput")
    w = nc.dram_tensor("w", [D, NIN], F32, kind="ExternalInput")
    g = nc.dram_tensor("g", [128, 8], F32, kind="ExternalInput")
    proj = nc.dram_tensor("proj", [TOK, NIN], F32, kind="ExternalOutput")

    g_sb = P.sb("g_sb", [128, 8])
    rstd = P.sb("rstd", [128, NT])
    ss = P.sb("ss", [128, NT])
    xt_in = [P.sb("xt_in%d" % i, [128, D]) for i in range(2)]
    junk = P.sb("junk", [128, D])
    xTs = P.sb("xTs", [128, 8, TOK])
    xg = P.sb("xg", [128, 8, TOK], BF16)
    wf = [P.sb("wf%d" % i, [128, 8, 512]) for i in range(2)]
    wb = [P.sb("wb%d" % i, [128, 8, 512], BF16) for i in range(2)]
    ost = [P.sb("ost%d" % i, [128, 512]) for i in range(3)]
    pss = [P.ps("ps%d" % i, [128, 512]) for i in range(4)]

    P.dma(g_sb[:, :], g[:, :], [], ["g_sb"])
    for t in range(NT):
        b = xt_in[t % 2]
        bn = "xt_in%d" % (t % 2)
        P.dma(b[:, :], x_tok[t * 128:(t + 1) * 128, :], [], [bn], eng="sp")
        P.act(junk[:, :], b[:, :], AF.Square, [bn], ["junk", ("ss", t)], accum_out=ss[:, t:t + 1])
    P.ts("dve", rstd[:, :], ss[:, :], 1.0 / D, EPS, ALU.mult, ALU.add, ["ss"], ["rstd"])
    P.act(rstd[:, :], rstd[:, :], AF.Sqrt, ["rstd"], ["rstd"])
    P.op("dve", lambda e: e.reciprocal(rstd[:, :], rstd[:, :]), ["rstd"], ["rstd"])
    xTv = xT.ap().rearrange("(c p) t -> p c t", p=128)
    for c in range(8):
        P.dma(xTs[:, c, :], xTv[:, c, :], [], [("xTs", c)], eng="pool")
        P.ts("dve", xg[:, c, :], xTs[:, c, :], g_sb[:, c:c + 1], None, ALU.mult, None,
             [("xTs", c), "g_sb"], [("xg", c)])
    wv = w.ap().rearrange("(c p) n -> p c n", p=128)
    nblk = (NIN + 511) // 512
    k = 0
    for cb in range(nblk):
        c0 = cb * 512
        cw = min(512, NIN - c0)
        wfb, wbb = wf[cb % 2], wb[cb % 2]
        wfn, wbn = "wf%d" % (cb % 2), "wb%d" % (cb % 2)
        P.dma(wfb[:, :, :cw], wv[:, :, c0:c0 + cw], [], [wfn], eng="sp")
        for c in range(8):
            P.cp("act" if c % 2 else "dve", wbb[:, c, :cw], wfb[:, c, :cw], [wfn], [(wbn, c)])
        for t in range(NT):
            ps = pss[k % 4]
            psn = "ps%d" % (k % 4)
            for c in range(8):
                P.mm(ps[:, :cw], xg[:, c, t * 128:(t + 1) * 128], wbb[:, c, :cw],
                     [("xg", c), (wbn, c)], [psn], start=(c == 0), stop=(c == 7))
            o = ost[k % 3]
            on = "ost%d" % (k % 3)
            P.act(o[:, :cw], ps[:, :cw], AF.Copy, [psn, "rstd"], [on], scale=rstd[:, t:t + 1])
            P.dma(proj[t * 128:(t + 1) * 128, c0:c0 + cw], o[:, :cw], [on], [("proj", k)], eng="sp", sk=on)
            k += 1
    P.emit(final_wait_keys=("proj",))
    return nc


def build_L1F(L=4096, ctx=None):
    nc, P, io = _ctx(ctx)
    I = _mkI(nc, io)
    x_tok = I("x_tok", [L, D])
    w = I("w", [D, NIN])
    g = I("g", [128, 8])
    c_ident = I("c_ident", [128, 128])
    FT = _mkO(nc, io, "FT", [6304, L])
    VT = _mkO(nc, io, "VT", [L, 512])
    HT = min(L, 1024)
    NH = L // HT
    NTT = HT // 128
    NB = L // 512
    g_sb = P.sb("g_sb", [128, 8])
    P.dma(g_sb[:, :], g[:, :], [], ["g_sb"])
    identf = P.sb("identf", [128, 128])
    P.dma(identf[:, :], c_ident[:, :], [], ["identf"])
    identb = P.sb("identb", [128, 128], BF16)
    P.cp("dve", identb[:, :], identf[:, :], ["identf"], ["identb"])
    hT = P.sb("hT", [128, 8, L], BF16)
    x1 = [P.sb("x1_%d" % i, [128, D]) for i in range(3)]
    junk = P.sb("junk", [128, D], BF16)
    ss = P.sb("ss", [128, NTT])
    hb = [P.sb("hb%d" % i, [128, D], BF16) for i in range(2)]
    wf = [P.sb("wf%d" % i, [128, 8, 512]) for i in range(2)]
    wb = [P.sb("wb%d" % i, [128, 8, 512], BF16) for i in range(2)]
    ost = [P.sb("ost%d" % i, [128, 512]) for i in range(4)]
    pss = [P.ps("ps%d" % i, [128, 512]) for i in range(6)]
    psb = P.ps("psb", [128, 1024], BF16)
    for half in range(NH):
        t0 = half * HT
        xs_ = []
        for tt_ in range(NTT):
            xi, xn = x1[tt_ % 3], "x1_%d" % (tt_ % 3)
            P.dma(xi[:, :], x_tok[t0 + tt_ * 128:t0 + (tt_ + 1) * 128, :], [], [xn], eng="sp")
            P.act(junk[:, :], xi[:, :], AF.Square, [xn], ["junk", ("ss", tt_)], accum_out=ss[:, tt_:tt_ + 1])
        P.ts("dve", ss[:, :], ss[:, :], 1.0 / D, EPS, ALU.mult, ALU.add, ["ss"], ["ss"])
        P.act(ss[:, :], ss[:, :], AF.Sqrt, ["ss"], ["ss"])
        P.op("dve", lambda e: e.reciprocal(ss[:, :], ss[:, :]), ["ss"], ["ss"])
        for tt_ in range(NTT):
            xi, xn = x1[tt_ % 3], "x1_%d" % (tt_ % 3)
            P.dma(xi[:, :], x_tok[t0 + tt_ * 128:t0 + (tt_ + 1) * 128, :], [], [xn], eng="sp")
            h_, hn = hb[tt_ % 2], "hb%d" % (tt_ % 2)
            P.act(h_[:, :], xi[:, :], AF.Copy, [xn, "ss"], [hn], scale=ss[:, tt_:tt_ + 1])
            for c in range(8):
                P.tr(psb[:, c * 128:(c + 1) * 128], h_[:, c * 128:(c + 1) * 128], identb[:, :], [hn, "identb"], ["psb"])
            for c in range(8):
                P.ts("dve", hT[:, c, t0 + tt_ * 128:t0 + (tt_ + 1) * 128], psb[:, c * 128:(c + 1) * 128], g_sb[:, c:c + 1], None, ALU.mult, None,
                     ["psb", "g_sb"], [("hT", c)])
    wv = w.ap().rearrange("(c p) n -> p c n", p=128)
    k = 0
    nblk = (NIN + 511) // 512
    for cb in range(nblk):
        c0 = cb * 512
        cw = min(512, NIN - c0)
        wfb, wbb = wf[cb % 2], wb[cb % 2]
        wfn, wbn = "wf%d" % (cb % 2), "wb%d" % (cb % 2)
        P.dma(wfb[:, :, :cw], wv[:, :, c0:c0 + cw], [], [wfn], eng=("sp" if cb % 2 else "pool"))
        for c in range(8):
            P.cp("act" if c % 2 else "dve", wbb[:, c, :cw], wfb[:, c, :cw], [wfn], [(wbn, c)])
        if c0 == 1536:
            for t in range(L // 128):
                ps, psn = pss[k % 6], "ps%d" % (k % 6)
                for c in range(8):
                    P.mm(ps[:, :], hT[:, c, t * 128:(t + 1) * 128], wbb[:, c, :], [("hT", c), (wbn, c)], [psn], start=(c == 0), stop=(c == 7))
                o, on = ost[k % 4], "ost%d" % (k % 4)
                P.cp("act" if k % 2 else "dve", o[:, :], ps[:, :], [psn], [on])
                P.dma(VT[t * 128:(t + 1) * 128, :], o[:, :], [on], [("VT", t)], sk=on, eng=("sp" if k % 2 else "pool"))
                k += 1
            continue
        r0 = c0 if c0 < 1536 else c0 - 512
        for ct in range((cw + 127) // 128):
            mw = min(128, cw - ct * 128)
            for b5 in range(NB):
                bs = slice(b5 * 512, b5 * 512 + 512)
                ps, psn = pss[k % 6], "ps%d" % (k % 6)
                for c in range(8):
                    P.mm(ps[:mw, :], wbb[:, c, ct * 128:ct * 128 + mw], hT[:, c, bs], [("hT", c), (wbn, c)], [psn], start=(c == 0), stop=(c == 7))
                o, on = ost[k % 4], "ost%d" % (k % 4)
                P.cp("act" if k % 2 else "dve", o[:mw, :], ps[:mw, :], [psn], [on])
                P.dma(FT[r0 + ct * 128:r0 + ct * 128 + mw, bs], o[:mw, :], [on], [("FT", k)], sk=on, eng=("sp" if k % 2 else "pool"))
                k += 1
    return _fin(P, nc, io, ("FT", "VT"))


def build_L1G(L=2048, ctx=None):
    nc, P, io = _ctx(ctx)
    I = _mkI(nc, io)
    x_tok = I("x_tok", [L, D])
    w = I("w", [D, NIN])
    g = I("g", [128, 8])
    c_ident = I("c_ident", [128, 128])
    G = _mkO(nc, io, "G", [3744, L])
    GL = _mkO(nc, io, "GL", [3072, L])
    VT = V(G.ap()[3232:3744, :].rearrange("a (b c) -> (a b) c", c=512))
    HT = min(L, 1024)
    NH = L // HT
    NTT = HT // 128
    NB = L // 512
    g_sb = P.sb("g_sb", [128, 8])
    P.dma(g_sb[:, :], g[:, :], [], ["g_sb"])
    identf = P.sb("identf", [128, 128])
    P.dma(identf[:, :], c_ident[:, :], [], ["identf"])
    identb = P.sb("identb", [128, 128], BF16)
    P.cp("dve", identb[:, :], identf[:, :], ["identf"], ["identb"])
    hT = P.sb("hT", [128, 8, L], BF16)
    x1 = [P.sb("x1_%d" % i, [128, D]) for i in range(3)]
    junk = P.sb("junk", [128, D], BF16)
    ss = P.sb("ss", [128, NTT])
    hb = [P.sb("hb%d" % i, [128, D], BF16) for i in range(2)]
    wf = [P.sb("wf%d" % i, [128, 8, 512]) for i in range(2)]
    wb = [P.sb("wb%d" % i, [128, 8, 512], BF16) for i in range(2)]
    ost = [P.sb("ost%d" % i, [128, 512]) for i in range(4)]
    pss = [P.ps("ps%d" % i, [128, 512]) for i in range(6)]
    psb = P.ps("psb", [128, 1024], BF16)
    for half in range(NH):
        t0 = half * HT
        xs_ = []
        for tt_ in range(NTT):
            xi, xn = x1[tt_ % 3], "x1_%d" % (tt_ % 3)
            P.dma(xi[:, :], x_tok[t0 + tt_ * 128:t0 + (tt_ + 1) * 128, :], [], [xn], eng="sp")
            P.act(junk[:, :], xi[:, :], AF.Square, [xn], ["junk", ("ss", tt_)], accum_out=ss[:, tt_:tt_ + 1])
        P.ts("dve", ss[:, :], ss[:, :], 1.0 / D, EPS, ALU.mult, ALU.add, ["ss"], ["ss"])
        P.act(ss[:, :], ss[:, :], AF.Sqrt, ["ss"], ["ss"])
        P.op("dve", lambda e: e.reciprocal(ss[:, :], ss[:, :]), ["ss"], ["ss"])
        for tt_ in range(NTT):
            xi, xn = x1[tt_ % 3], "x1_%d" % (tt_ % 3)
            P.dma(xi[:, :], x_tok[t0 + tt_ * 128:t0 + (tt_ + 1) * 128, :], [], [xn], eng="sp")
            h_, hn = hb[tt_ % 2], "hb%d" % (tt_ % 2)
            P.act(h_[:, :], xi[:, :], AF.Copy, [xn, "ss"], [hn], scale=ss[:, tt_:tt_ + 1])
            for c in range(8):
                P.tr(psb[:, c * 128:(c + 1) * 128], h_[:, c * 128:(c + 1) * 128], identb[:, :], [hn, "identb"], ["psb"])
            for c in range(8):
                P.ts("dve", hT[:, c, t0 + tt_ * 128:t0 + (tt_ + 1) * 128], psb[:, c * 128:(c + 1) * 128], g_sb[:, c:c + 1], None, ALU.mult, None,
                     ["psb", "g_sb"], [("hT", c)])
    wv = w.ap().rearrange("(c p) n -> p c n", p=128)
    k = 0
    nblk = (NIN + 511) // 512
    for cb in range(nblk):
        c0 = cb * 512
        cw = min(512, NIN - c0)
        wfb, wbb = wf[cb % 2], wb[cb % 2]
        wfn, wbn = "wf%d" % (cb % 2), "wb%d" % (cb % 2)
        P.dma(wfb[:, :, :cw], wv[:, :, c0:c0 + cw], [], [wfn], eng=("sp" if cb % 2 else "pool"))
        for c in range(8):
            P.cp("act" if c % 2 else "dve", wbb[:, c, :cw], wfb[:, c, :cw], [wfn], [(wbn, c)])
        if c0 == 1536:
            for t in range(L // 128):
                ps, psn = pss[k % 6], "ps%d" % (k % 6)
                for c in range(8):
                    P.mm(ps[:, :], hT[:, c, t * 128:(t + 1) * 128], wbb[:, c, :], [("hT", c), (wbn, c)], [psn], start=(c == 0), stop=(c == 7))
                o, on = ost[k % 4], "ost%d" % (k % 4)
                P.cp("act" if k % 2 else "dve", o[:, :], ps[:, :], [psn], [on])
                P.dma(VT[t * 128:(t + 1) * 128, :], o[:, :], [on], [("VT", t)], sk=on, eng=("sp" if k % 2 else "pool"))
                k += 1
            continue
        r0 = c0 if c0 < 1536 else c0 - 512
        for ct in range((cw + 127) // 128):
            mw = min(128, cw - ct * 128)
            for b5 in range(NB):
                bs = slice(b5 * 512, b5 * 512 + 512)
                ps, psn = pss[k % 6], "ps%d" % (k % 6)
                for c in range(8):
                    P.mm(ps[:mw, :], wbb[:, c, ct * 128:ct * 128 + mw], hT[:, c, bs], [("hT", c), (wbn, c)], [psn], start=(c == 0), stop=(c == 7))
                o, on = ost[k % 4], "ost%d" % (k % 4)
                P.cp("act" if k % 2 else "dve", o[:mw, :], ps[:mw, :], [psn], [on])
                ra, rb = r0 + ct * 128, r0 + ct * 128 + mw
                if ra < 3232:
                    n_ = min(rb, 3232) - ra
                    P.dma(G[ra:ra + n_, bs], o[0:n_, :], [on], [("G", k)], sk=on, eng=("sp" if k % 2 else "pool"))
                if rb > 3232:
                    p0 = max(ra, 3232) - ra
                    P.dma(GL[max(ra, 3232) - 3232:rb - 3232, bs], o[p0:mw, :], [on], [("GL", k)], sk=on, eng=("sp" if k % 2 else "pool"))
                k += 1
    return _fin(P, nc, io, ("G", "GL", "VT"))


def rw_consts(L):
    T = 64
    i = np.arange(128)
    same = (i[:, None] // T) == (i[None, :] // T)
    su = (same & (i[:, None] < i[None, :])).astype(np.float32)
    iu = (same & (i[:, None] <= i[None, :])).astype(np.float32)
    sl = (same & (i[:, None] > i[None, :])).astype(np.float32)
    mask4 = np.concatenate([su, iu, su, iu], 1)
    reset = np.ones((128, min(L, 1024)), np.float32)
    reset[:, ::T] = 0.0
    bones = ((i[:, None] // 64) == (i[None, :] // 64)).astype(np.float32)
    hsel = np.zeros((128, 2), np.float32)
    hsel[:64, 0] = 1
    hsel[64:, 1] = 1
    return dict(c_mask4=mask4, c_sl=sl, c_reset=reset, c_bones=bones, c_hsel=hsel,
                c_ident=np.eye(128, dtype=np.float32))


def build_L4(L=4096, has_vgate=False, LH=None, ctx=None, fm_out=False):
    nc, P, io = _ctx(ctx)
    NB = L // 128
    NCH = L // 64
    NCB = (L + 511) // 512
    I = _mkI(nc, io)
    zr, zk, zv = I("zr", [256, L]), I("zk", [256, L]), I("zv", [256, L])
    zw, za, zg = I("zw", [32, L]), I("za", [32, L]), I("zg", [96, L])
    mu3 = I("mu3", [128, 6])
    mulo = I("mulo", [96, 3])
    w2, a2, g2 = I("w2", [32, 256]), I("a2", [32, 256]), I("g2", [96, 256])
    pc = I("pc", [128, 10])
    lng, lnb = I("lng", [128, 256]), I("lnb", [128, 256])
    c_mask4, c_sl, c_reset = I("c_mask4", [128, 512]), I("c_sl", [128, 128]), I("c_reset", [128, min(L, 1024)])
    c_bones, c_hsel, c_ident = I("c_bones", [128, 128]), I("c_hsel", [128, 2]), I("c_ident", [128, 128])
    if has_vgate:
        zvfull = I("zvfull", [512, L])
        muvf = I("muvf", [128, 4])
        v1 = I("v1", [128, 4, 32])
        v2 = I("v2", [32, 256])
        v0 = I("v0", [128, 2])
        vfirst_in = I("vfirst_in", [256, L])
    y_out = _mkO(nc, io, "y_out", ([256, L] if fm_out else [L, 256]))
    vfirst_out = _mkO(nc, io, "vfirst_out", [256, L])

    def ld(name, src, shape, dt=F32, eng="sp"):
        t = P.sb(name, shape, dt)
        if len(shape) == 2:
            P.dma(t[:, :], src[:, :], [], [name], eng=eng)
        else:
            P.dma(t[:, :, :], src[:, :, :], [], [name], eng=eng)
        return t
    mu3s, mulos = ld("mu3s", mu3, [128, 6]), ld("mulos", mulo, [96, 3])
    pcs = ld("pcs", pc, [128, 10])
    lngs, lnbs = ld("lngs", lng, [128, 256]), ld("lnbs", lnb, [128, 256])
    mask4, msl, reset = ld("mask4", c_mask4, [128, 512]), ld("msl", c_sl, [128, 128]), ld("reset", c_reset, [128, min(L, 1024)])
    bones, hself, identf = ld("bones", c_bones, [128, 128]), ld("hself", c_hsel, [128, 2]), ld("identf", c_ident, [128, 128])
    w2f, a2f, g2f = ld("w2f", w2, [32, 256]), ld("a2f", a2, [32, 256]), ld("g2f", g2, [96, 256])
    identb = P.sb("identb", [128, 128], BF16)
    hselb = P.sb("hselb", [128, 2], BF16)
    w2b, a2b, g2b = P.sb("w2b", [32, 256], BF16), P.sb("a2b", [32, 256], BF16), P.sb("g2b", [96, 256], BF16)
    P.cp("dve", identb[:, :], identf[:, :], ["identf"], ["identb"])
    P.cp("dve", hselb[:, :], hself[:, :], ["hself"], ["hselb"])
    P.cp("dve", w2b[:, :], w2f[:, :], ["w2f"], ["w2b"])
    P.cp("dve", a2b[:, :], a2f[:, :], ["a2f"], ["a2b"])
    P.cp("dve", g2b[:, :], g2f[:, :], ["g2f"], ["g2b"])

    LH = LH or min(L, 1024)
    NQH = L // LH
    NCBH = (LH + 511) // 512
    S = [P.sb("S%d" % i, [128, LH + 1]) for i in range(5)]
    SN = ["S%d" % i for i in range(5)]
    AT, BT, KT, RT, VB, RK = [P.sb(n, [128, 2, L], BF16) for n in ("AT", "BT", "KT", "RT", "VB", "RK")]
    gam = P.sb("gam", [128, 2, NCH])
    lo_w, lo_a, lo_g = P.sb("lo_w", [32, LH], BF16), P.sb("lo_a", [32, LH], BF16), P.sb("lo_g", [96, L], BF16)
    resetb = P.sb("resetb", [128, LH], BF16)
    P.cp("dve", resetb[:, :], reset[:, 0:LH], ["reset"], ["resetb"])
    pss = [P.ps("ps%d" % i, [128, 512]) for i in range(7)]
    psb = P.ps("psb", [128, 1024], BF16)
    pk = [0]

    def nextps():
        k = pk[0] % 4
        pk[0] += 1
        return pss[k], "ps%d" % k

    def shift_mix(dst, dn, src_rows, rows, mucol, scratch, scn, raw, rawn, t0, eng="sp"):
        if t0 == 0:
            P.memset("pool", raw[:rows, 0:1], 0.0, [(rawn, 0)])
            P.dma(raw[:rows, 1:LH + 1], src_rows[:, 0:LH], [], [(rawn, 1)], eng=eng)
        else:
            P.dma(raw[:rows, 0:LH + 1], src_rows[:, t0 - 1:t0 + LH], [], [rawn], eng=eng)
        P.tt("dve", scratch[:rows, 0:LH], raw[:rows, 0:LH], raw[:rows, 1:LH + 1], ALU.subtract, [rawn], [scn])
        P.stt(dst[:rows, 0:LH], scratch[:rows, 0:LH], mucol, raw[:rows, 1:LH + 1], ALU.mult, ALU.add, [scn, rawn, "mu3s", "mulos", "muvfs"], [dn])

    if has_vgate:
        v1f = ld("v1f", v1, [128, 4, 32])
        v1b = P.sb("v1b", [128, 4, 32], BF16)
        P.cp("dve", v1b[:, :, :], v1f[:, :, :], ["v1f"], ["v1b"])
        v2f = ld("v2f", v2, [32, 256])
        v2b = P.sb("v2b", [32, 256], BF16)
        P.cp("dve", v2b[:, :], v2f[:, :], ["v2f"], ["v2b"])
        v0s = ld("v0s", v0, [128, 2])
        muvfs = ld("muvfs", muvf, [128, 4])
        vfb = P.sb("vfb", [128, 4, LH], BF16)
        t1b = P.sb("t1b", [32, LH], BF16)

    W = slice(0, LH)
    for hq in range(NQH):
        t0 = hq * LH
        TS = slice(t0, t0 + LH)
        shift_mix(S[2], SN[2], zw, 32, mulos[:32, 0:1], S[1], SN[1], S[0], SN[0], t0)
        P.act(lo_w[:, :], S[2][:32, W], AF.Tanh, [SN[2]], ["lo_w"])
        shift_mix(S[2], SN[2], za, 32, mulos[:32, 1:2], S[1], SN[1], S[0], SN[0], t0)
        P.cp("dve", lo_a[:, :], S[2][:32, W], [SN[2]], ["lo_a"])
        shift_mix(S[2], SN[2], zg, 96, mulos[:96, 2:3], S[1], SN[1], S[0], SN[0], t0)
        P.act(lo_g[:, TS], S[2][:96, W], AF.Sigmoid, [SN[2]], [("lo_g", hq)])
        if has_vgate:
            for c in range(4):
                shift_mix(S[2], SN[2], zvfull[c * 128:(c + 1) * 128, :], 128, muvfs[:, c:c + 1], S[1], SN[1], S[0], SN[0], t0)
                P.cp("pool", vfb[:, c, :], S[2][:, W], [SN[2]], [("vfb", c)])
            for cb in range(NCBH):
                c0 = cb * 512
                cw = min(512, LH - c0)
                ps, psn = nextps()
                for c in range(4):
                    P.mm(ps[:32, :cw], v1b[:, c, :], vfb[:, c, c0:c0 + cw], ["v1b", ("vfb", c)], [psn], start=(c == 0), stop=(c == 3))
                P.cp("act", t1b[:, c0:c0 + cw], ps[:32, :cw], [psn], [("t1b", cb)])
        for p in range(2):
            rows = slice(p * 128, (p + 1) * 128)
            col = lambda j: pcs[:, 2 * j + p:2 * j + p + 1]
            shift_mix(S[2], SN[2], zv[rows, :], 128, mu3s[:, 4 + p:5 + p], S[1], SN[1], S[0], SN[0], t0)
            if has_vgate:
                for cb in range(NCBH):
                    c0 = cb * 512
                    cw = min(512, LH - c0)
                    ps, psn = nextps()
                    P.mm(ps[:, :cw], v2b[:, p * 128:(p + 1) * 128], t1b[:, c0:c0 + cw], ["v2b", ("t1b", cb)], [psn])
                    P.act(S[3][:, c0:c0 + cw], ps[:, :cw], AF.Sigmoid, [psn, "v0s"], [(SN[3], cb)], bias=v0s[:, p:p + 1])
                P.dma(S[0][:, W], vfirst_in[rows, TS], [], [SN[0]])
                P.tt("dve", S[1][:, W], S[0][:, W], S[2][:, W], ALU.subtract, [SN[0], SN[2]], [SN[1]])
                P.tt("dve", S[1][:, W], S[1][:, W], S[3][:, W], ALU.mult, [SN[1], SN[3]], [SN[1]])
                P.tt("dve", S[2][:, W], S[2][:, W], S[1][:, W], ALU.add, [SN[2], SN[1]], [SN[2]])
                P.dma(vfirst_out[rows, TS], S[0][:, W], [SN[0]], [("vfirst_out", hq * 2 + p)], sk="vfo")
            else:
                P.dma(vfirst_out[rows, TS], S[2][:, W], [SN[2]], [("vfirst_out", hq * 2 + p)], sk="vfo")
            P.cp("pool", VB[:, p, TS], S[2][:, W], [SN[2]], [("VB", p)])
            for cb in range(NCBH):
                c0 = cb * 512
                cw = min(512, LH - c0)
                ps, psn = nextps()
                P.mm(ps[:, :cw], a2b[:, p * 128:(p + 1) * 128], lo_a[:, c0:c0 + cw], ["a2b", "lo_a"], [psn])
                P.act(S[3][:, c0:c0 + cw], ps[:, :cw], AF.Sigmoid, [psn, "pcs"], [(SN[3], cb)], bias=col(1))
                ps, psn = nextps()
                P.mm(ps[:, :cw], w2b[:, p * 128:(p + 1) * 128], lo_w[:, c0:c0 + cw], ["w2b", "lo_w"], [psn])
                P.act(S[4][:, c0:c0 + cw], ps[:, :cw], AF.Sigmoid, [psn, "pcs"], [(SN[4], cb)], bias=col(0))
            P.ts("dve", S[4][:, W], S[4][:, W], -float(np.exp(-0.5)), None, ALU.mult, None, [SN[4]], [SN[4]])
            shift_mix(S[2], SN[2], zk[rows, :], 128, mu3s[:, 2 + p:3 + p], S[1], SN[1], S[0], SN[0], t0)
            P.ts("dve", S[0][:, W], S[2][:, W], col(2), None, ALU.mult, None, [SN[2], "pcs"], [SN[0]])
            P.tt("pool", S[1][:, W], S[0][:, W], S[0][:, W], ALU.mult, [SN[0]], [SN[1]])
            for cb in range(NCBH):
                c0 = cb * 512
                cw = min(512, LH - c0)
                ps, psn = nextps()
                P.mm(ps[:, :cw], bones[:, :], S[1][:, c0:c0 + cw], ["bones", SN[1]], [psn])
                P.ts("dve", S[1][:, c0:c0 + cw], ps[:, :cw], 1e-24, None, ALU.max, None, [psn], [(SN[1], cb)])
            P.act(S[1][:, W], S[1][:, W], AF.Sqrt, [SN[1]], [SN[1]])
            P.op("dve", lambda e: e.reciprocal(S[1][:, W], S[1][:, W]), [SN[1]], [SN[1]])
            P.tt("dve", S[0][:, W], S[0][:, W], S[1][:, W], ALU.mult, [SN[0], SN[1]], [SN[0]])
            P.ts("dve", S[1][:, W], S[3][:, W], -1.0, col(3), ALU.add, ALU.mult, [SN[3], "pcs"], [SN[1]])
            P.stt(S[2][:, W], S[1][:, W], 1.0, S[2][:, W], ALU.add, ALU.mult, [SN[1], SN[2]], [SN[2]])
            P.op("dve", lambda e: e.tensor_tensor_scan(S[1][:, W], resetb[:, :], S[4][:, W], 0.0, ALU.mult, ALU.add),
                 ["resetb", SN[4]], [SN[1]])
            P.tt("pool", S[4][:, W], S[1][:, W], S[4][:, W], ALU.subtract, [SN[1], SN[4]], [SN[4]])
            P.act(S[4][:, W], S[4][:, W], AF.Exp, [SN[4]], [SN[4]])
            P.stt(AT[:, p, TS], S[0][:, W], -1.0, S[4][:, W], ALU.mult, ALU.mult, [SN[0], SN[4]], [("AT", p)])
            P.act(S[4][:, W], S[1][:, W], AF.Exp, [SN[1]], [SN[4]], scale=-1.0)
            P.tt("dve", S[0][:, W], S[0][:, W], S[3][:, W], ALU.mult, [SN[0], SN[3]], [SN[0]])
            P.tt("dve", BT[:, p, TS], S[0][:, W], S[4][:, W], ALU.mult, [SN[0], SN[4]], [("BT", p)])
            P.tt("pool", KT[:, p, TS], S[2][:, W], S[4][:, W], ALU.mult, [SN[2], SN[4]], [("KT", p)])
            P.act(S[4][:, W], S[1][:, W], AF.Exp, [SN[1]], [SN[4]])
            gv = ap3(S[4], 63, [[LH + 1, 128], [64, LH // 64]])
            P.cp("dve", gam[:, p, t0 // 64:(t0 + LH) // 64], gv, [SN[4]], [("gam", p)])
            shift_mix(S[0], SN[0], zr[rows, :], 128, mu3s[:, p:p + 1], S[1], SN[1], S[3], SN[3], t0)
            P.tt("dve", RT[:, p, TS], S[0][:, W], S[4][:, W], ALU.mult, [SN[0], SN[4]], [("RT", p)])
            P.stt(RK[:, p, TS], S[0][:, W], col(4), S[2][:, W], ALU.mult, ALU.mult, [SN[0], SN[2], "pcs"], [("RK", p)])

    A4 = [P.sb("A4_%d" % i, [128, 512], BF16) for i in range(4)]
    N0s = [P.sb("N0s_%d" % i, [128, 128], BF16) for i in range(4)]
    NL = [[P.sb("NL_%d_%d" % (i, j), [128, 256], BF16) for j in range(2)] for i in range(4)]
    TOK = [P.sb("TOK%d" % i, [128, 4, 128], BF16) for i in range(2)]
    Zb = [[P.sb("Zb_%d_%d" % (i, j), [128, 128], BF16) for j in range(2)] for i in range(4)]
    W1P = [P.sb("W1P%d" % i, [128, 128], BF16) for i in range(4)]
    XTP = [[P.sb("XTP_%d_%d" % (i, c), [128, 128]) for c in range(2)] for i in range(4)]
    Qg = [[P.sb("Qg_%d_%d" % (i, c), [128, 64]) for c in range(2)] for i in range(4)]
    RM = [[P.sb("RM_%d_%d" % (i, c), [128, 128], BF16) for c in range(2)] for i in range(4)]
    Hf = [[P.sb("Hf_%d_%d" % (p, h), [128, 64]) for h in range(2)] for p in range(2)]
    Hb = [[[P.sb("Hb_%d_%d_%d" % (p, h, c), [128, 64], BF16) for c in range(2)] for h in range(2)] for p in range(2)]
    Yall = [P.sb("Yall%d" % i, [128, 256]) for i in range(2)]
    rkc = [P.sb("rkc%d" % i, [128, 4]) for i in range(2)]
    gtk = [P.sb("gtk%d" % i, [128, 256]) for i in range(2)]
    st = P.sb("st", [128, 16])
    sq = P.sb("sq", [128, 256])
    yo = [P.sb("yo%d" % i, [128, 256]) for i in range(2)]
    yfm = [P.sb("yfm%d" % i, [128, 2, 128]) for i in range(2)]
    for i in range(4):
        P.memset("pool", W1P[i][:, :], 0.0, ["W1P%d" % i])
        for c in range(2):
            P.memset("pool", XTP[i][c][:, :], 0.0, ["XTP_%d_%d" % (i, c)])
            P.memset("pool", RM[i][c][:, :], 0.0, ["RM_%d_%d" % (i, c)])
    for p in range(2):
        for h in range(2):
            P.memset("pool", Hf[p][h][:, :], 0.0, ["Hf_%d_%d" % (p, h)])
            P.memset("pool", Hb[p][h][0][:, :], 0.0, ["Hb_%d_%d_0" % (p, h)])
    pk6 = [0]

    def nps():
        k = pk6[0] % 6
        pk6[0] += 1
        return pss[k], "ps%d" % k

    def unit(p, h, blk, Y, Yn):
        us = 2 * p + h
        cols = slice(blk * 128, (blk + 1) * 128)
        tk, tkn = TOK[p], "TOK%d" % p
        o = 64 * h
        hs = slice(o, o + 64)
        bt, kt_, at, rt = BT[hs, p, cols], KT[hs, p, cols], AT[hs, p, cols], RT[hs, p, cols]
        a4, a4n = A4[us], "A4_%d" % us
        n0, n0n = N0s[us], "N0s_%d" % us
        p1, p1n = nps()
        P.mm(p1[:, 0:128], bt, at, [("BT", p), ("AT", p)], [p1n])
        P.mm(p1[:, 128:256], bt, rt, [("BT", p), ("RT", p)], [p1n])
        P.mm(p1[:, 256:384], kt_, at, [("KT", p), ("AT", p)], [p1n])
        P.mm(p1[:, 384:512], kt_, rt, [("KT", p), ("RT", p)], [p1n])
        P.tt("dve", a4[:, :], p1[:, :], mask4[:, :], ALU.mult, [p1n, "mask4"], [a4n])
        p2, p2n = nps()
        P.mm(p2[:, 0:128], at, bt, [("BT", p), ("AT", p)], [p2n])
        P.tt("dve", n0[:, :], p2[:, 0:128], msl[:, :], ALU.mult, [p2n, "msl"], [n0n])
        yield
        L0, ArbT, AakT, ArkT = a4[:, 0:128], a4[:, 128:256], a4[:, 256:384], a4[:, 384:512]
        zc, zcn = Zb[us][0], "Zb_%d_0" % us
        p3, p3n = nps()
        P.mm(p3[:, 0:64], AakT, tk[:, 2, hs], [a4n, tkn], [p3n])
        P.cp("dve", zc[:, 0:64], tk[:, 3, hs], [tkn], [(zcn, 0)])
        P.cp("act", zc[:, 64:128], p3[:, 0:64], [p3n], [(zcn, 1)])
        yield
        Nj, Lj, njn, ljn = n0[:, :], L0, n0n, a4n
        zi = 0
        for j in range(6):
            pz, pzn = nps()
            zc, zcn = Zb[us][zi], "Zb_%d_%d" % (us, zi)
            zn_, znn = Zb[us][1 - zi], "Zb_%d_%d" % (us, 1 - zi)
            P.mm(pz[:, 0:128], Lj, zc[:, :], [ljn, zcn], [pzn])
            if j < 5:
                pq, pqn = nps()
                nl, nln = NL[us][j % 2], "NL_%d_%d" % (us, j % 2)
                if j < 4:
                    P.mm(pq[:, 0:128], Lj, Nj, [ljn, njn], [pqn])
                P.mm(pq[:, 128:256], Nj, Lj, [ljn, njn], [pqn])
            P.tt("dve", zn_[:, :], pz[:, 0:128], zc[:, :], ALU.add, [pzn, zcn], [znn])
            if j == 5:
                P.cp("pool", W1P[us][:, hs], zn_[:, 0:64], [znn], ["W1P%d" % us])
            else:
                if j < 4:
                    P.cp("act", nl[:, :], pq[:, 0:256], [pqn], [nln])
                else:
                    P.cp("act", nl[:, 128:256], pq[:, 128:256], [pqn], [nln])
                Nj, Lj, njn, ljn = nl[:, 0:128], nl[:, 128:256], nln, nln
            zi = 1 - zi
            yield
        Zf, Zfn = Zb[us][zi], "Zb_%d_%d" % (us, zi)
        U0 = Zf[:, 64:128]
        w1p, w1n = W1P[us], "W1P%d" % us
        pr, prn = nps()
        P.mm(pr[:, 0:128], w1p[:, :], ArbT, [w1n, a4n], [prn])
        for c in range(2):
            cs = slice(64 * c, 64 * c + 64)
            P.tt("dve", RM[us][c][hs, cs], pr[hs, cs], RT[hs, p, blk * 128 + 64 * c:blk * 128 + 64 * c + 64], ALU.add,
                 [prn, ("RT", p)], ["RM_%d_%d" % (us, c)])
        yield
        hf, hfn = Hf[p][h], "Hf_%d_%d" % (p, h)
        for c in range(2):
            ch = blk * 2 + c
            rs = slice(64 * c, 64 * c + 64)
            px, pxn = nps()
            P.mm(px[:, 0:64], w1p[rs, :], tk[rs, 0, hs], [w1n, tkn], [pxn])
            P.mm(px[:, 64:128], tk[rs, 0, :], U0[rs, :], [tkn, Zfn], [pxn], start=True, stop=False)
            P.mm(px[:, 64:128], tk[rs, 1, :], tk[rs, 2, hs], [tkn], [pxn], start=False, stop=True)
            xt, xtn = XTP[us][c], "XTP_%d_%d" % (us, c)
            P.tt("dve", xt[hs, hs], px[hs, 0:64], identf[hs, hs], ALU.add, [pxn, "identf"], [xtn])
            qg, qgn = Qg[us][c], "Qg_%d_%d" % (us, c)
            P.ts("dve", qg[hs, :], px[hs, 64:128], gam[hs, p, ch:ch + 1], None, ALU.mult, None, [pxn, ("gam", p)], [qgn])
            yield
        ph, phn = nps()
        P.mm(ph[:, 0:64], XTP[us][0][hs, :], hf[hs, :], ["XTP_%d_0" % us, hfn], [phn])
        P.stt(hf[hs, :], ph[hs, 0:64], gam[hs, p, 2 * blk:2 * blk + 1], Qg[us][0][hs, :], ALU.mult, ALU.add,
              [phn, ("gam", p), "Qg_%d_0" % us], [hfn])
        P.cp("pool", Hb[p][h][1][hs, :], hf[hs, :], [hfn], ["Hb_%d_%d_1" % (p, h)])
        yield
        ph, phn = nps()
        P.mm(ph[:, 0:64], XTP[us][1][hs, :], hf[hs, :], ["XTP_%d_1" % us, hfn], [phn])
        P.stt(hf[hs, :], ph[hs, 0:64], gam[hs, p, 2 * blk + 1:2 * blk + 2], Qg[us][1][hs, :], ALU.mult, ALU.add,
              [phn, ("gam", p), "Qg_%d_1" % us], [hfn])
        py, pyn = nps()
        P.mm(py[:, 0:64], ArbT, U0, [a4n, Zfn], [pyn], start=True, stop=False)
        P.mm(py[:, 0:64], ArkT, tk[:, 2, hs], [a4n, tkn], [pyn], start=False, stop=False)
        P.mm(py[:, 0:64], RM[us][0][hs, :], Hb[p][h][0][hs, :], ["RM_%d_0" % us, "Hb_%d_%d_0" % (p, h)], [pyn], start=False, stop=False)
        P.mm(py[:, 0:64], RM[us][1][hs, :], Hb[p][h][1][hs, :], ["RM_%d_1" % us, "Hb_%d_%d_1" % (p, h)], [pyn], start=False, stop=True)
        P.cp("act", Y[:, us * 64:(us + 1) * 64], py[:, 0:64], [pyn], [(Yn, us)])
        P.cp("pool", Hb[p][h][0][hs, :], hf[hs, :], [hfn], ["Hb_%d_%d_0" % (p, h)])
        yield

    for blk in range(NB):
        cols = slice(blk * 128, (blk + 1) * 128)
        Y, Yn = Yall[blk % 2], "Yall%d" % (blk % 2)
        pg, pgn = pss[6], "ps6"
        P.mm(pg[:, 0:256], lo_g[:, cols], g2b[:, :], ["lo_g", "g2b"], [pgn])
        for p in range(2):
            P.mm(pg[:, 256 + 2 * p:258 + 2 * p], RK[:, p, cols], hselb[:, :], [("RK", p), "hselb"], [pgn])
        gt, gtn = gtk[blk % 2], "gtk%d" % (blk % 2)
        rk_, rkn = rkc[blk % 2], "rkc%d" % (blk % 2)
        P.cp("act", gt[:, :], pg[:, 0:256], [pgn], [gtn])
        P.cp("act", rk_[:, :], pg[:, 256:260], [pgn], [rkn])
        for p in range(2):
            tk, tkn = TOK[p], "TOK%d" % p
            for j, (X, xn) in enumerate(((BT, "BT"), (KT, "KT"), (VB, "VB"), (AT, "AT"))):
                P.tr(psb[:, j * 128:(j + 1) * 128], X[:, p, cols], identb[:, :], [(xn, p), "identb"], ["psb"])
            P.cp("act", tk[:, :, :], psb[:, 0:512].rearrange("p (j c) -> p j c", j=4), ["psb"], [tkn])
        gens = [unit(p, h, blk, Y, Yn) for p in range(2) for h in range(2)]
        live = list(gens)
        while live:
            nxt = []
            for g_ in live:
                try:
                    next(g_)
                    nxt.append(g_)
                except StopIteration:
                    pass
            live = nxt
        Y3 = Y[:, :].rearrange("p (h v) -> p h v", h=4)
        P.op("dve", lambda e, Y3=Y3: e.tensor_reduce(st[:, 0:4], Y3, mybir.AxisListType.X, ALU.add), [Yn], [("st", 0)])
        P.tt("pool", sq[:, :], Y[:, :], Y[:, :], ALU.mult, [Yn], ["sq"])
        sq3 = sq[:, :].rearrange("p (h v) -> p h v", h=4)
        P.op("dve", lambda e, sq3=sq3: e.tensor_reduce(st[:, 4:8], sq3, mybir.AxisListType.X, ALU.add), ["sq"], [("st", 1)])
        P.ts("dve", st[:, 0:4], st[:, 0:4], 1.0 / 64, None, ALU.mult, None, [("st", 0)], [("st", 0)])
        P.tt("dve", st[:, 8:12], st[:, 0:4], st[:, 0:4], ALU.mult, [("st", 0)], [("st", 2)])
        P.stt(st[:, 4:8], st[:, 4:8], 1.0 / 64, st[:, 8:12], ALU.mult, ALU.subtract, [("st", 1), ("st", 2)], [("st", 1)])
        P.ts("dve", st[:, 4:8], st[:, 4:8], 64e-5, None, ALU.add, None, [("st", 1)], [("st", 1)])
        P.act(st[:, 4:8], st[:, 4:8], AF.Sqrt, [("st", 1)], [("st", 1)])
        P.op("dve", lambda e: e.reciprocal(st[:, 4:8], st[:, 4:8]), [("st", 1)], [("st", 1)])
        o_, on = yo[blk % 2], "yo%d" % (blk % 2)
        for hh in range(4):
            cs = slice(hh * 64, hh * 64 + 64)
            P.ts("dve", o_[:, cs], Y[:, cs], st[:, hh:hh + 1], st[:, 4 + hh:5 + hh], ALU.subtract, ALU.mult, [Yn, "st"], [(on, hh)])
        P.tt("dve", o_[:, :], o_[:, :], lngs[:, :], ALU.mult, [on, "lngs"], [on])
        P.tt("pool", o_[:, :], o_[:, :], lnbs[:, :], ALU.add, [on, "lnbs"], [on])
        for hh in range(4):
            cs = slice(hh * 64, hh * 64 + 64)
            pp, h2 = hh // 2, hh % 2
            P.stt(o_[:, cs], TOK[pp][:, 2, 64 * h2:64 * h2 + 64], rk_[:, hh:hh + 1], o_[:, cs], ALU.mult, ALU.add,
                  ["TOK%d" % pp, rkn, on], [on])
        P.tt("dve", o_[:, :], o_[:, :], gt[:, :], ALU.mult, [on, gtn], [on])
        if fm_out:
            pt_, ptn = nextps()
            for c in range(2):
                P.tr(pt_[:, c * 128:(c + 1) * 128], o_[:, c * 128:(c + 1) * 128], identf[:, :], [on, "identf"], [ptn])
            yf, yfn = yfm[blk % 2], "yfm%d" % (blk % 2)
            P.cp("act", yf[:, :, :], pt_[:, 0:256].rearrange("p (c t) -> p c t", c=2), [ptn], [yfn])
            P.dma(y_out[0:256, blk * 128:(blk + 1) * 128].rearrange("(c p) t -> p c t", p=128), yf[:, :, :], [yfn], [("y_out", blk)], sk=yfn)
        else:
            P.dma(y_out[blk * 128:(blk + 1) * 128, :], o_[:, :], [on], [("y_out", blk)], sk=on)
    return _fin(P, nc, io, ("y_out", "vfirst_out"))


def l4_inputs(z, prm, hg, L, vfirst=None):
    c0 = 256 * hg
    f = lambda a: np.ascontiguousarray(a, dtype=np.float32)
    mu = prm["mu"]
    d = dict(zr=f(z[:, c0:c0 + 256].T), zk=f(z[:, 512 + c0:512 + c0 + 256].T), zv=f(z[:, 1024 + c0:1024 + c0 + 256].T),
             zw=f(z[:, 1536:1568].T), za=f(z[:, 1568:1600].T), zg=f(z[:, 1600:1696].T))
    mu3 = np.zeros((128, 6), np.float32)
    for j in range(3):
        for p in range(2):
            mu3[:, 2 * j + p] = mu[512 * j + c0 + 128 * p:512 * j + c0 + 128 * p + 128]
    d["mu3"] = mu3
    mulo = np.zeros((96, 3), np.float32)
    mulo[:32, 0] = mu[1536:1568]
    mulo[:32, 1] = mu[1568:1600]
    mulo[:, 2] = mu[1600:1696]
    d["mulo"] = mulo
    d["w2"] = f(prm["w2"][:, c0:c0 + 256]); d["a2"] = f(prm["a2"][:, c0:c0 + 256]); d["g2"] = f(prm["g2"][:, c0:c0 + 256])
    pc = np.zeros((128, 10), np.float32)
    for j, nm in enumerate(("w0", "a0", "k_k", "k_a", "r_k")):
        v = prm[nm].reshape(-1)
        for p in range(2):
            pc[:, 2 * j + p] = v[c0 + 128 * p:c0 + 128 * p + 128]
    d["pc"] = pc
    d["lng"] = f(np.broadcast_to(prm["ln_g"][c0:c0 + 256][None, :], (128, 256)))
    d["lnb"] = f(np.broadcast_to(prm["ln_b"][c0:c0 + 256][None, :], (128, 256)))
    d.update(rw_consts(L))
    if vfirst is not None:
        d["zvfull"] = f(z[:, 1024:1536].T)
        d["muvf"] = f(mu[1024:1536].reshape(4, 128).T)
        d["v1"] = f(prm["v1"].reshape(4, 128, 32).transpose(1, 0, 2))
        d["v2"] = f(prm["v2"][:, c0:c0 + 256])
        v0 = np.zeros((128, 2), np.float32)
        for p in range(2):
            v0[:, p] = prm["v0"][c0 + 128 * p:c0 + 128 * p + 128]
        d["v0"] = v0
        d["vfirst_in"] = f(vfirst)
    return d


def at_consts():
    tk = np.arange(128)[:, None]
    tq = np.arange(512)[None, :]
    m = np.stack([((128 * j + tk) <= tq).astype(np.float32) for j in range(4)], 1)
    i = np.arange(128)
    b64 = ((i[:, None] // 64) == (i[None, :] // 64)).astype(np.float32)
    return dict(c_cmask=m, c_b64=b64, c_ones=np.ones((128, 128), np.float32))


def build_L3(L=4096, ctx=None):
    nc, P, io = _ctx(ctx)
    NT = L // 128
    NQ = L // 512
    I = _mkI(nc, io)
    qT, kT = I("qT", [2, 128, L]), I("kT", [2, 128, L])
    vtok = I("vtok", [2, L, 128])
    gains = I("gains", [128, 4])
    lamv = I("lamv", [128, 4, 64])
    laminit = I("laminit", [128, 1])
    c_cmask, c_b64, c_ones = I("c_cmask", [128, 4, 512]), I("c_b64", [128, 128]), I("c_ones", [128, 128])
    oT = _mkO(nc, io, "oT", [2, 128, L])

    def ld(name, src, shape):
        t = P.sb(name, shape)
        if len(shape) == 2:
            P.dma(t[:, :], src[:, :], [], [name])
        else:
            P.dma(t[:, :, :], src[:, :, :], [], [name])
        return t
    gs = ld("gs", gains, [128, 4])
    lv = ld("lv", lamv, [128, 4, 64])
    li = ld("li", laminit, [128, 1])
    cmf = ld("cmf", c_cmask, [128, 4, 512])
    b64 = ld("b64", c_b64, [128, 128])
    onesf = ld("onesf", c_ones, [128, 128])
    cmb = P.sb("cmb", [128, 4, 512], BF16)
    onesb = P.sb("onesb", [128, 128], BF16)
    P.cp("dve", cmb[:, :, :], cmf[:, :, :], ["cmf"], ["cmb"])
    P.cp("dve", onesb[:, :], onesf[:, :], ["onesf"], ["onesb"])
    lt = P.sb("lt", [128, 2, 64])
    ls = P.sb("ls", [128, 4])
    P.tt("dve", lt[:, 0, :], lv[:, 0, :], lv[:, 1, :], ALU.mult, ["lv"], [("lt", 0)])
    P.tt("dve", lt[:, 1, :], lv[:, 2, :], lv[:, 3, :], ALU.mult, ["lv"], [("lt", 1)])
    P.op("dve", lambda e: e.tensor_reduce(ls[:, 0:2], lt[:, :, :], mybir.AxisListType.X, ALU.add), ["lt"], ["ls"])
    P.act(ls[:, 0:2], ls[:, 0:2], AF.Exp, ["ls"], ["ls"])
    P.tt("dve", ls[:, 2:3], ls[:, 1:2], ls[:, 0:1], ALU.subtract, ["ls"], ["ls"])
    P.tt("dve", ls[:, 3:4], ls[:, 2:3], li[:, :], ALU.subtract, ["ls", "li"], ["ls"])
    gq = P.sb("gq", [128, 1])
    P.ts("dve", gq[:, :], gs[:, 0:1], 0.125, None, ALU.mult, None, ["gs"], ["gq"])
    gsub = P.sb("gsub", [128, 1])
    P.tt("dve", gsub[:, :], gs[:, 2:3], gs[:, 3:4], ALU.mult, ["gs"], ["gsub"])

    raw = P.sb("raw", [128, L])
    sqt = P.sb("sqt", [128, L])
    qn = P.sb("qn", [128, L], BF16)
    kn = P.sb("kn", [128, L], BF16)
    vf = P.sb("vf", [128, NT, 128])
    vb = P.sb("vb", [128, NT, 128], BF16)
    pt = [P.sb("pt%d" % i, [128, 512], BF16) for i in range(4)]
    ob = P.sb("ob", [128, 512])
    o1 = P.sb("o1", [128, 512])
    rz = P.sb("rz", [128, 512])
    osq = P.sb("osq", [128, 512])
    oo = [P.sb("oo%d" % i, [128, 512]) for i in range(2)]
    psS = [P.ps("psS%d" % i, [128, 512]) for i in range(2)]
    psO = [P.ps("psO%d" % i, [128, 512]) for i in range(2)]
    psZ = [P.ps("psZ%d" % i, [128, 512]) for i in range(2)]
    psM = P.ps("psM", [128, 512])

    def qknorm(src, dst, dn, gcol):
        P.dma(raw[:, :], src, [], ["raw"])
        P.tt("pool", sqt[:, :], raw[:, :], raw[:, :], ALU.mult, ["raw"], ["sqt"])
        for cb in range(NQ):
            cs = slice(cb * 512, cb * 512 + 512)
            P.mm(psM[:, :], b64[:, :], sqt[:, cs], ["b64", "sqt"], ["psM"])
            P.ts("dve", sqt[:, cs], psM[:, :], 1.0 / 64, EPS, ALU.mult, ALU.add, ["psM"], [("sqt", cb)])
        P.act(sqt[:, :], sqt[:, :], AF.Sqrt, ["sqt"], ["sqt"])
        P.op("dve", lambda e: e.reciprocal(sqt[:, :], sqt[:, :]), ["sqt"], ["sqt"])
        P.stt(dst[:, :], raw[:, :], gcol, sqt[:, :], ALU.mult, ALU.mult, ["raw", "sqt", "gs", "gq"], [dn])

    qn2 = [qn, P.sb("qn_b", [128, L], BF16)]
    kn2 = [kn, P.sb("kn_b", [128, L], BF16)]
    vb2 = [vb, P.sb("vb_b", [128, NT, 128], BF16)]
    qnn, knn, vbn = ["qn", "qn_b"], ["kn", "kn_b"], ["vb", "vb_b"]
    sO = [P.sb("sO%d" % i, [128, 512]) for i in range(2)]
    sZ = [P.sb("sZ%d" % i, [128, 512]) for i in range(2)]
    for h in range(2):
        qknorm(qT[h, :, :], qn2[h], qnn[h], gq[:, 0:1])
        qknorm(kT[h, :, :], kn2[h], knn[h], gs[:, 1:2])
        P.dma(vf[:, :, :], vtok[h, :, :].rearrange("(n p) d -> p n d", p=128), [], ["vf"])
        P.cp("pool", vb2[h][:, :, :], vf[:, :, :], ["vf"], [vbn[h]])
    kcount = 0
    for h in range(2):
        qn_, kn_, vb_ = qn2[h], kn2[h], vb2[h]
        for qb in range(NQ):
            qs = slice(qb * 512, qb * 512 + 512)
            ntk = 4 * qb + 4
            its = [(j, c2) for j in range(ntk) for c2 in range(2)]
            bufs = []

            def emit_S(idx):
                nonlocal kcount
                j, c2 = its[idx]
                hs = slice(64 * c2, 64 * c2 + 64)
                s_, sn = psS[kcount % 2], "psS%d" % (kcount % 2)
                p_, pn = pt[kcount % 4], "pt%d" % (kcount % 4)
                kcount += 1
                P.mm(s_[:, :], kn_[hs, j * 128:j * 128 + 128], qn_[hs, qs], [knn[h], qnn[h]], [sn])
                bufs.append((s_, sn, p_, pn))
            emit_S(0)
            for idx, (j, c2) in enumerate(its):
                if idx + 1 < len(its):
                    emit_S(idx + 1)
                s_, sn, p_, pn = bufs[idx]
                P.act(p_[:, :], s_[:, :], AF.Exp, [sn], [pn])
                if j >= 4 * qb:
                    P.tt("dve", p_[:, :], p_[:, :], cmb[:, j - 4 * qb, :], ALU.mult, [pn, "cmb"], [pn])
                P.mm(psO[c2][:, :], vb_[:, j, :], p_[:, :], [vbn[h], pn], ["psO%d" % c2], start=(j == 0), stop=(j == ntk - 1))
                P.mm(psZ[c2][:, :], onesb[:, :], p_[:, :], ["onesb", pn], ["psZ%d" % c2], start=(j == 0), stop=(j == ntk - 1))
            for c2 in range(2):
                P.cp("act", sO[c2][:, :], psO[c2][:, :], ["psO%d" % c2], ["sO%d" % c2])
                P.cp("act", sZ[c2][:, :], psZ[c2][:, :], ["psZ%d" % c2], ["sZ%d" % c2])
            P.op("dve", lambda e: e.reciprocal(rz[:, :], sZ[0][:, :]), ["sZ0"], ["rz"])
            P.tt("dve", ob[:, :], sO[0][:, :], rz[:, :], ALU.mult, ["sO0", "rz"], ["ob"])
            P.op("dve", lambda e: e.reciprocal(rz[:, :], sZ[1][:, :]), ["sZ1"], ["rz"])
            P.tt("pool", o1[:, :], sO[1][:, :], rz[:, :], ALU.mult, ["sO1", "rz"], ["o1"])
            P.stt(ob[:, :], o1[:, :], ls[:, 3:4], ob[:, :], ALU.mult, ALU.add, ["o1", "ls", "ob"], ["ob"])
            P.tt("pool", osq[:, :], ob[:, :], ob[:, :], ALU.mult, ["ob"], ["osq"])
            P.mm(psM[:, :], onesf[:, :], osq[:, :], ["onesf", "osq"], ["psM"])
            P.ts("dve", osq[:, :], psM[:, :], 1.0 / 128, 1e-5, ALU.mult, ALU.add, ["psM"], ["osq"])
            P.act(osq[:, :], osq[:, :], AF.Sqrt, ["osq"], ["osq"])
            P.op("dve", lambda e: e.reciprocal(osq[:, :], osq[:, :]), ["osq"], ["osq"])
            o_, on = oo[qb % 2], "oo%d" % (qb % 2)
            P.stt(o_[:, :], ob[:, :], gsub[:, 0:1], osq[:, :], ALU.mult, ALU.mult, ["ob", "gsub", "osq"], [on])
            P.dma(oT[h, :, qs], o_[:, :], [on], [("oT", h * NQ + qb)], sk=on)
    return _fin(P, nc, io, ("oT",))


def l3_inputs(q, k, v, prm, hp, L, lam_init):
    f = lambda a: np.ascontiguousarray(a, dtype=np.float32)
    hs = [2 * hp, 2 * hp + 1]
    d = dict(qT=f(np.stack([q[:, h * 128:(h + 1) * 128].T for h in hs])),
             kT=f(np.stack([k[:, h * 128:(h + 1) * 128].T for h in hs])),
             vtok=f(np.stack([v[:, h * 128:(h + 1) * 128] for h in hs])))
    g = np.zeros((128, 4), np.float32)
    g[:, 0] = prm["q_gain"].reshape(-1)
    g[:, 1] = prm["k_gain"].reshape(-1)
    g[:, 2] = prm["subln_g"]
    g[:, 3] = 1.0 - lam_init
    d["gains"] = g
    d["lamv"] = f(np.broadcast_to(prm["lam"][None], (128, 4, 64)))
    d["laminit"] = np.full((128, 1), lam_init, np.float32)
    d.update(at_consts())
    return d


PI = float(np.pi)


def build_L2(L=4096, ctx=None):
    nc, P, io = _ctx(ctx)
    NQ = L // 512
    I = _mkI(nc, io)
    uT = I("uT", [256, L])
    bpr, bpi = I("bpr", [128, 8, 128]), I("bpi", [128, 8, 128])
    cpr, cpi = I("cpr", [128, 8, 128]), I("cpi", [128, 8, 128])
    prm = I("prm", [128, 3, 8])
    dcol = I("dcol", [128, 2])
    yT = _mkO(nc, io, "yT", [256, L])

    def ld(name, src, shape):
        t = P.sb(name, shape)
        if len(shape) == 2:
            P.dma(t[:, :], src[:, :], [], [name])
        else:
            P.dma(t[:, :, :], src[:, :, :], [], [name])
        return t
    prs = ld("prs", prm, [128, 3, 8])
    ds = ld("ds", dcol, [128, 2])
    wts = {}
    wf32 = {}
    for nm, src in (("bpr", bpr), ("bpi", bpi), ("cpr", cpr), ("cpi", cpi)):
        f = ld(nm + "f", src, [128, 8, 128])
        wf32[nm] = f
        b = P.sb(nm + "b", [128, 8, 128], BF16)
        if nm.startswith("b"):
            P.cp("dve", b[:, :, :], f[:, :, :], [nm + "f"], [nm + "b"])
        wts[nm] = b
    LHu = L // 2
    iaf = P.sb("iaf", [128, LHu])
    ub = P.sb("ub", [128, 2, L], BF16)
    for t in range(2):
        for hu in range(2):
            P.dma(iaf[:, :], uT[t * 128:(t + 1) * 128, hu * LHu:(hu + 1) * LHu], [], ["iaf"])
            P.cp("dve", ub[:, t, hu * LHu:(hu + 1) * LHu], iaf[:, :], ["iaf"], [("ub", t)])
    q = P.sb("q", [128, 16, 8])
    Q = lambda i: q[:, i, :]
    lr, li_, ldt = prs[:, 0, :], prs[:, 1, :], prs[:, 2, :]
    negpi = P.sb("negpi", [128, 1])
    P.memset("pool", negpi[:, :], -PI, ["negpi"])
    P.act(Q(0), ldt, AF.Exp, ["prs"], [("q", 0)])
    P.tt("dve", Q(1), lr, Q(0), ALU.mult, ["prs", ("q", 0)], [("q", 1)])
    P.tt("dve", Q(2), li_, Q(0), ALU.mult, ["prs", ("q", 0)], [("q", 2)])
    P.act(Q(3), Q(1), AF.Exp, [("q", 1)], [("q", 3)])
    NA = L // 64
    pw = P.sb("pw", [128, 24, 2, 8])
    wt = P.sb("wt", [128, 6, 8])
    W = lambda i: wt[:, i, :]

    def square_norm(cin, sin_, cout, sout, rd, wr):
        P.tt("dve", W(0), cin, cin, ALU.mult, rd, [("wt", 0)])
        P.tt("dve", W(1), sin_, sin_, ALU.mult, rd, [("wt", 1)])
        P.tt("dve", W(2), cin, sin_, ALU.mult, rd, [("wt", 2)])
        P.tt("dve", W(0), W(0), W(1), ALU.subtract, [("wt", 0), ("wt", 1)], [("wt", 0)])
        P.ts("dve", W(2), W(2), 2.0, None, ALU.mult, None, [("wt", 2)], [("wt", 2)])
        P.tt("dve", W(3), W(0), W(0), ALU.mult, [("wt", 0)], [("wt", 3)])
        P.tt("dve", W(4), W(2), W(2), ALU.mult, [("wt", 2)], [("wt", 4)])
        P.tt("dve", W(3), W(3), W(4), ALU.add, [("wt", 3), ("wt", 4)], [("wt", 3)])
        P.act(W(3), W(3), AF.Sqrt, [("wt", 3)], [("wt", 3)])
        P.op("dve", lambda e: e.reciprocal(W(3), W(3)), [("wt", 3)], [("wt", 3)])
        P.tt("dve", cout, W(0), W(3), ALU.mult, [("wt", 0), ("wt", 3)], wr)
        P.tt("dve", sout, W(2), W(3), ALU.mult, [("wt", 2), ("wt", 3)], wr)
    P.act(pw[:, 0, 1, :], Q(2), AF.Sin, [("q", 2)], [("pw", 0)], scale=1.0 / 16)
    P.act(Q(5), Q(2), AF.Sin, [("q", 2)], [("q", 5)], scale=1.0 / 32)
    P.tt("dve", Q(5), Q(5), Q(5), ALU.mult, [("q", 5)], [("q", 5)])
    P.ts("dve", pw[:, 0, 0, :], Q(5), -2.0, 1.0, ALU.mult, ALU.add, [("q", 5)], [("pw", 0)])
    NPW = 4 + 6 + max(1, int(np.log2(NA)))
    for k_ in range(1, NPW):
        square_norm(pw[:, k_ - 1, 0, :], pw[:, k_ - 1, 1, :], pw[:, k_, 0, :], pw[:, k_, 1, :], [("pw", k_ - 1)], [("pw", k_)])
    P.cp("dve", Q(5), pw[:, 4, 0, :], [("pw", 4)], [("q", 5)])
    P.cp("dve", Q(4), pw[:, 4, 1, :], [("pw", 4)], [("q", 4)])
    EBc, EBs = P.sb("EBc", [128, 8, 64]), P.sb("EBs", [128, 8, 64])
    EAc, EAs = P.sb("EAc", [128, 8, NA]), P.sb("EAs", [128, 8, NA])
    tbl = P.sb("tbl", [128, 2, 8, 64])

    def build_table(Ec, Es, ecn, esn, n_tot, k0, rowlen):
        P.memset("pool", Ec[:, :, 0:1], 1.0, [ecn])
        P.memset("pool", Es[:, :, 0:1], 0.0, [esn])
        n = 1
        k_ = k0
        while n < n_tot:
            wc = ap3(pw, (k_ * 2 + 0) * 8, [[24 * 2 * 8, 128], [1, 8], [0, n]])
            ws = ap3(pw, (k_ * 2 + 1) * 8, [[24 * 2 * 8, 128], [1, 8], [0, n]])
            t0_, t1_ = tbl[:, 0, :, 0:n], tbl[:, 1, :, 0:n]
            rd = [ecn, esn, ("pw", k_)]
            P.tt("dve", t0_, Es[:, :, 0:n], ws, ALU.mult, rd, [("tbl", 0)])
            P.tt("dve", t1_, Ec[:, :, 0:n], ws, ALU.mult, rd, [("tbl", 1)])
            P.tt("dve", Ec[:, :, n:2 * n], Ec[:, :, 0:n], wc, ALU.mult, rd, [ecn])
            P.tt("dve", Es[:, :, n:2 * n], Es[:, :, 0:n], wc, ALU.mult, rd, [esn])
            P.tt("dve", Ec[:, :, n:2 * n], Ec[:, :, n:2 * n], t0_, ALU.subtract, [ecn, ("tbl", 0)], [ecn])
            P.tt("dve", Es[:, :, n:2 * n], Es[:, :, n:2 * n], t1_, ALU.add, [esn, ("tbl", 1)], [esn])
            n *= 2
            k_ += 1
    build_table(EBc, EBs, "EBc", "EBs", 64, 4, 64)
    build_table(EAc, EAs, "EAc", "EAs", NA, 10, NA)
    P.tt("dve", Q(6), Q(3), Q(5), ALU.mult, [("q", 3), ("q", 5)], [("q", 6)])
    P.tt("dve", Q(7), Q(3), Q(4), ALU.mult, [("q", 3), ("q", 4)], [("q", 7)])
    P.tt("dve", Q(8), lr, lr, ALU.mult, ["prs"], [("q", 8)])
    P.tt("dve", Q(9), li_, li_, ALU.mult, ["prs"], [("q", 9)])
    P.tt("dve", Q(8), Q(8), Q(9), ALU.add, [("q", 8), ("q", 9)], [("q", 8)])
    P.op("dve", lambda e: e.reciprocal(Q(8), Q(8)), [("q", 8)], [("q", 8)])
    P.ts("dve", Q(9), Q(6), -1.0, None, ALU.add, None, [("q", 6)], [("q", 9)])
    P.tt("dve", Q(10), Q(9), lr, ALU.mult, [("q", 9), "prs"], [("q", 10)])
    P.tt("dve", Q(11), Q(7), li_, ALU.mult, [("q", 7), "prs"], [("q", 11)])
    P.tt("dve", Q(10), Q(10), Q(11), ALU.add, [("q", 10), ("q", 11)], [("q", 10)])
    P.tt("dve", Q(10), Q(10), Q(8), ALU.mult, [("q", 10), ("q", 8)], [("q", 10)])
    P.tt("dve", Q(11), Q(7), lr, ALU.mult, [("q", 7), "prs"], [("q", 11)])
    P.tt("dve", Q(12), Q(9), li_, ALU.mult, [("q", 9), "prs"], [("q", 12)])
    P.tt("dve", Q(11), Q(11), Q(12), ALU.subtract, [("q", 11), ("q", 12)], [("q", 11)])
    P.tt("dve", Q(11), Q(11), Q(8), ALU.mult, [("q", 11), ("q", 8)], [("q", 11)])
    P.ts("dve", Q(14), Q(10), -1.0, None, ALU.mult, None, [("q", 10)], [("q", 14)])
    MAG, TH, KR, KI, NKR = 3, 2, 10, 11, 14
    ctmp = P.sb("ctmp", [128, 2, 128])
    for gp in range(8):
        krc, kic, nkr = q[:, KR, gp:gp + 1], q[:, KI, gp:gp + 1], q[:, NKR, gp:gp + 1]
        cre, cim = wf32["cpr"][:, gp, :], wf32["cpi"][:, gp, :]
        P.ts("dve", ctmp[:, 0, :], cim, kic, None, ALU.mult, None, ["cpif", "q"], [("ctmp", 0)])
        P.stt(wts["cpr"][:, gp, :], cre, krc, ctmp[:, 0, :], ALU.mult, ALU.subtract, ["cprf", "q", ("ctmp", 0)], [("cprb", gp)])
        P.ts("dve", ctmp[:, 1, :], cim, krc, None, ALU.mult, None, ["cpif", "q"], [("ctmp", 1)])
        P.stt(wts["cpi"][:, gp, :], cre, kic, ctmp[:, 1, :], ALU.mult, ALU.add, ["cprf", "q", ("ctmp", 1)], [("cpib", gp)])

    xb = P.sb("xb", [128, 4, 2, L], BF16)
    NS = 2
    def mk(n, dt=F32):
        return [[P.sb("%s%d_%d" % (n, ln, i), [128, 512], dt) for i in range(NS)] for ln in range(2)]
    cc, ss_, T1, T2, tmp = [mk(n) for n in ("cc", "ss", "T1", "T2", "tmp")]
    tmp2 = [[P.sb("tmp2%d_0" % ln, [128, 512])] * NS for ln in range(2)]
    mr = [[P.sb("mr%d_0" % ln, [128, 512])] * NS for ln in range(2)]
    mi = [[P.sb("mi%d_0" % ln, [128, 512])] * NS for ln in range(2)]
    wr = [[P.sb("wr%d_%d" % (ln, i), [128, 512]) for i in range(2)] for ln in range(2)]
    wi = [[P.sb("wi%d_%d" % (ln, i), [128, 512]) for i in range(2)] for ln in range(2)]
    magt = [P.sb("magt%d" % ln, [128, 512]) for ln in range(2)]
    onest = P.sb("onest", [128, 512])
    P.memset("pool", onest[:, :], 1.0, ["onest"])
    ust = [P.sb("ust0", [128, 512])] * 2
    yst = [P.sb("yst%d" % i, [128, 512]) for i in range(2)]
    psr = [[P.ps("psr%d_%d" % (ln, i), [128, 512]) for i in range(1)] for ln in range(2)]
    psi = [[P.ps("psi%d_%d" % (ln, i), [128, 512]) for i in range(1)] for ln in range(2)]
    psy = [P.ps("psy%d" % i, [128, 512]) for i in range(2)]
    k = 0
    for tile in range(2):
        for glp in range(2):
          lanes = []
          for ln in range(2):
            gl = 2 * glp + ln
            gp = 4 * tile + gl
            sc = lambda idx, gp=gp: q[:, idx, gp:gp + 1]
            P.ts("dve", magt[ln][:, :], onest[:, :], sc(MAG), None, ALU.mult, None, ["onest", "q"], ["magt%d" % ln])
            def stage_a(qb, i, gp=gp, gl=gl, tile=tile, sc=sc, ln=ln):
                n = lambda s_: "%s%d_%d" % (s_, ln, i)
                cs = slice(qb * 512, qb * 512 + 512)
                P.mm(psr[ln][0][:, :], wts["bpr"][:, gp, :], ub[:, tile, cs], ["bprb", ("ub", tile)], ["psr%d_0" % ln])
                P.mm(psi[ln][0][:, :], wts["bpi"][:, gp, :], ub[:, tile, cs], ["bpib", ("ub", tile)], ["psi%d_0" % ln])
                a0 = 8 * qb
                bro = lambda T_, n_: ap3(T_, gp * n_ + a0, [[8 * n_, 128], [1, 8], [0, 64]])
                brb = lambda T_: ap3(T_, gp * 64, [[8 * 64, 128], [0, 8], [1, 64]])
                v3 = lambda t_: t_[:, :].rearrange("p (a b) -> p a b", a=8)
                P.cp("act", T1[ln][i][:, :], psr[ln][0][:, :], ["psr%d_0" % ln], [n("T1")])
                P.cp("act", T2[ln][i][:, :], psi[ln][0][:, :], ["psi%d_0" % ln], [n("T2")])
                P.tt("pool", v3(tmp[ln][i]), bro(EAs, NA), brb(EBs), ALU.mult, ["EAs", "EBs"], [n("tmp")])
                P.tt("dve", v3(cc[ln][i]), bro(EAc, NA), brb(EBc), ALU.mult, ["EAc", "EBc"], [n("cc")])
                P.tt("pool", v3(tmp2[ln][i]), bro(EAc, NA), brb(EBs), ALU.mult, ["EAc", "EBs"], [("tmp2%d" % ln)])
                P.tt("dve", v3(ss_[ln][i]), bro(EAs, NA), brb(EBc), ALU.mult, ["EAs", "EBc"], [n("ss")])
                P.tt("dve", cc[ln][i][:, :], cc[ln][i][:, :], tmp[ln][i][:, :], ALU.subtract, [n("cc"), n("tmp")], [n("cc")])
                P.tt("dve", ss_[ln][i][:, :], ss_[ln][i][:, :], tmp2[ln][i][:, :], ALU.add, [n("ss"), ("tmp2%d" % ln)], [n("ss")])

            def stage_b(qb, i, gp=gp, gl=gl, tile=tile, sc=sc, ln=ln):
                n = lambda s_: "%s%d_%d" % (s_, ln, i)
                cs = slice(qb * 512, qb * 512 + 512)
                P.tt("dve", mr[ln][i][:, :], cc[ln][i][:, :], T1[ln][i][:, :], ALU.mult, [n("cc"), n("T1")], ["mr%d" % ln])
                P.tt("dve", tmp[ln][i][:, :], ss_[ln][i][:, :], T2[ln][i][:, :], ALU.mult, [n("ss"), n("T2")], [n("tmp")])
                P.tt("dve", mr[ln][i][:, :], mr[ln][i][:, :], tmp[ln][i][:, :], ALU.add, ["mr%d" % ln, n("tmp")], ["mr%d" % ln])
                P.tt("dve", mi[ln][i][:, :], cc[ln][i][:, :], T2[ln][i][:, :], ALU.mult, [n("cc"), n("T2")], ["mi%d" % ln])
                P.tt("dve", tmp2[ln][i][:, :], ss_[ln][i][:, :], T1[ln][i][:, :], ALU.mult, [n("ss"), n("T1")], [("tmp2%d" % ln)])
                P.tt("dve", mi[ln][i][:, :], mi[ln][i][:, :], tmp2[ln][i][:, :], ALU.subtract, ["mi%d" % ln, ("tmp2%d" % ln)], ["mi%d" % ln])
                w_r, w_i = wr[ln][qb % 2], wi[ln][qb % 2]
                wrn, win = "wr%d_%d" % (ln, qb % 2), "wi%d_%d" % (ln, qb % 2)
                if qb == 0:
                    ini_r, ini_i, extra = 0.0, 0.0, []
                else:
                    ini_r, ini_i = wr[ln][1 - qb % 2][:, 511:512], wi[ln][1 - qb % 2][:, 511:512]
                    extra = ["wr%d_%d" % (ln, 1 - qb % 2), "wi%d_%d" % (ln, 1 - qb % 2)]
                P.op("dve", lambda e, w_r=w_r, m=mr[ln][i], ini=ini_r, mg=magt[ln]: e.tensor_tensor_scan(w_r[:, :], mg[:, :], m[:, :], ini, ALU.mult, ALU.add),
                     ["magt%d" % ln, "mr%d" % ln] + extra, [wrn])
                P.op("dve", lambda e, w_i=w_i, m=mi[ln][i], ini=ini_i, mg=magt[ln]: e.tensor_tensor_scan(w_i[:, :], mg[:, :], m[:, :], ini, ALU.mult, ALU.add),
                     ["magt%d" % ln, "mi%d" % ln] + extra, [win])
                P.tt("dve", tmp[ln][i][:, :], ss_[ln][i][:, :], w_i[:, :], ALU.mult, [n("ss"), win], [n("tmp")])
                P.tt("dve", tmp2[ln][i][:, :], cc[ln][i][:, :], w_r[:, :], ALU.mult, [n("cc"), wrn], [("tmp2%d" % ln)])
                P.tt("dve", xb[:, gl, 0, cs], tmp2[ln][i][:, :], tmp[ln][i][:, :], ALU.subtract, [("tmp2%d" % ln), n("tmp")], [("xb", gl * 2)])
                P.tt("dve", T1[ln][i][:, :], ss_[ln][i][:, :], w_r[:, :], ALU.mult, [n("ss"), wrn], [n("T1")])
                P.tt("dve", T2[ln][i][:, :], cc[ln][i][:, :], w_i[:, :], ALU.mult, [n("cc"), win], [n("T2")])
                P.stt(xb[:, gl, 1, cs], T1[ln][i][:, :], -1.0, T2[ln][i][:, :], ALU.mult, ALU.subtract, [n("T1"), n("T2")], [("xb", gl * 2 + 1)])

            lanes.append((stage_a, stage_b))
          for (sa, sb_) in lanes:
              sa(0, 0)
          for qb in range(NQ):
              if qb + 1 < NQ:
                  for (sa, sb_) in lanes:
                      sa(qb + 1, (qb + 1) % NS)
              for (sa, sb_) in lanes:
                  sb_(qb, qb % NS)
        for qb in range(NQ):
            cs = slice(qb * 512, qb * 512 + 512)
            py, pyn = psy[qb % 2], "psy%d" % (qb % 2)
            for gl in range(4):
                gp = 4 * tile + gl
                P.mm(py[:, :], wts["cpr"][:, gp, :], xb[:, gl, 0, cs], ["cprb", ("xb", gl * 2)], [pyn], start=(gl == 0), stop=False)
                P.mm(py[:, :], wts["cpi"][:, gp, :], xb[:, gl, 1, cs], ["cpib", ("xb", gl * 2 + 1)], [pyn], start=False, stop=(gl == 3))
            us, usn = ust[0], "ust0"
            ys, ysn = yst[qb % 2], "yst%d" % (qb % 2)
            P.dma(us[:, :], uT[tile * 128:(tile + 1) * 128, cs], [], [usn])
            P.stt(ys[:, :], us[:, :], ds[:, tile:tile + 1], py[:, :], ALU.mult, ALU.add, [usn, "ds", pyn], [ysn])
            P.dma(yT[tile * 128:(tile + 1) * 128, cs], ys[:, :], [ysn], [("yT", tile * NQ + qb)], sk=ysn)
    return _fin(P, nc, io, ("yT",))


def l2_inputs(u, prm, gh, L):
    f = lambda a: np.ascontiguousarray(a, dtype=np.float32)
    d = dict(uT=f(u[:, 256 * gh:256 * gh + 256].T))
    bpr = np.zeros((128, 8, 128), np.float32); bpi = np.zeros_like(bpr)
    cpr = np.zeros((128, 8, 128), np.float32); cpi = np.zeros_like(cpr)
    pr = np.zeros((128, 3, 8), np.float32)
    for gp in range(8):
        gl = gp % 4
        for gpar in range(2):
            g = 16 * gh + 2 * gp + gpar
            chs = slice(16 * (2 * gl + gpar), 16 * (2 * gl + gpar) + 16)
            ps = slice(64 * gpar, 64 * gpar + 64)
            bpr[chs, gp, ps] = prm["b_re"][g].T
            bpi[chs, gp, ps] = prm["b_im"][g].T
            cpr[ps, gp, chs] = prm["c_re"][g].T
            cpi[ps, gp, chs] = prm["c_im"][g].T
            pr[ps, 0, gp] = prm["lam_re"][g]
            pr[ps, 1, gp] = prm["lam_im"][g]
            pr[ps, 2, gp] = prm["log_dt"][g]
    d.update(bpr=bpr, bpi=bpi, cpr=cpr, cpi=cpi, prm=pr)
    d["dcol"] = f(prm["d"][256 * gh:256 * gh + 256].reshape(2, 128).T)
    return d


DFF = 2816


def build_L5a(TOK=2048, HT=1024, ctx=None):
    nc, P, io = _ctx(ctx)
    NH = TOK // HT
    NTT = HT // 128
    NB5 = HT // 512
    I = _mkI(nc, io)
    x_tok = I("x_tok", [TOK, D])
    ypreT, oT, ycT = I("ypreT", [512, TOK]), I("oT", [512, TOK]), I("ycT", [512, TOK])
    glT = I("glT", [3072, TOK])
    w_glu, w_br, w_out = I("w_glu", [512, 512]), I("w_br", [3, 512, 1024]), I("w_out", [1024, 1024])
    x_out = _mkO(nc, io, "x_out", [TOK, D])

    wst = [P.sb("wst%d" % i, [128, 1024]) for i in range(3)]
    wk = [0]

    def load_w(dst, dn, sub, src_ap, ncol):
        i = wk[0] % 3
        wk[0] += 1
        P.dma(wst[i][:, :ncol], src_ap, [], ["wst%d" % i], eng=("sp" if i % 2 == 0 else "pool"))
        P.cp("act" if i % 2 else "dve", dst, wst[i][:, :ncol], ["wst%d" % i], [(dn, sub)])

    wglu = P.sb("wglu", [128, 4, 512], BF16)
    wbr = P.sb("wbr", [128, 12, 1024], BF16)
    wout = P.sb("wout", [128, 8, 1024], BF16)
    for c in range(4):
        load_w(wglu[:, c, :], "wglu", c, w_glu[c * 128:(c + 1) * 128, :], 512)
    for n in range(3):
        for c in range(4):
            load_w(wbr[:, n * 4 + c, :], "wbr", n * 4 + c, w_br[n, c * 128:(c + 1) * 128, :], 1024)
    for c in range(8):
        load_w(wout[:, c, :], "wout", c, w_out[c * 128:(c + 1) * 128, :], 1024)

    fin = P.sb("fin", [128, 4, HT])
    t1 = P.sb("t1", [128, 4, HT])
    yb = [P.sb("yb%d" % n, [128, 4, HT], BF16) for n in range(3)]
    ya2 = P.sb("ya2", [128, 4, HT], BF16)
    gst = [P.sb("gst%d" % i, [128, 512]) for i in range(2)]
    sg = [P.sb("sg%d" % i, [128, 512]) for i in range(2)]
    pr = [P.sb("pr%d" % i, [128, 512]) for i in range(2)]
    macc = P.sb("macc", [128, 512])
    mT = P.sb("mT", [128, 8, HT], BF16)
    x1 = [P.sb("x1_%d" % i, [128, D]) for i in range(2)]
    xin = [P.sb("xin%d" % i, [128, D]) for i in range(2)]
    pss = [P.ps("ps%d" % i, [128, 512]) for i in range(6)]
    pk = [0]

    def nextps():
        k = pk[0] % 6
        pk[0] += 1
        return pss[k], "ps%d" % k

    for half in range(NH):
        t0 = half * HT
        tsl = slice(t0, t0 + HT)
        fmv = lambda a: a.ap().rearrange("(c p) t -> p c t", p=128)
        P.dma(fin[:, :, :], fmv(ypreT)[:, :, tsl], [], ["fin"])
        P.tt("pool", t1[:, :, :], fin[:, :, :], fin[:, :, :], ALU.mult, ["fin"], ["t1"])
        P.ts("dve", t1[:, :, :], t1[:, :, :], 0.044715, 1.0, ALU.mult, ALU.add, ["t1"], ["t1"])
        P.tt("pool", t1[:, :, :], t1[:, :, :], fin[:, :, :], ALU.mult, ["t1", "fin"], ["t1"])
        P.act(t1[:, :, :], t1[:, :, :], AF.Sigmoid, ["t1"], ["t1"], scale=1.5957691216057308)
        P.tt("dve", t1[:, :, :], t1[:, :, :], fin[:, :, :], ALU.mult, ["t1", "fin"], ["t1"])
        P.cp("pool", ya2[:, :, :], t1[:, :, :], ["t1"], ["ya2"])
        for jc in range(4):
            for b5 in range(NB5):
                bs = slice(b5 * 512, b5 * 512 + 512)
                ps, psn = nextps()
                for c in range(4):
                    P.mm(ps[:, :], wglu[:, c, jc * 128:(jc + 1) * 128], ya2[:, c, bs], [("wglu", c), "ya2"], [psn], start=(c == 0), stop=(c == 3))
                s_, sn = sg[(jc * NB5 + b5) % 2], "sg%d" % ((jc * NB5 + b5) % 2)
                P.act(s_[:, :], ps[:, :], AF.Sigmoid, [psn], [sn])
                P.tt("dve", yb[0][:, jc, bs], t1[:, jc, bs], s_[:, :], ALU.mult, ["t1", sn], [("yb0", jc)])
        P.dma(fin[:, :, :], fmv(oT)[:, :, tsl], ["fin"], ["fin"])
        P.cp("dve", yb[1][:, :, :], fin[:, :, :], ["fin"], ["yb1"])
        P.dma(fin[:, :, :], fmv(ycT)[:, :, tsl], ["fin"], ["fin"])
        P.cp("dve", yb[2][:, :, :], fin[:, :, :], ["fin"], ["yb2"])
        kk = 0
        for dc in range(8):
            for b5 in range(NB5):
                bs = slice(b5 * 512, b5 * 512 + 512)
                for n in range(3):
                    g_, gn = gst[kk % 2], "gst%d" % (kk % 2)
                    s_, sn = sg[kk % 2], "sg%d" % (kk % 2)
                    p_, pn = pr[kk % 2], "pr%d" % (kk % 2)
                    kk += 1
                    r0 = n * 1024 + dc * 128
                    P.dma(g_[:, :], glT[r0:r0 + 128, t0 + b5 * 512:t0 + b5 * 512 + 512], [], [gn], eng=("sp" if kk % 2 else "pool"))
                    P.act(s_[:, :], g_[:, :], AF.Sigmoid, [gn], [sn])
                    ps, psn = nextps()
                    for c in range(4):
                        P.mm(ps[:, :], wbr[:, n * 4 + c, dc * 128:(dc + 1) * 128], yb[n][:, c, bs], [("wbr", n * 4 + c), "yb%d" % n], [psn],
                             start=(c == 0), stop=(c == 3))
                    if n == 0:
                        P.tt("dve", macc[:, :], ps[:, :], s_[:, :], ALU.mult, [psn, sn], ["macc"])
                    else:
                        P.tt("dve", p_[:, :], ps[:, :], s_[:, :], ALU.mult, [psn, sn], [pn])
                        if n == 1:
                            P.tt("pool", macc[:, :], macc[:, :], p_[:, :], ALU.add, ["macc", pn], ["macc"])
                        else:
                            P.tt("pool", mT[:, dc, bs], macc[:, :], p_[:, :], ALU.add, ["macc", pn], [("mT", dc)])
        for tt_ in range(NTT):
            xi, xn = xin[tt_ % 2], "xin%d" % (tt_ % 2)
            xo_, xon = x1[tt_ % 2], "x1_%d" % (tt_ % 2)
            P.dma(xi[:, :], x_tok[t0 + tt_ * 128:t0 + (tt_ + 1) * 128, :], [], [xn])
            for ch in range(2):
                ps, psn = nextps()
                for c in range(8):
                    P.mm(ps[:, :], mT[:, c, tt_ * 128:(tt_ + 1) * 128], wout[:, c, ch * 512:(ch + 1) * 512], [("mT", c), ("wout", c)], [psn],
                         start=(c == 0), stop=(c == 7))
                P.tt("dve", xo_[:, ch * 512:(ch + 1) * 512], ps[:, :], xi[:, ch * 512:(ch + 1) * 512], ALU.add, [psn, xn], [(xon, ch)])
            P.dma(x_out[t0 + tt_ * 128:t0 + (tt_ + 1) * 128, :], xo_[:, :], [xon], [("x_out", half * NTT + tt_)], sk=xon)
    return _fin(P, nc, io, ("x_out",))


def build_L5b(TOK=2048, HT=1024, ctx=None):
    nc, P, io = _ctx(ctx)
    NH = TOK // HT
    NTT = HT // 128
    NB5 = HT // 512
    I = _mkI(nc, io)
    x_tok = I("x_tok", [TOK, D])
    g2 = I("g2", [128, 8])
    w_fi, w_fo = I("w_fi", [1024, 2 * DFF]), I("w_fo", [DFF, 1024])
    c_ident = I("c_ident", [128, 128])
    x_out = _mkO(nc, io, "x_out", [TOK, D])
    g2s = P.sb("g2s", [128, 8])
    P.dma(g2s[:, :], g2[:, :], [], ["g2s"])
    identf = P.sb("identf", [128, 128])
    P.dma(identf[:, :], c_ident[:, :], [], ["identf"])
    identb = P.sb("identb", [128, 128], BF16)
    P.cp("dve", identb[:, :], identf[:, :], ["identf"], ["identb"])
    wst = [P.sb("wst%d" % i, [128, 1024]) for i in range(3)]
    wk = [0]

    def load_w(dst, dn, sub, src_ap, ncol):
        i = wk[0] % 3
        wk[0] += 1
        P.dma(wst[i][:, :ncol], src_ap, [], ["wst%d" % i], eng=("sp" if i % 2 == 0 else "pool"))
        P.cp("act" if i % 2 else "dve", dst, wst[i][:, :ncol], ["wst%d" % i], [(dn, sub)])
    wfo = P.sb("wfo", [128, 22, 1024], BF16)
    for c in range(22):
        load_w(wfo[:, c, :], "wfo", c, w_fo[c * 128:(c + 1) * 128, :], 1024)
    x1 = P.sb("x1", [128, NTT, D])
    junk = P.sb("junk", [128, D], BF16)
    ss = P.sb("ss5", [128, NTT])
    hb = [P.sb("hb%d" % i, [128, D], BF16) for i in range(2)]
    h2T = P.sb("h2T", [128, 8, HT], BF16)
    wfi_f = [P.sb("wfif%d" % i, [128, 8, 256]) for i in range(2)]
    wfi_b = [P.sb("wfib%d" % i, [128, 8, 256], BF16) for i in range(2)]
    actT = P.sb("actT", [128, 22, HT], BF16)
    sil = [P.sb("sil%d" % i, [128, 512]) for i in range(2)]
    xo = [P.sb("xo%d" % i, [128, 512]) for i in range(2)]
    pss = [P.ps("ps%d" % i, [128, 512]) for i in range(6)]
    psb = P.ps("psb", [128, 1024], BF16)
    pk = [0]

    def nextps():
        k = pk[0] % 6
        pk[0] += 1
        return pss[k], "ps%d" % k

    for half in range(NH):
        t0 = half * HT
        for tt_ in range(NTT):
            P.dma(x1[:, tt_, :], x_tok[t0 + tt_ * 128:t0 + (tt_ + 1) * 128, :], [], [("x1", tt_)], eng=("sp" if tt_ % 2 else "pool"))
            P.act(junk[:, :], x1[:, tt_, :], AF.Square, [("x1", tt_)], ["junk", ("ss5", tt_)], accum_out=ss[:, tt_:tt_ + 1])
        P.ts("dve", ss[:, :], ss[:, :], 1.0 / D, EPS, ALU.mult, ALU.add, ["ss5"], ["ss5"])
        P.act(ss[:, :], ss[:, :], AF.Sqrt, ["ss5"], ["ss5"])
        P.op("dve", lambda e: e.reciprocal(ss[:, :], ss[:, :]), ["ss5"], ["ss5"])
        for tt_ in range(NTT):
            h_, hn = hb[tt_ % 2], "hb%d" % (tt_ % 2)
            P.act(h_[:, :], x1[:, tt_, :], AF.Copy, [("x1", tt_), "ss5"], [hn], scale=ss[:, tt_:tt_ + 1])
            for c in range(8):
                P.tr(psb[:, c * 128:(c + 1) * 128], h_[:, c * 128:(c + 1) * 128], identb[:, :], [hn, "identb"], ["psb"])
            for c in range(8):
                P.ts("dve", h2T[:, c, tt_ * 128:(tt_ + 1) * 128], psb[:, c * 128:(c + 1) * 128], g2s[:, c:c + 1], None, ALU.mult, None,
                     ["psb", "g2s"], [("h2T", c)])
        for fc in range(22):
            wf, wfn = wfi_f[fc % 2], "wfif%d" % (fc % 2)
            wb_, wbn = wfi_b[fc % 2], "wfib%d" % (fc % 2)
            wv = w_fi.ap().rearrange("(c p) n -> p c n", p=128)
            P.dma(wf[:, :, 0:128], wv[:, :, fc * 128:(fc + 1) * 128], [], [(wfn, 0)], eng="sp", sk=wfn + "a")
            P.dma(wf[:, :, 128:256], wv[:, :, DFF + fc * 128:DFF + (fc + 1) * 128], [], [(wfn, 1)], eng="pool", sk=wfn + "b")
            P.cp("act", wb_[:, :, 0:128], wf[:, :, 0:128], [wfn], [(wbn, 0)])
            P.cp("dve", wb_[:, :, 128:256], wf[:, :, 128:256], [wfn], [(wbn, 1)])
            for b5 in range(NB5):
                bs = slice(b5 * 512, b5 * 512 + 512)
                pg, pgn = nextps()
                pu, pun = nextps()
                for c in range(8):
                    P.mm(pg[:, :], wb_[:, c, 0:128], h2T[:, c, bs], [wbn, ("h2T", c)], [pgn], start=(c == 0), stop=(c == 7))
                for c in range(8):
                    P.mm(pu[:, :], wb_[:, c, 128:256], h2T[:, c, bs], [wbn, ("h2T", c)], [pun], start=(c == 0), stop=(c == 7))
                s_, sn = sil[(fc * NB5 + b5) % 2], "sil%d" % ((fc * NB5 + b5) % 2)
                P.act(s_[:, :], pg[:, :], AF.Silu, [pgn], [sn])
                P.tt("dve", actT[:, fc, bs], pu[:, :], s_[:, :], ALU.mult, [pun, sn], [("actT", fc)])
        for tt_ in range(NTT):
            for ch in range(2):
                ps, psn = nextps()
                for fc in range(22):
                    P.mm(ps[:, :], actT[:, fc, tt_ * 128:(tt_ + 1) * 128], wfo[:, fc, ch * 512:(ch + 1) * 512], [("actT", fc), ("wfo", fc)], [psn],
                         start=(fc == 0), stop=(fc == 21))
                o_, on = xo[(tt_ * 2 + ch) % 2], "xo%d" % ((tt_ * 2 + ch) % 2)
                P.tt("dve", o_[:, :], ps[:, :], x1[:, tt_, ch * 512:(ch + 1) * 512], ALU.add, [psn, ("x1", tt_)], [on])
                P.dma(x_out[t0 + tt_ * 128:t0 + (tt_ + 1) * 128, ch * 512:(ch + 1) * 512], o_[:, :], [on],
                      [("x_out", (half * NTT + tt_) * 2 + ch)], sk=on)
    return _fin(P, nc, io, ("x_out",))


def _hw_runner(nc, in_maps):
    n = len(in_maps)
    maps = list(in_maps) + [in_maps[-1]] * (NCORES - n)
    res = run_bass_kernel_spmd(nc, maps, core_ids=list(range(NCORES)))
    return res.results[:n]


def kernel_impl(inp, runner=_hw_runner, TOKL=2048, HT=512):
    f = lambda a: np.ascontiguousarray(a, dtype=np.float32)
    x = np.asarray(inp["x"], dtype=np.float32)
    B, L, _ = x.shape
    NTOK = B * L
    ntc = NTOK // TOKL
    xcur = x.reshape(NTOK, D)
    depth = inp["w_in"].shape[0]
    progs = {}

    def prog(name, fn):
        if name not in progs:
            progs[name] = fn()
        return progs[name]
    ident = np.eye(128, dtype=np.float32)
    vfirst = None
    for i in range(depth):
        lam_init = 0.8 - 0.6 * float(np.exp(-0.3 * i))
        nc = prog("L1", lambda: build_L1(TOKL))
        g1 = f(np.asarray(inp["norm1_g"][i]).reshape(8, 128).T)
        w_in = f(inp["w_in"][i])
        maps = []
        for c in range(ntc):
            xs = xcur[c * TOKL:(c + 1) * TOKL]
            maps.append(dict(x_tok=f(xs), xT=f(xs.T), w=w_in, g=g1))
        res = runner(nc, maps)
        proj = np.concatenate([r["proj"] for r in res], 0).reshape(B, L, NIN)
        u, q, k, v = proj[..., 0:512], proj[..., 512:1024], proj[..., 1024:1536], proj[..., 1536:2048]
        z, gl = proj[..., 2048:3744], proj[..., 3744:]
        nc = prog("L2", lambda: build_L2(L))
        p2 = dict(lam_re=np.asarray(inp["s5_lambda_re"][i]), lam_im=np.asarray(inp["s5_lambda_im"][i]), log_dt=np.asarray(inp["s5_log_dt"][i]),
                  b_re=np.asarray(inp["s5_b_re"][i]), b_im=np.asarray(inp["s5_b_im"][i]), c_re=np.asarray(inp["s5_c_re"][i]),
                  c_im=np.asarray(inp["s5_c_im"][i]), d=np.asarray(inp["s5_d"][i]))
        res = runner(nc, [l2_inputs(u[c // 2], p2, c % 2, L) for c in range(2 * B)])
        ypre = np.stack([np.concatenate([res[2 * b]["yT"], res[2 * b + 1]["yT"]], 0) for b in range(B)])
        nc = prog("L3", lambda: build_L3(L))
        p3 = dict(q_gain=np.asarray(inp["da_q_gain"][i]), k_gain=np.asarray(inp["da_k_gain"][i]), lam=np.asarray(inp["da_lambda"][i]),
                  subln_g=np.asarray(inp["da_subln_g"][i]))
        res = runner(nc, [l3_inputs(q[c // 2], k[c // 2], v[c // 2], p3, c % 2, L, lam_init) for c in range(2 * B)])
        oat = np.stack([np.concatenate([res[2 * b]["oT"].reshape(256, L), res[2 * b + 1]["oT"].reshape(256, L)], 0) for b in range(B)])
        vg = i > 0
        nc = prog("L4v" if vg else "L4", lambda: build_L4(L, has_vgate=vg))
        p4 = dict(mu=np.asarray(inp["rw_mu"][i]), w0=np.asarray(inp["rw_w0"][i]), w2=np.asarray(inp["rw_w2"][i]), a0=np.asarray(inp["rw_a0"][i]),
                  a2=np.asarray(inp["rw_a2"][i]), g2=np.asarray(inp["rw_g2"][i]), k_k=np.asarray(inp["rw_k_k"][i]), k_a=np.asarray(inp["rw_k_a"][i]),
                  r_k=np.asarray(inp["rw_r_k"][i]), ln_g=np.asarray(inp["rw_ln_g"][i]), ln_b=np.asarray(inp["rw_ln_b"][i]))
        if vg:
            p4.update(v0=np.asarray(inp["rw_v0"][i - 1]), v1=np.asarray(inp["rw_v1"][i - 1]), v2=np.asarray(inp["rw_v2"][i - 1]))
        maps = []
        for c in range(2 * B):
            b, hg = c // 2, c % 2
            maps.append(l4_inputs(z[b], p4, hg, L, vfirst=(vfirst[b][256 * hg:256 * hg + 256] if vg else None)))
        res = runner(nc, maps)
        yc = np.stack([np.concatenate([res[2 * b]["y_out"], res[2 * b + 1]["y_out"]], 1) for b in range(B)])
        if not vg:
            vfirst = np.stack([np.concatenate([res[2 * b]["vfirst_out"], res[2 * b + 1]["vfirst_out"]], 0) for b in range(B)])
        nc = prog("L5a", lambda: build_L5a(TOKL, HT))
        wa = dict(w_glu=f(inp["s5_w_glu"][i]), w_br=f(inp["w_branch"][i]), w_out=f(inp["w_out"][i]))
        maps = []
        for c in range(ntc):
            t0 = c * TOKL
            b, s0 = t0 // L, t0 % L
            sl = slice(s0, s0 + TOKL)
            maps.append(dict(wa, x_tok=f(xcur[t0:t0 + TOKL]), ypreT=f(ypre[b][:, sl]), oT=f(oat[b][:, sl]), ycT=f(yc[b][sl].T), glT=f(gl[b][sl].T)))
        res = runner(nc, maps)
        x1 = np.concatenate([r["x_out"] for r in res], 0)
        nc = prog("L5b", lambda: build_L5b(TOKL, HT))
        wb_ = dict(g2=f(np.asarray(inp["norm2_g"][i]).reshape(8, 128).T), w_fi=f(inp["w_ffn_in"][i]), w_fo=f(inp["w_ffn_out"][i]), c_ident=ident)
        res = runner(nc, [dict(wb_, x_tok=f(x1[c * TOKL:(c + 1) * TOKL])) for c in range(ntc)])
        xcur = np.concatenate([r["x_out"] for r in res], 0)
    return xcur.reshape(B, L, D).astype(np.float32)


def build_fused(L=4096, depth=2):
    nc = bass.Bass("TRN2", target_bir_lowering=False)
    sh = Shared(nc)
    E = lambda n, s: nc.dram_tensor(n, list(s), F32, kind="ExternalInput")
    x_in = E("x_in", [L, D])
    x_out = nc.dram_tensor("x_final", [L, D], F32, kind="ExternalOutput")
    FT = nc.dram_tensor("FT", [6304, L], F32)
    VT = nc.dram_tensor("VT", [L, 512], F32)
    YB = nc.dram_tensor("YB", [1536, L], F32)
    X1 = nc.dram_tensor("X1", [L, D], F32)
    XS = nc.dram_tensor("XS", [L, D], F32)
    VF = nc.dram_tensor("VF", [512, L], F32)
    VF2 = nc.dram_tensor("VF2", [512, L], F32)
    cs = {}
    LHr = min(L, 1024)
    for n, shp in (("c_ident", [128, 128]), ("c_cmask", [128, 4, 512]), ("c_b64", [128, 128]), ("c_ones", [128, 128]),
                   ("c_mask4", [128, 512]), ("c_sl", [128, 128]), ("c_reset", [128, LHr]), ("c_bones", [128, 128]), ("c_hsel", [128, 2])):
        cs[n] = E(n, shp)
    TP = min(2048, L)
    for i in range(depth):
        sfx = "_%d" % i
        xin = x_in if i == 0 else XS
        xo = x_out if i == depth - 1 else XS
        io = dict(x_tok=xin, w=E("w_in" + sfx, [D, NIN]), g=E("g1" + sfx, [128, 8]), c_ident=cs["c_ident"], FT=FT, VT=VT)
        build_L1F(L, ctx=(nc, sh, io))
        s5 = {n: E("s5" + n + sfx, [2, 128, 8, 128]) for n in ("bpr", "bpi", "cpr", "cpi")}
        s5prm = E("s5prm" + sfx, [2, 128, 3, 8])
        s5d = E("s5d" + sfx, [2, 128, 2])
        for gh in range(2):
            io = dict(uT=V(FT.ap()[256 * gh:256 * gh + 256, :]), prm=V(s5prm.ap()[gh]), dcol=V(s5d.ap()[gh]),
                      yT=V(YB.ap()[256 * gh:256 * gh + 256, :]))
            for n in s5:
                io[n] = V(s5[n].ap()[gh])
            build_L2(L, ctx=(nc, sh, io))
        gains = E("da_gains" + sfx, [128, 4])
        lamv = E("da_lamv" + sfx, [128, 4, 64])
        laminit = E("da_laminit" + sfx, [128, 1])
        for hp in range(2):
            hv = lambda a, r0: V(a.ap()[r0:r0 + 256, :].rearrange("(h p) l -> h p l", h=2))
            io = dict(qT=hv(FT, 512 + 256 * hp), kT=hv(FT, 1024 + 256 * hp),
                      vtok=V(VT.ap()[:, 256 * hp:256 * hp + 256].rearrange("l (h d) -> h l d", h=2)),
                      gains=gains, lamv=lamv, laminit=laminit, c_cmask=cs["c_cmask"], c_b64=cs["c_b64"], c_ones=cs["c_ones"],
                      oT=hv(YB, 512 + 256 * hp))
            build_L3(L, ctx=(nc, sh, io))
        vg = i > 0
        r4 = dict(mu3=E("rw_mu3" + sfx, [2, 128, 6]), mulo=E("rw_mulo" + sfx, [96, 3]), w2=E("rw_w2p" + sfx, [2, 32, 256]),
                  a2=E("rw_a2p" + sfx, [2, 32, 256]), g2=E("rw_g2p" + sfx, [2, 96, 256]), pc=E("rw_pc" + sfx, [2, 128, 10]),
                  lng=E("rw_lng" + sfx, [2, 128, 256]), lnb=E("rw_lnb" + sfx, [2, 128, 256]))
        if vg:
            r4.update(muvf=E("rw_muvf" + sfx, [128, 4]), v1=E("rw_v1p" + sfx, [128, 4, 32]), v2=E("rw_v2p" + sfx, [2, 32, 256]),
                      v0=E("rw_v0p" + sfx, [2, 128, 2]))
        for hg in range(2):
            R = lambda r0, n: V(FT.ap()[r0:r0 + n, :])
            io = dict(zr=R(1536 + 256 * hg, 256), zk=R(2048 + 256 * hg, 256), zv=R(2560 + 256 * hg, 256),
                      zw=R(3072, 32), za=R(3104, 32), zg=R(3136, 96), mulo=r4["mulo"],
                      y_out=V(YB.ap()[1024 + 256 * hg:1024 + 256 * hg + 256, :]),
                      vfirst_out=V((VF2 if vg else VF).ap()[256 * hg:256 * hg + 256, :]))
            for n in ("mu3", "w2", "a2", "g2", "pc", "lng", "lnb"):
                io[n] = V(r4[n].ap()[hg])
            for n in ("c_mask4", "c_sl", "c_reset", "c_bones", "c_hsel", "c_ident"):
                io[n] = cs[n]
            if vg:
                io.update(zvfull=R(2560, 512), muvf=r4["muvf"], v1=r4["v1"], v2=V(r4["v2"].ap()[hg]), v0=V(r4["v0"].ap()[hg]),
                          vfirst_in=V(VF.ap()[256 * hg:256 * hg + 256, :]))
            build_L4(L, has_vgate=vg, ctx=(nc, sh, io), fm_out=True)
        wa = dict(w_glu=E("w_glu" + sfx, [512, 512]), w_br=E("w_br" + sfx, [3, 512, 1024]), w_out=E("w_out" + sfx, [1024, 1024]))
        for th in range(L // TP):
            tsl = slice(th * TP, th * TP + TP)
            io = dict(wa, x_tok=V(xin.ap()[tsl, :]), ypreT=V(YB.ap()[0:512, tsl]), oT=V(YB.ap()[512:1024, tsl]), ycT=V(YB.ap()[1024:1536, tsl]),
                      glT=V(FT.ap()[3232:6304, tsl]), x_out=V(X1.ap()[tsl, :]))
            build_L5a(TP, 512, ctx=(nc, sh, io))
        wb_ = dict(g2=E("g2n" + sfx, [128, 8]), w_fi=E("w_fi" + sfx, [1024, 2 * DFF]), w_fo=E("w_fo" + sfx, [DFF, 1024]), c_ident=cs["c_ident"])
        for th in range(L // TP):
            tsl = slice(th * TP, th * TP + TP)
            io = dict(wb_, x_tok=V(X1.ap()[tsl, :]), x_out=V(xo.ap()[tsl, :]))
            if i == depth - 1:
                io["_final"] = True
            build_L5b(TP, 512, ctx=(nc, sh, io))
    sh.close()
    return nc


def fused_inputs(inp, L):
    f = lambda a: np.ascontiguousarray(a, dtype=np.float32)
    d = {}
    d.update(at_consts())
    rc = rw_consts(L)
    d.update(rc)
    depth = inp["w_in"].shape[0]
    zL = np.zeros((L, 512), np.float32)
    zz = np.zeros((L, 1696), np.float32)
    for i in range(depth):
        sfx = "_%d" % i
        lam_init = 0.8 - 0.6 * float(np.exp(-0.3 * i))
        d["w_in" + sfx] = f(inp["w_in"][i])
        d["g1" + sfx] = f(np.asarray(inp["norm1_g"][i]).reshape(8, 128).T)
        p2 = dict(lam_re=np.asarray(inp["s5_lambda_re"][i]), lam_im=np.asarray(inp["s5_lambda_im"][i]), log_dt=np.asarray(inp["s5_log_dt"][i]),
                  b_re=np.asarray(inp["s5_b_re"][i]), b_im=np.asarray(inp["s5_b_im"][i]), c_re=np.asarray(inp["s5_c_re"][i]),
                  c_im=np.asarray(inp["s5_c_im"][i]), d=np.asarray(inp["s5_d"][i]))
        l2 = [l2_inputs(zL, p2, gh, L) for gh in range(2)]
        for n in ("bpr", "bpi", "cpr", "cpi"):
            d["s5" + n + sfx] = f(np.stack([l2[gh][n] for gh in range(2)]))
        d["s5prm" + sfx] = f(np.stack([l2[gh]["prm"] for gh in range(2)]))
        d["s5d" + sfx] = f(np.stack([l2[gh]["dcol"] for gh in range(2)]))
        p3 = dict(q_gain=np.asarray(inp["da_q_gain"][i]), k_gain=np.asarray(inp["da_k_gain"][i]), lam=np.asarray(inp["da_lambda"][i]),
                  subln_g=np.asarray(inp["da_subln_g"][i]))
        l3 = l3_inputs(zL, zL, zL, p3, 0, L, lam_init)
        d["da_gains" + sfx], d["da_lamv" + sfx], d["da_laminit" + sfx] = l3["gains"], l3["lamv"], l3["laminit"]
        p4 = dict(mu=np.asarray(inp["rw_mu"][i]), w0=np.asarray(inp["rw_w0"][i]), w2=np.asarray(inp["rw_w2"][i]), a0=np.asarray(inp["rw_a0"][i]),
                  a2=np.asarray(inp["rw_a2"][i]), g2=np.asarray(inp["rw_g2"][i]), k_k=np.asarray(inp["rw_k_k"][i]), k_a=np.asarray(inp["rw_k_a"][i]),
                  r_k=np.asarray(inp["rw_r_k"][i]), ln_g=np.asarray(inp["rw_ln_g"][i]), ln_b=np.asarray(inp["rw_ln_b"][i]))
        vg = i > 0
        if vg:
            p4.update(v0=np.asarray(inp["rw_v0"][i - 1]), v1=np.asarray(inp["rw_v1"][i - 1]), v2=np.asarray(inp["rw_v2"][i - 1]))
        l4 = [l4_inputs(zz, p4, hg, L, vfirst=(np.zeros((256, L), np.float32) if vg else None)) for hg in range(2)]
        for n, m in (("mu3", "rw_mu3"), ("w2", "rw_w2p"), ("a2", "rw_a2p"), ("g2", "rw_g2p"), ("pc", "rw_pc"), ("lng", "rw_lng"), ("lnb", "rw_lnb")):
            d[m + sfx] = f(np.stack([l4[hg][n] for hg in range(2)]))
        d["rw_mulo" + sfx] = l4[0]["mulo"]
        if vg:
            d["rw_muvf" + sfx] = l4[0]["muvf"]
            d["rw_v1p" + sfx] = l4[0]["v1"]
            d["rw_v2p" + sfx] = f(np.stack([l4[hg]["v2"] for hg in range(2)]))
            d["rw_v0p" + sfx] = f(np.stack([l4[hg]["v0"] for hg in range(2)]))
        d["w_glu" + sfx] = f(inp["s5_w_glu"][i])
        d["w_br" + sfx] = f(inp["w_branch"][i])
        d["w_out" + sfx] = f(inp["w_out"][i])
        d["g2n" + sfx] = f(np.asarray(inp["norm2_g"][i]).reshape(8, 128).T)
        d["w_fi" + sfx] = f(inp["w_ffn_in"][i])
        d["w_fo" + sfx] = f(inp["w_ffn_out"][i])
    return d


def kernel_fused(inp, runner=None):
    x = np.asarray(inp["x"], dtype=np.float32)
    B, L, _ = x.shape
    depth = inp["w_in"].shape[0]
    nc = build_fused(L, depth)
    common = fused_inputs(inp, L)
    ncore = NCORES if runner is None else B
    maps = [dict(common, x_in=np.ascontiguousarray(x[c % B])) for c in range(ncore)]
    if runner is None:
        res = run_bass_kernel_spmd(nc, maps, core_ids=list(range(NCORES))).results
    else:
        res = runner(nc, maps)
    return np.stack([res[b]["x_final"] for b in range(B)]).astype(np.float32)


I32 = mybir.dt.int32


def _dyn_dma(P, hreg, tmpregs, ctr, dst, tensor, base, hscale, dims, reads, writes, sk):
    def fn(e):
        r = tmpregs[ctr[0] % len(tmpregs)]
        ctr[0] += 1
        e.reg_mul(r, hreg, hscale)
        e.reg_add(r, r, base)
        return e.dma_start(out=dst, in_=bass.AP(tensor, r, [list(d) for d in dims]))
    return P.op("pool", fn, reads, writes, dma=True, sk=sk)


def phase_load_h(nc, sh, regs, hid):
    P = Prog(nc, sh)
    hreg, _ = regs
    hs = P.stack.enter_context(nc.sbuf_tensor(P.pfx + "hs", [1, 1], I32))
    P.dma(hs[:, :], hid[:, :], [], ["hs"], eng="pool")
    P.op("pool", lambda e: e.reg_load(hreg, hs[:1, :1]), ["hs"], ["hreg"])
    P.emit()


def _chunked_allgather(P, cc, src, dst, chunks, groups):
    for ci, (r0, n) in enumerate(chunks):
        P.op("pool", lambda e, r0=r0, n=n: e.collective_compute("AllGather", ALU.bypass, replica_groups=groups,
                                                                ins=[src.ap()[r0:r0 + n, :]], outs=[dst.ap()[2 * r0:2 * r0 + 2 * n, :]]),
             [], [("gat", ci)], dma=True, cc=cc)


def phase_xchg_G(nc, sh, regs, hid, G_in, G_out, FTm, VTm, TOKH, groups):
    P = Prog(nc, sh)
    hreg, tmpregs = regs
    ctr = [0]
    chunks = [(256 * c, 256) for c in range(12)] + [(3072, 160), (3232, 256), (3488, 256)]
    _chunked_allgather(P, 0, G_in, G_out, chunks, groups)
    HT2 = TOKH // 2
    for t in range(2):
        tc = slice(t * TOKH, (t + 1) * TOKH)
        dstA = FTm.ap()[0:1536, tc].rearrange("(s r) c -> s r c", s=6)
        _dyn_dma(P, hreg, tmpregs, ctr, dstA, G_out, (256 * t) * TOKH, 512 * TOKH, [[1024 * TOKH, 6], [TOKH, 256], [1, TOKH]],
                 ["gat"], [("FTm", 3 * t)], "rlA%d" % t)
        _dyn_dma(P, hreg, tmpregs, ctr, FTm.ap()[1536:1792, tc], G_out, (5632 + 256 * t) * TOKH, -512 * TOKH, [[TOKH, 256], [1, TOKH]],
                 ["gat"], [("FTm", 3 * t + 1)], "rlB%d" % t)
        P.dma(FTm.ap()[1792:1952, tc], G_out.ap()[6144 + 160 * t:6144 + 160 * t + 160, :], ["gat"], [("FTm", 3 * t + 2)], eng="pool", sk="rlC%d" % t)
        for q_ in range(2):
            _dyn_dma(P, hreg, tmpregs, ctr, VTm.ap()[t * TOKH + q_ * HT2:t * TOKH + (q_ + 1) * HT2, :], G_out,
                     (6464 + 512 * q_ + 256 * t) * TOKH, 256, [[512, HT2], [1, 256]], ["gat"], [("VTm", 2 * t + q_)], "rlv%d" % (2 * t + q_))
    P.emit()


def phase_xchg_Y(nc, sh, regs, hid, Y_in, Y_out, YBh, L, TOKH, groups):
    P = Prog(nc, sh)
    hreg, tmpregs = regs
    ctr = [0]
    _chunked_allgather(P, 1, Y_in, Y_out, [(256 * c, 256) for c in range(6)], groups)
    for r in range(2):
        dst = YBh.ap().rearrange("(n q) c -> n q c", n=3)[:, r * 256:(r + 1) * 256, :]
        _dyn_dma(P, hreg, tmpregs, ctr, dst, Y_out, (256 * r) * TOKH, 1536 * TOKH, [[512 * TOKH, 3], [TOKH, 256], [1, TOKH]],
                 ["gat"], [("YBh", r)], "rly%d" % r)
    P.emit()


def build_fused8(L=4096, depth=2, ncores=8):
    nc = bass.Bass("TRN2", target_bir_lowering=False)
    sh = Shared(nc)
    TOKH = L // 2
    groups = [[2 * i, 2 * i + 1] for i in range(ncores // 2)]
    hreg = sh.stack.enter_context(nc.gpsimd.register("hreg"))
    tmpregs = [sh.stack.enter_context(nc.gpsimd.register("tmpr%d" % i)) for i in range(2)]
    regs = (hreg, tmpregs)
    E = lambda n, s: nc.dram_tensor(n, list(s), F32, kind="ExternalInput")
    x_in = E("x_in", [TOKH, D])
    hid = nc.dram_tensor("hid", [1, 1], I32, kind="ExternalInput")
    x_out = nc.dram_tensor("x_final", [TOKH, D], F32, kind="ExternalOutput")
    G_in = nc.dram_tensor("G_in", [3744, TOKH], F32)
    G_out = nc.dram_tensor("G_out", [2 * 3744, TOKH], F32)
    GL = nc.dram_tensor("GL", [3072, TOKH], F32)
    FTm = nc.dram_tensor("FTm", [1952, L], F32)
    VTm = nc.dram_tensor("VTm", [L, 256], F32)
    Y_in = nc.dram_tensor("Y_in", [2 * 768, TOKH], F32)
    Y_out = nc.dram_tensor("Y_out", [4 * 768, TOKH], F32)
    ysl = lambda r0, n: [Y_in.ap()[j * 768 + r0:j * 768 + r0 + n, :] for j in range(2)]
    YBh = nc.dram_tensor("YBh", [1536, TOKH], F32)
    X1 = nc.dram_tensor("X1", [TOKH, D], F32)
    XS = nc.dram_tensor("XS", [TOKH, D], F32)
    VF = nc.dram_tensor("VF", [256, L], F32)
    VF2 = nc.dram_tensor("VF2", [256, L], F32)
    cs = {}
    LHr = min(L, 1024)
    for n, shp in (("c_ident", [128, 128]), ("c_cmask", [128, 4, 512]), ("c_b64", [128, 128]), ("c_ones", [128, 128]),
                   ("c_mask4", [128, 512]), ("c_sl", [128, 128]), ("c_reset", [128, LHr]), ("c_bones", [128, 128]), ("c_hsel", [128, 2])):
        cs[n] = E(n, shp)
    R = lambda r0, n: V(FTm.ap()[r0:r0 + n, :])
    phase_load_h(nc, sh, regs, hid)
    for i in range(depth):
        sfx = "_%d" % i
        xin = x_in if i == 0 else XS
        xo = x_out if i == depth - 1 else XS
        io = dict(x_tok=xin, w=E("w_in" + sfx, [D, NIN]), g=E("g1" + sfx, [128, 8]), c_ident=cs["c_ident"], G=G_in, GL=GL)
        build_L1G(TOKH, ctx=(nc, sh, io))
        phase_xchg_G(nc, sh, regs, hid, G_in, G_out, FTm, VTm, TOKH, groups)
        io = dict(uT=R(0, 256), prm=E("s5prm" + sfx, [128, 3, 8]), dcol=E("s5d" + sfx, [128, 2]), yT=VTok2(ysl(0, 256), TOKH))
        for n in ("bpr", "bpi", "cpr", "cpi"):
            io[n] = E("s5" + n + sfx, [128, 8, 128])
        build_L2(L, ctx=(nc, sh, io))
        hv = lambda ap_: V(ap_.rearrange("(h p) l -> h p l", h=2))
        io = dict(qT=hv(FTm.ap()[256:512, :]), kT=hv(FTm.ap()[512:768, :]), vtok=V(VTm.ap().rearrange("l (h d) -> h l d", h=2)),
                  gains=E("da_gains" + sfx, [128, 4]), lamv=E("da_lamv" + sfx, [128, 4, 64]), laminit=E("da_laminit" + sfx, [128, 1]),
                  c_cmask=cs["c_cmask"], c_b64=cs["c_b64"], c_ones=cs["c_ones"],
                  oT=VTok2([a_.rearrange("(h p) l -> h p l", h=2) for a_ in ysl(256, 256)], TOKH))
        build_L3(L, ctx=(nc, sh, io))
        vg = i > 0
        io = dict(zr=R(768, 256), zk=R(1024, 256), zv=R(1280, 256), zw=R(1792, 32), za=R(1824, 32), zg=R(1856, 96),
                  mu3=E("rw_mu3" + sfx, [128, 6]), mulo=E("rw_mulo" + sfx, [96, 3]), w2=E("rw_w2p" + sfx, [32, 256]),
                  a2=E("rw_a2p" + sfx, [32, 256]), g2=E("rw_g2p" + sfx, [96, 256]), pc=E("rw_pc" + sfx, [128, 10]),
                  lng=E("rw_lng" + sfx, [128, 256]), lnb=E("rw_lnb" + sfx, [128, 256]),
                  y_out=VTok2(ysl(512, 256), TOKH), vfirst_out=(VF2 if vg else VF))
        for n in ("c_mask4", "c_sl", "c_reset", "c_bones", "c_hsel", "c_ident"):
            io[n] = cs[n]
        if vg:
            io.update(zvfull=R(1280, 512), muvf=E("rw_muvf" + sfx, [128, 4]), v1=E("rw_v1p" + sfx, [128, 4, 32]),
                      v2=E("rw_v2p" + sfx, [32, 256]), v0=E("rw_v0p" + sfx, [128, 2]), vfirst_in=VF)
        build_L4(L, has_vgate=vg, ctx=(nc, sh, io), fm_out=True)
        phase_xchg_Y(nc, sh, regs, hid, Y_in, Y_out, YBh, L, TOKH, groups)
        io = dict(w_glu=E("w_glu" + sfx, [512, 512]), w_br=E("w_br" + sfx, [3, 512, 1024]), w_out=E("w_out" + sfx, [1024, 1024]),
                  x_tok=xin, ypreT=V(YBh.ap()[0:512, :]), oT=V(YBh.ap()[512:1024, :]), ycT=V(YBh.ap()[1024:1536, :]), glT=GL, x_out=X1)
        build_L5a(TOKH, 512, ctx=(nc, sh, io))
        io = dict(g2=E("g2n" + sfx, [128, 8]), w_fi=E("w_fi" + sfx, [1024, 2 * DFF]), w_fo=E("w_fo" + sfx, [DFF, 1024]), c_ident=cs["c_ident"],
                  x_tok=X1, x_out=xo)
        if i == depth - 1:
            io["_final"] = True
        build_L5b(TOKH, min(1024, TOKH), ctx=(nc, sh, io))
    sh.close()
    return nc


def fused8_inputs(inp, L, h):
    f = lambda a: np.ascontiguousarray(a, dtype=np.float32)
    d = {}
    d.update(at_consts())
    d.update(rw_consts(L))
    d["hid"] = np.array([[h]], np.int32)
    depth = inp["w_in"].shape[0]
    zL = np.zeros((L, 512), np.float32)
    zz = np.zeros((L, 1696), np.float32)
    for i in range(depth):
        sfx = "_%d" % i
        lam_init = 0.8 - 0.6 * float(np.exp(-0.3 * i))
        d["w_in" + sfx] = f(inp["w_in"][i])
        d["g1" + sfx] = f(np.asarray(inp["norm1_g"][i]).reshape(8, 128).T)
        p2 = dict(lam_re=np.asarray(inp["s5_lambda_re"][i]), lam_im=np.asarray(inp["s5_lambda_im"][i]), log_dt=np.asarray(inp["s5_log_dt"][i]),
                  b_re=np.asarray(inp["s5_b_re"][i]), b_im=np.asarray(inp["s5_b_im"][i]), c_re=np.asarray(inp["s5_c_re"][i]),
                  c_im=np.asarray(inp["s5_c_im"][i]), d=np.asarray(inp["s5_d"][i]))
        l2 = l2_inputs(zL, p2, h, L)
        for n in ("bpr", "bpi", "cpr", "cpi"):
            d["s5" + n + sfx] = l2[n]
        d["s5prm" + sfx], d["s5d" + sfx] = l2["prm"], l2["dcol"]
        p3 = dict(q_gain=np.asarray(inp["da_q_gain"][i]), k_gain=np.asarray(inp["da_k_gain"][i]), lam=np.asarray(inp["da_lambda"][i]),
                  subln_g=np.asarray(inp["da_subln_g"][i]))
        l3 = l3_inputs(zL, zL, zL, p3, 0, L, lam_init)
        d["da_gains" + sfx], d["da_lamv" + sfx], d["da_laminit" + sfx] = l3["gains"], l3["lamv"], l3["laminit"]
        p4 = dict(mu=np.asarray(inp["rw_mu"][i]), w0=np.asarray(inp["rw_w0"][i]), w2=np.asarray(inp["rw_w2"][i]), a0=np.asarray(inp["rw_a0"][i]),
                  a2=np.asarray(inp["rw_a2"][i]), g2=np.asarray(inp["rw_g2"][i]), k_k=np.asarray(inp["rw_k_k"][i]), k_a=np.asarray(inp["rw_k_a"][i]),
                  r_k=np.asarray(inp["rw_r_k"][i]), ln_g=np.asarray(inp["rw_ln_g"][i]), ln_b=np.asarray(inp["rw_ln_b"][i]))
        vg = i > 0
        if vg:
            p4.update(v0=np.asarray(inp["rw_v0"][i - 1]), v1=np.asarray(inp["rw_v1"][i - 1]), v2=np.asarray(inp["rw_v2"][i - 1]))
        l4 = l4_inputs(zz, p4, h, L, vfirst=(np.zeros((256, L), np.float32) if vg else None))
        for n, m in (("mu3", "rw_mu3"), ("mulo", "rw_mulo"), ("w2", "rw_w2p"), ("a2", "rw_a2p"), ("g2", "rw_g2p"), ("pc", "rw_pc"),
                     ("lng", "rw_lng"), ("lnb", "rw_lnb")):
            d[m + sfx] = l4[n]
        if vg:
            perm = np.concatenate([np.arange(256 * h, 256 * h + 256), np.arange(256 * (1 - h), 256 * (1 - h) + 256)])
            muv = np.asarray(inp["rw_mu"][i])[1024:1536][perm]
            d["rw_muvf" + sfx] = f(muv.reshape(4, 128).T)
            d["rw_v1p" + sfx] = f(np.asarray(inp["rw_v1"][i - 1])[perm].reshape(4, 128, 32).transpose(1, 0, 2))
            d["rw_v2p" + sfx] = l4["v2"]
            d["rw_v0p" + sfx] = l4["v0"]
        d["w_glu" + sfx] = f(inp["s5_w_glu"][i])
        d["w_br" + sfx] = f(inp["w_branch"][i])
        d["w_out" + sfx] = f(inp["w_out"][i])
        d["g2n" + sfx] = f(np.asarray(inp["norm2_g"][i]).reshape(8, 128).T)
        d["w_fi" + sfx] = f(inp["w_ffn_in"][i])
        d["w_fo" + sfx] = f(inp["w_ffn_out"][i])
    return d


def kernel_fused8(inp, runner=None):
    x = np.asarray(inp["x"], dtype=np.float32)
    B, L, _ = x.shape
    depth = inp["w_in"].shape[0]
    ncores = 2 * B
    nc = build_fused8(L, depth, ncores)
    TOKH = L // 2
    common = [fused8_inputs(inp, L, h) for h in range(2)]
    maps = [dict(common[c % 2], x_in=np.ascontiguousarray(x[c // 2, (c % 2) * TOKH:(c % 2 + 1) * TOKH])) for c in range(ncores)]
    if runner is None:
        res = run_bass_kernel_spmd(nc, maps, core_ids=list(range(ncores))).results
    else:
        res = runner(nc, maps)
    out = np.stack([np.concatenate([res[2 * b]["x_final"], res[2 * b + 1]["x_final"]], 0) for b in range(B)])
    return out.astype(np.float32)


def kernel(**inputs):
    return kernel_fused8(inputs)
```

```python
from contextlib import ExitStack
import numpy as np
import concourse.bass as bass
import concourse.mybir as mybir

F32 = mybir.dt.float32
BF16 = mybir.dt.bfloat16
ALU = mybir.AluOpType
AF = mybir.ActivationFunctionType

ENGS = ["pe", "act", "dve", "pool", "sp"]
SAME_ENG_SYNC = True


class Shared:
    NDQ = 92

    def __init__(self, nc):
        self.nc = nc
        self.stack = ExitStack()
        names = ["e_" + e for e in ENGS] + ["dq%d" % i for i in range(self.NDQ)] + ["cc0", "cc1"]
        self.sems = {n: self.stack.enter_context(nc.semaphore(n)) for n in names}
        self.cnt = {n: 0 for n in names}

    def close(self):
        self.stack.close()


class V:
    def __init__(self, ap):
        self._ap = ap

    def ap(self):
        return self._ap

    def __getitem__(self, idx):
        return self._ap[idx]


class VTok2:
    def __init__(self, aps, tokh):
        self.aps, self.tokh = aps, tokh

    def __getitem__(self, idx):
        idx = list(idx)
        sl = idx[-1]
        j = sl.start // self.tokh
        assert (sl.stop - 1) // self.tokh == j
        idx[-1] = slice(sl.start - j * self.tokh, sl.stop - j * self.tokh)
        return self.aps[j][tuple(idx)]


class Prog:
    _pid = [0]

    def __init__(self, nc, shared=None):
        self.shared = shared
        Prog._pid[0] += 1
        self.pfx = "p%d_" % Prog._pid[0]
        self.nc = nc
        self.ins = []
        self.byeng = {e: [] for e in ENGS}
        self.lastw = {}
        self.rds = {}
        self.stack = ExitStack()
        self.n_ps = 0

    def sb(self, name, shape, dt=F32):
        return self.stack.enter_context(self.nc.sbuf_tensor(self.pfx + name, list(shape), dt))

    def ps(self, name, shape, dt=F32):
        return self.stack.enter_context(self.nc.psum_tensor(self.pfx + name, list(shape), dt))

    def dram(self, name, shape, dt=F32, kind="Internal"):
        return self.nc.dram_tensor(name, list(shape), dt, kind=kind)

    @staticmethod
    def _ov(a, b):
        return a is None or b is None or a == b

    def _norm(self, keys):
        out = []
        for k in keys:
            if isinstance(k, tuple):
                out.append((k[0], None if k[0].startswith("ps") else k[1]))
            else:
                out.append((k, None))
        return out

    def op(self, eng, fn, reads=(), writes=(), dma=False, sk=None, cc=None):
        reads = self._norm(reads)
        writes = self._norm(writes)
        i = len(self.ins)
        deps = set()
        for (t, s) in reads:
            for s2, w in self.lastw.get(t, {}).items():
                if self._ov(s, s2):
                    deps.add(w)
        for (t, s) in writes:
            for s2, w in self.lastw.get(t, {}).items():
                if self._ov(s, s2):
                    deps.add(w)
            for s2, rl in self.rds.get(t, {}).items():
                if self._ov(s, s2):
                    deps.update(rl)
        deps.discard(i)
        for (t, s) in reads:
            self.rds.setdefault(t, {}).setdefault(s, []).append(i)
        for (t, s) in writes:
            d = self.lastw.setdefault(t, {})
            r = self.rds.setdefault(t, {})
            if s is None:
                d.clear()
                r.clear()
            else:
                r[s] = []
            d[s] = i
        if eng == "pe" or not SAME_ENG_SYNC:
            deps = {d for d in deps if self.ins[d]["eng"] != eng or self.ins[d]["dma"] or dma}
        self.ins.append(dict(eng=eng, fn=fn, deps=deps, dma=dma, cc=cc, key=(writes[0] if writes else None),
                             sk=(sk if sk is not None else (writes[0] if writes else None))))
        self.byeng[eng].append(i)
        return i

    def mm(self, out, lhsT, rhs, reads, writes, start=True, stop=True):
        return self.op("pe", lambda e: e.matmul(out, lhsT, rhs, start=start, stop=stop), reads, writes)

    def tr(self, out, in_, ident, reads, writes):
        return self.op("pe", lambda e: e.transpose(out, in_, ident), reads, writes)

    def act(self, out, in_, func, reads, writes, eng="act", **kw):
        return self.op(eng, lambda e: e.activation(out, in_, func, **kw), reads, writes)

    def tt(self, eng, out, in0, in1, op, reads, writes):
        return self.op(eng, lambda e: e.tensor_tensor(out, in0, in1, op), reads, writes)

    def ts(self, eng, out, in0, s1, s2, op0, op1, reads, writes, **kw):
        if op1 is None:
            return self.op(eng, lambda e: e.tensor_scalar(out, in0, s1, None, op0, **kw), reads, writes)
        return self.op(eng, lambda e: e.tensor_scalar(out, in0, s1, s2, op0, op1, **kw), reads, writes)

    def stt(self, out, in0, scalar, in1, op0, op1, reads, writes):
        return self.op("dve", lambda e: e.scalar_tensor_tensor(out, in0, scalar, in1, op0, op1), reads, writes)

    def cp(self, eng, out, in_, reads, writes):
        if eng == "act":
            return self.op(eng, lambda e: e.copy(out, in_), reads, writes)
        return self.op(eng, lambda e: e.tensor_copy(out, in_), reads, writes)

    def dma(self, out, in_, reads, writes, eng="sp", sk=None, **kw):
        return self.op(eng, lambda e: e.dma_start(out, in_, **kw), reads, writes, dma=True, sk=sk)

    def memset(self, eng, ap, val, writes):
        return self.op(eng, lambda e: e.memset(ap, val), (), writes)

    def emit(self, final_wait_keys=()):
        nc = self.nc
        ins = self.ins
        sh = self.shared
        own = sh is None
        if own:
            sh = Shared(nc)
        has_dep = [False] * len(ins)
        for i, d in enumerate(ins):
            for j in d["deps"]:
                has_dep[j] = True
        final_ids = [i for i, d in enumerate(ins) if d["dma"] and d["key"] is not None and d["key"][0] in final_wait_keys]
        for i, d in enumerate(ins):
            if d["dma"]:
                has_dep[i] = True
        for e in ENGS:
            for i in reversed(self.byeng[e]):
                if not ins[i]["dma"]:
                    has_dep[i] = True
                    break
        start = dict(sh.cnt)
        dma_sem_of = {}
        dma_eng = {}
        nsp, npool = [0], [0]
        sig = [None] * len(ins)
        for e in ENGS:
            for i in self.byeng[e]:
                d = ins[i]
                if not has_dep[i]:
                    continue
                if d["cc"] is not None:
                    sn = "cc%d" % d["cc"]
                    sh.cnt[sn] += 1
                    sig[i] = (sn, sh.cnt[sn])
                elif d["dma"]:
                    k = d["sk"]
                    if k not in dma_sem_of:
                        if e == "pool":
                            npool[0] += 1
                            assert npool[0] <= 34, "too many pool dma sem keys"
                            dma_sem_of[k] = "dq%d" % (Shared.NDQ - npool[0])
                        else:
                            nsp[0] += 1
                            assert nsp[0] <= Shared.NDQ - 34, "too many sp dma sem keys"
                            dma_sem_of[k] = "dq%d" % (nsp[0] - 1)
                        dma_eng[k] = e
                    assert dma_eng[k] == e, ("dma sem key used from two queues", k)
                    sn = dma_sem_of[k]
                    sh.cnt[sn] += 16
                    sig[i] = (sn, sh.cnt[sn])
                else:
                    sh.cnt["e_" + e] += 1
                    sig[i] = ("e_" + e, sh.cnt["e_" + e])
        sems = sh.sems
        block = self.stack.enter_context(nc.Block())
        engobj = dict(pe="tensor", act="scalar", dve="vector", pool="gpsimd", sp="sync")
        last_phase_final = [(sig[i][0], sig[i][1]) for i in final_ids]

        def make(e):
            def body(eng):
                waited = {}
                for sn, v in start.items():
                    if v > 0 and sn != "e_" + e:
                        eng.wait_ge(sems[sn], v)
                    waited[sn] = v
                for i in self.byeng[e]:
                    d = ins[i]
                    need = {}
                    for j in d["deps"]:
                        s_ = sig[j]
                        assert s_ is not None
                        if need.get(s_[0], 0) < s_[1]:
                            need[s_[0]] = s_[1]
                    for sn, v in need.items():
                        if waited.get(sn, 0) < v:
                            eng.wait_ge(sems[sn], v)
                            waited[sn] = v
                    inst = d["fn"](eng)
                    if sig[i] is not None:
                        inst.then_inc(sems[sig[i][0]], 16 if (d["dma"] and d["cc"] is None) else 1)
                if e == "sp":
                    for (sn, v) in last_phase_final:
                        if waited.get(sn, 0) < v:
                            eng.wait_ge(sems[sn], v)
                            waited[sn] = v
            return body

        for e in ENGS:
            getattr(block, engobj[e])(make(e))
        self.stack.close()
        if own:
            sh.close()
        return nc


from concourse.bass_utils import run_bass_kernel_spmd

NCORES = 8
D = 1024
NIN = 6816
EPS = 1e-6


def _ctx(ctx):
    if ctx is None:
        nc = bass.Bass("TRN2", target_bir_lowering=False)
        return nc, Prog(nc), None
    nc, shared, io = ctx
    return nc, Prog(nc, shared), io


def _mkI(nc, io):
    if io is None:
        return lambda n, s: nc.dram_tensor(n, list(s), F32, kind="ExternalInput")
    return lambda n, s: io[n]


def _mkO(nc, io, n, s):
    if io is None:
        return nc.dram_tensor(n, list(s), F32, kind="ExternalOutput")
    return io[n]


def _fin(P, nc, io, outs):
    if io is None:
        P.emit(final_wait_keys=outs)
    else:
        P.emit(final_wait_keys=(outs if io.get("_final") else ()))
    return nc


def ap3(t, off, dims):
    return bass.AP(t, off, [list(d) for d in dims])


def build_L1(TOK=2048):
    nc = bass.Bass("TRN2", target_bir_lowering=False)
    P = Prog(nc)
    NT = TOK // 128
    x_tok = nc.dram_tensor("x_tok", [TOK, D], F32, kind="ExternalInput")
    xT = nc.dram_tensor("xT", [D, TOK], F32, kind="ExternalInput")
    w = nc.dram_tensor("w", [D, NIN], F32, kind="ExternalInput")
    g = nc.dram_tensor("g", [128, 8], F32, kind="ExternalInput")
    proj = nc.dram_tensor("proj", [TOK, NIN], F32, kind="ExternalOutput")

    g_sb = P.sb("g_sb", [128, 8])
    rstd = P.sb("rstd", [128, NT])
    ss = P.sb("ss", [128, NT])
    xt_in = [P.sb("xt_in%d" % i, [128, D]) for i in range(2)]
    junk = P.sb("junk", [128, D])
    xTs = P.sb("xTs", [128, 8, TOK])
    xg = P.sb("xg", [128, 8, TOK], BF16)
    wf = [P.sb("wf%d" % i, [128, 8, 512]) for i in range(2)]
    wb = [P.sb("wb%d" % i, [128, 8, 512], BF16) for i in range(2)]
    ost = [P.sb("ost%d" % i, [128, 512]) for i in range(3)]
    pss = [P.ps("ps%d" % i, [128, 512]) for i in range(4)]

    P.dma(g_sb[:, :], g[:, :], [], ["g_sb"])
    for t in range(NT):
        b = xt_in[t % 2]
        bn = "xt_in%d" % (t % 2)
        P.dma(b[:, :], x_tok[t * 128:(t + 1) * 128, :], [], [bn], eng="sp")
        P.act(junk[:, :], b[:, :], AF.Square, [bn], ["junk", ("ss", t)], accum_out=ss[:, t:t + 1])
    P.ts("dve", rstd[:, :], ss[:, :], 1.0 / D, EPS, ALU.mult, ALU.add, ["ss"], ["rstd"])
    P.act(rstd[:, :], rstd[:, :], AF.Sqrt, ["rstd"], ["rstd"])
    P.op("dve", lambda e: e.reciprocal(rstd[:, :], rstd[:, :]), ["rstd"], ["rstd"])
    xTv = xT.ap().rearrange("(c p) t -> p c t", p=128)
    for c in range(8):
        P.dma(xTs[:, c, :], xTv[:, c, :], [], [("xTs", c)], eng="pool")
        P.ts("dve", xg[:, c, :], xTs[:, c, :], g_sb[:, c:c + 1], None, ALU.mult, None,
             [("xTs", c), "g_sb"], [("xg", c)])
    wv = w.ap().rearrange("(c p) n -> p c n", p=128)
    nblk = (NIN + 511) // 512
    k = 0
    for cb in range(nblk):
        c0 = cb * 512
        cw = min(512, NIN - c0)
        wfb, wbb = wf[cb % 2], wb[cb % 2]
        wfn, wbn = "wf%d" % (cb % 2), "wb%d" % (cb % 2)
        P.dma(wfb[:, :, :cw], wv[:, :, c0:c0 + cw], [], [wfn], eng="sp")
        for c in range(8):
            P.cp("act" if c % 2 else "dve", wbb[:, c, :cw], wfb[:, c, :cw], [wfn], [(wbn, c)])
        for t in range(NT):
            ps = pss[k % 4]
            psn = "ps%d" % (k % 4)
            for c in range(8):
                P.mm(ps[:, :cw], xg[:, c, t * 128:(t + 1) * 128], wbb[:, c, :cw],
                     [("xg", c), (wbn, c)], [psn], start=(c == 0), stop=(c == 7))
            o = ost[k % 3]
            on = "ost%d" % (k % 3)
            P.act(o[:, :cw], ps[:, :cw], AF.Copy, [psn, "rstd"], [on], scale=rstd[:, t:t + 1])
            P.dma(proj[t * 128:(t + 1) * 128, c0:c0 + cw], o[:, :cw], [on], [("proj", k)], eng="sp", sk=on)
            k += 1
    P.emit(final_wait_keys=("proj",))
    return nc


def build_L1F(L=4096, ctx=None):
    nc, P, io = _ctx(ctx)
    I = _mkI(nc, io)
    x_tok = I("x_tok", [L, D])
    w = I("w", [D, NIN])
    g = I("g", [128, 8])
    c_ident = I("c_ident", [128, 128])
    FT = _mkO(nc, io, "FT", [6304, L])
    VT = _mkO(nc, io, "VT", [L, 512])
    HT = min(L, 1024)
    NH = L // HT
    NTT = HT // 128
    NB = L // 512
    g_sb = P.sb("g_sb", [128, 8])
    P.dma(g_sb[:, :], g[:, :], [], ["g_sb"])
    identf = P.sb("identf", [128, 128])
    P.dma(identf[:, :], c_ident[:, :], [], ["identf"])
    identb = P.sb("identb", [128, 128], BF16)
    P.cp("dve", identb[:, :], identf[:, :], ["identf"], ["identb"])
    hT = P.sb("hT", [128, 8, L], BF16)
    x1 = [P.sb("x1_%d" % i, [128, D]) for i in range(3)]
    junk = P.sb("junk", [128, D], BF16)
    ss = P.sb("ss", [128, NTT])
    hb = [P.sb("hb%d" % i, [128, D], BF16) for i in range(2)]
    wf = [P.sb("wf%d" % i, [128, 8, 512]) for i in range(2)]
    wb = [P.sb("wb%d" % i, [128, 8, 512], BF16) for i in range(2)]
    ost = [P.sb("ost%d" % i, [128, 512]) for i in range(4)]
    pss = [P.ps("ps%d" % i, [128, 512]) for i in range(6)]
    psb = P.ps("psb", [128, 1024], BF16)
    for half in range(NH):
        t0 = half * HT
        xs_ = []
        for tt_ in range(NTT):
            xi, xn = x1[tt_ % 3], "x1_%d" % (tt_ % 3)
            P.dma(xi[:, :], x_tok[t0 + tt_ * 128:t0 + (tt_ + 1) * 128, :], [], [xn], eng="sp")
            P.act(junk[:, :], xi[:, :], AF.Square, [xn], ["junk", ("ss", tt_)], accum_out=ss[:, tt_:tt_ + 1])
        P.ts("dve", ss[:, :], ss[:, :], 1.0 / D, EPS, ALU.mult, ALU.add, ["ss"], ["ss"])
        P.act(ss[:, :], ss[:, :], AF.Sqrt, ["ss"], ["ss"])
        P.op("dve", lambda e: e.reciprocal(ss[:, :], ss[:, :]), ["ss"], ["ss"])
        for tt_ in range(NTT):
            xi, xn = x1[tt_ % 3], "x1_%d" % (tt_ % 3)
            P.dma(xi[:, :], x_tok[t0 + tt_ * 128:t0 + (tt_ + 1) * 128, :], [], [xn], eng="sp")
            h_, hn = hb[tt_ % 2], "hb%d" % (tt_ % 2)
            P.act(h_[:, :], xi[:, :], AF.Copy, [xn, "ss"], [hn], scale=ss[:, tt_:tt_ + 1])
            for c in range(8):
                P.tr(psb[:, c * 128:(c + 1) * 128], h_[:, c * 128:(c + 1) * 128], identb[:, :], [hn, "identb"], ["psb"])
            for c in range(8):
                P.ts("dve", hT[:, c, t0 + tt_ * 128:t0 + (tt_ + 1) * 128], psb[:, c * 128:(c + 1) * 128], g_sb[:, c:c + 1], None, ALU.mult, None,
                     ["psb", "g_sb"], [("hT", c)])
    wv = w.ap().rearrange("(c p) n -> p c n", p=128)
    k = 0
    nblk = (NIN + 511) // 512
    for cb in range(nblk):
        c0 = cb * 512
        cw = min(512, NIN - c0)
        wfb, wbb = wf[cb % 2], wb[cb % 2]
        wfn, wbn = "wf%d" % (cb % 2), "wb%d" % (cb % 2)
        P.dma(wfb[:, :, :cw], wv[:, :, c0:c0 + cw], [], [wfn], eng=("sp" if cb % 2 else "pool"))
        for c in range(8):
            P.cp("act" if c % 2 else "dve", wbb[:, c, :cw], wfb[:, c, :cw], [wfn], [(wbn, c)])
        if c0 == 1536:
            for t in range(L // 128):
                ps, psn = pss[k % 6], "ps%d" % (k % 6)
                for c in range(8):
                    P.mm(ps[:, :], hT[:, c, t * 128:(t + 1) * 128], wbb[:, c, :], [("hT", c), (wbn, c)], [psn], start=(c == 0), stop=(c == 7))
                o, on = ost[k % 4], "ost%d" % (k % 4)
                P.cp("act" if k % 2 else "dve", o[:, :], ps[:, :], [psn], [on])
                P.dma(VT[t * 128:(t + 1) * 128, :], o[:, :], [on], [("VT", t)], sk=on, eng=("sp" if k % 2 else "pool"))
                k += 1
            continue
        r0 = c0 if c0 < 1536 else c0 - 512
        for ct in range((cw + 127) // 128):
            mw = min(128, cw - ct * 128)
            for b5 in range(NB):
                bs = slice(b5 * 512, b5 * 512 + 512)
                ps, psn = pss[k % 6], "ps%d" % (k % 6)
                for c in range(8):
                    P.mm(ps[:mw, :], wbb[:, c, ct * 128:ct * 128 + mw], hT[:, c, bs], [("hT", c), (wbn, c)], [psn], start=(c == 0), stop=(c == 7))
                o, on = ost[k % 4], "ost%d" % (k % 4)
                P.cp("act" if k % 2 else "dve", o[:mw, :], ps[:mw, :], [psn], [on])
                P.dma(FT[r0 + ct * 128:r0 + ct * 128 + mw, bs], o[:mw, :], [on], [("FT", k)], sk=on, eng=("sp" if k % 2 else "pool"))
                k += 1
    return _fin(P, nc, io, ("FT", "VT"))


def build_L1G(L=2048, ctx=None):
    nc, P, io = _ctx(ctx)
    I = _mkI(nc, io)
    x_tok = I("x_tok", [L, D])
    w = I("w", [D, NIN])
    g = I("g", [128, 8])
    c_ident = I("c_ident", [128, 128])
    G = _mkO(nc, io, "G", [3744, L])
    GL = _mkO(nc, io, "GL", [3072, L])
    VT = V(G.ap()[3232:3744, :].rearrange("a (b c) -> (a b) c", c=512))
    HT = min(L, 1024)
    NH = L // HT
    NTT = HT // 128
    NB = L // 512
    g_sb = P.sb("g_sb", [128, 8])
    P.dma(g_sb[:, :], g[:, :], [], ["g_sb"])
    identf = P.sb("identf", [128, 128])
    P.dma(identf[:, :], c_ident[:, :], [], ["identf"])
    identb = P.sb("identb", [128, 128], BF16)
    P.cp("dve", identb[:, :], identf[:, :], ["identf"], ["identb"])
    hT = P.sb("hT", [128, 8, L], BF16)
    x1 = [P.sb("x1_%d" % i, [128, D]) for i in range(3)]
    junk = P.sb("junk", [128, D], BF16)
    ss = P.sb("ss", [128, NTT])
    hb = [P.sb("hb%d" % i, [128, D], BF16) for i in range(2)]
    wf = [P.sb("wf%d" % i, [128, 8, 512]) for i in range(2)]
    wb = [P.sb("wb%d" % i, [128, 8, 512], BF16) for i in range(2)]
    ost = [P.sb("ost%d" % i, [128, 512]) for i in range(4)]
    pss = [P.ps("ps%d" % i, [128, 512]) for i in range(6)]
    psb = P.ps("psb", [128, 1024], BF16)
    for half in range(NH):
        t0 = half * HT
        xs_ = []
        for tt_ in range(NTT):
            xi, xn = x1[tt_ % 3], "x1_%d" % (tt_ % 3)
            P.dma(xi[:, :], x_tok[t0 + tt_ * 128:t0 + (tt_ + 1) * 128, :], [], [xn], eng="sp")
            P.act(junk[:, :], xi[:, :], AF.Square, [xn], ["junk", ("ss", tt_)], accum_out=ss[:, tt_:tt_ + 1])
        P.ts("dve", ss[:, :], ss[:, :], 1.0 / D, EPS, ALU.mult, ALU.add, ["ss"], ["ss"])
        P.act(ss[:, :], ss[:, :], AF.Sqrt, ["ss"], ["ss"])
        P.op("dve", lambda e: e.reciprocal(ss[:, :], ss[:, :]), ["ss"], ["ss"])
        for tt_ in range(NTT):
            xi, xn = x1[tt_ % 3], "x1_%d" % (tt_ % 3)
            P.dma(xi[:, :], x_tok[t0 + tt_ * 128:t0 + (tt_ + 1) * 128, :], [], [xn], eng="sp")
            h_, hn = hb[tt_ % 2], "hb%d" % (tt_ % 2)
            P.act(h_[:, :], xi[:, :], AF.Copy, [xn, "ss"], [hn], scale=ss[:, tt_:tt_ + 1])
            for c in range(8):
                P.tr(psb[:, c * 128:(c + 1) * 128], h_[:, c * 128:(c + 1) * 128], identb[:, :], [hn, "identb"], ["psb"])
            for c in range(8):
                P.ts("dve", hT[:, c, t0 + tt_ * 128:t0 + (tt_ + 1) * 128], psb[:, c * 128:(c + 1) * 128], g_sb[:, c:c + 1], None, ALU.mult, None,
                     ["psb", "g_sb"], [("hT", c)])
    wv = w.ap().rearrange("(c p) n -> p c n", p=128)
    k = 0
    nblk = (NIN + 511) // 512
    for cb in range(nblk):
        c0 = cb * 512
        cw = min(512, NIN - c0)
        wfb, wbb = wf[cb % 2], wb[cb % 2]
        wfn, wbn = "wf%d" % (cb % 2), "wb%d" % (cb % 2)
        P.dma(wfb[:, :, :cw], wv[:, :, c0:c0 + cw], [], [wfn], eng=("sp" if cb % 2 else "pool"))
        for c in range(8):
            P.cp("act" if c % 2 else "dve", wbb[:, c, :cw], wfb[:, c, :cw], [wfn], [(wbn, c)])
        if c0 == 1536:
            for t in range(L // 128):
                ps, psn = pss[k % 6], "ps%d" % (k % 6)
                for c in range(8):
                    P.mm(ps[:, :], hT[:, c, t * 128:(t + 1) * 128], wbb[:, c, :], [("hT", c), (wbn, c)], [psn], start=(c == 0), stop=(c == 7))
                o, on = ost[k % 4], "ost%d" % (k % 4)
                P.cp("act" if k % 2 else "dve", o[:, :], ps[:, :], [psn], [on])
                P.dma(VT[t * 128:(t + 1) * 128, :], o[:, :], [on], [("VT", t)], sk=on, eng=("sp" if k % 2 else "pool"))
                k += 1
            continue
        r0 = c0 if c0 < 1536 else c0 - 512
        for ct in range((cw + 127) // 128):
            mw = min(128, cw - ct * 128)
            for b5 in range(NB):
                bs = slice(b5 * 512, b5 * 512 + 512)
                ps, psn = pss[k % 6], "ps%d" % (k % 6)
                for c in range(8):
                    P.mm(ps[:mw, :], wbb[:, c, ct * 128:ct * 128 + mw], hT[:, c, bs], [("hT", c), (wbn, c)], [psn], start=(c == 0), stop=(c == 7))
                o, on = ost[k % 4], "ost%d" % (k % 4)
                P.cp("act" if k % 2 else "dve", o[:mw, :], ps[:mw, :], [psn], [on])
                ra, rb = r0 + ct * 128, r0 + ct * 128 + mw
                if ra < 3232:
                    n_ = min(rb, 3232) - ra
                    P.dma(G[ra:ra + n_, bs], o[0:n_, :], [on], [("G", k)], sk=on, eng=("sp" if k % 2 else "pool"))
                if rb > 3232:
                    p0 = max(ra, 3232) - ra
                    P.dma(GL[max(ra, 3232) - 3232:rb - 3232, bs], o[p0:mw, :], [on], [("GL", k)], sk=on, eng=("sp" if k % 2 else "pool"))
                k += 1
    return _fin(P, nc, io, ("G", "GL", "VT"))


def rw_consts(L):
    T = 64
    i = np.arange(128)
    same = (i[:, None] // T) == (i[None, :] // T)
    su = (same & (i[:, None] < i[None, :])).astype(np.float32)
    iu = (same & (i[:, None] <= i[None, :])).astype(np.float32)
    sl = (same & (i[:, None] > i[None, :])).astype(np.float32)
    mask4 = np.concatenate([su, iu, su, iu], 1)
    reset = np.ones((128, min(L, 1024)), np.float32)
    reset[:, ::T] = 0.0
    bones = ((i[:, None] // 64) == (i[None, :] // 64)).astype(np.float32)
    hsel = np.zeros((128, 2), np.float32)
    hsel[:64, 0] = 1
    hsel[64:, 1] = 1
    return dict(c_mask4=mask4, c_sl=sl, c_reset=reset, c_bones=bones, c_hsel=hsel,
                c_ident=np.eye(128, dtype=np.float32))


def build_L4(L=4096, has_vgate=False, LH=None, ctx=None, fm_out=False):
    nc, P, io = _ctx(ctx)
    NB = L // 128
    NCH = L // 64
    NCB = (L + 511) // 512
    I = _mkI(nc, io)
    zr, zk, zv = I("zr", [256, L]), I("zk", [256, L]), I("zv", [256, L])
    zw, za, zg = I("zw", [32, L]), I("za", [32, L]), I("zg", [96, L])
    mu3 = I("mu3", [128, 6])
    mulo = I("mulo", [96, 3])
    w2, a2, g2 = I("w2", [32, 256]), I("a2", [32, 256]), I("g2", [96, 256])
    pc = I("pc", [128, 10])
    lng, lnb = I("lng", [128, 256]), I("lnb", [128, 256])
    c_mask4, c_sl, c_reset = I("c_mask4", [128, 512]), I("c_sl", [128, 128]), I("c_reset", [128, min(L, 1024)])
    c_bones, c_hsel, c_ident = I("c_bones", [128, 128]), I("c_hsel", [128, 2]), I("c_ident", [128, 128])
    if has_vgate:
        zvfull = I("zvfull", [512, L])
        muvf = I("muvf", [128, 4])
        v1 = I("v1", [128, 4, 32])
        v2 = I("v2", [32, 256])
        v0 = I("v0", [128, 2])
        vfirst_in = I("vfirst_in", [256, L])
    y_out = _mkO(nc, io, "y_out", ([256, L] if fm_out else [L, 256]))
    vfirst_out = _mkO(nc, io, "vfirst_out", [256, L])

    def ld(name, src, shape, dt=F32, eng="sp"):
        t = P.sb(name, shape, dt)
        if len(shape) == 2:
            P.dma(t[:, :], src[:, :], [], [name], eng=eng)
        else:
            P.dma(t[:, :, :], src[:, :, :], [], [name], eng=eng)
        return t
    mu3s, mulos = ld("mu3s", mu3, [128, 6]), ld("mulos", mulo, [96, 3])
    pcs = ld("pcs", pc, [128, 10])
    lngs, lnbs = ld("lngs", lng, [128, 256]), ld("lnbs", lnb, [128, 256])
    mask4, msl, reset = ld("mask4", c_mask4, [128, 512]), ld("msl", c_sl, [128, 128]), ld("reset", c_reset, [128, min(L, 1024)])
    bones, hself, identf = ld("bones", c_bones, [128, 128]), ld("hself", c_hsel, [128, 2]), ld("identf", c_ident, [128, 128])
    w2f, a2f, g2f = ld("w2f", w2, [32, 256]), ld("a2f", a2, [32, 256]), ld("g2f", g2, [96, 256])
    identb = P.sb("identb", [128, 128], BF16)
    hselb = P.sb("hselb", [128, 2], BF16)
    w2b, a2b, g2b = P.sb("w2b", [32, 256], BF16), P.sb("a2b", [32, 256], BF16), P.sb("g2b", [96, 256], BF16)
    P.cp("dve", identb[:, :], identf[:, :], ["identf"], ["identb"])
    P.cp("dve", hselb[:, :], hself[:, :], ["hself"], ["hselb"])
    P.cp("dve", w2b[:, :], w2f[:, :], ["w2f"], ["w2b"])
    P.cp("dve", a2b[:, :], a2f[:, :], ["a2f"], ["a2b"])
    P.cp("dve", g2b[:, :], g2f[:, :], ["g2f"], ["g2b"])

    LH = LH or min(L, 1024)
    NQH = L // LH
    NCBH = (LH + 511) // 512
    S = [P.sb("S%d" % i, [128, LH + 1]) for i in range(5)]
    SN = ["S%d" % i for i in range(5)]
    AT, BT, KT, RT, VB, RK = [P.sb(n, [128, 2, L], BF16) for n in ("AT", "BT", "KT", "RT", "VB", "RK")]
    gam = P.sb("gam", [128, 2, NCH])
    lo_w, lo_a, lo_g = P.sb("lo_w", [32, LH], BF16), P.sb("lo_a", [32, LH], BF16), P.sb("lo_g", [96, L], BF16)
    resetb = P.sb("resetb", [128, LH], BF16)
    P.cp("dve", resetb[:, :], reset[:, 0:LH], ["reset"], ["resetb"])
    pss = [P.ps("ps%d" % i, [128, 512]) for i in range(7)]
    psb = P.ps("psb", [128, 1024], BF16)
    pk = [0]

    def nextps():
        k = pk[0] % 4
        pk[0] += 1
        return pss[k], "ps%d" % k

    def shift_mix(dst, dn, src_rows, rows, mucol, scratch, scn, raw, rawn, t0, eng="sp"):
        if t0 == 0:
            P.memset("pool", raw[:rows, 0:1], 0.0, [(rawn, 0)])
            P.dma(raw[:rows, 1:LH + 1], src_rows[:, 0:LH], [], [(rawn, 1)], eng=eng)
        else:
            P.dma(raw[:rows, 0:LH + 1], src_rows[:, t0 - 1:t0 + LH], [], [rawn], eng=eng)
        P.tt("dve", scratch[:rows, 0:LH], raw[:rows, 0:LH], raw[:rows, 1:LH + 1], ALU.subtract, [rawn], [scn])
        P.stt(dst[:rows, 0:LH], scratch[:rows, 0:LH], mucol, raw[:rows, 1:LH + 1], ALU.mult, ALU.add, [scn, rawn, "mu3s", "mulos", "muvfs"], [dn])

    if has_vgate:
        v1f = ld("v1f", v1, [128, 4, 32])
        v1b = P.sb("v1b", [128, 4, 32], BF16)
        P.cp("dve", v1b[:, :, :], v1f[:, :, :], ["v1f"], ["v1b"])
        v2f = ld("v2f", v2, [32, 256])
        v2b = P.sb("v2b", [32, 256], BF16)
        P.cp("dve", v2b[:, :], v2f[:, :], ["v2f"], ["v2b"])
        v0s = ld("v0s", v0, [128, 2])
        muvfs = ld("muvfs", muvf, [128, 4])
        vfb = P.sb("vfb", [128, 4, LH], BF16)
        t1b = P.sb("t1b", [32, LH], BF16)

    W = slice(0, LH)
    for hq in range(NQH):
        t0 = hq * LH
        TS = slice(t0, t0 + LH)
        shift_mix(S[2], SN[2], zw, 32, mulos[:32, 0:1], S[1], SN[1], S[0], SN[0], t0)
        P.act(lo_w[:, :], S[2][:32, W], AF.Tanh, [SN[2]], ["lo_w"])
        shift_mix(S[2], SN[2], za, 32, mulos[:32, 1:2], S[1], SN[1], S[0], SN[0], t0)
        P.cp("dve", lo_a[:, :], S[2][:32, W], [SN[2]], ["lo_a"])
        shift_mix(S[2], SN[2], zg, 96, mulos[:96, 2:3], S[1], SN[1], S[0], SN[0], t0)
        P.act(lo_g[:, TS], S[2][:96, W], AF.Sigmoid, [SN[2]], [("lo_g", hq)])
        if has_vgate:
            for c in range(4):
                shift_mix(S[2], SN[2], zvfull[c * 128:(c + 1) * 128, :], 128, muvfs[:, c:c + 1], S[1], SN[1], S[0], SN[0], t0)
                P.cp("act", vfb[:, c, :], S[2][:, W], [SN[2]], [("vfb", c)])
            for cb in range(NCBH):
                c0 = cb * 512
                cw = min(512, LH - c0)
                ps, psn = nextps()
                for c in range(4):
                    P.mm(ps[:32, :cw], v1b[:, c, :], vfb[:, c, c0:c0 + cw], ["v1b", ("vfb", c)], [psn], start=(c == 0), stop=(c == 3))
                P.cp("act", t1b[:, c0:c0 + cw], ps[:32, :cw], [psn], [("t1b", cb)])
        for p in range(2):
            rows = slice(p * 128, (p + 1) * 128)
            col = lambda j: pcs[:, 2 * j + p:2 * j + p + 1]
            shift_mix(S[2], SN[2], zv[rows, :], 128, mu3s[:, 4 + p:5 + p], S[1], SN[1], S[0], SN[0], t0)
            if has_vgate:
                for cb in range(NCBH):
                    c0 = cb * 512
                    cw = min(512, LH - c0)
                    ps, psn = nextps()
                    P.mm(ps[:, :cw], v2b[:, p * 128:(p + 1) * 128], t1b[:, c0:c0 + cw], ["v2b", ("t1b", cb)], [psn])
                    P.act(S[3][:, c0:c0 + cw], ps[:, :cw], AF.Sigmoid, [psn, "v0s"], [(SN[3], cb)], bias=v0s[:, p:p + 1])
                P.dma(S[0][:, W], vfirst_in[rows, TS], [], [SN[0]])
                P.tt("dve", S[1][:, W], S[0][:, W], S[2][:, W], ALU.subtract, [SN[0], SN[2]], [SN[1]])
                P.tt("dve", S[1][:, W], S[1][:, W], S[3][:, W], ALU.mult, [SN[1], SN[3]], [SN[1]])
                P.tt("dve", S[2][:, W], S[2][:, W], S[1][:, W], ALU.add, [SN[2], SN[1]], [SN[2]])
                P.dma(vfirst_out[rows, TS], S[0][:, W], [SN[0]], [("vfirst_out", hq * 2 + p)], sk="vfo")
            else:
                P.dma(vfirst_out[rows, TS], S[2][:, W], [SN[2]], [("vfirst_out", hq * 2 + p)], sk="vfo")
            P.cp("act", VB[:, p, TS], S[2][:, W], [SN[2]], [("VB", p)])
            for cb in range(NCBH):
                c0 = cb * 512
                cw = min(512, LH - c0)
                ps, psn = nextps()
                P.mm(ps[:, :cw], a2b[:, p * 128:(p + 1) * 128], lo_a[:, c0:c0 + cw], ["a2b", "lo_a"], [psn])
                P.act(S[3][:, c0:c0 + cw], ps[:, :cw], AF.Sigmoid, [psn, "pcs"], [(SN[3], cb)], bias=col(1))
                ps, psn = nextps()
                P.mm(ps[:, :cw], w2b[:, p * 128:(p + 1) * 128], lo_w[:, c0:c0 + cw], ["w2b", "lo_w"], [psn])
                P.act(S[4][:, c0:c0 + cw], ps[:, :cw], AF.Sigmoid, [psn, "pcs"], [(SN[4], cb)], bias=col(0))
            P.ts("dve", S[4][:, W], S[4][:, W], -float(np.exp(-0.5)), None, ALU.mult, None, [SN[4]], [SN[4]])
            shift_mix(S[2], SN[2], zk[rows, :], 128, mu3s[:, 2 + p:3 + p], S[1], SN[1], S[0], SN[0], t0)
            P.ts("dve", S[0][:, W], S[2][:, W], col(2), None, ALU.mult, None, [SN[2], "pcs"], [SN[0]])
            P.act(S[1][:, W], S[0][:, W], AF.Square, [SN[0]], [SN[1]])
            for cb in range(NCBH):
                c0 = cb * 512
                cw = min(512, LH - c0)
                ps, psn = nextps()
                P.mm(ps[:, :cw], bones[:, :], S[1][:, c0:c0 + cw], ["bones", SN[1]], [psn])
                P.ts("dve", S[1][:, c0:c0 + cw], ps[:, :cw], 1e-24, None, ALU.max, None, [psn], [(SN[1], cb)])
            P.act(S[1][:, W], S[1][:, W], AF.Sqrt, [SN[1]], [SN[1]])
            P.op("dve", lambda e: e.reciprocal(S[1][:, W], S[1][:, W]), [SN[1]], [SN[1]])
            P.tt("dve", S[0][:, W], S[0][:, W], S[1][:, W], ALU.mult, [SN[0], SN[1]], [SN[0]])
            P.ts("dve", S[1][:, W], S[3][:, W], -1.0, col(3), ALU.add, ALU.mult, [SN[3], "pcs"], [SN[1]])
            P.stt(S[2][:, W], S[1][:, W], 1.0, S[2][:, W], ALU.add, ALU.mult, [SN[1], SN[2]], [SN[2]])
            P.op("dve", lambda e: e.tensor_tensor_scan(S[1][:, W], resetb[:, :], S[4][:, W], 0.0, ALU.mult, ALU.add),
                 ["resetb", SN[4]], [SN[1]])
            P.tt("dve", S[4][:, W], S[1][:, W], S[4][:, W], ALU.subtract, [SN[1], SN[4]], [SN[4]])
            P.act(S[4][:, W], S[4][:, W], AF.Exp, [SN[4]], [SN[4]])
            P.stt(AT[:, p, TS], S[0][:, W], -1.0, S[4][:, W], ALU.mult, ALU.mult, [SN[0], SN[4]], [("AT", p)])
            P.act(S[4][:, W], S[1][:, W], AF.Exp, [SN[1]], [SN[4]], scale=-1.0)
            P.tt("dve", S[0][:, W], S[0][:, W], S[3][:, W], ALU.mult, [SN[0], SN[3]], [SN[0]])
            P.tt("dve", BT[:, p, TS], S[0][:, W], S[4][:, W], ALU.mult, [SN[0], SN[4]], [("BT", p)])
            P.tt("pool", KT[:, p, TS], S[2][:, W], S[4][:, W], ALU.mult, [SN[2], SN[4]], [("KT", p)])
            P.act(S[4][:, W], S[1][:, W], AF.Exp, [SN[1]], [SN[4]])
            gv = ap3(S[4], 63, [[LH + 1, 128], [64, LH // 64]])
            P.cp("dve", gam[:, p, t0 // 64:(t0 + LH) // 64], gv, [SN[4]], [("gam", p)])
            shift_mix(S[0], SN[0], zr[rows, :], 128, mu3s[:, p:p + 1], S[1], SN[1], S[3], SN[3], t0)
            P.tt("dve", RT[:, p, TS], S[0][:, W], S[4][:, W], ALU.mult, [SN[0], SN[4]], [("RT", p)])
            P.stt(RK[:, p, TS], S[0][:, W], col(4), S[2][:, W], ALU.mult, ALU.mult, [SN[0], SN[2], "pcs"], [("RK", p)])

    A4 = [P.sb("A4_%d" % i, [128, 512], BF16) for i in range(4)]
    N0s = [P.sb("N0s_%d" % i, [128, 128], BF16) for i in range(4)]
    NL = [[P.sb("NL_%d_%d" % (i, j), [128, 256], BF16) for j in range(2)] for i in range(4)]
    TOK = [P.sb("TOK%d" % i, [128, 4, 128], BF16) for i in range(2)]
    Zb = [[P.sb("Zb_%d_%d" % (i, j), [128, 128], BF16) for j in range(2)] for i in range(4)]
    W1P = [P.sb("W1P%d" % i, [128, 128], BF16) for i in range(4)]
    XTP = [[P.sb("XTP_%d_%d" % (i, c), [128, 128]) for c in range(2)] for i in range(4)]
    Qg = [[P.sb("Qg_%d_%d" % (i, c), [128, 64]) for c in range(2)] for i in range(4)]
    RM = [[P.sb("RM_%d_%d" % (i, c), [128, 128], BF16) for c in range(2)] for i in range(4)]
    Hf = [[P.sb("Hf_%d_%d" % (p, h), [128, 64]) for h in range(2)] for p in range(2)]
    Hb = [[[P.sb("Hb_%d_%d_%d" % (p, h, c), [128, 64], BF16) for c in range(2)] for h in range(2)] for p in range(2)]
    Yall = [P.sb("Yall%d" % i, [128, 256]) for i in range(2)]
    rkc = [P.sb("rkc%d" % i, [128, 4]) for i in range(2)]
    gtk = [P.sb("gtk%d" % i, [128, 256]) for i in range(2)]
    st = P.sb("st", [128, 16])
    sq = P.sb("sq", [128, 256])
    yo = [P.sb("yo%d" % i, [128, 256]) for i in range(2)]
    yfm = [P.sb("yfm%d" % i, [128, 2, 128]) for i in range(2)]
    for i in range(4):
        P.memset("pool", W1P[i][:, :], 0.0, ["W1P%d" % i])
        for c in range(2):
            P.memset("pool", XTP[i][c][:, :], 0.0, ["XTP_%d_%d" % (i, c)])
            P.memset("pool", RM[i][c][:, :], 0.0, ["RM_%d_%d" % (i, c)])
    for p in range(2):
        for h in range(2):
            P.memset("pool", Hf[p][h][:, :], 0.0, ["Hf_%d_%d" % (p, h)])
            P.memset("pool", Hb[p][h][0][:, :], 0.0, ["Hb_%d_%d_0" % (p, h)])
    pk6 = [0]

    def nps():
        k = pk6[0] % 6
        pk6[0] += 1
        return pss[k], "ps%d" % k

    def unit(p, h, blk, Y, Yn):
        us = 2 * p + h
        cols = slice(blk * 128, (blk + 1) * 128)
        tk, tkn = TOK[p], "TOK%d" % p
        o = 64 * h
        hs = slice(o, o + 64)
        bt, kt_, at, rt = BT[hs, p, cols], KT[hs, p, cols], AT[hs, p, cols], RT[hs, p, cols]
        a4, a4n = A4[us], "A4_%d" % us
        n0, n0n = N0s[us], "N0s_%d" % us
        p1, p1n = nps()
        P.mm(p1[:, 0:128], bt, at, [("BT", p), ("AT", p)], [p1n])
        P.mm(p1[:, 128:256], bt, rt, [("BT", p), ("RT", p)], [p1n])
        P.mm(p1[:, 256:384], kt_, at, [("KT", p), ("AT", p)], [p1n])
        P.mm(p1[:, 384:512], kt_, rt, [("KT", p), ("RT", p)], [p1n])
        P.tt("dve", a4[:, :], p1[:, :], mask4[:, :], ALU.mult, [p1n, "mask4"], [a4n])
        p2, p2n = nps()
        P.mm(p2[:, 0:128], at, bt, [("BT", p), ("AT", p)], [p2n])
        P.tt("dve", n0[:, :], p2[:, 0:128], msl[:, :], ALU.mult, [p2n, "msl"], [n0n])
        yield
        L0, ArbT, AakT, ArkT = a4[:, 0:128], a4[:, 128:256], a4[:, 256:384], a4[:, 384:512]
        zc, zcn = Zb[us][0], "Zb_%d_0" % us
        p3, p3n = nps()
        P.mm(p3[:, 0:64], AakT, tk[:, 2, hs], [a4n, tkn], [p3n])
        P.cp("dve", zc[:, 0:64], tk[:, 3, hs], [tkn], [(zcn, 0)])
        P.cp("act", zc[:, 64:128], p3[:, 0:64], [p3n], [(zcn, 1)])
        yield
        Nj, Lj, njn, ljn = n0[:, :], L0, n0n, a4n
        zi = 0
        for j in range(6):
            pz, pzn = nps()
            zc, zcn = Zb[us][zi], "Zb_%d_%d" % (us, zi)
            zn_, znn = Zb[us][1 - zi], "Zb_%d_%d" % (us, 1 - zi)
            P.mm(pz[:, 0:128], Lj, zc[:, :], [ljn, zcn], [pzn])
            if j < 5:
                pq, pqn = nps()
                nl, nln = NL[us][j % 2], "NL_%d_%d" % (us, j % 2)
                if j < 4:
                    P.mm(pq[:, 0:128], Lj, Nj, [ljn, njn], [pqn])
                P.mm(pq[:, 128:256], Nj, Lj, [ljn, njn], [pqn])
            P.tt("dve", zn_[:, :], pz[:, 0:128], zc[:, :], ALU.add, [pzn, zcn], [znn])
            if j == 5:
                P.cp("pool", W1P[us][:, hs], zn_[:, 0:64], [znn], ["W1P%d" % us])
            else:
                if j < 4:
                    P.cp("act", nl[:, :], pq[:, 0:256], [pqn], [nln])
                else:
                    P.cp("act", nl[:, 128:256], pq[:, 128:256], [pqn], [nln])
                Nj, Lj, njn, ljn = nl[:, 0:128], nl[:, 128:256], nln, nln
            zi = 1 - zi
            yield
        Zf, Zfn = Zb[us][zi], "Zb_%d_%d" % (us, zi)
        U0 = Zf[:, 64:128]
        w1p, w1n = W1P[us], "W1P%d" % us
        pr, prn = nps()
        P.mm(pr[:, 0:128], w1p[:, :], ArbT, [w1n, a4n], [prn])
        for c in range(2):
            cs = slice(64 * c, 64 * c + 64)
            P.tt("dve", RM[us][c][hs, cs], pr[hs, cs], RT[hs, p, blk * 128 + 64 * c:blk * 128 + 64 * c + 64], ALU.add,
                 [prn, ("RT", p)], ["RM_%d_%d" % (us, c)])
        yield
        hf, hfn = Hf[p][h], "Hf_%d_%d" % (p, h)
        for c in range(2):
            ch = blk * 2 + c
            rs = slice(64 * c, 64 * c + 64)
            px, pxn = nps()
            P.mm(px[:, 0:64], w1p[rs, :], tk[rs, 0, hs], [w1n, tkn], [pxn])
            P.mm(px[:, 64:128], tk[rs, 0, :], U0[rs, :], [tkn, Zfn], [pxn], start=True, stop=False)
            P.mm(px[:, 64:128], tk[rs, 1, :], tk[rs, 2, hs], [tkn], [pxn], start=False, stop=True)
            xt, xtn = XTP[us][c], "XTP_%d_%d" % (us, c)
            P.tt("dve", xt[hs, hs], px[hs, 0:64], identf[hs, hs], ALU.add, [pxn, "identf"], [xtn])
            qg, qgn = Qg[us][c], "Qg_%d_%d" % (us, c)
            P.ts("dve", qg[hs, :], px[hs, 64:128], gam[hs, p, ch:ch + 1], None, ALU.mult, None, [pxn, ("gam", p)], [qgn])
            yield
        ph, phn = nps()
        P.mm(ph[:, 0:64], XTP[us][0][hs, :], hf[hs, :], ["XTP_%d_0" % us, hfn], [phn])
        P.stt(hf[hs, :], ph[hs, 0:64], gam[hs, p, 2 * blk:2 * blk + 1], Qg[us][0][hs, :], ALU.mult, ALU.add,
              [phn, ("gam", p), "Qg_%d_0" % us], [hfn])
        P.cp("pool", Hb[p][h][1][hs, :], hf[hs, :], [hfn], ["Hb_%d_%d_1" % (p, h)])
        yield
        ph, phn = nps()
        P.mm(ph[:, 0:64], XTP[us][1][hs, :], hf[hs, :], ["XTP_%d_1" % us, hfn], [phn])
        P.stt(hf[hs, :], ph[hs, 0:64], gam[hs, p, 2 * blk + 1:2 * blk + 2], Qg[us][1][hs, :], ALU.mult, ALU.add,
              [phn, ("gam", p), "Qg_%d_1" % us], [hfn])
        py, pyn = nps()
        P.mm(py[:, 0:64], ArbT, U0, [a4n, Zfn], [pyn], start=True, stop=False)
        P.mm(py[:, 0:64], ArkT, tk[:, 2, hs], [a4n, tkn], [pyn], start=False, stop=False)
        P.mm(py[:, 0:64], RM[us][0][hs, :], Hb[p][h][0][hs, :], ["RM_%d_0" % us, "Hb_%d_%d_0" % (p, h)], [pyn], start=False, stop=False)
        P.mm(py[:, 0:64], RM[us][1][hs, :], Hb[p][h][1][hs, :], ["RM_%d_1" % us, "Hb_%d_%d_1" % (p, h)], [pyn], start=False, stop=True)
        P.cp("act", Y[:, us * 64:(us + 1) * 64], py[:, 0:64], [pyn], [(Yn, us)])
        P.cp("pool", Hb[p][h][0][hs, :], hf[hs, :], [hfn], ["Hb_%d_%d_0" % (p, h)])
        yield

    for blk in range(NB):
        cols = slice(blk * 128, (blk + 1) * 128)
        Y, Yn = Yall[blk % 2], "Yall%d" % (blk % 2)
        pg, pgn = pss[6], "ps6"
        P.mm(pg[:, 0:256], lo_g[:, cols], g2b[:, :], ["lo_g", "g2b"], [pgn])
        for p in range(2):
            P.mm(pg[:, 256 + 2 * p:258 + 2 * p], RK[:, p, cols], hselb[:, :], [("RK", p), "hselb"], [pgn])
        gt, gtn = gtk[blk % 2], "gtk%d" % (blk % 2)
        rk_, rkn = rkc[blk % 2], "rkc%d" % (blk % 2)
        P.cp("act", gt[:, :], pg[:, 0:256], [pgn], [gtn])
        P.cp("act", rk_[:, :], pg[:, 256:260], [pgn], [rkn])
        for p in range(2):
            tk, tkn = TOK[p], "TOK%d" % p
            for j, (X, xn) in enumerate(((BT, "BT"), (KT, "KT"), (VB, "VB"), (AT, "AT"))):
                P.tr(psb[:, j * 128:(j + 1) * 128], X[:, p, cols], identb[:, :], [(xn, p), "identb"], ["psb"])
            P.cp("act", tk[:, :, :], psb[:, 0:512].rearrange("p (j c) -> p j c", j=4), ["psb"], [tkn])
        gens = [unit(p, h, blk, Y, Yn) for p in range(2) for h in range(2)]
        live = list(gens)
        while live:
            nxt = []
            for g_ in live:
                try:
                    next(g_)
                    nxt.append(g_)
                except StopIteration:
                    pass
            live = nxt
        Y3 = Y[:, :].rearrange("p (h v) -> p h v", h=4)
        P.op("dve", lambda e, Y3=Y3: e.tensor_reduce(st[:, 0:4], Y3, mybir.AxisListType.X, ALU.add), [Yn], [("st", 0)])
        P.act(sq[:, :], Y[:, :], AF.Square, [Yn], ["sq"])
        sq3 = sq[:, :].rearrange("p (h v) -> p h v", h=4)
        P.op("dve", lambda e, sq3=sq3: e.tensor_reduce(st[:, 4:8], sq3, mybir.AxisListType.X, ALU.add), ["sq"], [("st", 1)])
        P.ts("dve", st[:, 0:4], st[:, 0:4], 1.0 / 64, None, ALU.mult, None, [("st", 0)], [("st", 0)])
        P.tt("dve", st[:, 8:12], st[:, 0:4], st[:, 0:4], ALU.mult, [("st", 0)], [("st", 2)])
        P.stt(st[:, 4:8], st[:, 4:8], 1.0 / 64, st[:, 8:12], ALU.mult, ALU.subtract, [("st", 1), ("st", 2)], [("st", 1)])
        P.ts("dve", st[:, 4:8], st[:, 4:8], 64e-5, None, ALU.add, None, [("st", 1)], [("st", 1)])
        P.act(st[:, 4:8], st[:, 4:8], AF.Sqrt, [("st", 1)], [("st", 1)])
        P.op("dve", lambda e: e.reciprocal(st[:, 4:8], st[:, 4:8]), [("st", 1)], [("st", 1)])
        o_, on = yo[blk % 2], "yo%d" % (blk % 2)
        for hh in range(4):
            cs = slice(hh * 64, hh * 64 + 64)
            P.ts("dve", o_[:, cs], Y[:, cs], st[:, hh:hh + 1], st[:, 4 + hh:5 + hh], ALU.subtract, ALU.mult, [Yn, "st"], [(on, hh)])
        P.tt("dve", o_[:, :], o_[:, :], lngs[:, :], ALU.mult, [on, "lngs"], [on])
        P.tt("dve", o_[:, :], o_[:, :], lnbs[:, :], ALU.add, [on, "lnbs"], [on])
        for hh in range(4):
            cs = slice(hh * 64, hh * 64 + 64)
            pp, h2 = hh // 2, hh % 2
            P.stt(o_[:, cs], TOK[pp][:, 2, 64 * h2:64 * h2 + 64], rk_[:, hh:hh + 1], o_[:, cs], ALU.mult, ALU.add,
                  ["TOK%d" % pp, rkn, on], [on])
        P.tt("dve", o_[:, :], o_[:, :], gt[:, :], ALU.mult, [on, gtn], [on])
        if fm_out:
            pt_, ptn = nextps()
            for c in range(2):
                P.tr(pt_[:, c * 128:(c + 1) * 128], o_[:, c * 128:(c + 1) * 128], identf[:, :], [on, "identf"], [ptn])
            yf, yfn = yfm[blk % 2], "yfm%d" % (blk % 2)
            P.cp("act", yf[:, :, :], pt_[:, 0:256].rearrange("p (c t) -> p c t", c=2), [ptn], [yfn])
            P.dma(y_out[0:256, blk * 128:(blk + 1) * 128].rearrange("(c p) t -> p c t", p=128), yf[:, :, :], [yfn], [("y_out", blk)], sk=yfn)
        else:
            P.dma(y_out[blk * 128:(blk + 1) * 128, :], o_[:, :], [on], [("y_out", blk)], sk=on)
    return _fin(P, nc, io, ("y_out", "vfirst_out"))


def l4_inputs(z, prm, hg, L, vfirst=None):
    c0 = 256 * hg
    f = lambda a: np.ascontiguousarray(a, dtype=np.float32)
    mu = prm["mu"]
    d = dict(zr=f(z[:, c0:c0 + 256].T), zk=f(z[:, 512 + c0:512 + c0 + 256].T), zv=f(z[:, 1024 + c0:1024 + c0 + 256].T),
             zw=f(z[:, 1536:1568].T), za=f(z[:, 1568:1600].T), zg=f(z[:, 1600:1696].T))
    mu3 = np.zeros((128, 6), np.float32)
    for j in range(3):
        for p in range(2):
            mu3[:, 2 * j + p] = mu[512 * j + c0 + 128 * p:512 * j + c0 + 128 * p + 128]
    d["mu3"] = mu3
    mulo = np.zeros((96, 3), np.float32)
    mulo[:32, 0] = mu[1536:1568]
    mulo[:32, 1] = mu[1568:1600]
    mulo[:, 2] = mu[1600:1696]
    d["mulo"] = mulo
    d["w2"] = f(prm["w2"][:, c0:c0 + 256]); d["a2"] = f(prm["a2"][:, c0:c0 + 256]); d["g2"] = f(prm["g2"][:, c0:c0 + 256])
    pc = np.zeros((128, 10), np.float32)
    for j, nm in enumerate(("w0", "a0", "k_k", "k_a", "r_k")):
        v = prm[nm].reshape(-1)
        for p in range(2):
            pc[:, 2 * j + p] = v[c0 + 128 * p:c0 + 128 * p + 128]
    d["pc"] = pc
    d["lng"] = f(np.broadcast_to(prm["ln_g"][c0:c0 + 256][None, :], (128, 256)))
    d["lnb"] = f(np.broadcast_to(prm["ln_b"][c0:c0 + 256][None, :], (128, 256)))
    d.update(rw_consts(L))
    if vfirst is not None:
        d["zvfull"] = f(z[:, 1024:1536].T)
        d["muvf"] = f(mu[1024:1536].reshape(4, 128).T)
        d["v1"] = f(prm["v1"].reshape(4, 128, 32).transpose(1, 0, 2))
        d["v2"] = f(prm["v2"][:, c0:c0 + 256])
        v0 = np.zeros((128, 2), np.float32)
        for p in range(2):
            v0[:, p] = prm["v0"][c0 + 128 * p:c0 + 128 * p + 128]
        d["v0"] = v0
        d["vfirst_in"] = f(vfirst)
    return d


def at_consts():
    tk = np.arange(128)[:, None]
    tq = np.arange(512)[None, :]
    m = np.stack([((128 * j + tk) <= tq).astype(np.float32) for j in range(4)], 1)
    i = np.arange(128)
    b64 = ((i[:, None] // 64) == (i[None, :] // 64)).astype(np.float32)
    return dict(c_cmask=m, c_b64=b64, c_ones=np.ones((128, 128), np.float32))


def build_L3(L=4096, ctx=None):
    nc, P, io = _ctx(ctx)
    NT = L // 128
    NQ = L // 512
    I = _mkI(nc, io)
    qT, kT = I("qT", [2, 128, L]), I("kT", [2, 128, L])
    vtok = I("vtok", [2, L, 128])
    gains = I("gains", [128, 4])
    lamv = I("lamv", [128, 4, 64])
    laminit = I("laminit", [128, 1])
    c_cmask, c_b64, c_ones = I("c_cmask", [128, 4, 512]), I("c_b64", [128, 128]), I("c_ones", [128, 128])
    oT = _mkO(nc, io, "oT", [2, 128, L])

    def ld(name, src, shape):
        t = P.sb(name, shape)
        if len(shape) == 2:
            P.dma(t[:, :], src[:, :], [], [name])
        else:
            P.dma(t[:, :, :], src[:, :, :], [], [name])
        return t
    gs = ld("gs", gains, [128, 4])
    lv = ld("lv", lamv, [128, 4, 64])
    li = ld("li", laminit, [128, 1])
    cmf = ld("cmf", c_cmask, [128, 4, 512])
    b64 = ld("b64", c_b64, [128, 128])
    onesf = ld("onesf", c_ones, [128, 128])
    cmb = P.sb("cmb", [128, 4, 512], BF16)
    onesb = P.sb("onesb", [128, 128], BF16)
    P.cp("dve", cmb[:, :, :], cmf[:, :, :], ["cmf"], ["cmb"])
    P.cp("dve", onesb[:, :], onesf[:, :], ["onesf"], ["onesb"])
    lt = P.sb("lt", [128, 2, 64])
    ls = P.sb("ls", [128, 4])
    P.tt("dve", lt[:, 0, :], lv[:, 0, :], lv[:, 1, :], ALU.mult, ["lv"], [("lt", 0)])
    P.tt("dve", lt[:, 1, :], lv[:, 2, :], lv[:, 3, :], ALU.mult, ["lv"], [("lt", 1)])
    P.op("dve", lambda e: e.tensor_reduce(ls[:, 0:2], lt[:, :, :], mybir.AxisListType.X, ALU.add), ["lt"], ["ls"])
    P.act(ls[:, 0:2], ls[:, 0:2], AF.Exp, ["ls"], ["ls"])
    P.tt("dve", ls[:, 2:3], ls[:, 1:2], ls[:, 0:1], ALU.subtract, ["ls"], ["ls"])
    P.tt("dve", ls[:, 3:4], ls[:, 2:3], li[:, :], ALU.subtract, ["ls", "li"], ["ls"])
    gq = P.sb("gq", [128, 1])
    P.ts("dve", gq[:, :], gs[:, 0:1], 0.125, None, ALU.mult, None, ["gs"], ["gq"])
    gsub = P.sb("gsub", [128, 1])
    P.tt("dve", gsub[:, :], gs[:, 2:3], gs[:, 3:4], ALU.mult, ["gs"], ["gsub"])

    raw = P.sb("raw", [128, L])
    sqt = P.sb("sqt", [128, L])
    qn = P.sb("qn", [128, L], BF16)
    kn = P.sb("kn", [128, L], BF16)
    vf = P.sb("vf", [128, NT, 128])
    vb = P.sb("vb", [128, NT, 128], BF16)
    pt = [P.sb("pt%d" % i, [128, 512], BF16) for i in range(4)]
    ob = P.sb("ob", [128, 512])
    o1 = P.sb("o1", [128, 512])
    rz = P.sb("rz", [128, 512])
    osq = P.sb("osq", [128, 512])
    oo = [P.sb("oo%d" % i, [128, 512]) for i in range(2)]
    psS = [P.ps("psS%d" % i, [128, 512]) for i in range(2)]
    psO = [P.ps("psO%d" % i, [128, 512]) for i in range(2)]
    psZ = [P.ps("psZ%d" % i, [128, 512]) for i in range(2)]
    psM = P.ps("psM", [128, 512])

    def qknorm(src, dst, dn, gcol):
        P.dma(raw[:, :], src, [], ["raw"])
        P.act(sqt[:, :], raw[:, :], AF.Square, ["raw"], ["sqt"])
        for cb in range(NQ):
            cs = slice(cb * 512, cb * 512 + 512)
            P.mm(psM[:, :], b64[:, :], sqt[:, cs], ["b64", "sqt"], ["psM"])
            P.ts("dve", sqt[:, cs], psM[:, :], 1.0 / 64, EPS, ALU.mult, ALU.add, ["psM"], [("sqt", cb)])
        P.act(sqt[:, :], sqt[:, :], AF.Sqrt, ["sqt"], ["sqt"])
        P.op("dve", lambda e: e.reciprocal(sqt[:, :], sqt[:, :]), ["sqt"], ["sqt"])
        P.stt(dst[:, :], raw[:, :], gcol, sqt[:, :], ALU.mult, ALU.mult, ["raw", "sqt", "gs", "gq"], [dn])

    qn2 = [qn, P.sb("qn_b", [128, L], BF16)]
    kn2 = [kn, P.sb("kn_b", [128, L], BF16)]
    vb2 = [vb, P.sb("vb_b", [128, NT, 128], BF16)]
    qnn, knn, vbn = ["qn", "qn_b"], ["kn", "kn_b"], ["vb", "vb_b"]
    sO = [P.sb("sO%d" % i, [128, 512]) for i in range(2)]
    sZ = [P.sb("sZ%d" % i, [128, 512]) for i in range(2)]
    for h in range(2):
        qknorm(qT[h, :, :], qn2[h], qnn[h], gq[:, 0:1])
        qknorm(kT[h, :, :], kn2[h], knn[h], gs[:, 1:2])
        P.dma(vf[:, :, :], vtok[h, :, :].rearrange("(n p) d -> p n d", p=128), [], ["vf"])
        P.cp("dve", vb2[h][:, :, :], vf[:, :, :], ["vf"], [vbn[h]])
    kcount = 0
    for h in range(2):
        qn_, kn_, vb_ = qn2[h], kn2[h], vb2[h]
        for qb in range(NQ):
            qs = slice(qb * 512, qb * 512 + 512)
            ntk = 4 * qb + 4
            its = [(j, c2) for j in range(ntk) for c2 in range(2)]
            bufs = []

            def emit_S(idx):
                nonlocal kcount
                j, c2 = its[idx]
                hs = slice(64 * c2, 64 * c2 + 64)
                s_, sn = psS[kcount % 2], "psS%d" % (kcount % 2)
                p_, pn = pt[kcount % 4], "pt%d" % (kcount % 4)
                kcount += 1
                P.mm(s_[:, :], kn_[hs, j * 128:j * 128 + 128], qn_[hs, qs], [knn[h], qnn[h]], [sn])
                bufs.append((s_, sn, p_, pn))
            emit_S(0)
            for idx, (j, c2) in enumerate(its):
                if idx + 1 < len(its):
                    emit_S(idx + 1)
                s_, sn, p_, pn = bufs[idx]
                P.act(p_[:, :], s_[:, :], AF.Exp, [sn], [pn])
                if j >= 4 * qb:
                    P.tt("dve", p_[:, :], p_[:, :], cmb[:, j - 4 * qb, :], ALU.mult, [pn, "cmb"], [pn])
                P.mm(psO[c2][:, :], vb_[:, j, :], p_[:, :], [vbn[h], pn], ["psO%d" % c2], start=(j == 0), stop=(j == ntk - 1))
                P.mm(psZ[c2][:, :], onesb[:, :], p_[:, :], ["onesb", pn], ["psZ%d" % c2], start=(j == 0), stop=(j == ntk - 1))
            for c2 in range(2):
                P.cp("act", sO[c2][:, :], psO[c2][:, :], ["psO%d" % c2], ["sO%d" % c2])
                P.cp("act", sZ[c2][:, :], psZ[c2][:, :], ["psZ%d" % c2], ["sZ%d" % c2])
            P.op("dve", lambda e: e.reciprocal(rz[:, :], sZ[0][:, :]), ["sZ0"], ["rz"])
            P.tt("dve", ob[:, :], sO[0][:, :], rz[:, :], ALU.mult, ["sO0", "rz"], ["ob"])
            P.op("dve", lambda e: e.reciprocal(rz[:, :], sZ[1][:, :]), ["sZ1"], ["rz"])
            P.tt("dve", o1[:, :], sO[1][:, :], rz[:, :], ALU.mult, ["sO1", "rz"], ["o1"])
            P.stt(ob[:, :], o1[:, :], ls[:, 3:4], ob[:, :], ALU.mult, ALU.add, ["o1", "ls", "ob"], ["ob"])
            P.act(osq[:, :], ob[:, :], AF.Square, ["ob"], ["osq"])
            P.mm(psM[:, :], onesf[:, :], osq[:, :], ["onesf", "osq"], ["psM"])
            P.ts("dve", osq[:, :], psM[:, :], 1.0 / 128, 1e-5, ALU.mult, ALU.add, ["psM"], ["osq"])
            P.act(osq[:, :], osq[:, :], AF.Sqrt, ["osq"], ["osq"])
            P.op("dve", lambda e: e.reciprocal(osq[:, :], osq[:, :]), ["osq"], ["osq"])
            o_, on = oo[qb % 2], "oo%d" % (qb % 2)
            P.stt(o_[:, :], ob[:, :], gsub[:, 0:1], osq[:, :], ALU.mult, ALU.mult, ["ob", "gsub", "osq"], [on])
            P.dma(oT[h, :, qs], o_[:, :], [on], [("oT", h * NQ + qb)], sk=on)
    return _fin(P, nc, io, ("oT",))


def l3_inputs(q, k, v, prm, hp, L, lam_init):
    f = lambda a: np.ascontiguousarray(a, dtype=np.float32)
    hs = [2 * hp, 2 * hp + 1]
    d = dict(qT=f(np.stack([q[:, h * 128:(h + 1) * 128].T for h in hs])),
             kT=f(np.stack([k[:, h * 128:(h + 1) * 128].T for h in hs])),
             vtok=f(np.stack([v[:, h * 128:(h + 1) * 128] for h in hs])))
    g = np.zeros((128, 4), np.float32)
    g[:, 0] = prm["q_gain"].reshape(-1)
    g[:, 1] = prm["k_gain"].reshape(-1)
    g[:, 2] = prm["subln_g"]
    g[:, 3] = 1.0 - lam_init
    d["gains"] = g
    d["lamv"] = f(np.broadcast_to(prm["lam"][None], (128, 4, 64)))
    d["laminit"] = np.full((128, 1), lam_init, np.float32)
    d.update(at_consts())
    return d


PI = float(np.pi)


def build_L2(L=4096, ctx=None):
    nc, P, io = _ctx(ctx)
    NQ = L // 512
    I = _mkI(nc, io)
    uT = I("uT", [256, L])
    bpr, bpi = I("bpr", [128, 8, 128]), I("bpi", [128, 8, 128])
    cpr, cpi = I("cpr", [128, 8, 128]), I("cpi", [128, 8, 128])
    prm = I("prm", [128, 3, 8])
    dcol = I("dcol", [128, 2])
    yT = _mkO(nc, io, "yT", [256, L])

    def ld(name, src, shape):
        t = P.sb(name, shape)
        if len(shape) == 2:
            P.dma(t[:, :], src[:, :], [], [name])
        else:
            P.dma(t[:, :, :], src[:, :, :], [], [name])
        return t
    prs = ld("prs", prm, [128, 3, 8])
    ds = ld("ds", dcol, [128, 2])
    wts = {}
    wf32 = {}
    for nm, src in (("bpr", bpr), ("bpi", bpi), ("cpr", cpr), ("cpi", cpi)):
        f = ld(nm + "f", src, [128, 8, 128])
        wf32[nm] = f
        b = P.sb(nm + "b", [128, 8, 128], BF16)
        if nm.startswith("b"):
            P.cp("dve", b[:, :, :], f[:, :, :], [nm + "f"], [nm + "b"])
        wts[nm] = b
    LHu = L // 2
    iaf = P.sb("iaf", [128, LHu])
    ub = P.sb("ub", [128, 2, L], BF16)
    for t in range(2):
        for hu in range(2):
            P.dma(iaf[:, :], uT[t * 128:(t + 1) * 128, hu * LHu:(hu + 1) * LHu], [], ["iaf"])
            P.cp("dve", ub[:, t, hu * LHu:(hu + 1) * LHu], iaf[:, :], ["iaf"], [("ub", t)])
    q = P.sb("q", [128, 16, 8])
    Q = lambda i: q[:, i, :]
    lr, li_, ldt = prs[:, 0, :], prs[:, 1, :], prs[:, 2, :]
    negpi = P.sb("negpi", [128, 1])
    P.memset("pool", negpi[:, :], -PI, ["negpi"])
    P.act(Q(0), ldt, AF.Exp, ["prs"], [("q", 0)])
    P.tt("dve", Q(1), lr, Q(0), ALU.mult, ["prs", ("q", 0)], [("q", 1)])
    P.tt("dve", Q(2), li_, Q(0), ALU.mult, ["prs", ("q", 0)], [("q", 2)])
    P.act(Q(3), Q(1), AF.Exp, [("q", 1)], [("q", 3)])
    NA = L // 64
    pw = P.sb("pw", [128, 24, 2, 8])
    wt = P.sb("wt", [128, 6, 8])
    W = lambda i: wt[:, i, :]

    def square_norm(cin, sin_, cout, sout, rd, wr):
        P.tt("dve", W(0), cin, cin, ALU.mult, rd, [("wt", 0)])
        P.tt("dve", W(1), sin_, sin_, ALU.mult, rd, [("wt", 1)])
        P.tt("dve", W(2), cin, sin_, ALU.mult, rd, [("wt", 2)])
        P.tt("dve", W(0), W(0), W(1), ALU.subtract, [("wt", 0), ("wt", 1)], [("wt", 0)])
        P.ts("dve", W(2), W(2), 2.0, None, ALU.mult, None, [("wt", 2)], [("wt", 2)])
        P.tt("dve", W(3), W(0), W(0), ALU.mult, [("wt", 0)], [("wt", 3)])
        P.tt("dve", W(4), W(2), W(2), ALU.mult, [("wt", 2)], [("wt", 4)])
        P.tt("dve", W(3), W(3), W(4), ALU.add, [("wt", 3), ("wt", 4)], [("wt", 3)])
        P.act(W(3), W(3), AF.Sqrt, [("wt", 3)], [("wt", 3)])
        P.op("dve", lambda e: e.reciprocal(W(3), W(3)), [("wt", 3)], [("wt", 3)])
        P.tt("dve", cout, W(0), W(3), ALU.mult, [("wt", 0), ("wt", 3)], wr)
        P.tt("dve", sout, W(2), W(3), ALU.mult, [("wt", 2), ("wt", 3)], wr)
    P.act(pw[:, 0, 1, :], Q(2), AF.Sin, [("q", 2)], [("pw", 0)], scale=1.0 / 16)
    P.act(Q(5), Q(2), AF.Sin, [("q", 2)], [("q", 5)], scale=1.0 / 32)
    P.tt("dve", Q(5), Q(5), Q(5), ALU.mult, [("q", 5)], [("q", 5)])
    P.ts("dve", pw[:, 0, 0, :], Q(5), -2.0, 1.0, ALU.mult, ALU.add, [("q", 5)], [("pw", 0)])
    NPW = 4 + 6 + max(1, int(np.log2(NA)))
    for k_ in range(1, NPW):
        square_norm(pw[:, k_ - 1, 0, :], pw[:, k_ - 1, 1, :], pw[:, k_, 0, :], pw[:, k_, 1, :], [("pw", k_ - 1)], [("pw", k_)])
    P.cp("dve", Q(5), pw[:, 4, 0, :], [("pw", 4)], [("q", 5)])
    P.cp("dve", Q(4), pw[:, 4, 1, :], [("pw", 4)], [("q", 4)])
    EBc, EBs = P.sb("EBc", [128, 8, 64]), P.sb("EBs", [128, 8, 64])
    EAc, EAs = P.sb("EAc", [128, 8, NA]), P.sb("EAs", [128, 8, NA])
    tbl = P.sb("tbl", [128, 2, 8, 64])

    def build_table(Ec, Es, ecn, esn, n_tot, k0, rowlen):
        P.memset("pool", Ec[:, :, 0:1], 1.0, [ecn])
        P.memset("pool", Es[:, :, 0:1], 0.0, [esn])
        n = 1
        k_ = k0
        while n < n_tot:
            wc = ap3(pw, (k_ * 2 + 0) * 8, [[24 * 2 * 8, 128], [1, 8], [0, n]])
            ws = ap3(pw, (k_ * 2 + 1) * 8, [[24 * 2 * 8, 128], [1, 8], [0, n]])
            t0_, t1_ = tbl[:, 0, :, 0:n], tbl[:, 1, :, 0:n]
            rd = [ecn, esn, ("pw", k_)]
            P.tt("dve", t0_, Es[:, :, 0:n], ws, ALU.mult, rd, [("tbl", 0)])
            P.tt("dve", t1_, Ec[:, :, 0:n], ws, ALU.mult, rd, [("tbl", 1)])
            P.tt("dve", Ec[:, :, n:2 * n], Ec[:, :, 0:n], wc, ALU.mult, rd, [ecn])
            P.tt("dve", Es[:, :, n:2 * n], Es[:, :, 0:n], wc, ALU.mult, rd, [esn])
            P.tt("dve", Ec[:, :, n:2 * n], Ec[:, :, n:2 * n], t0_, ALU.subtract, [ecn, ("tbl", 0)], [ecn])
            P.tt("dve", Es[:, :, n:2 * n], Es[:, :, n:2 * n], t1_, ALU.add, [esn, ("tbl", 1)], [esn])
            n *= 2
            k_ += 1
    build_table(EBc, EBs, "EBc", "EBs", 64, 4, 64)
    build_table(EAc, EAs, "EAc", "EAs", NA, 10, NA)
    P.tt("dve", Q(6), Q(3), Q(5), ALU.mult, [("q", 3), ("q", 5)], [("q", 6)])
    P.tt("dve", Q(7), Q(3), Q(4), ALU.mult, [("q", 3), ("q", 4)], [("q", 7)])
    P.tt("dve", Q(8), lr, lr, ALU.mult, ["prs"], [("q", 8)])
    P.tt("dve", Q(9), li_, li_, ALU.mult, ["prs"], [("q", 9)])
    P.tt("dve", Q(8), Q(8), Q(9), ALU.add, [("q", 8), ("q", 9)], [("q", 8)])
    P.op("dve", lambda e: e.reciprocal(Q(8), Q(8)), [("q", 8)], [("q", 8)])
    P.ts("dve", Q(9), Q(6), -1.0, None, ALU.add, None, [("q", 6)], [("q", 9)])
    P.tt("dve", Q(10), Q(9), lr, ALU.mult, [("q", 9), "prs"], [("q", 10)])
    P.tt("dve", Q(11), Q(7), li_, ALU.mult, [("q", 7), "prs"], [("q", 11)])
    P.tt("dve", Q(10), Q(10), Q(11), ALU.add, [("q", 10), ("q", 11)], [("q", 10)])
    P.tt("dve", Q(10), Q(10), Q(8), ALU.mult, [("q", 10), ("q", 8)], [("q", 10)])
    P.tt("dve", Q(11), Q(7), lr, ALU.mult, [("q", 7), "prs"], [("q", 11)])
    P.tt("dve", Q(12), Q(9), li_, ALU.mult, [("q", 9), "prs"], [("q", 12)])
    P.tt("dve", Q(11), Q(11), Q(12), ALU.subtract, [("q", 11), ("q", 12)], [("q", 11)])
    P.tt("dve", Q(11), Q(11), Q(8), ALU.mult, [("q", 11), ("q", 8)], [("q", 11)])
    P.ts("dve", Q(14), Q(10), -1.0, None, ALU.mult, None, [("q", 10)], [("q", 14)])
    MAG, TH, KR, KI, NKR = 3, 2, 10, 11, 14
    ctmp = P.sb("ctmp", [128, 2, 128])
    for gp in range(8):
        krc, kic, nkr = q[:, KR, gp:gp + 1], q[:, KI, gp:gp + 1], q[:, NKR, gp:gp + 1]
        cre, cim = wf32["cpr"][:, gp, :], wf32["cpi"][:, gp, :]
        P.ts("dve", ctmp[:, 0, :], cim, kic, None, ALU.mult, None, ["cpif", "q"], [("ctmp", 0)])
        P.stt(wts["cpr"][:, gp, :], cre, krc, ctmp[:, 0, :], ALU.mult, ALU.subtract, ["cprf", "q", ("ctmp", 0)], [("cprb", gp)])
        P.ts("dve", ctmp[:, 1, :], cim, krc, None, ALU.mult, None, ["cpif", "q"], [("ctmp", 1)])
        P.stt(wts["cpi"][:, gp, :], cre, kic, ctmp[:, 1, :], ALU.mult, ALU.add, ["cprf", "q", ("ctmp", 1)], [("cpib", gp)])

    xb = P.sb("xb", [128, 4, 2, L], BF16)
    NS = 2
    def mk(n, dt=F32):
        return [[P.sb("%s%d_%d" % (n, ln, i), [128, 512], dt) for i in range(NS)] for ln in range(2)]
    cc, ss_, T1, T2, tmp = [mk(n) for n in ("cc", "ss", "T1", "T2", "tmp")]
    tmp2 = [[P.sb("tmp2%d_0" % ln, [128, 512])] * NS for ln in range(2)]
    mr = [[P.sb("mr%d_0" % ln, [128, 512])] * NS for ln in range(2)]
    mi = [[P.sb("mi%d_0" % ln, [128, 512])] * NS for ln in range(2)]
    wr = [[P.sb("wr%d_%d" % (ln, i), [128, 512]) for i in range(2)] for ln in range(2)]
    wi = [[P.sb("wi%d_%d" % (ln, i), [128, 512]) for i in range(2)] for ln in range(2)]
    magt = [P.sb("magt%d" % ln, [128, 512]) for ln in range(2)]
    onest = P.sb("onest", [128, 512])
    P.memset("pool", onest[:, :], 1.0, ["onest"])
    ust = [P.sb("ust0", [128, 512])] * 2
    yst = [P.sb("yst%d" % i, [128, 512]) for i in range(2)]
    psr = [[P.ps("psr%d_%d" % (ln, i), [128, 512]) for i in range(1)] for ln in range(2)]
    psi = [[P.ps("psi%d_%d" % (ln, i), [128, 512]) for i in range(1)] for ln in range(2)]
    psy = [P.ps("psy%d" % i, [128, 512]) for i in range(2)]
    k = 0
    for tile in range(2):
        for glp in range(2):
          lanes = []
          for ln in range(2):
            gl = 2 * glp + ln
            gp = 4 * tile + gl
            sc = lambda idx, gp=gp: q[:, idx, gp:gp + 1]
            P.ts("dve", magt[ln][:, :], onest[:, :], sc(MAG), None, ALU.mult, None, ["onest", "q"], ["magt%d" % ln])
            def stage_a(qb, i, gp=gp, gl=gl, tile=tile, sc=sc, ln=ln):
                n = lambda s_: "%s%d_%d" % (s_, ln, i)
                cs = slice(qb * 512, qb * 512 + 512)
                P.mm(psr[ln][0][:, :], wts["bpr"][:, gp, :], ub[:, tile, cs], ["bprb", ("ub", tile)], ["psr%d_0" % ln])
                P.mm(psi[ln][0][:, :], wts["bpi"][:, gp, :], ub[:, tile, cs], ["bpib", ("ub", tile)], ["psi%d_0" % ln])
                a0 = 8 * qb
                bro = lambda T_, n_: ap3(T_, gp * n_ + a0, [[8 * n_, 128], [1, 8], [0, 64]])
                brb = lambda T_: ap3(T_, gp * 64, [[8 * 64, 128], [0, 8], [1, 64]])
                v3 = lambda t_: t_[:, :].rearrange("p (a b) -> p a b", a=8)
                P.cp("act", T1[ln][i][:, :], psr[ln][0][:, :], ["psr%d_0" % ln], [n("T1")])
                P.cp("act", T2[ln][i][:, :], psi[ln][0][:, :], ["psi%d_0" % ln], [n("T2")])
                P.tt("pool", v3(tmp[ln][i]), bro(EAs, NA), brb(EBs), ALU.mult, ["EAs", "EBs"], [n("tmp")])
                P.tt("dve", v3(cc[ln][i]), bro(EAc, NA), brb(EBc), ALU.mult, ["EAc", "EBc"], [n("cc")])
                P.tt("pool", v3(tmp2[ln][i]), bro(EAc, NA), brb(EBs), ALU.mult, ["EAc", "EBs"], [("tmp2%d" % ln)])
                P.tt("dve", v3(ss_[ln][i]), bro(EAs, NA), brb(EBc), ALU.mult, ["EAs", "EBc"], [n("ss")])
                P.tt("dve", cc[ln][i][:, :], cc[ln][i][:, :], tmp[ln][i][:, :], ALU.subtract, [n("cc"), n("tmp")], [n("cc")])
                P.tt("dve", ss_[ln][i][:, :], ss_[ln][i][:, :], tmp2[ln][i][:, :], ALU.add, [n("ss"), ("tmp2%d" % ln)], [n("ss")])

            def stage_b(qb, i, gp=gp, gl=gl, tile=tile, sc=sc, ln=ln):
                n = lambda s_: "%s%d_%d" % (s_, ln, i)
                cs = slice(qb * 512, qb * 512 + 512)
                P.tt("dve", mr[ln][i][:, :], cc[ln][i][:, :], T1[ln][i][:, :], ALU.mult, [n("cc"), n("T1")], ["mr%d" % ln])
                P.tt("dve", tmp[ln][i][:, :], ss_[ln][i][:, :], T2[ln][i][:, :], ALU.mult, [n("ss"), n("T2")], [n("tmp")])
                P.tt("dve", mr[ln][i][:, :], mr[ln][i][:, :], tmp[ln][i][:, :], ALU.add, ["mr%d" % ln, n("tmp")], ["mr%d" % ln])
                P.tt("dve", mi[ln][i][:, :], cc[ln][i][:, :], T2[ln][i][:, :], ALU.mult, [n("cc"), n("T2")], ["mi%d" % ln])
                P.tt("dve", tmp2[ln][i][:, :], ss_[ln][i][:, :], T1[ln][i][:, :], ALU.mult, [n("ss"), n("T1")], [("tmp2%d" % ln)])
                P.tt("dve", mi[ln][i][:, :], mi[ln][i][:, :], tmp2[ln][i][:, :], ALU.subtract, ["mi%d" % ln, ("tmp2%d" % ln)], ["mi%d" % ln])
                w_r, w_i = wr[ln][qb % 2], wi[ln][qb % 2]
                wrn, win = "wr%d_%d" % (ln, qb % 2), "wi%d_%d" % (ln, qb % 2)
                if qb == 0:
                    ini_r, ini_i, extra = 0.0, 0.0, []
                else:
                    ini_r, ini_i = wr[ln][1 - qb % 2][:, 511:512], wi[ln][1 - qb % 2][:, 511:512]
                    extra = ["wr%d_%d" % (ln, 1 - qb % 2), "wi%d_%d" % (ln, 1 - qb % 2)]
                P.op("dve", lambda e, w_r=w_r, m=mr[ln][i], ini=ini_r, mg=magt[ln]: e.tensor_tensor_scan(w_r[:, :], mg[:, :], m[:, :], ini, ALU.mult, ALU.add),
                     ["magt%d" % ln, "mr%d" % ln] + extra, [wrn])
                P.op("dve", lambda e, w_i=w_i, m=mi[ln][i], ini=ini_i, mg=magt[ln]: e.tensor_tensor_scan(w_i[:, :], mg[:, :], m[:, :], ini, ALU.mult, ALU.add),
                     ["magt%d" % ln, "mi%d" % ln] + extra, [win])
                P.tt("dve", tmp[ln][i][:, :], ss_[ln][i][:, :], w_i[:, :], ALU.mult, [n("ss"), win], [n("tmp")])
                P.tt("dve", tmp2[ln][i][:, :], cc[ln][i][:, :], w_r[:, :], ALU.mult, [n("cc"), wrn], [("tmp2%d" % ln)])
                P.tt("dve", xb[:, gl, 0, cs], tmp2[ln][i][:, :], tmp[ln][i][:, :], ALU.subtract, [("tmp2%d" % ln), n("tmp")], [("xb", gl * 2)])
                P.tt("dve", T1[ln][i][:, :], ss_[ln][i][:, :], w_r[:, :], ALU.mult, [n("ss"), wrn], [n("T1")])
                P.tt("dve", T2[ln][i][:, :], cc[ln][i][:, :], w_i[:, :], ALU.mult, [n("cc"), win], [n("T2")])
                P.stt(xb[:, gl, 1, cs], T1[ln][i][:, :], -1.0, T2[ln][i][:, :], ALU.mult, ALU.subtract, [n("T1"), n("T2")], [("xb", gl * 2 + 1)])

            lanes.append((stage_a, stage_b))
          for (sa, sb_) in lanes:
              sa(0, 0)
          for qb in range(NQ):
              if qb + 1 < NQ:
                  for (sa, sb_) in lanes:
                      sa(qb + 1, (qb + 1) % NS)
              for (sa, sb_) in lanes:
                  sb_(qb, qb % NS)
        for qb in range(NQ):
            cs = slice(qb * 512, qb * 512 + 512)
            py, pyn = psy[qb % 2], "psy%d" % (qb % 2)
            for gl in range(4):
                gp = 4 * tile + gl
                P.mm(py[:, :], wts["cpr"][:, gp, :], xb[:, gl, 0, cs], ["cprb", ("xb", gl * 2)], [pyn], start=(gl == 0), stop=False)
                P.mm(py[:, :], wts["cpi"][:, gp, :], xb[:, gl, 1, cs], ["cpib", ("xb", gl * 2 + 1)], [pyn], start=False, stop=(gl == 3))
            us, usn = ust[0], "ust0"
            ys, ysn = yst[qb % 2], "yst%d" % (qb % 2)
            P.dma(us[:, :], uT[tile * 128:(tile + 1) * 128, cs], [], [usn])
            P.stt(ys[:, :], us[:, :], ds[:, tile:tile + 1], py[:, :], ALU.mult, ALU.add, [usn, "ds", pyn], [ysn])
            P.dma(yT[tile * 128:(tile + 1) * 128, cs], ys[:, :], [ysn], [("yT", tile * NQ + qb)], sk=ysn)
    return _fin(P, nc, io, ("yT",))


def l2_inputs(u, prm, gh, L):
    f = lambda a: np.ascontiguousarray(a, dtype=np.float32)
    d = dict(uT=f(u[:, 256 * gh:256 * gh + 256].T))
    bpr = np.zeros((128, 8, 128), np.float32); bpi = np.zeros_like(bpr)
    cpr = np.zeros((128, 8, 128), np.float32); cpi = np.zeros_like(cpr)
    pr = np.zeros((128, 3, 8), np.float32)
    for gp in range(8):
        gl = gp % 4
        for gpar in range(2):
            g = 16 * gh + 2 * gp + gpar
            chs = slice(16 * (2 * gl + gpar), 16 * (2 * gl + gpar) + 16)
            ps = slice(64 * gpar, 64 * gpar + 64)
            bpr[chs, gp, ps] = prm["b_re"][g].T
            bpi[chs, gp, ps] = prm["b_im"][g].T
            cpr[ps, gp, chs] = prm["c_re"][g].T
            cpi[ps, gp, chs] = prm["c_im"][g].T
            pr[ps, 0, gp] = prm["lam_re"][g]
            pr[ps, 1, gp] = prm["lam_im"][g]
            pr[ps, 2, gp] = prm["log_dt"][g]
    d.update(bpr=bpr, bpi=bpi, cpr=cpr, cpi=cpi, prm=pr)
    d["dcol"] = f(prm["d"][256 * gh:256 * gh + 256].reshape(2, 128).T)
    return d


DFF = 2816


def build_L5a(TOK=2048, HT=1024, ctx=None):
    nc, P, io = _ctx(ctx)
    NH = TOK // HT
    NTT = HT // 128
    NB5 = HT // 512
    I = _mkI(nc, io)
    x_tok = I("x_tok", [TOK, D])
    ypreT, oT, ycT = I("ypreT", [512, TOK]), I("oT", [512, TOK]), I("ycT", [512, TOK])
    glT = I("glT", [3072, TOK])
    w_glu, w_br, w_out = I("w_glu", [512, 512]), I("w_br", [3, 512, 1024]), I("w_out", [1024, 1024])
    x_out = _mkO(nc, io, "x_out", [TOK, D])

    wst = [P.sb("wst%d" % i, [128, 1024]) for i in range(3)]
    wk = [0]

    def load_w(dst, dn, sub, src_ap, ncol):
        i = wk[0] % 3
        wk[0] += 1
        P.dma(wst[i][:, :ncol], src_ap, [], ["wst%d" % i], eng=("sp" if i % 2 == 0 else "pool"))
        P.cp("act" if i % 2 else "dve", dst, wst[i][:, :ncol], ["wst%d" % i], [(dn, sub)])

    wglu = P.sb("wglu", [128, 4, 512], BF16)
    wbr = P.sb("wbr", [128, 12, 1024], BF16)
    wout = P.sb("wout", [128, 8, 1024], BF16)
    for c in range(4):
        load_w(wglu[:, c, :], "wglu", c, w_glu[c * 128:(c + 1) * 128, :], 512)
    for n in range(3):
        for c in range(4):
            load_w(wbr[:, n * 4 + c, :], "wbr", n * 4 + c, w_br[n, c * 128:(c + 1) * 128, :], 1024)
    for c in range(8):
        load_w(wout[:, c, :], "wout", c, w_out[c * 128:(c + 1) * 128, :], 1024)

    fin = P.sb("fin", [128, 4, HT])
    t1 = P.sb("t1", [128, 4, HT])
    yb = [P.sb("yb%d" % n, [128, 4, HT], BF16) for n in range(3)]
    ya2 = P.sb("ya2", [128, 4, HT], BF16)
    gst = [P.sb("gst%d" % i, [128, 512]) for i in range(2)]
    sg = [P.sb("sg%d" % i, [128, 512]) for i in range(2)]
    pr = [P.sb("pr%d" % i, [128, 512]) for i in range(2)]
    macc = P.sb("macc", [128, 512])
    mT = P.sb("mT", [128, 8, HT], BF16)
    x1 = [P.sb("x1_%d" % i, [128, D]) for i in range(2)]
    xin = [P.sb("xin%d" % i, [128, D]) for i in range(2)]
    pss = [P.ps("ps%d" % i, [128, 512]) for i in range(6)]
    pk = [0]

    def nextps():
        k = pk[0] % 6
        pk[0] += 1
        return pss[k], "ps%d" % k

    for half in range(NH):
        t0 = half * HT
        tsl = slice(t0, t0 + HT)
        fmv = lambda a: a.ap().rearrange("(c p) t -> p c t", p=128)
        P.dma(fin[:, :, :], fmv(ypreT)[:, :, tsl], [], ["fin"])
        P.act(t1[:, :, :], fin[:, :, :], AF.Square, ["fin"], ["t1"])
        P.ts("dve", t1[:, :, :], t1[:, :, :], 0.044715, 1.0, ALU.mult, ALU.add, ["t1"], ["t1"])
        P.tt("dve", t1[:, :, :], t1[:, :, :], fin[:, :, :], ALU.mult, ["t1", "fin"], ["t1"])
        P.act(t1[:, :, :], t1[:, :, :], AF.Sigmoid, ["t1"], ["t1"], scale=1.5957691216057308)
        P.tt("dve", t1[:, :, :], t1[:, :, :], fin[:, :, :], ALU.mult, ["t1", "fin"], ["t1"])
        P.cp("act", ya2[:, :, :], t1[:, :, :], ["t1"], ["ya2"])
        for jc in range(4):
            for b5 in range(NB5):
                bs = slice(b5 * 512, b5 * 512 + 512)
                ps, psn = nextps()
                for c in range(4):
                    P.mm(ps[:, :], wglu[:, c, jc * 128:(jc + 1) * 128], ya2[:, c, bs], [("wglu", c), "ya2"], [psn], start=(c == 0), stop=(c == 3))
                s_, sn = sg[(jc * NB5 + b5) % 2], "sg%d" % ((jc * NB5 + b5) % 2)
                P.act(s_[:, :], ps[:, :], AF.Sigmoid, [psn], [sn])
                P.tt("dve", yb[0][:, jc, bs], t1[:, jc, bs], s_[:, :], ALU.mult, ["t1", sn], [("yb0", jc)])
        P.dma(fin[:, :, :], fmv(oT)[:, :, tsl], ["fin"], ["fin"])
        P.cp("dve", yb[1][:, :, :], fin[:, :, :], ["fin"], ["yb1"])
        P.dma(fin[:, :, :], fmv(ycT)[:, :, tsl], ["fin"], ["fin"])
        P.cp("dve", yb[2][:, :, :], fin[:, :, :], ["fin"], ["yb2"])
        kk = 0
        for dc in range(8):
            for b5 in range(NB5):
                bs = slice(b5 * 512, b5 * 512 + 512)
                for n in range(3):
                    g_, gn = gst[kk % 2], "gst%d" % (kk % 2)
                    s_, sn = sg[kk % 2], "sg%d" % (kk % 2)
                    p_, pn = pr[kk % 2], "pr%d" % (kk % 2)
                    kk += 1
                    r0 = n * 1024 + dc * 128
                    P.dma(g_[:, :], glT[r0:r0 + 128, t0 + b5 * 512:t0 + b5 * 512 + 512], [], [gn], eng=("sp" if kk % 2 else "pool"))
                    P.act(s_[:, :], g_[:, :], AF.Sigmoid, [gn], [sn])
                    ps, psn = nextps()
                    for c in range(4):
                        P.mm(ps[:, :], wbr[:, n * 4 + c, dc * 128:(dc + 1) * 128], yb[n][:, c, bs], [("wbr", n * 4 + c), "yb%d" % n], [psn],
                             start=(c == 0), stop=(c == 3))
                    if n == 0:
                        P.tt("dve", macc[:, :], ps[:, :], s_[:, :], ALU.mult, [psn, sn], ["macc"])
                    else:
                        P.tt("dve", p_[:, :], ps[:, :], s_[:, :], ALU.mult, [psn, sn], [pn])
                        if n == 1:
                            P.tt("dve", macc[:, :], macc[:, :], p_[:, :], ALU.add, ["macc", pn], ["macc"])
                        else:
                            P.tt("dve", mT[:, dc, bs], macc[:, :], p_[:, :], ALU.add, ["macc", pn], [("mT", dc)])
        for tt_ in range(NTT):
            xi, xn = xin[tt_ % 2], "xin%d" % (tt_ % 2)
            xo_, xon = x1[tt_ % 2], "x1_%d" % (tt_ % 2)
            P.dma(xi[:, :], x_tok[t0 + tt_ * 128:t0 + (tt_ + 1) * 128, :], [], [xn])
            for ch in range(2):
                ps, psn = nextps()
                for c in range(8):
                    P.mm(ps[:, :], mT[:, c, tt_ * 128:(tt_ + 1) * 128], wout[:, c, ch * 512:(ch + 1) * 512], [("mT", c), ("wout", c)], [psn],
                         start=(c == 0), stop=(c == 7))
                P.tt("dve", xo_[:, ch * 512:(ch + 1) * 512], ps[:, :], xi[:, ch * 512:(ch + 1) * 512], ALU.add, [psn, xn], [(xon, ch)])
            P.dma(x_out[t0 + tt_ * 128:t0 + (tt_ + 1) * 128, :], xo_[:, :], [xon], [("x_out", half * NTT + tt_)], sk=xon)
    return _fin(P, nc, io, ("x_out",))


def build_L5b(TOK=2048, HT=1024, ctx=None):
    nc, P, io = _ctx(ctx)
    NH = TOK // HT
    NTT = HT // 128
    NB5 = HT // 512
    I = _mkI(nc, io)
    x_tok = I("x_tok", [TOK, D])
    g2 = I("g2", [128, 8])
    w_fi, w_fo = I("w_fi", [1024, 2 * DFF]), I("w_fo", [DFF, 1024])
    c_ident = I("c_ident", [128, 128])
    x_out = _mkO(nc, io, "x_out", [TOK, D])
    g2s = P.sb("g2s", [128, 8])
    P.dma(g2s[:, :], g2[:, :], [], ["g2s"])
    identf = P.sb("identf", [128, 128])
    P.dma(identf[:, :], c_ident[:, :], [], ["identf"])
    identb = P.sb("identb", [128, 128], BF16)
    P.cp("dve", identb[:, :], identf[:, :], ["identf"], ["identb"])
    wst = [P.sb("wst%d" % i, [128, 1024]) for i in range(3)]
    wk = [0]

    def load_w(dst, dn, sub, src_ap, ncol):
        i = wk[0] % 3
        wk[0] += 1
        P.dma(wst[i][:, :ncol], src_ap, [], ["wst%d" % i], eng=("sp" if i % 2 == 0 else "pool"))
        P.cp("act" if i % 2 else "dve", dst, wst[i][:, :ncol], ["wst%d" % i], [(dn, sub)])
    wfo = P.sb("wfo", [128, 22, 1024], BF16)
    for c in range(22):
        load_w(wfo[:, c, :], "wfo", c, w_fo[c * 128:(c + 1) * 128, :], 1024)
    x1 = P.sb("x1", [128, NTT, D])
    junk = P.sb("junk", [128, D], BF16)
    ss = P.sb("ss5", [128, NTT])
    hb = [P.sb("hb%d" % i, [128, D], BF16) for i in range(2)]
    h2T = P.sb("h2T", [128, 8, HT], BF16)
    wfi_f = [P.sb("wfif%d" % i, [128, 8, 256]) for i in range(2)]
    wfi_b = [P.sb("wfib%d" % i, [128, 8, 256], BF16) for i in range(2)]
    actT = P.sb("actT", [128, 22, HT], BF16)
    sil = [P.sb("sil%d" % i, [128, 512]) for i in range(2)]
    xo = [P.sb("xo%d" % i, [128, 512]) for i in range(2)]
    pss = [P.ps("ps%d" % i, [128, 512]) for i in range(6)]
    psb = P.ps("psb", [128, 1024], BF16)
    pk = [0]

    def nextps():
        k = pk[0] % 6
        pk[0] += 1
        return pss[k], "ps%d" % k

    for half in range(NH):
        t0 = half * HT
        for tt_ in range(NTT):
            P.dma(x1[:, tt_, :], x_tok[t0 + tt_ * 128:t0 + (tt_ + 1) * 128, :], [], [("x1", tt_)], eng=("sp" if tt_ % 2 else "pool"))
            P.act(junk[:, :], x1[:, tt_, :], AF.Square, [("x1", tt_)], ["junk", ("ss5", tt_)], accum_out=ss[:, tt_:tt_ + 1])
        P.ts("dve", ss[:, :], ss[:, :], 1.0 / D, EPS, ALU.mult, ALU.add, ["ss5"], ["ss5"])
        P.act(ss[:, :], ss[:, :], AF.Sqrt, ["ss5"], ["ss5"])
        P.op("dve", lambda e: e.reciprocal(ss[:, :], ss[:, :]), ["ss5"], ["ss5"])
        for tt_ in range(NTT):
            h_, hn = hb[tt_ % 2], "hb%d" % (tt_ % 2)
            P.act(h_[:, :], x1[:, tt_, :], AF.Copy, [("x1", tt_), "ss5"], [hn], scale=ss[:, tt_:tt_ + 1])
            for c in range(8):
                P.tr(psb[:, c * 128:(c + 1) * 128], h_[:, c * 128:(c + 1) * 128], identb[:, :], [hn, "identb"], ["psb"])
            for c in range(8):
                P.ts("dve", h2T[:, c, tt_ * 128:(tt_ + 1) * 128], psb[:, c * 128:(c + 1) * 128], g2s[:, c:c + 1], None, ALU.mult, None,
                     ["psb", "g2s"], [("h2T", c)])
        for fc in range(22):
            wf, wfn = wfi_f[fc % 2], "wfif%d" % (fc % 2)
            wb_, wbn = wfi_b[fc % 2], "wfib%d" % (fc % 2)
            wv = w_fi.ap().rearrange("(c p) n -> p c n", p=128)
            P.dma(wf[:, :, 0:128], wv[:, :, fc * 128:(fc + 1) * 128], [], [(wfn, 0)], eng="sp", sk=wfn + "a")
            P.dma(wf[:, :, 128:256], wv[:, :, DFF + fc * 128:DFF + (fc + 1) * 128], [], [(wfn, 1)], eng="pool", sk=wfn + "b")
            P.cp("act", wb_[:, :, 0:128], wf[:, :, 0:128], [wfn], [(wbn, 0)])
            P.cp("dve", wb_[:, :, 128:256], wf[:, :, 128:256], [wfn], [(wbn, 1)])
            for b5 in range(NB5):
                bs = slice(b5 * 512, b5 * 512 + 512)
                pg, pgn = nextps()
                pu, pun = nextps()
                for c in range(8):
                    P.mm(pg[:, :], wb_[:, c, 0:128], h2T[:, c, bs], [wbn, ("h2T", c)], [pgn], start=(c == 0), stop=(c == 7))
                for c in range(8):
                    P.mm(pu[:, :], wb_[:, c, 128:256], h2T[:, c, bs], [wbn, ("h2T", c)], [pun], start=(c == 0), stop=(c == 7))
                s_, sn = sil[(fc * NB5 + b5) % 2], "sil%d" % ((fc * NB5 + b5) % 2)
                P.act(s_[:, :], pg[:, :], AF.Silu, [pgn], [sn])
                P.tt("dve", actT[:, fc, bs], pu[:, :], s_[:, :], ALU.mult, [pun, sn], [("actT", fc)])
        for tt_ in range(NTT):
            for ch in range(2):
                ps, psn = nextps()
                for fc in range(22):
                    P.mm(ps[:, :], actT[:, fc, tt_ * 128:(tt_ + 1) * 128], wfo[:, fc, ch * 512:(ch + 1) * 512], [("actT", fc), ("wfo", fc)], [psn],
                         start=(fc == 0), stop=(fc == 21))
                o_, on = xo[(tt_ * 2 + ch) % 2], "xo%d" % ((tt_ * 2 + ch) % 2)
                P.tt("dve", o_[:, :], ps[:, :], x1[:, tt_, ch * 512:(ch + 1) * 512], ALU.add, [psn, ("x1", tt_)], [on])
                P.dma(x_out[t0 + tt_ * 128:t0 + (tt_ + 1) * 128, ch * 512:(ch + 1) * 512], o_[:, :], [on],
                      [("x_out", (half * NTT + tt_) * 2 + ch)], sk=on)
    return _fin(P, nc, io, ("x_out",))


def _hw_runner(nc, in_maps):
    n = len(in_maps)
    maps = list(in_maps) + [in_maps[-1]] * (NCORES - n)
    res = run_bass_kernel_spmd(nc, maps, core_ids=list(range(NCORES)))
    return res.results[:n]


def kernel_impl(inp, runner=_hw_runner, TOKL=2048, HT=512):
    f = lambda a: np.ascontiguousarray(a, dtype=np.float32)
    x = np.asarray(inp["x"], dtype=np.float32)
    B, L, _ = x.shape
    NTOK = B * L
    ntc = NTOK // TOKL
    xcur = x.reshape(NTOK, D)
    depth = inp["w_in"].shape[0]
    progs = {}

    def prog(name, fn):
        if name not in progs:
            progs[name] = fn()
        return progs[name]
    ident = np.eye(128, dtype=np.float32)
    vfirst = None
    for i in range(depth):
        lam_init = 0.8 - 0.6 * float(np.exp(-0.3 * i))
        nc = prog("L1", lambda: build_L1(TOKL))
        g1 = f(np.asarray(inp["norm1_g"][i]).reshape(8, 128).T)
        w_in = f(inp["w_in"][i])
        maps = []
        for c in range(ntc):
            xs = xcur[c * TOKL:(c + 1) * TOKL]
            maps.append(dict(x_tok=f(xs), xT=f(xs.T), w=w_in, g=g1))
        res = runner(nc, maps)
        proj = np.concatenate([r["proj"] for r in res], 0).reshape(B, L, NIN)
        u, q, k, v = proj[..., 0:512], proj[..., 512:1024], proj[..., 1024:1536], proj[..., 1536:2048]
        z, gl = proj[..., 2048:3744], proj[..., 3744:]
        nc = prog("L2", lambda: build_L2(L))
        p2 = dict(lam_re=np.asarray(inp["s5_lambda_re"][i]), lam_im=np.asarray(inp["s5_lambda_im"][i]), log_dt=np.asarray(inp["s5_log_dt"][i]),
                  b_re=np.asarray(inp["s5_b_re"][i]), b_im=np.asarray(inp["s5_b_im"][i]), c_re=np.asarray(inp["s5_c_re"][i]),
                  c_im=np.asarray(inp["s5_c_im"][i]), d=np.asarray(inp["s5_d"][i]))
        res = runner(nc, [l2_inputs(u[c // 2], p2, c % 2, L) for c in range(2 * B)])
        ypre = np.stack([np.concatenate([res[2 * b]["yT"], res[2 * b + 1]["yT"]], 0) for b in range(B)])
        nc = prog("L3", lambda: build_L3(L))
        p3 = dict(q_gain=np.asarray(inp["da_q_gain"][i]), k_gain=np.asarray(inp["da_k_gain"][i]), lam=np.asarray(inp["da_lambda"][i]),
                  subln_g=np.asarray(inp["da_subln_g"][i]))
        res = runner(nc, [l3_inputs(q[c // 2], k[c // 2], v[c // 2], p3, c % 2, L, lam_init) for c in range(2 * B)])
        oat = np.stack([np.concatenate([res[2 * b]["oT"].reshape(256, L), res[2 * b + 1]["oT"].reshape(256, L)], 0) for b in range(B)])
        vg = i > 0
        nc = prog("L4v" if vg else "L4", lambda: build_L4(L, has_vgate=vg))
        p4 = dict(mu=np.asarray(inp["rw_mu"][i]), w0=np.asarray(inp["rw_w0"][i]), w2=np.asarray(inp["rw_w2"][i]), a0=np.asarray(inp["rw_a0"][i]),
                  a2=np.asarray(inp["rw_a2"][i]), g2=np.asarray(inp["rw_g2"][i]), k_k=np.asarray(inp["rw_k_k"][i]), k_a=np.asarray(inp["rw_k_a"][i]),
                  r_k=np.asarray(inp["rw_r_k"][i]), ln_g=np.asarray(inp["rw_ln_g"][i]), ln_b=np.asarray(inp["rw_ln_b"][i]))
        if vg:
            p4.update(v0=np.asarray(inp["rw_v0"][i - 1]), v1=np.asarray(inp["rw_v1"][i - 1]), v2=np.asarray(inp["rw_v2"][i - 1]))
        maps = []
        for c in range(2 * B):
            b, hg = c // 2, c % 2
            maps.append(l4_inputs(z[b], p4, hg, L, vfirst=(vfirst[b][256 * hg:256 * hg + 256] if vg else None)))
        res = runner(nc, maps)
        yc = np.stack([np.concatenate([res[2 * b]["y_out"], res[2 * b + 1]["y_out"]], 1) for b in range(B)])
        if not vg:
            vfirst = np.stack([np.concatenate([res[2 * b]["vfirst_out"], res[2 * b + 1]["vfirst_out"]], 0) for b in range(B)])
        nc = prog("L5a", lambda: build_L5a(TOKL, HT))
        wa = dict(w_glu=f(inp["s5_w_glu"][i]), w_br=f(inp["w_branch"][i]), w_out=f(inp["w_out"][i]))
        maps = []
        for c in range(ntc):
            t0 = c * TOKL
            b, s0 = t0 // L, t0 % L
            sl = slice(s0, s0 + TOKL)
            maps.append(dict(wa, x_tok=f(xcur[t0:t0 + TOKL]), ypreT=f(ypre[b][:, sl]), oT=f(oat[b][:, sl]), ycT=f(yc[b][sl].T), glT=f(gl[b][sl].T)))
        res = runner(nc, maps)
        x1 = np.concatenate([r["x_out"] for r in res], 0)
        nc = prog("L5b", lambda: build_L5b(TOKL, HT))
        wb_ = dict(g2=f(np.asarray(inp["norm2_g"][i]).reshape(8, 128).T), w_fi=f(inp["w_ffn_in"][i]), w_fo=f(inp["w_ffn_out"][i]), c_ident=ident)
        res = runner(nc, [dict(wb_, x_tok=f(x1[c * TOKL:(c + 1) * TOKL])) for c in range(ntc)])
        xcur = np.concatenate([r["x_out"] for r in res], 0)
    return xcur.reshape(B, L, D).astype(np.float32)


def build_fused(L=4096, depth=2):
    nc = bass.Bass("TRN2", target_bir_lowering=False)
    sh = Shared(nc)
    E = lambda n, s: nc.dram_tensor(n, list(s), F32, kind="ExternalInput")
    x_in = E("x_in", [L, D])
    x_out = nc.dram_tensor("x_final", [L, D], F32, kind="ExternalOutput")
    FT = nc.dram_tensor("FT", [6304, L], F32)
    VT = nc.dram_tensor("VT", [L, 512], F32)
    YB = nc.dram_tensor("YB", [1536, L], F32)
    X1 = nc.dram_tensor("X1", [L, D], F32)
    XS = nc.dram_tensor("XS", [L, D], F32)
    VF = nc.dram_tensor("VF", [512, L], F32)
    VF2 = nc.dram_tensor("VF2", [512, L], F32)
    cs = {}
    LHr = min(L, 1024)
    for n, shp in (("c_ident", [128, 128]), ("c_cmask", [128, 4, 512]), ("c_b64", [128, 128]), ("c_ones", [128, 128]),
                   ("c_mask4", [128, 512]), ("c_sl", [128, 128]), ("c_reset", [128, LHr]), ("c_bones", [128, 128]), ("c_hsel", [128, 2])):
        cs[n] = E(n, shp)
    TP = min(2048, L)
    for i in range(depth):
        sfx = "_%d" % i
        xin = x_in if i == 0 else XS
        xo = x_out if i == depth - 1 else XS
        io = dict(x_tok=xin, w=E("w_in" + sfx, [D, NIN]), g=E("g1" + sfx, [128, 8]), c_ident=cs["c_ident"], FT=FT, VT=VT)
        build_L1F(L, ctx=(nc, sh, io))
        s5 = {n: E("s5" + n + sfx, [2, 128, 8, 128]) for n in ("bpr", "bpi", "cpr", "cpi")}
        s5prm = E("s5prm" + sfx, [2, 128, 3, 8])
        s5d = E("s5d" + sfx, [2, 128, 2])
        for gh in range(2):
            io = dict(uT=V(FT.ap()[256 * gh:256 * gh + 256, :]), prm=V(s5prm.ap()[gh]), dcol=V(s5d.ap()[gh]),
                      yT=V(YB.ap()[256 * gh:256 * gh + 256, :]))
            for n in s5:
                io[n] = V(s5[n].ap()[gh])
            build_L2(L, ctx=(nc, sh, io))
        gains = E("da_gains" + sfx, [128, 4])
        lamv = E("da_lamv" + sfx, [128, 4, 64])
        laminit = E("da_laminit" + sfx, [128, 1])
        for hp in range(2):
            hv = lambda a, r0: V(a.ap()[r0:r0 + 256, :].rearrange("(h p) l -> h p l", h=2))
            io = dict(qT=hv(FT, 512 + 256 * hp), kT=hv(FT, 1024 + 256 * hp),
                      vtok=V(VT.ap()[:, 256 * hp:256 * hp + 256].rearrange("l (h d) -> h l d", h=2)),
                      gains=gains, lamv=lamv, laminit=laminit, c_cmask=cs["c_cmask"], c_b64=cs["c_b64"], c_ones=cs["c_ones"],
                      oT=hv(YB, 512 + 256 * hp))
            build_L3(L, ctx=(nc, sh, io))
        vg = i > 0
        r4 = dict(mu3=E("rw_mu3" + sfx, [2, 128, 6]), mulo=E("rw_mulo" + sfx, [96, 3]), w2=E("rw_w2p" + sfx, [2, 32, 256]),
                  a2=E("rw_a2p" + sfx, [2, 32, 256]), g2=E("rw_g2p" + sfx, [2, 96, 256]), pc=E("rw_pc" + sfx, [2, 128, 10]),
                  lng=E("rw_lng" + sfx, [2, 128, 256]), lnb=E("rw_lnb" + sfx, [2, 128, 256]))
        if vg:
            r4.update(muvf=E("rw_muvf" + sfx, [128, 4]), v1=E("rw_v1p" + sfx, [128, 4, 32]), v2=E("rw_v2p" + sfx, [2, 32, 256]),
                      v0=E("rw_v0p" + sfx, [2, 128, 2]))
        for hg in range(2):
            R = lambda r0, n: V(FT.ap()[r0:r0 + n, :])
            io = dict(zr=R(1536 + 256 * hg, 256), zk=R(2048 + 256 * hg, 256), zv=R(2560 + 256 * hg, 256),
                      zw=R(3072, 32), za=R(3104, 32), zg=R(3136, 96), mulo=r4["mulo"],
                      y_out=V(YB.ap()[1024 + 256 * hg:1024 + 256 * hg + 256, :]),
                      vfirst_out=V((VF2 if vg else VF).ap()[256 * hg:256 * hg + 256, :]))
            for n in ("mu3", "w2", "a2", "g2", "pc", "lng", "lnb"):
                io[n] = V(r4[n].ap()[hg])
            for n in ("c_mask4", "c_sl", "c_reset", "c_bones", "c_hsel", "c_ident"):
                io[n] = cs[n]
            if vg:
                io.update(zvfull=R(2560, 512), muvf=r4["muvf"], v1=r4["v1"], v2=V(r4["v2"].ap()[hg]), v0=V(r4["v0"].ap()[hg]),
                          vfirst_in=V(VF.ap()[256 * hg:256 * hg + 256, :]))
            build_L4(L, has_vgate=vg, ctx=(nc, sh, io), fm_out=True)
        wa = dict(w_glu=E("w_glu" + sfx, [512, 512]), w_br=E("w_br" + sfx, [3, 512, 1024]), w_out=E("w_out" + sfx, [1024, 1024]))
        for th in range(L // TP):
            tsl = slice(th * TP, th * TP + TP)
            io = dict(wa, x_tok=V(xin.ap()[tsl, :]), ypreT=V(YB.ap()[0:512, tsl]), oT=V(YB.ap()[512:1024, tsl]), ycT=V(YB.ap()[1024:1536, tsl]),
                      glT=V(FT.ap()[3232:6304, tsl]), x_out=V(X1.ap()[tsl, :]))
            build_L5a(TP, 512, ctx=(nc, sh, io))
        wb_ = dict(g2=E("g2n" + sfx, [128, 8]), w_fi=E("w_fi" + sfx, [1024, 2 * DFF]), w_fo=E("w_fo" + sfx, [DFF, 1024]), c_ident=cs["c_ident"])
        for th in range(L // TP):
            tsl = slice(th * TP, th * TP + TP)
            io = dict(wb_, x_tok=V(X1.ap()[tsl, :]), x_out=V(xo.ap()[tsl, :]))
            if i == depth - 1:
                io["_final"] = True
            build_L5b(TP, 512, ctx=(nc, sh, io))
    sh.close()
    return nc


def fused_inputs(inp, L):
    f = lambda a: np.ascontiguousarray(a, dtype=np.float32)
    d = {}
    d.update(at_consts())
    rc = rw_consts(L)
    d.update(rc)
    depth = inp["w_in"].shape[0]
    zL = np.zeros((L, 512), np.float32)
    zz = np.zeros((L, 1696), np.float32)
    for i in range(depth):
        sfx = "_%d" % i
        lam_init = 0.8 - 0.6 * float(np.exp(-0.3 * i))
        d["w_in" + sfx] = f(inp["w_in"][i])
        d["g1" + sfx] = f(np.asarray(inp["norm1_g"][i]).reshape(8, 128).T)
        p2 = dict(lam_re=np.asarray(inp["s5_lambda_re"][i]), lam_im=np.asarray(inp["s5_lambda_im"][i]), log_dt=np.asarray(inp["s5_log_dt"][i]),
                  b_re=np.asarray(inp["s5_b_re"][i]), b_im=np.asarray(inp["s5_b_im"][i]), c_re=np.asarray(inp["s5_c_re"][i]),
                  c_im=np.asarray(inp["s5_c_im"][i]), d=np.asarray(inp["s5_d"][i]))
        l2 = [l2_inputs(zL, p2, gh, L) for gh in range(2)]
        for n in ("bpr", "bpi", "cpr", "cpi"):
            d["s5" + n + sfx] = f(np.stack([l2[gh][n] for gh in range(2)]))
        d["s5prm" + sfx] = f(np.stack([l2[gh]["prm"] for gh in range(2)]))
        d["s5d" + sfx] = f(np.stack([l2[gh]["dcol"] for gh in range(2)]))
        p3 = dict(q_gain=np.asarray(inp["da_q_gain"][i]), k_gain=np.asarray(inp["da_k_gain"][i]), lam=np.asarray(inp["da_lambda"][i]),
                  subln_g=np.asarray(inp["da_subln_g"][i]))
        l3 = l3_inputs(zL, zL, zL, p3, 0, L, lam_init)
        d["da_gains" + sfx], d["da_lamv" + sfx], d["da_laminit" + sfx] = l3["gains"], l3["lamv"], l3["laminit"]
        p4 = dict(mu=np.asarray(inp["rw_mu"][i]), w0=np.asarray(inp["rw_w0"][i]), w2=np.asarray(inp["rw_w2"][i]), a0=np.asarray(inp["rw_a0"][i]),
                  a2=np.asarray(inp["rw_a2"][i]), g2=np.asarray(inp["rw_g2"][i]), k_k=np.asarray(inp["rw_k_k"][i]), k_a=np.asarray(inp["rw_k_a"][i]),
                  r_k=np.asarray(inp["rw_r_k"][i]), ln_g=np.asarray(inp["rw_ln_g"][i]), ln_b=np.asarray(inp["rw_ln_b"][i]))
        vg = i > 0
        if vg:
            p4.update(v0=np.asarray(inp["rw_v0"][i - 1]), v1=np.asarray(inp["rw_v1"][i - 1]), v2=np.asarray(inp["rw_v2"][i - 1]))
        l4 = [l4_inputs(zz, p4, hg, L, vfirst=(np.zeros((256, L), np.float32) if vg else None)) for hg in range(2)]
        for n, m in (("mu3", "rw_mu3"), ("w2", "rw_w2p"), ("a2", "rw_a2p"), ("g2", "rw_g2p"), ("pc", "rw_pc"), ("lng", "rw_lng"), ("lnb", "rw_lnb")):
            d[m + sfx] = f(np.stack([l4[hg][n] for hg in range(2)]))
        d["rw_mulo" + sfx] = l4[0]["mulo"]
        if vg:
            d["rw_muvf" + sfx] = l4[0]["muvf"]
            d["rw_v1p" + sfx] = l4[0]["v1"]
            d["rw_v2p" + sfx] = f(np.stack([l4[hg]["v2"] for hg in range(2)]))
            d["rw_v0p" + sfx] = f(np.stack([l4[hg]["v0"] for hg in range(2)]))
        d["w_glu" + sfx] = f(inp["s5_w_glu"][i])
        d["w_br" + sfx] = f(inp["w_branch"][i])
        d["w_out" + sfx] = f(inp["w_out"][i])
        d["g2n" + sfx] = f(np.asarray(inp["norm2_g"][i]).reshape(8, 128).T)
        d["w_fi" + sfx] = f(inp["w_ffn_in"][i])
        d["w_fo" + sfx] = f(inp["w_ffn_out"][i])
    return d


def kernel_fused(inp, runner=None):
    x = np.asarray(inp["x"], dtype=np.float32)
    B, L, _ = x.shape
    depth = inp["w_in"].shape[0]
    nc = build_fused(L, depth)
    common = fused_inputs(inp, L)
    ncore = NCORES if runner is None else B
    maps = [dict(common, x_in=np.ascontiguousarray(x[c % B])) for c in range(ncore)]
    if runner is None:
        res = run_bass_kernel_spmd(nc, maps, core_ids=list(range(NCORES))).results
    else:
        res = runner(nc, maps)
    return np.stack([res[b]["x_final"] for b in range(B)]).astype(np.float32)


I32 = mybir.dt.int32


def _dyn_dma(P, hreg, tmpregs, ctr, dst, tensor, base, hscale, dims, reads, writes, sk):
    def fn(e):
        r = tmpregs[ctr[0] % len(tmpregs)]
        ctr[0] += 1
        e.reg_mul(r, hreg, hscale)
        e.reg_add(r, r, base)
        return e.dma_start(out=dst, in_=bass.AP(tensor, r, [list(d) for d in dims]))
    return P.op("pool", fn, reads, writes, dma=True, sk=sk)


def phase_load_h(nc, sh, regs, hid):
    P = Prog(nc, sh)
    hreg, _ = regs
    hs = P.stack.enter_context(nc.sbuf_tensor(P.pfx + "hs", [1, 1], I32))
    P.dma(hs[:, :], hid[:, :], [], ["hs"], eng="pool")
    P.op("pool", lambda e: e.reg_load(hreg, hs[:1, :1]), ["hs"], ["hreg"])
    P.emit()


def _chunked_allgather(P, cc, src, dst, chunks, groups):
    for ci, (r0, n) in enumerate(chunks):
        P.op("pool", lambda e, r0=r0, n=n: e.collective_compute("AllGather", ALU.bypass, replica_groups=groups,
                                                                ins=[src.ap()[r0:r0 + n, :]], outs=[dst.ap()[2 * r0:2 * r0 + 2 * n, :]]),
             [], [("gat", ci)], dma=True, cc=cc)


def phase_xchg_G(nc, sh, regs, hid, G_in, G_out, FTm, VTm, TOKH, groups):
    P = Prog(nc, sh)
    hreg, tmpregs = regs
    ctr = [0]
    chunks = [(256 * c, 256) for c in range(12)] + [(3072, 160), (3232, 256), (3488, 256)]
    _chunked_allgather(P, 0, G_in, G_out, chunks, groups)
    HT2 = TOKH // 2
    for t in range(2):
        tc = slice(t * TOKH, (t + 1) * TOKH)
        dstA = FTm.ap()[0:1536, tc].rearrange("(s r) c -> s r c", s=6)
        _dyn_dma(P, hreg, tmpregs, ctr, dstA, G_out, (256 * t) * TOKH, 512 * TOKH, [[1024 * TOKH, 6], [TOKH, 256], [1, TOKH]],
                 ["gat"], [("FTm", 3 * t)], "rlA%d" % t)
        _dyn_dma(P, hreg, tmpregs, ctr, FTm.ap()[1536:1792, tc], G_out, (5632 + 256 * t) * TOKH, -512 * TOKH, [[TOKH, 256], [1, TOKH]],
                 ["gat"], [("FTm", 3 * t + 1)], "rlB%d" % t)
        P.dma(FTm.ap()[1792:1952, tc], G_out.ap()[6144 + 160 * t:6144 + 160 * t + 160, :], ["gat"], [("FTm", 3 * t + 2)], eng="pool", sk="rlC%d" % t)
        for q_ in range(2):
            _dyn_dma(P, hreg, tmpregs, ctr, VTm.ap()[t * TOKH + q_ * HT2:t * TOKH + (q_ + 1) * HT2, :], G_out,
                     (6464 + 512 * q_ + 256 * t) * TOKH, 256, [[512, HT2], [1, 256]], ["gat"], [("VTm", 2 * t + q_)], "rlv%d" % (2 * t + q_))
    P.emit()


def phase_xchg_Y(nc, sh, regs, hid, Y_in, Y_out, YBh, L, TOKH, groups):
    P = Prog(nc, sh)
    hreg, tmpregs = regs
    ctr = [0]
    _chunked_allgather(P, 1, Y_in, Y_out, [(256 * c, 256) for c in range(6)], groups)
    for r in range(2):
        dst = YBh.ap().rearrange("(n q) c -> n q c", n=3)[:, r * 256:(r + 1) * 256, :]
        _dyn_dma(P, hreg, tmpregs, ctr, dst, Y_out, (256 * r) * TOKH, 1536 * TOKH, [[512 * TOKH, 3], [TOKH, 256], [1, TOKH]],
                 ["gat"], [("YBh", r)], "rly%d" % r)
    P.emit()


def build_fused8(L=4096, depth=2, ncores=8):
    nc = bass.Bass("TRN2", target_bir_lowering=False)
    sh = Shared(nc)
    TOKH = L // 2
    groups = [[2 * i, 2 * i + 1] for i in range(ncores // 2)]
    hreg = sh.stack.enter_context(nc.gpsimd.register("hreg"))
    tmpregs = [sh.stack.enter_context(nc.gpsimd.register("tmpr%d" % i)) for i in range(2)]
    regs = (hreg, tmpregs)
    E = lambda n, s: nc.dram_tensor(n, list(s), F32, kind="ExternalInput")
    x_in = E("x_in", [TOKH, D])
    hid = nc.dram_tensor("hid", [1, 1], I32, kind="ExternalInput")
    x_out = nc.dram_tensor("x_final", [TOKH, D], F32, kind="ExternalOutput")
    G_in = nc.dram_tensor("G_in", [3744, TOKH], F32)
    G_out = nc.dram_tensor("G_out", [2 * 3744, TOKH], F32)
    GL = nc.dram_tensor("GL", [3072, TOKH], F32)
    FTm = nc.dram_tensor("FTm", [1952, L], F32)
    VTm = nc.dram_tensor("VTm", [L, 256], F32)
    Y_in = nc.dram_tensor("Y_in", [2 * 768, TOKH], F32)
    Y_out = nc.dram_tensor("Y_out", [4 * 768, TOKH], F32)
    ysl = lambda r0, n: [Y_in.ap()[j * 768 + r0:j * 768 + r0 + n, :] for j in range(2)]
    YBh = nc.dram_tensor("YBh", [1536, TOKH], F32)
    X1 = nc.dram_tensor("X1", [TOKH, D], F32)
    XS = nc.dram_tensor("XS", [TOKH, D], F32)
    VF = nc.dram_tensor("VF", [256, L], F32)
    VF2 = nc.dram_tensor("VF2", [256, L], F32)
    cs = {}
    LHr = min(L, 1024)
    for n, shp in (("c_ident", [128, 128]), ("c_cmask", [128, 4, 512]), ("c_b64", [128, 128]), ("c_ones", [128, 128]),
                   ("c_mask4", [128, 512]), ("c_sl", [128, 128]), ("c_reset", [128, LHr]), ("c_bones", [128, 128]), ("c_hsel", [128, 2])):
        cs[n] = E(n, shp)
    R = lambda r0, n: V(FTm.ap()[r0:r0 + n, :])
    phase_load_h(nc, sh, regs, hid)
    for i in range(depth):
        sfx = "_%d" % i
        xin = x_in if i == 0 else XS
        xo = x_out if i == depth - 1 else XS
        io = dict(x_tok=xin, w=E("w_in" + sfx, [D, NIN]), g=E("g1" + sfx, [128, 8]), c_ident=cs["c_ident"], G=G_in, GL=GL)
        build_L1G(TOKH, ctx=(nc, sh, io))
        phase_xchg_G(nc, sh, regs, hid, G_in, G_out, FTm, VTm, TOKH, groups)
        io = dict(uT=R(0, 256), prm=E("s5prm" + sfx, [128, 3, 8]), dcol=E("s5d" + sfx, [128, 2]), yT=VTok2(ysl(0, 256), TOKH))
        for n in ("bpr", "bpi", "cpr", "cpi"):
            io[n] = E("s5" + n + sfx, [128, 8, 128])
        build_L2(L, ctx=(nc, sh, io))
        hv = lambda ap_: V(ap_.rearrange("(h p) l -> h p l", h=2))
        io = dict(qT=hv(FTm.ap()[256:512, :]), kT=hv(FTm.ap()[512:768, :]), vtok=V(VTm.ap().rearrange("l (h d) -> h l d", h=2)),
                  gains=E("da_gains" + sfx, [128, 4]), lamv=E("da_lamv" + sfx, [128, 4, 64]), laminit=E("da_laminit" + sfx, [128, 1]),
                  c_cmask=cs["c_cmask"], c_b64=cs["c_b64"], c_ones=cs["c_ones"],
                  oT=VTok2([a_.rearrange("(h p) l -> h p l", h=2) for a_ in ysl(256, 256)], TOKH))
        build_L3(L, ctx=(nc, sh, io))
        vg = i > 0
        io = dict(zr=R(768, 256), zk=R(1024, 256), zv=R(1280, 256), zw=R(1792, 32), za=R(1824, 32), zg=R(1856, 96),
                  mu3=E("rw_mu3" + sfx, [128, 6]), mulo=E("rw_mulo" + sfx, [96, 3]), w2=E("rw_w2p" + sfx, [32, 256]),
                  a2=E("rw_a2p" + sfx, [32, 256]), g2=E("rw_g2p" + sfx, [96, 256]), pc=E("rw_pc" + sfx, [128, 10]),
                  lng=E("rw_lng" + sfx, [128, 256]), lnb=E("rw_lnb" + sfx, [128, 256]),
                  y_out=VTok2(ysl(512, 256), TOKH), vfirst_out=(VF2 if vg else VF))
        for n in ("c_mask4", "c_sl", "c_reset", "c_bones", "c_hsel", "c_ident"):
            io[n] = cs[n]
        if vg:
            io.update(zvfull=R(1280, 512), muvf=E("rw_muvf" + sfx, [128, 4]), v1=E("rw_v1p" + sfx, [128, 4, 32]),
                      v2=E("rw_v2p" + sfx, [32, 256]), v0=E("rw_v0p" + sfx, [128, 2]), vfirst_in=VF)
        build_L4(L, has_vgate=vg, ctx=(nc, sh, io), fm_out=True)
        phase_xchg_Y(nc, sh, regs, hid, Y_in, Y_out, YBh, L, TOKH, groups)
        io = dict(w_glu=E("w_glu" + sfx, [512, 512]), w_br=E("w_br" + sfx, [3, 512, 1024]), w_out=E("w_out" + sfx, [1024, 1024]),
                  x_tok=xin, ypreT=V(YBh.ap()[0:512, :]), oT=V(YBh.ap()[512:1024, :]), ycT=V(YBh.ap()[1024:1536, :]), glT=GL, x_out=X1)
        build_L5a(TOKH, 512, ctx=(nc, sh, io))
        io = dict(g2=E("g2n" + sfx, [128, 8]), w_fi=E("w_fi" + sfx, [1024, 2 * DFF]), w_fo=E("w_fo" + sfx, [DFF, 1024]), c_ident=cs["c_ident"],
                  x_tok=X1, x_out=xo)
        if i == depth - 1:
            io["_final"] = True
        build_L5b(TOKH, min(1024, TOKH), ctx=(nc, sh, io))
    sh.close()
    return nc


def fused8_inputs(inp, L, h):
    f = lambda a: np.ascontiguousarray(a, dtype=np.float32)
    d = {}
    d.update(at_consts())
    d.update(rw_consts(L))
    d["hid"] = np.array([[h]], np.int32)
    depth = inp["w_in"].shape[0]
    zL = np.zeros((L, 512), np.float32)
    zz = np.zeros((L, 1696), np.float32)
    for i in range(depth):
        sfx = "_%d" % i
        lam_init = 0.8 - 0.6 * float(np.exp(-0.3 * i))
        d["w_in" + sfx] = f(inp["w_in"][i])
        d["g1" + sfx] = f(np.asarray(inp["norm1_g"][i]).reshape(8, 128).T)
        p2 = dict(lam_re=np.asarray(inp["s5_lambda_re"][i]), lam_im=np.asarray(inp["s5_lambda_im"][i]), log_dt=np.asarray(inp["s5_log_dt"][i]),
                  b_re=np.asarray(inp["s5_b_re"][i]), b_im=np.asarray(inp["s5_b_im"][i]), c_re=np.asarray(inp["s5_c_re"][i]),
                  c_im=np.asarray(inp["s5_c_im"][i]), d=np.asarray(inp["s5_d"][i]))
        l2 = l2_inputs(zL, p2, h, L)
        for n in ("bpr", "bpi", "cpr", "cpi"):
            d["s5" + n + sfx] = l2[n]
        d["s5prm" + sfx], d["s5d" + sfx] = l2["prm"], l2["dcol"]
        p3 = dict(q_gain=np.asarray(inp["da_q_gain"][i]), k_gain=np.asarray(inp["da_k_gain"][i]), lam=np.asarray(inp["da_lambda"][i]),
                  subln_g=np.asarray(inp["da_subln_g"][i]))
        l3 = l3_inputs(zL, zL, zL, p3, 0, L, lam_init)
        d["da_gains" + sfx], d["da_lamv" + sfx], d["da_laminit" + sfx] = l3["gains"], l3["lamv"], l3["laminit"]
        p4 = dict(mu=np.asarray(inp["rw_mu"][i]), w0=np.asarray(inp["rw_w0"][i]), w2=np.asarray(inp["rw_w2"][i]), a0=np.asarray(inp["rw_a0"][i]),
                  a2=np.asarray(inp["rw_a2"][i]), g2=np.asarray(inp["rw_g2"][i]), k_k=np.asarray(inp["rw_k_k"][i]), k_a=np.asarray(inp["rw_k_a"][i]),
                  r_k=np.asarray(inp["rw_r_k"][i]), ln_g=np.asarray(inp["rw_ln_g"][i]), ln_b=np.asarray(inp["rw_ln_b"][i]))
        vg = i > 0
        if vg:
            p4.update(v0=np.asarray(inp["rw_v0"][i - 1]), v1=np.asarray(inp["rw_v1"][i - 1]), v2=np.asarray(inp["rw_v2"][i - 1]))
        l4 = l4_inputs(zz, p4, h, L, vfirst=(np.zeros((256, L), np.float32) if vg else None))
        for n, m in (("mu3", "rw_mu3"), ("mulo", "rw_mulo"), ("w2", "rw_w2p"), ("a2", "rw_a2p"), ("g2", "rw_g2p"), ("pc", "rw_pc"),
                     ("lng", "rw_lng"), ("lnb", "rw_lnb")):
            d[m + sfx] = l4[n]
        if vg:
            perm = np.concatenate([np.arange(256 * h, 256 * h + 256), np.arange(256 * (1 - h), 256 * (1 - h) + 256)])
            muv = np.asarray(inp["rw_mu"][i])[1024:1536][perm]
            d["rw_muvf" + sfx] = f(muv.reshape(4, 128).T)
            d["rw_v1p" + sfx] = f(np.asarray(inp["rw_v1"][i - 1])[perm].reshape(4, 128, 32).transpose(1, 0, 2))
            d["rw_v2p" + sfx] = l4["v2"]
            d["rw_v0p" + sfx] = l4["v0"]
        d["w_glu" + sfx] = f(inp["s5_w_glu"][i])
        d["w_br" + sfx] = f(inp["w_branch"][i])
        d["w_out" + sfx] = f(inp["w_out"][i])
        d["g2n" + sfx] = f(np.asarray(inp["norm2_g"][i]).reshape(8, 128).T)
        d["w_fi" + sfx] = f(inp["w_ffn_in"][i])
        d["w_fo" + sfx] = f(inp["w_ffn_out"][i])
    return d


def kernel_fused8(inp, runner=None):
    x = np.asarray(inp["x"], dtype=np.float32)
    B, L, _ = x.shape
    depth = inp["w_in"].shape[0]
    ncores = 2 * B
    nc = build_fused8(L, depth, ncores)
    TOKH = L // 2
    common = [fused8_inputs(inp, L, h) for h in range(2)]
    maps = [dict(common[c % 2], x_in=np.ascontiguousarray(x[c // 2, (c % 2) * TOKH:(c % 2 + 1) * TOKH])) for c in range(ncores)]
    if runner is None:
        res = run_bass_kernel_spmd(nc, maps, core_ids=list(range(ncores))).results
    else:
        res = runner(nc, maps)
    out = np.stack([np.concatenate([res[2 * b]["x_final"], res[2 * b + 1]["x_final"]], 0) for b in range(B)])
    return out.astype(np.float32)


def kernel(**inputs):
    return kernel_fused8(inputs)
```
